# Optimizing a Trainium2 kernel written in Bass

```python
import math
import jax, jax.numpy as jnp
from jax import lax
import numpy as np

D_MODEL = 1024
BATCH = 16
SEQ = 256
DEPTH = 2
DEC_BATCH = 8
DEC_SEQ = 2048
PAST_LEN = 256

GRID_W = 64
EPS = 1e-6
ROPE_BASE = 10000.0
ATTN_BLOCK = 128
RET_HEADS = 4
RET_DK = 64
RET_DV = 64
RET_CHUNK = 128
FNET_GROUPS = 4
FNET_GC = 64
MLA_HEADS = 4
MLA_Q_LORA = 256
MLA_KV_LORA = 128
MLA_NOPE = 64
MLA_ROPE = 32
MLA_V = 64
HY_CH = 256
HY_ORDER = 2
HY_BANDS = 16
HY_EMB = 1 + 2 * HY_BANDS
HY_FFN = 64
HY_FAST_DECAY = 0.3
HY_SLOW_DECAY = 1.5
HY_TARGET = 1e-2
D_FF = ((8 * D_MODEL + 3 * 256 - 1) // (3 * 256)) * 256

RET_W = RET_HEADS * RET_DV
FNET_W = FNET_GROUPS * FNET_GC
MLA_W = MLA_HEADS * MLA_V
MIX_W = RET_W + FNET_W + MLA_W + HY_CH
IN_SIZES = (RET_HEADS * RET_DK, RET_HEADS * RET_DK, RET_W, RET_W, FNET_W,
            MLA_Q_LORA, MLA_KV_LORA, MLA_ROPE, 3 * HY_CH)
IN_W = sum(IN_SIZES)
F32 = jnp.float32

kernel_name = 'hybrid_retention_fnet_mla_hyena_prefix_step'


def rmsnorm(x, g):
    xf = x.astype(F32)
    y = xf * lax.rsqrt(jnp.mean(xf * xf, axis=-1, keepdims=True) + EPS)
    return (y * g.astype(F32)).astype(x.dtype)


def grid_positions(L):
    rows = L // GRID_W
    row = jnp.repeat(jnp.arange(rows, dtype=jnp.int32), GRID_W)
    col = jnp.tile(jnp.arange(GRID_W, dtype=jnp.int32), rows)
    return row, col


def axial_rope(x):
    L, R = x.shape[1], x.shape[-1]
    half = R // 4
    row, col = grid_positions(L)
    inv = ROPE_BASE ** (-jnp.arange(half, dtype=F32) / half)

    def rot(xa, pos):
        ang = pos.astype(F32)[:, None] * inv[None, :]
        cos = jnp.cos(ang)[None, :, None, :]
        sin = jnp.sin(ang)[None, :, None, :]
        x1, x2 = jnp.split(xa.astype(F32), 2, axis=-1)
        return jnp.concatenate([x1 * cos - x2 * sin, x2 * cos + x1 * sin], axis=-1)

    xr, xc = jnp.split(x, 2, axis=-1)
    return jnp.concatenate([rot(xr, row), rot(xc, col)], axis=-1).astype(x.dtype)


def retention_scan(q, k, v, log_gamma, s0):
    B, L, H, dk = q.shape
    dv = v.shape[-1]
    C = RET_CHUNK
    n = L // C
    qc = q.reshape(B, n, C, H, dk)
    kc = k.reshape(B, n, C, H, dk)
    vc = v.reshape(B, n, C, H, dv)
    i = jnp.arange(C, dtype=F32)
    diff = i[:, None] - i[None, :]
    dmask = jnp.where(diff[None] >= 0,
                      jnp.exp(jnp.maximum(diff, 0.0)[None] * log_gamma[:, None, None]), 0.0)
    scores = jnp.einsum('bnchd,bnmhd->bnhcm', qc, kc) * dmask[None, None]
    inner = jnp.einsum('bnhcm,bnmhe->bnche', scores, vc)
    xi = jnp.exp((i[:, None] + 1.0) * log_gamma[None, :])
    zeta = jnp.exp((C - 1.0 - i)[:, None] * log_gamma[None, :])
    g_chunk = jnp.exp(C * log_gamma)
    kv = jnp.einsum('bnchd,bnche->nbhde', kc * zeta[None, None, :, :, None], vc)

    def step(S, kv_j):
        return g_chunk[None, :, None, None] * S + kv_j, S

    s_final, s_prev = lax.scan(step, s0, kv)
    cross = jnp.einsum('bnchd,nbhde->bnche', qc * xi[None, None, :, :, None], s_prev)
    return (inner + cross).reshape(B, L, H, dv), s_final


def retention(rq, rk, rv, rg, decay, s0, latent):
    B, L, _ = rq.shape
    q = rq.reshape(B, L, RET_HEADS, RET_DK)
    k = rk.reshape(B, L, RET_HEADS, RET_DK) * (RET_DK ** -0.5)
    v = rv.reshape(B, L, RET_HEADS, RET_DV)
    if latent:
        q = axial_rope(q)
        k = axial_rope(k)
    q, k, v = q.astype(F32), k.astype(F32), v.astype(F32)
    log_g = jax.nn.log_sigmoid(decay.astype(F32))
    s0 = s0.astype(F32)
    o_f, s_f = retention_scan(q, k, v, log_g[0], s0[:, 0])
    o_b, s_b = retention_scan(q[:, ::-1], k[:, ::-1], v[:, ::-1], log_g[1], s0[:, 1])
    o = o_f + o_b[:, ::-1]
    mu = jnp.mean(o, axis=-1, keepdims=True)
    var = jnp.mean(jnp.square(o - mu), axis=-1, keepdims=True)
    o = ((o - mu) * lax.rsqrt(var + EPS)).reshape(B, L, RET_W)
    out = jax.nn.silu(rg.astype(F32)) * o
    return out.astype(rq.dtype), jnp.stack([s_f, s_b], axis=1)


def fourier_mix(f):
    B, L, _ = f.shape
    ff = f.astype(F32).reshape(B, L, FNET_GROUPS, FNET_GC)
    y = jnp.fft.fftn(ff, axes=(1, 3), norm='ortho').real
    return y.reshape(B, L, FNET_W).astype(f.dtype)


def mla_keys(ckv, kr, w_ukv):
    B, L, _ = ckv.shape
    kv = (ckv @ w_ukv).reshape(B, L, MLA_HEADS, MLA_NOPE + MLA_V)
    k_nope, v = jnp.split(kv, [MLA_NOPE], axis=-1)
    k = jnp.concatenate([k_nope, jnp.broadcast_to(kr, (B, L, MLA_HEADS, MLA_ROPE)).astype(k_nope.dtype)], axis=-1)
    return k, v


def block_attention(q, k, v):
    B, Lq, H, Dh = q.shape
    nb = Lq // ATTN_BLOCK
    scale = Dh ** -0.5
    qb = jnp.moveaxis(q.reshape(B, nb, ATTN_BLOCK, H, Dh), 1, 0)

    def one(qi):
        s = jnp.einsum('bqhd,bkhd->bhqk', qi, k, preferred_element_type=F32) * scale
        pr = jax.nn.softmax(s, axis=-1)
        return jnp.einsum('bhqk,bkhe->bqhe', pr.astype(v.dtype), v)

    o = lax.map(one, qb)
    return jnp.moveaxis(o, 0, 1).reshape(B, Lq, H, v.shape[-1])


def short_conv3(u, w, b):
    up = jnp.pad(u, ((0, 0), (1, 1), (0, 0)))
    return up[:, :-2] * w[0] + up[:, 1:-1] * w[1] + up[:, 2:] * w[2] + b


def hyena_filters(L, w1, b1, w2, b2, w3):
    pos = jnp.arange(L, dtype=F32)
    t = pos / L
    bands = jnp.arange(1, HY_BANDS + 1, dtype=F32)
    ang = (2.0 * math.pi / L) * pos[:, None] * bands[None, :]
    z = jnp.concatenate([t[:, None], jnp.sin(ang), jnp.cos(ang)], axis=-1)
    h = jnp.sin(z @ w1.astype(F32) + b1.astype(F32))
    h = jnp.sin(h @ w2.astype(F32) + b2.astype(F32))
    h = (h @ w3.astype(F32)).reshape(L, HY_ORDER, 2, HY_CH)
    deltas = jnp.abs(jnp.linspace(math.log(HY_TARGET) / HY_SLOW_DECAY,
                                  math.log(HY_TARGET) / HY_FAST_DECAY, HY_CH, dtype=F32))
    window = jnp.exp(-t[:, None] * deltas[None, :])
    return h * window[:, None, None, :]


def long_conv(u, hf, hb):
    L, C = hf.shape
    g = jnp.concatenate([hf, jnp.zeros((1, C), F32), hb[1:][::-1]], axis=0)
    g = g / (jnp.sum(jnp.abs(g), axis=0, keepdims=True) + EPS)
    U = jnp.fft.rfft(u, n=2 * L, axis=1)
    G = jnp.fft.rfft(g, n=2 * L, axis=0)
    return jnp.fft.irfft(U * G[None], n=2 * L, axis=1)[:, :L]


def hyena(u, p):
    B, L, _ = u.shape
    uc = short_conv3(u, p['hy_short_w'], p['hy_short_b']).astype(F32)
    v, x1, x2 = jnp.split(uc, 3, axis=-1)
    h = hyena_filters(L, p['hy_w1'], p['hy_b1'], p['hy_w2'], p['hy_b2'], p['hy_w3'])
    d_skip = p['hy_bias'].astype(F32)
    z = v
    for o, gate in enumerate((x1, x2)):
        z = gate * (long_conv(z, h[:, o, 0], h[:, o, 1]) + d_skip[o] * z)
    return z.astype(u.dtype)


def mixer(h, p, latent, ctx_ckv, ctx_krope, s0):
    B, L, _ = h.shape
    idx = np.cumsum(IN_SIZES)[:-1].tolist()
    rq, rk, rv, rg, fu, cq, ckv, kr, hu = jnp.split(h @ p['w_in'], idx, axis=-1)
    ret_out, s_ret = retention(rq, rk, rv, rg, p['ret_decay'], s0, latent)
    four_out = fourier_mix(fu)
    cq = rmsnorm(cq, p['mla_q_norm'])
    ckv = rmsnorm(ckv, p['mla_kv_norm'])
    q = (cq @ p['mla_w_uq']).reshape(B, L, MLA_HEADS, MLA_NOPE + MLA_ROPE)
    q_nope, q_rope = jnp.split(q, [MLA_NOPE], axis=-1)
    kr_h = kr[:, :, None, :]
    kr_rot = axial_rope(kr_h) if latent else kr_h
    if latent:
        q_rope = axial_rope(q_rope)
    q = jnp.concatenate([q_nope, q_rope], axis=-1)
    k, v = mla_keys(ckv, kr_rot, p['mla_w_ukv'])
    if latent:
        k_c, v_c = mla_keys(ctx_ckv, ctx_krope[:, :, None, :], p['mla_w_ukv'])
        k = jnp.concatenate([k, k_c], axis=1)
        v = jnp.concatenate([v, v_c], axis=1)
    att = block_attention(q, k, v).reshape(B, L, MLA_W)
    hy = hyena(hu, p)
    out = jnp.concatenate([ret_out, four_out, att, hy], axis=-1) @ p['w_out']
    return out, ckv, kr, s_ret


def layer(x, mod, p, latent, ctx_ckv, ctx_krope, s0):
    sh1, sc1, g1, sh2, sc2, g2 = jnp.split(mod.astype(x.dtype), 6, axis=-1)
    norm = p['norm_g']
    h = rmsnorm(x, norm[0]) * (1.0 + sc1) + sh1
    m, ckv, kr, s_ret = mixer(h, p, latent, ctx_ckv, ctx_krope, s0)
    x = x + g1 * rmsnorm(m, norm[1])
    h = rmsnorm(x, norm[2]) * (1.0 + sc2) + sh2
    f = (jax.nn.silu(h @ p['w_gate']) * (h @ p['w_up'])) @ p['w_down']
    x = x + g2 * rmsnorm(f, norm[3])
    return x, ckv, kr, s_ret


def setup_inputs(seed: int = 0) -> dict:
    key = jax.random.key(seed)
    ks = jax.random.split(key, 28)
    nrm = lambda k, shape, s: jax.random.normal(k, shape, F32) * s
    ret_init = jnp.log(jnp.exp2(5.0 + jnp.arange(RET_HEADS, dtype=F32)) - 1.0)
    return {
        'x_prompt': nrm(ks[0], (BATCH, SEQ, D_MODEL), 1.0),
        'x_sample': nrm(ks[1], (DEC_BATCH, DEC_SEQ, D_MODEL), 1.0),
        'cache_ckv': nrm(ks[2], (DEC_BATCH, DEPTH, PAST_LEN, MLA_KV_LORA), 1.0),
        'cache_krope': nrm(ks[3], (DEC_BATCH, DEPTH, PAST_LEN, MLA_ROPE), 1.0),
        'state_ret': nrm(ks[4], (DEC_BATCH, DEPTH, 2, RET_HEADS, RET_DK, RET_DV), 0.5),
        'c': nrm(ks[5], (DEC_BATCH, D_MODEL), 1.0),
        'c_ctx': nrm(ks[6], (D_MODEL,), 1.0),
        'w_ada': nrm(ks[7], (DEPTH, D_MODEL, 6 * D_MODEL), 0.5 * D_MODEL ** -0.5),
        'b_ada': nrm(ks[8], (DEPTH, 6 * D_MODEL), 0.01),
        'norm_g': 1.0 + nrm(ks[9], (DEPTH, 4, D_MODEL), 0.01),
        'w_in': nrm(ks[10], (DEPTH, D_MODEL, IN_W), D_MODEL ** -0.5),
        'w_out': nrm(ks[11], (DEPTH, MIX_W, D_MODEL), MIX_W ** -0.5),
        'ret_decay': ret_init + nrm(ks[12], (DEPTH, 2, RET_HEADS), 0.1),
        'mla_q_norm': 1.0 + nrm(ks[13], (DEPTH, MLA_Q_LORA), 0.01),
        'mla_kv_norm': 1.0 + nrm(ks[14], (DEPTH, MLA_KV_LORA), 0.01),
        'mla_w_uq': nrm(ks[15], (DEPTH, MLA_Q_LORA, MLA_HEADS * (MLA_NOPE + MLA_ROPE)), MLA_Q_LORA ** -0.5),
        'mla_w_ukv': nrm(ks[16], (DEPTH, MLA_KV_LORA, MLA_HEADS * (MLA_NOPE + MLA_V)), MLA_KV_LORA ** -0.5),
        'hy_short_w': nrm(ks[17], (DEPTH, 3, 3 * HY_CH), 3 ** -0.5),
        'hy_short_b': nrm(ks[18], (DEPTH, 3 * HY_CH), 0.01),
        'hy_w1': nrm(ks[19], (DEPTH, HY_EMB, HY_FFN), HY_EMB ** -0.5),
        'hy_b1': nrm(ks[20], (DEPTH, HY_FFN), 0.1),
        'hy_w2': nrm(ks[21], (DEPTH, HY_FFN, HY_FFN), HY_FFN ** -0.5),
        'hy_b2': nrm(ks[22], (DEPTH, HY_FFN), 0.1),
        'hy_w3': nrm(ks[23], (DEPTH, HY_FFN, HY_ORDER * 2 * HY_CH), HY_FFN ** -0.5),
        'hy_bias': nrm(ks[24], (DEPTH, HY_ORDER, HY_CH), 0.1),
        'w_gate': nrm(ks[25], (DEPTH, D_MODEL, D_FF), D_MODEL ** -0.5),
        'w_up': nrm(ks[26], (DEPTH, D_MODEL, D_FF), D_MODEL ** -0.5),
        'w_down': nrm(ks[27], (DEPTH, D_FF, D_MODEL), D_FF ** -0.5),
    }


def reference(x_prompt, x_sample, cache_ckv, cache_krope, state_ret, c, c_ctx,
              w_ada, b_ada, norm_g, w_in, w_out, ret_decay, mla_q_norm, mla_kv_norm,
              mla_w_uq, mla_w_ukv, hy_short_w, hy_short_b, hy_w1, hy_b1, hy_w2, hy_b2,
              hy_w3, hy_bias, w_gate, w_up, w_down):
    xp = x_prompt
    xs = x_sample
    s_zero = jnp.zeros((xp.shape[0], 2, RET_HEADS, RET_DK, RET_DV), F32)
    silu_ctx = jax.nn.silu(c_ctx.astype(F32))
    silu_c = jax.nn.silu(c.astype(F32))
    new_ckv, new_kr, new_s = [], [], []
    for l in range(DEPTH):
        p = {
            'norm_g': norm_g[l], 'w_in': w_in[l], 'w_out': w_out[l], 'ret_decay': ret_decay[l],
            'mla_q_norm': mla_q_norm[l], 'mla_kv_norm': mla_kv_norm[l],
            'mla_w_uq': mla_w_uq[l], 'mla_w_ukv': mla_w_ukv[l],
            'hy_short_w': hy_short_w[l], 'hy_short_b': hy_short_b[l],
            'hy_w1': hy_w1[l], 'hy_b1': hy_b1[l], 'hy_w2': hy_w2[l], 'hy_b2': hy_b2[l],
            'hy_w3': hy_w3[l], 'hy_bias': hy_bias[l],
            'w_gate': w_gate[l], 'w_up': w_up[l], 'w_down': w_down[l],
        }
        wa = w_ada[l].astype(F32)
        ba = b_ada[l].astype(F32)
        mod_ctx = (silu_ctx @ wa + ba)[None, None, :]
        mod_lat = (silu_c @ wa + ba)[:, None, :]
        xp, ckv, kr, s_ret = layer(xp, mod_ctx, p, False, None, None, s_zero)
        new_ckv.append(ckv)
        new_kr.append(kr)
        new_s.append(s_ret.astype(xp.dtype))
        xs, _, _, _ = layer(xs, mod_lat, p, True, cache_ckv[:, l], cache_krope[:, l], state_ret[:, l])
    return (xp, xs, jnp.stack(new_ckv, axis=1), jnp.stack(new_kr, axis=1), jnp.stack(new_s, axis=1))
```

```python
import contextlib
import math
import numpy as np
import ml_dtypes
import concourse.bass as bass
import concourse.mybir as mybir
from concourse.bass_utils import run_bass_kernel_spmd

F32 = mybir.dt.float32
BF16 = mybir.dt.bfloat16
ALU = mybir.AluOpType
AF = mybir.ActivationFunctionType
AX = mybir.AxisListType
NPBF = ml_dtypes.bfloat16

NDMA_SLOTS = 8
D = 1024
DFF = 2816
NFF = 22
NT = 20
EPS = 1e-6
SEQS = [(0, 16, True), (16, 2, False), (18, 2, False)]


class Buf:
    def __init__(self, name, col0, ncols):
        self.name, self.col0, self.ncols = name, col0, ncols
        self.subs = set()
        self.inherit = set()

    def k(self, sub=None):
        self.subs.add(sub)
        return (self, sub)


class Prog:
    def __init__(self, nc):
        self.nc = nc
        self.ops = []
        self.last_writer = {}
        self.readers = {}
        self.stack = contextlib.ExitStack()
        self.live = []
        self.dead = []
        self.psn = 0

    def make_arena(self, ncols):
        self.arena = self.stack.enter_context(self.nc.sbuf_tensor("arena", [128, ncols], F32))
        self.arena_cols = ncols
        self.ps = [self.stack.enter_context(self.nc.psum_tensor(f"ps{i}", [128, 512], F32)) for i in range(8)]

    def bank(self):
        i = self.psn % 8
        self.psn += 1
        return i

    def alloc(self, name, ncols):
        ncols = int(math.ceil(ncols))
        segs = sorted((b.col0, b.ncols) for b in self.live)
        pos, found = 0, None
        for c0, n in segs:
            if c0 - pos >= ncols:
                found = pos
                break
            pos = max(pos, c0 + n)
        if found is None:
            if self.arena_cols - pos >= ncols:
                found = pos
            else:
                raise RuntimeError(f"arena OOM {name} {ncols}: live={[(b.name, b.ncols) for b in self.live]}")
        b = Buf(name, found, ncols)
        self.live.append(b)
        for ob in self.dead:
            if ob.col0 < found + ncols and found < ob.col0 + ob.ncols:
                for sk in ob.subs:
                    kk = (ob, sk)
                    w = self.last_writer.get(kk)
                    if w is not None:
                        b.inherit.add(w)
                    b.inherit.update(self.readers.get(kk, ()))
                b.inherit.update(ob.inherit)
        return b

    def free(self, *bs):
        for b in bs:
            self.live.remove(b)
            self.dead.append(b)

    def f32(self, b, p0=0, p1=128):
        return self.arena[p0:p1, b.col0:b.col0 + b.ncols]

    def bf(self, b, p0=0, p1=128):
        return self.arena[p0:p1, b.col0:b.col0 + b.ncols].bitcast(BF16)

    def op(self, eng, fn, R=(), W=(), dma=False):
        idx = len(self.ops)
        deps = set()
        for k in list(R) + list(W):
            if isinstance(k, tuple) and isinstance(k[0], Buf) and k[0].inherit:
                deps.update(k[0].inherit)
        for k in R:
            w = self.last_writer.get(k)
            if w is not None:
                deps.add(w)
        for k in W:
            w = self.last_writer.get(k)
            if w is not None:
                deps.add(w)
            deps.update(self.readers.get(k, ()))
        for k in R:
            self.readers.setdefault(k, []).append(idx)
        for k in W:
            self.last_writer[k] = idx
            self.readers[k] = []
        self.ops.append(dict(eng=eng, fn=fn, deps=deps, dma=dma, signal=False))
        return idx

    def emit(self):
        nc, ops = self.nc, self.ops
        engs = ["pe", "act", "dve", "pool", "sp"]
        per = {e: [] for e in engs}
        for i, o in enumerate(ops):
            per[o["eng"]].append(i)
        for e in engs:
            seen = {pe_: -1 for pe_ in engs}
            for i in per[e]:
                o = ops[i]
                need = {}
                o["wdeps"] = []
                for d in o["deps"]:
                    od = ops[d]
                    if od["dma"]:
                        o["wdeps"].append(d)
                    else:
                        need[od["eng"]] = max(need.get(od["eng"], -1), d)
                for pe_, d in need.items():
                    if d > seen[pe_]:
                        seen[pe_] = d
                        o["wdeps"].append(d)
                        ops[d]["signal"] = True
        sems = {e: self.stack.enter_context(nc.semaphore("s_" + e)) for e in engs}
        dsems = {e: [self.stack.enter_context(nc.semaphore(f"d_{e}{i}")) for i in range(NDMA_SLOTS)]
                 for e in ("sp", "pool", "act")}
        cnt = {e: 0 for e in engs}
        dcnt = {e: 0 for e in dsems}
        for o in ops:
            e = o["eng"]
            if o["dma"]:
                j = dcnt[e]
                dcnt[e] += 1
                o["sem"] = dsems[e][j % NDMA_SLOTS]
                o["val"] = 16 * (j // NDMA_SLOTS + 1)
                o["prev"] = 16 * (j // NDMA_SLOTS)
            elif o["signal"]:
                cnt[e] += 1
                o["sem"] = sems[e]
                o["val"] = cnt[e]
        nw = [0]

        def run_engine(ename, eobj):
            waited = {}
            for i in per[ename]:
                o = ops[i]
                wl = {}
                for d in o["wdeps"]:
                    od = ops[d]
                    s = od["sem"]
                    if od["val"] > wl.get(s.name, (s, 0))[1]:
                        wl[s.name] = (s, od["val"])
                if o["dma"] and o["prev"] > 0:
                    s = o["sem"]
                    if o["prev"] > wl.get(s.name, (s, 0))[1]:
                        wl[s.name] = (s, o["prev"])
                for nm, (s, v) in wl.items():
                    if waited.get(nm, 0) >= v:
                        continue
                    eobj.wait_ge(s, v)
                    nw[0] += 1
                    waited[nm] = v
                ins = o["fn"](eobj)
                if o["dma"]:
                    ins.then_inc(o["sem"], 16)
                elif o["signal"]:
                    ins.then_inc(o["sem"], 1)
            if ename in dsems:
                n = dcnt[ename]
                for slot in range(NDMA_SLOTS):
                    k = (n - slot + NDMA_SLOTS - 1) // NDMA_SLOTS
                    if k > 0 and waited.get(dsems[ename][slot].name, 0) < 16 * k:
                        eobj.wait_ge(dsems[ename][slot], 16 * k)

        with nc.Block() as block:
            @block.tensor
            def _(e):
                run_engine("pe", e)

            @block.scalar
            def _(e):
                run_engine("act", e)

            @block.vector
            def _(e):
                run_engine("dve", e)

            @block.gpsimd
            def _(e):
                run_engine("pool", e)

            @block.sync
            def _(e):
                run_engine("sp", e)
        self.stats = dict(nops=len(ops), nwaits=nw[0], cnt=cnt, dcnt=dcnt)


def MM(specs):
    def f(e):
        ins = None
        for (o, l, r, st, sp) in specs:
            ins = e.matmul(o, lhsT=l, rhs=r, start=st, stop=sp)
        return ins
    return f


def ACT(out, in_, func, bias=None, scale=1.0, accum=None):
    def f(e):
        kw = {}
        if bias is not None:
            kw["bias"] = bias
        if accum is not None:
            kw["accum_out"] = accum
        return e.activation(out=out, in_=in_, func=func, scale=scale, **kw)
    return f


def TT(out, a, b, op):
    return lambda e: e.tensor_tensor(out=out, in0=a, in1=b, op=op)


def TS(out, a, s1, op0, s2=None, op1=None):
    if op1 is None:
        return lambda e: e.tensor_scalar(out=out, in0=a, scalar1=s1, scalar2=None, op0=op0)
    return lambda e: e.tensor_scalar(out=out, in0=a, scalar1=s1, scalar2=s2, op0=op0, op1=op1)


def STT(out, a, s, b, op0, op1):
    return lambda e: e.scalar_tensor_tensor(out=out, in0=a, scalar=s, in1=b, op0=op0, op1=op1)


def CP(out, in_):
    return lambda e: e.tensor_copy(out=out, in_=in_)


def RED(out, in_, op=None):
    return lambda e: e.tensor_reduce(out=out, in_=in_, axis=AX.X, op=op or ALU.add)


def RECIP(out, in_):
    return lambda e: e.reciprocal(out=out, in_=in_)


def MSET(out, v):
    return lambda e: e.memset(out, v)


def DMA(out, in_):
    return lambda e: e.dma_start(out=out, in_=in_)


def DMAS(out, in_):
    return lambda e: e.dma_start(out=out, in_=in_, allow_slow_non_contiguous=True)


def bc(ap, dims):
    return bass.AP(ap.tensor, ap.offset, [list(ap.ap[0])] + [list(d) for d in dims])


def parity_perm(L):
    nj = L // 256
    idx = np.zeros((2, nj, 128), np.int64)
    for pi in range(2):
        for j in range(nj):
            idx[pi, j] = pi + 2 * (128 * j + np.arange(128))
    return idx


def host_consts():
    C = {}
    C["ident"] = np.eye(128, dtype=np.float32).astype(NPBF)
    tok = np.arange(2048)
    row, col = tok // 64, tok % 64
    for nm, half in (("ret", 16), ("mla", 8)):
        inv = 10000.0 ** (-np.arange(half, dtype=np.float64) / half)
        ang = np.stack([row[:, None] * inv[None], col[:, None] * inv[None]], axis=1)
        ang = ang.reshape(16, 128, 2, half).transpose(1, 0, 2, 3).reshape(128, 16 * 2 * half)
        C["cos_" + nm] = np.cos(ang).astype(np.float32)
        C["sin_" + nm] = np.sin(ang).astype(np.float32)
    m = np.arange(128)[:, None].astype(np.float64)
    c = np.arange(128)[None, :].astype(np.float64)
    C["ret_dpos"] = np.tile(np.maximum(c - m, 0), (1, 4)).astype(np.float32)
    C["ret_dneg"] = np.tile(np.maximum(m - c, 0), (1, 4)).astype(np.float32)
    C["ret_mge"] = np.tile((c >= m) * 0.125, (1, 4)).astype(np.float32)
    C["ret_mle"] = np.tile((c <= m) * 0.125, (1, 4)).astype(np.float32)
    p = np.arange(128, dtype=np.float64)
    C["ret_cols"] = np.stack([p + 1, 127 - p, 128 - p, p], axis=1).astype(np.float32)
    a = 2 * np.pi * np.outer(np.arange(64), np.arange(64)) / 64
    C64 = np.kron(np.eye(2), np.cos(a))
    S64 = np.kron(np.eye(2), np.sin(a))
    C["f_c64"] = C64.astype(NPBF)
    C["f_s64n"] = (-S64).astype(NPBF)
    for L in (2048, 256):
        idx = parity_perm(L)
        nj = L // 256
        l = idx.astype(np.float64)
        pp = np.arange(L // 2, dtype=np.float64)
        ang = 2 * np.pi * l[..., None] * pp / L
        sc = 1.0 / math.sqrt(L * 64)
        C[f"f_cl{L}"] = (np.cos(ang) * sc).transpose(2, 0, 1, 3).reshape(128, -1).astype(NPBF)
        C[f"f_sl{L}"] = (np.sin(ang) * sc).transpose(2, 0, 1, 3).reshape(128, -1).astype(NPBF)
        pos = idx.reshape(-1).astype(np.float64)
        t = pos / L
        bands = np.arange(1, 17, dtype=np.float64)
        ang2 = (2 * np.pi / L) * pos[:, None] * bands[None]
        z = np.concatenate([t[:, None], np.sin(ang2), np.cos(ang2)], axis=1)
        C[f"h_zT{L}"] = np.ascontiguousarray(z.T).astype(np.float32)
        deltas = np.abs(np.linspace(math.log(1e-2) / 1.5, math.log(1e-2) / 0.3, 256))
        win = np.exp(-t[:, None] * deltas[None])
        C[f"h_win{L}"] = win.reshape(2 * nj, 128, 256).transpose(1, 0, 2).reshape(128, -1).astype(np.float32)
        N = 2 * L
        nf = L // 256 if L >= 256 else 1
        F = L // 2
        f = np.arange(F, dtype=np.float64) + 0.5
        s = idx.astype(np.float64)
        psi = 2 * np.pi * s[..., None] * f / N
        fch = F // 128
        cf = np.cos(psi).reshape(2, nj, 128, fch, 128).transpose(3, 2, 0, 1, 4).reshape(fch, 128, -1)
        sf = (-np.sin(psi)).reshape(2, nj, 128, fch, 128).transpose(3, 2, 0, 1, 4).reshape(fch, 128, -1)
        C[f"h_cf{L}"] = cf.astype(NPBF)
        C[f"h_sf{L}"] = sf.astype(NPBF)
        tt = np.stack([2 * np.arange(L // 2), 2 * np.arange(L // 2) + 1]).astype(np.float64)
        psi2 = 2 * np.pi * f[:, None, None] * tt[None] / N
        ci = (2.0 / N) * np.cos(psi2)
        si = -(2.0 / N) * np.sin(psi2)
        C[f"h_ci{L}"] = ci.reshape(fch, 128, 2, L // 2).transpose(1, 0, 2, 3).reshape(128, -1).astype(NPBF)
        C[f"h_si{L}"] = si.reshape(fch, 128, 2, L // 2).transpose(1, 0, 2, 3).reshape(128, -1).astype(NPBF)
    return C


_CONSTS = None


def get_consts():
    global _CONSTS
    if _CONSTS is None:
        _CONSTS = host_consts()
    return _CONSTS


WEIGHT_SPECS = [
    ("w_ada", (2, 1024, 6144)), ("b_ada", (2, 6144)), ("norm_g", (2, 4, 1024)), ("w_in", (2, 1024, 2464)),
    ("w_out", (2, 1024, 1024)), ("ret_decay", (2, 2, 4)), ("mla_q_norm", (2, 256)), ("mla_kv_norm", (2, 128)),
    ("mla_w_uq", (2, 256, 384)), ("mla_w_ukv", (2, 128, 512)), ("hy_short_w", (2, 3, 768)),
    ("hy_short_b", (2, 768)), ("hy_w1", (2, 33, 64)), ("hy_b1", (2, 64)), ("hy_w2", (2, 64, 64)),
    ("hy_b2", (2, 64)), ("hy_w3", (2, 64, 1024)), ("hy_bias", (2, 2, 256)), ("w_gate", (2, 1024, 2816)),
    ("w_up", (2, 1024, 2816)), ("w_down", (2, 2816, 1024)),
]
CORE_SPECS = [("xin", (2560, 1024)), ("ckv_ctx", (2, 256, 128)), ("kr_ctx", (2, 256, 32)),
              ("s0", (2, 2, 4, 64, 64)), ("cvec", (2, 1024))]
OUT_SPECS = [("y", (2560, 1024)), ("o_ckv", (2, 2, 256, 128)), ("o_kr", (2, 2, 256, 32)),
             ("o_st", (2, 2, 2, 4, 64, 64))]


def build(dbg=(), stop_after=None):
    nc = bass.Bass("TRN2", target_bir_lowering=False)
    P = Prog(nc)
    C = get_consts()
    T = {}
    for nm, shp in WEIGHT_SPECS + CORE_SPECS:
        T[nm] = nc.dram_tensor(nm, list(shp), F32, kind="ExternalInput").ap()
    for nm, arr in C.items():
        T[nm] = nc.dram_tensor(nm, list(arr.shape), BF16 if arr.dtype == NPBF else F32, kind="ExternalInput").ap()
    for nm, shp in OUT_SPECS:
        T[nm] = nc.dram_tensor(nm, list(shp), F32, kind="ExternalOutput").ap()
    T["xa"] = nc.dram_tensor("xa", [2560, 1024], F32, kind="Internal").ap()
    T["xb"] = nc.dram_tensor("xb", [2560, 1024], F32, kind="Internal").ap()
    DBG = {}

    P.make_arena(53184)
    PS = P.ps
    op = P.op

    def psk(i):
        return ("ps", i)

    def dump(name, ap, keys, shape=None):
        if name not in dbg:
            return
        shape = list(shape or ap.shape)
        d = nc.dram_tensor("dbg_" + name, shape, F32, kind="ExternalOutput").ap()
        DBG[name] = d
        if len(shape) == 3:
            for i_ in range(shape[1]):
                op("pool", DMA(d[:, i_, :], ap[:, i_, :]), R=keys, dma=True)
        else:
            op("pool", DMA(d, ap), R=keys, dma=True)

    cb = P.alloc("consts", 64 + 1 + 8 + 64 * 3 + 128)
    cw = P.f32(cb)
    o = [0]

    def take(n, dt=F32, src=None):
        src = cw if src is None else src
        v = src[:, o[0]:o[0] + n]
        o[0] += n
        return v.bitcast(BF16) if dt == BF16 else v
    ident = take(64, BF16)
    epsc = take(1)
    cols8 = take(8)
    c64, s64n, onesb = take(64, BF16), take(64, BF16), take(64, BF16)
    onesf = take(128)
    KC = cb.k()
    for dst, nm in ((ident, "ident"), (c64, "f_c64"), (s64n, "f_s64n")):
        op("sp", DMA(dst, T[nm]), W=[KC], dma=True)
    op("pool", MSET(epsc, EPS), W=[KC])
    op("pool", MSET(cols8[:, 0:1], -math.pi), W=[KC])
    op("pool", MSET(onesb, 1.0), W=[KC])
    op("pool", MSET(onesf, 1.0), W=[KC])
    MC = {}

    def mixer_consts():
        mb = P.alloc("mconsts", 4 * 512 + 4 + 2 * 512 + 2 * 256)
        o[0] = 0
        mw = P.f32(mb)
        for nm, n in (("ret_dpos", 512), ("ret_dneg", 512), ("ret_mge", 512), ("ret_mle", 512), ("ret_cols", 4),
                      ("cos_ret", 512), ("sin_ret", 512), ("cos_mla", 256), ("sin_mla", 256)):
            MC[nm] = take(n, src=mw)
            op("sp", DMA(MC[nm], T[nm]), W=[mb.k()], dma=True)
        MC["buf"] = mb
        MC["key"] = mb.k()

    mcolb = P.alloc("modcols", 2 * 4 * 8)
    mcol = P.f32(mcolb).rearrange("p (r q k) -> p r q k", r=2, q=4)
    gbb = P.alloc("gbc", 4 * 1024)
    gbc = P.f32(gbb).rearrange("p (r q n) -> p r q n", r=2, q=2)
    PR = (0, 32)

    def layer_mod(l):
        rb = P.alloc("rows", 6144 + 4096 + 6144)
        rows = P.f32(rb)[0:33, :]
        m = rows[:, 0:6144]
        ngr = rows[:, 6144:10240]
        rowt = rows[:, 10240:16384]
        KR = rb.k()
        scb = P.alloc("silu_c", 8 * 34 // 2)
        sct = P.bf(scb).rearrange("p (k r) -> p k r", r=34)
        cfb = P.alloc("c_f32", 16)
        cf32 = P.f32(cfb).rearrange("p (k r) -> p k r", r=2)
        for r in range(2):
            op("sp", DMAS(cf32[:, :, r:r + 1], bass.AP(T["cvec"].tensor, r * 1024, [[1, 128], [128, 8], [1, 1]])),
               W=[cfb.k()], dma=True)
        op("pool", MSET(sct, 0.0), W=[scb.k()])
        for r in range(2):
            op("act", ACT(sct[:, :, PR[r]:PR[r] + 1], cf32[:, :, r:r + 1], AF.Silu), R=[cfb.k()], W=[scb.k()])
        op("sp", DMA(ngr, bass.AP(T["norm_g"].tensor, l * 4096, [[0, 33], [1, 4096]])), W=[KR], dma=True)
        wab = [P.alloc(f"wada{i}", 8 * 512 // 2) for i in range(2)]
        badb = P.alloc("bada", 2 * 512)
        for nb in range(12):
            wb_ = wab[nb % 2]
            par = nb % 2
            wv = P.bf(wb_).rearrange("p (k n) -> p k n", k=8)
            op("pool", DMA(wv, T["w_ada"][l][:, nb * 512:(nb + 1) * 512].rearrange("(k p) n -> p k n", p=128)),
               W=[wb_.k()], dma=True)
            bv = P.f32(badb)[0:33, par * 512:par * 512 + 512]
            op("sp", DMA(bv, bass.AP(T["b_ada"].tensor, l * 6144 + nb * 512, [[0, 33], [1, 512]])),
               W=[badb.k(par)], dma=True)
            b = P.bank()
            op("pe", MM([(PS[b][0:33, :], sct[:, k, 0:33], wv[:, k, :], k == 0, k == 7) for k in range(8)]),
               R=[scb.k(), wb_.k()], W=[psk(b)])
            op("dve", TT(m[:, nb * 512:(nb + 1) * 512], PS[b][0:33, :], bv, ALU.add),
               R=[psk(b), badb.k(par)], W=[KR])
        for r in range(2):
            dump(f"mod{l}{r}", m[PR[r]:PR[r] + 1, :], [KR])
        op("dve", STT(rowt[:, 0:1024], m[:, 1024:2048], 1.0, ngr[:, 0:1024], ALU.add, ALU.mult), R=[KR], W=[KR])
        op("dve", CP(rowt[:, 1024:2048], m[:, 0:1024]), R=[KR], W=[KR])
        op("dve", STT(rowt[:, 2048:3072], m[:, 4096:5120], 1.0, ngr[:, 2048:3072], ALU.add, ALU.mult), R=[KR], W=[KR])
        op("dve", CP(rowt[:, 3072:4096], m[:, 3072:4096]), R=[KR], W=[KR])
        op("dve", TT(rowt[:, 4096:5120], m[:, 2048:3072], ngr[:, 1024:2048], ALU.mult), R=[KR], W=[KR])
        op("dve", TT(rowt[:, 5120:6144], m[:, 5120:6144], ngr[:, 3072:4096], ALU.mult), R=[KR], W=[KR])
        for r in range(2):
            pr = PR[r]
            b = P.bank()
            op("pe", MM([(PS[b][:, q * 8 + k:q * 8 + k + 1], rowt[pr:pr + 1, q * 1024 + k * 128:q * 1024 + (k + 1) * 128],
                          onesf[pr:pr + 1, 0:1], True, True) for q in range(4) for k in range(8)]),
               R=[KR, KC], W=[psk(b)])
            op("dve", CP(mcol[:, r], PS[b][:, 0:32].rearrange("p (q k) -> p q k", q=4)), R=[psk(b)], W=[mcolb.k()])
            for q in range(2):
                for hf in range(2):
                    b = P.bank()
                    c0 = (4 + q) * 1024 + hf * 512
                    op("pe", MM([(PS[b][:, :], onesf[pr:pr + 1, 0:128], rowt[pr:pr + 1, c0:c0 + 512], True, True)]),
                       R=[KR, KC], W=[psk(b)])
                    op("act", ACT(gbc[:, r, q, hf * 512:(hf + 1) * 512], PS[b][:, :], AF.Copy), R=[psk(b)], W=[gbb.k()])
        P.free(rb, scb, cfb, wab[0], wab[1], badb)

    def rstd_from_ss(ss, out, n, keys_r, keys_w, tmp):
        op("act", ACT(tmp, ss, AF.Sqrt, bias=epsc[0:ss.shape[0], :], scale=1.0 / n), R=keys_r + [KC], W=[keys_w[1]])
        op("dve", RECIP(out, tmp), R=[keys_w[1]], W=[keys_w[0]])

    xtb = [None, None]
    xnb = [P.alloc(f"xn{i}", 512) for i in range(2)]
    stb = P.alloc("stats", 64)
    stv = P.f32(stb)
    tcount = [0]

    def norm_transpose(src, tile, r, q0, dstT, dcol, xkeep=None):
        i = tcount[0] % 2
        tcount[0] += 1
        xt = P.f32(xtb[i]) if xkeep is None else xkeep[0]
        kx = xtb[i].k() if xkeep is None else xkeep[1]
        op("sp", DMA(xt, src[tile * 128:(tile + 1) * 128, :]), R=[("dram", src.tensor.name, tile)], W=[kx], dma=True)
        ss, rs, tm = stv[:, i * 4:i * 4 + 1], stv[:, i * 4 + 1:i * 4 + 2], stv[:, i * 4 + 2:i * 4 + 3]
        op("act", ACT(P.bf(xnb[i]), xt, AF.Square, accum=ss), R=[kx], W=[xnb[i].k(), stb.k(("ss", i))])
        rstd_from_ss(ss, rs, 1024.0, [stb.k(("ss", i))], [stb.k(("rs", i)), stb.k(("tm", i))], tm)
        xn = P.bf(xnb[i])
        op("dve", TS(xn, xt, rs, ALU.mult), R=[kx, stb.k(("rs", i))], W=[xnb[i].k()])
        for half in range(2):
            b = P.bank()
            op("pe", MM([(PS[b][:, j * 128:(j + 1) * 128], xn[:, (half * 4 + j) * 128:(half * 4 + j + 1) * 128], ident, True, True)
                         for j in range(4)]), R=[xnb[i].k(), KC], W=[psk(b)])
            for j in range(4):
                kc = half * 4 + j
                o_ = dstT[:, kc, dcol:dcol + 128]
                src_ps = PS[b][:, j * 128:(j + 1) * 128]
                A, B = mcol[:, r, q0, kc:kc + 1], mcol[:, r, q0 + 1, kc:kc + 1]
                if half == 0:
                    op("act", ACT(o_, src_ps, AF.Identity, bias=B, scale=A), R=[psk(b), mcolb.k()], W=[dstT_key[0]])
                else:
                    op("dve", TS(o_, src_ps, A, ALU.mult, B, ALU.add), R=[psk(b), mcolb.k()], W=[dstT_key[0]])

    dstT_key = [None]
    J2 = [None]
    ROPEB = [None]

    def resid_update(ps2, tile, r, q, xt, kx, dst, ri):
        junk2b = J2[0]
        ssa, ssb, ss, rs, tm = (stv[:, 16 + ri * 8 + j:16 + ri * 8 + j + 1] for j in range(5))
        kk = stb.k(("ru", ri))
        for hf, sx in ((0, ssa), (1, ssb)):
            op("act", ACT(P.f32(junk2b)[:, hf * 512:(hf + 1) * 512], PS[ps2[hf]][:, :], AF.Square, accum=sx),
               R=[psk(ps2[hf])], W=[junk2b.k(hf), kk])
        op("dve", TT(ss, ssa, ssb, ALU.add), R=[kk], W=[kk])
        op("act", ACT(tm, ss, AF.Sqrt, bias=epsc, scale=1.0 / 1024), R=[kk, KC], W=[kk])
        op("dve", RECIP(rs, tm), R=[kk], W=[kk])
        for hf in range(2):
            tmp = P.f32(junk2b)[:, hf * 512:(hf + 1) * 512]
            op("dve", STT(tmp, PS[ps2[hf]][:, :], rs, gbc[:, r, q, hf * 512:(hf + 1) * 512], ALU.mult, ALU.mult),
               R=[psk(ps2[hf]), kk, gbb.k()], W=[junk2b.k(hf)])
            op("pool", TT(xt[:, hf * 512:(hf + 1) * 512], tmp, xt[:, hf * 512:(hf + 1) * 512], ALU.add),
               R=[junk2b.k(hf), kx], W=[kx])
        op("sp", DMA(dst[tile * 128:(tile + 1) * 128, :], xt), R=[kx], W=[("dram", dst.tensor.name, tile)], dma=True)

    junk2b = None

    def ffn(l, src, dst):
        wgb = P.alloc("wg", 8 * DFF // 2)
        wub = P.alloc("wu", 8 * DFF // 2)
        wdb = P.alloc("wd", NFF * 1024 // 2)
        wg = P.bf(wgb).rearrange("p (k n) -> p k n", k=8)
        wu = P.bf(wub).rearrange("p (k n) -> p k n", k=8)
        wd = P.bf(wdb).rearrange("p (k n) -> p k n", k=NFF)
        for k in range(8):
            op("pool", DMA(wg[:, k, :], T["w_gate"][l][k * 128:(k + 1) * 128, :]), W=[wgb.k()], dma=True)
            op("pool", DMA(wu[:, k, :], T["w_up"][l][k * 128:(k + 1) * 128, :]), W=[wub.k()], dma=True)
        for k in range(0, NFF, 2):
            op("pool", DMA(wd[:, k:k + 2, :], T["w_down"][l][k * 128:(k + 2) * 128, :].rearrange("(k p) n -> p k n", p=128)),
               W=[wdb.k()], dma=True)
        h2b = P.alloc("h2T", 8 * 512 // 2)
        h2T = P.bf(h2b).rearrange("p (k n) -> p k n", k=8)
        aTb = P.alloc("aT", NFF * 512 // 2)
        aT = P.bf(aTb).rearrange("p (k n) -> p k n", k=NFF)
        xgb = P.alloc("xgrp", 4 * 1024)
        sgb = [P.alloc(f"sg{i}", 256) for i in range(2)]
        J2[0] = P.alloc("junk2", 1024)
        for g in range(5):
            r = 1 if g < 4 else 0
            dstT_key[0] = h2b.k()
            for j in range(4):
                tile = g * 4 + j
                xt = P.f32(xgb)[:, j * 1024:(j + 1) * 1024]
                norm_transpose(src, tile, r, 2, h2T, j * 128, xkeep=(xt, xgb.k(j)))
            for fc in range(NFF):
                bg, bu = P.bank(), P.bank()
                op("pe", MM([(PS[bg][:, :], wg[:, k, fc * 128:(fc + 1) * 128], h2T[:, k, :], k == 0, k == 7) for k in range(8)]),
                   R=[wgb.k(), h2b.k()], W=[psk(bg)])
                op("pe", MM([(PS[bu][:, :], wu[:, k, fc * 128:(fc + 1) * 128], h2T[:, k, :], k == 0, k == 7) for k in range(8)]),
                   R=[wub.k(), h2b.k()], W=[psk(bu)])
                sg = P.bf(sgb[fc % 2])
                op("act", ACT(sg, PS[bg][:, :], AF.Silu), R=[psk(bg)], W=[sgb[fc % 2].k()])
                op("dve", TT(aT[:, fc, :], sg, PS[bu][:, :], ALU.mult), R=[sgb[fc % 2].k(), psk(bu)], W=[aTb.k(fc)])
            for j in range(4):
                tile = g * 4 + j
                b0, b1 = P.bank(), P.bank()
                for hf, b in ((0, b0), (1, b1)):
                    op("pe", MM([(PS[b][:, :], aT[:, fc, j * 128:(j + 1) * 128], wd[:, fc, hf * 512:(hf + 1) * 512], fc == 0, fc == NFF - 1)
                                 for fc in range(NFF)]), R=[aTb.k(fc) for fc in range(NFF)] + [wdb.k()], W=[psk(b)])
                xt = P.f32(xgb)[:, j * 1024:(j + 1) * 1024]
                resid_update((b0, b1), tile, r, 1, xt, xgb.k(j), dst, j % 2)
        P.free(wgb, wub, wdb, h2b, aTb, xgb, sgb[0], sgb[1], J2[0])

    def mixer(l, src, dst, parts=("ret", "mla", "four", "hy"), do_out=True):
        mixer_consts()
        KMC = MC["key"]
        xtb[0], xtb[1] = P.alloc("xt0", 1024), P.alloc("xt1", 1024)
        hTb = P.alloc("hT", 8 * 2560 // 2)
        hT = P.bf(hTb).rearrange("p (k n) -> p k n", k=8)
        catb = P.alloc("catT", 8 * 2560 // 2)
        catT = P.bf(catb).rearrange("p (k n) -> p k n", k=8)
        KH = hTb.k()
        if dbg:
            op("pool", MSET(catT, 0.0), W=[catb.k(c) for c in range(8)])
        dstT_key[0] = KH
        for t in range(NT):
            norm_transpose(src, t, 1 if t < 16 else 0, 0, hT, t * 128)
        dump(f"hT{l}", hT, [KH])
        P.free(xtb[0], xtb[1])
        ROPEB[0] = P.alloc("ropeb", 128 + 512)
        if "ret" in parts:
            retention_all(l, hT, KH, catT, catb, KMC)
        if "mla" in parts:
            mla_all(l, hT, KH, catT, catb, KMC)
        P.free(ROPEB[0], MC["buf"])
        if "four" in parts:
            fourier_all(l, hT, KH, catT, catb)
        if "hy" in parts:
            hyena_all(l, hT, KH, catT, catb, hTb)
        else:
            P.free(hTb)
        dump(f"catT{l}", catT, [catb.k(c) for c in range(8)])
        if do_out:
            xtb[0], xtb[1] = P.alloc("xt0", 1024), P.alloc("xt1", 1024)
            J2[0] = P.alloc("junk2", 1024)
            wob = P.alloc("wout", 8 * 1024 // 2)
            wo = P.bf(wob).rearrange("p (k n) -> p k n", k=8)
            op("pool", DMA(wo, T["w_out"][l].rearrange("(k p) n -> p k n", p=128)), W=[wob.k()], dma=True)
            for t in range(NT):
                r = 1 if t < 16 else 0
                i2 = t % 2
                xt = P.f32(xtb[i2])
                op("sp", DMA(xt, src[t * 128:(t + 1) * 128, :]), R=[("dram", src.tensor.name, t)], W=[xtb[i2].k()], dma=True)
                b0, b1 = P.bank(), P.bank()
                for hf, b in ((0, b0), (1, b1)):
                    op("pe", MM([(PS[b][:, :], catT[:, k, t * 128:(t + 1) * 128], wo[:, k, hf * 512:(hf + 1) * 512], k == 0, k == 7)
                                 for k in range(8)]), R=[catb.k(c) for c in range(8)] + [wob.k()], W=[psk(b)])
                resid_update((b0, b1), t, r, 0, xt, xtb[i2].k(), dst, i2)
            P.free(wob, xtb[0], xtb[1], J2[0])
        P.free(catb)

    def retention_all(l, hT, KH, catT, catb, KMC):
        wAb = P.alloc("wA", 8 * 1024 // 2)
        wA = P.bf(wAb).rearrange("p (k n) -> p k n", k=8)
        op("pool", DMA(wA, T["w_in"][l][:, 0:1024].rearrange("(k p) n -> p k n", p=128)), W=[wAb.k()], dma=True)
        rtb = P.alloc("rtabs", 8 + 8 + 8 + 16 + 512 + 4 + 4)
        rt = P.f32(rtb)
        KT_ = rtb.k()
        decb = rt[:, 0:8]
        lgb = rt[:, 8:16]
        tmp8 = rt[:, 16:24]
        xz = rt[:, 24:40].rearrange("p (q h) -> p q h", q=4)
        dmk = rt[:, 40:552]
        lgsel = rt[:, 552:556].rearrange("p (d q) -> p d q", d=2)
        gsel = rt[:, 556:560].rearrange("p (d q) -> p d q", d=2)
        op("sp", DMA(decb, bass.AP(T["ret_decay"].tensor, l * 8, [[0, 128], [1, 8]])), W=[KT_], dma=True)
        op("act", ACT(tmp8, decb, AF.Exp, scale=-1.0), R=[KT_], W=[KT_])
        op("act", ACT(tmp8, tmp8, AF.Ln, bias=onesf[:, 0:1], scale=1.0), R=[KT_, KC], W=[KT_])
        op("dve", TS(lgb, tmp8, -1.0, ALU.mult), R=[KT_], W=[KT_])
        rc = MC["ret_cols"]
        op("act", ACT(xz[:, 0, :], lgb[:, 0:4], AF.Exp, scale=rc[:, 0:1]), R=[KT_, KMC], W=[KT_])
        op("act", ACT(xz[:, 1, :], lgb[:, 0:4], AF.Exp, scale=rc[:, 1:2]), R=[KT_, KMC], W=[KT_])
        op("act", ACT(xz[:, 2, :], lgb[:, 4:8], AF.Exp, scale=rc[:, 2:3]), R=[KT_, KMC], W=[KT_])
        op("act", ACT(xz[:, 3, :], lgb[:, 4:8], AF.Exp, scale=rc[:, 3:4]), R=[KT_, KMC], W=[KT_])
        for qq in (1, 3):
            op("dve", TS(xz[:, qq, :], xz[:, qq, :], 0.125, ALU.mult), R=[KT_], W=[KT_])
        t1b = P.alloc("rt_tmp", 1024)
        t1 = P.f32(t1b)
        for h in range(4):
            hs = (h % 2) * 2 + h // 2
            sl = slice(hs * 128, (hs + 1) * 128)
            op("act", ACT(t1[:, sl], MC["ret_dpos"][:, sl], AF.Exp, scale=lgb[:, h:h + 1]), R=[KT_, KMC], W=[t1b.k()])
            op("act", ACT(t1[:, 512 + hs * 128:512 + (hs + 1) * 128], MC["ret_dneg"][:, sl], AF.Exp, scale=lgb[:, 4 + h:5 + h]),
               R=[KT_, KMC], W=[t1b.k()])
        op("dve", TT(t1[:, 0:512], t1[:, 0:512], MC["ret_mge"], ALU.mult), R=[t1b.k(), KMC], W=[t1b.k()])
        op("dve", TT(t1[:, 512:1024], t1[:, 512:1024], MC["ret_mle"], ALU.mult), R=[t1b.k(), KMC], W=[t1b.k()])
        op("dve", TT(dmk, t1[:, 0:512], t1[:, 512:1024], ALU.add), R=[t1b.k()], W=[KT_])
        P.free(t1b)
        dv = lgb.rearrange("p (d q a) -> p d q a", d=2, a=2)
        for a in range(2):
            op("dve", CP(lgsel[a * 64:(a + 1) * 64], dv[a * 64:(a + 1) * 64, :, :, a]), R=[KT_], W=[KT_])
        op("act", ACT(gsel, lgsel, AF.Exp, scale=128.0), R=[KT_], W=[KT_])

        import os as _os
        _seqs = [SEQS[int(i_)] for i_ in _os.environ.get("DBG_SEQS", "0,1,2").split(",")]
        _stop = int(_os.environ.get("RET_STOP", "9"))
        if _stop <= 1:
            return
        for (t0, nts, latent) in _seqs:
            if latent:
                nts = int(_os.environ.get("DBG_NT0", nts))
            L = nts * 128
            qTb = P.alloc("qT", 2 * L // 2)
            kTb = P.alloc("kT", 2 * L // 2)
            qT = P.bf(qTb).rearrange("p (q n) -> p q n", q=2)
            kT = P.bf(kTb).rearrange("p (q n) -> p q n", q=2)
            vtb = P.alloc("v_tok", nts * 256 // 2)
            vt = P.bf(vtb).rearrange("p (t n) -> p t n", t=nts)
            gtb = P.alloc("gate_tok", nts * 256 // 2)
            gt = P.bf(gtb).rearrange("p (t n) -> p t n", t=nts)
            kvb = P.alloc("kv_all", nts * 256)
            kva = P.f32(kvb).rearrange("p (t d q n) -> p t d q n", t=nts, d=2, q=2)
            sbb = P.alloc("S_bf", nts * 256 // 2)
            sbf = P.bf(sbb).rearrange("p (t d q n) -> p t d q n", t=nts, d=2, q=2)
            stf = P.alloc("S_f32", 256)
            Sst = P.f32(stf).rearrange("p (d q n) -> p d q n", d=2, q=2)
            qkb = [P.alloc(f"qkrot{i}", 256) for i in range(2)]
            kzb = [P.alloc(f"kz{i}", 256) for i in range(2)]
            rpb = P.alloc("ropetmp", 1024) if latent else None
            for j in range(nts):
                t = t0 + j
                cols = slice(t * 128, (t + 1) * 128)
                bq, bv = P.bank(), P.bank()
                op("pe", MM([(PS[bq][:, :], hT[:, k, cols], wA[:, k, 0:512], k == 0, k == 7) for k in range(8)]),
                   R=[KH, wAb.k()], W=[psk(bq)])
                op("pe", MM([(PS[bv][:, :], hT[:, k, cols], wA[:, k, 512:1024], k == 0, k == 7) for k in range(8)]),
                   R=[KH, wAb.k()], W=[psk(bv)])
                qk = P.bf(qkb[j % 2])
                kq = qkb[j % 2].k()
                if latent:
                    rp = P.f32(rpb)
                    raw = rp[:, 0:512]
                    op("act", ACT(raw, PS[bq][:, :], AF.Copy), R=[psk(bq)], W=[rpb.k(0)])
                    src5 = raw.rearrange("p (h r x j) -> p h r x j", h=8, r=2, x=2)
                    dst5 = qk.rearrange("p (h r x j) -> p h r x j", h=8, r=2, x=2)
                    tm5 = rp[:, 512:1024].rearrange("p (u h r j) -> p u h r j", u=2, h=8, r=2)
                    cs = bc(MC["cos_ret"][:, j * 32:(j + 1) * 32], [[0, 8], [16, 2], [1, 16]])
                    sn = bc(MC["sin_ret"][:, j * 32:(j + 1) * 32], [[0, 8], [16, 2], [1, 16]])
                    x1, x2 = src5[:, :, :, 0, :], src5[:, :, :, 1, :]
                    op("pool", TT(tm5[:, 0], x1, cs, ALU.mult), R=[rpb.k(0), KMC], W=[rpb.k(1)])
                    op("pool", TT(tm5[:, 1], x2, sn, ALU.mult), R=[rpb.k(0), KMC], W=[rpb.k(2)])
                    op("pool", TT(dst5[:, :, :, 0, :], tm5[:, 0], tm5[:, 1], ALU.subtract), R=[rpb.k(1), rpb.k(2)], W=[kq])
                    op("pool", TT(tm5[:, 0], x2, cs, ALU.mult), R=[rpb.k(0), KMC], W=[rpb.k(1)])
                    op("pool", TT(tm5[:, 1], x1, sn, ALU.mult), R=[rpb.k(0), KMC], W=[rpb.k(2)])
                    op("pool", TT(dst5[:, :, :, 1, :], tm5[:, 0], tm5[:, 1], ALU.add), R=[rpb.k(1), rpb.k(2)], W=[kq])
                else:
                    op("act", ACT(qk, PS[bq][:, :], AF.Copy), R=[psk(bq)], W=[kq])
                op("act", ACT(vt[:, j, :], PS[bv][:, 0:256], AF.Copy), R=[psk(bv)], W=[vtb.k(j)])
                op("act", ACT(gt[:, j, :], PS[bv][:, 256:512], AF.Silu), R=[psk(bv)], W=[gtb.k(j)])
                if _os.environ.get("DBG_SKIPKZ"):
                    continue
                kz = P.bf(kzb[j % 2]).rearrange("p (d h n) -> p d h n", d=2, h=4)
                k4 = qk[:, 256:512].rearrange("p (h n) -> p h n", h=4)
                for d_, qq in ((0, 1), (1, 3)):
                    op("pool", TT(kz[:, d_], k4, bc(xz[:, qq, :], [[1, 4], [0, 64]]), ALU.mult), R=[kq, KT_], W=[kzb[j % 2].k(d_)])
                if _os.environ.get("DBG_SKIPT"):
                    continue
                bt, bt2 = P.bank(), P.bank()
                op("pe", MM([(PS[bt][:, i * 128:(i + 1) * 128], qk[:, i * 128:(i + 1) * 128], ident, True, True) for i in range(2)]),
                   R=[kq, KC], W=[psk(bt)])
                op("pe", MM([(PS[bt2][:, i * 128:(i + 1) * 128], qk[:, (2 + i) * 128:(3 + i) * 128], ident, True, True) for i in range(2)]),
                   R=[kq, KC], W=[psk(bt2)])
                jc = slice(j * 128, (j + 1) * 128)
                op("act", ACT(qT[:, :, jc], PS[bt][:, 0:256].rearrange("p (q n) -> p q n", q=2), AF.Copy), R=[psk(bt)], W=[qTb.k(j)])
                op("dve", CP(kT[:, :, jc], PS[bt2][:, 0:256].rearrange("p (q n) -> p q n", q=2)), R=[psk(bt2)], W=[kTb.k(j)])
                if _os.environ.get("DBG_SKIPKV"):
                    continue
                bk = P.bank()
                op("pe", MM([(PS[bk][:, d_ * 256 + hp * 128:d_ * 256 + (hp + 1) * 128], kz[:, d_, 2 * hp:2 * hp + 2, :],
                              vt[:, j, hp * 128:(hp + 1) * 128], True, True) for d_ in range(2) for hp in range(2)]),
                   R=[kzb[j % 2].k(0), kzb[j % 2].k(1), vtb.k(j)], W=[psk(bk)])
                for a in range(2):
                    srcv = bass.AP(PS[bk][:, :].tensor,
                                   PS[bk][a * 64:(a + 1) * 64, a * 64:a * 64 + 1].offset,
                                   [list(PS[bk][a * 64:(a + 1) * 64, :].ap[0]), [256, 2], [128, 2], [1, 64]])
                    op("act" if j % 2 == 0 else "dve",
                       (ACT(kva[a * 64:(a + 1) * 64, j], srcv, AF.Copy) if j % 2 == 0 else CP(kva[a * 64:(a + 1) * 64, j], srcv)),
                       R=[psk(bk)], W=[kvb.k((j, a))])
            if _stop <= 2:
                continue
            KS = stf.k()
            if latent:
                for d_ in range(2):
                    for a in range(2):
                        op("sp", DMA(Sst[a * 64:(a + 1) * 64, d_], bass.AP(T["s0"].tensor, ((l * 2 + d_) * 4 + a) * 4096,
                                                                           [[64, 64], [2 * 4096, 2], [1, 64]])), W=[KS], dma=True)
            else:
                op("pool", MSET(P.f32(stf), 0.0), W=[KS])
            for d_ in range(2):
                order = range(nts) if d_ == 0 else range(nts - 1, -1, -1)
                for j in order:
                    op("act", ACT(sbf[:, j, d_], Sst[:, d_], AF.Copy), R=[KS], W=[sbb.k((j, d_))])
                    for hp in range(2):
                        op("dve", STT(Sst[:, d_, hp, :], Sst[:, d_, hp, :], gsel[:, d_, hp:hp + 1], kva[:, j, d_, hp, :], ALU.mult, ALU.add),
                           R=[KS, KT_, kvb.k((j, 0)), kvb.k((j, 1))], W=[KS])
            if not latent:
                pb = (t0 - 16) // 2
                for d_ in range(2):
                    for a in range(2):
                        op("sp", DMA(bass.AP(T["o_st"].tensor, (((pb * 2 + l) * 2 + d_) * 4 + a) * 4096, [[64, 64], [2 * 4096, 2], [1, 64]]),
                                     Sst[a * 64:(a + 1) * 64, d_]), R=[KS], dma=True)
            if _stop <= 3:
                continue
            P.free(kvb, qkb[0], qkb[1], kzb[0], kzb[1])
            if rpb is not None:
                P.free(rpb)
            ptb = [P.alloc(f"PT{i}", 256) for i in range(2)]
            ob = [P.alloc(f"o_acc{i}", 256 * 3) for i in range(2)]
            gnb = P.alloc("gn", 32)
            gn = P.f32(gnb)
            rob = [P.alloc(f"ret_o{i}", 128) for i in range(2)]
            for j in range(nts):
                t = t0 + j
                jc = slice(j * 128, (j + 1) * 128)
                bsa = [P.bank(), P.bank()]
                PT = P.bf(ptb[j % 2])
                for a in range(2):
                    op("pe", MM([(PS[bsa[a]][:, hp * 128:(hp + 1) * 128], kT[a * 64:(a + 1) * 64, hp, jc],
                                  qT[a * 64:(a + 1) * 64, hp, jc], True, True) for hp in range(2)]),
                       R=[kTb.k(j), qTb.k(j)], W=[psk(bsa[a])])
                    op("dve", TT(PT[:, a * 256:(a + 1) * 256], PS[bsa[a]][:, 0:256], dmk[:, a * 256:(a + 1) * 256], ALU.mult),
                       R=[psk(bsa[a]), KT_], W=[ptb[j % 2].k(a)])
                bo = P.bank()
                op("pe", MM([(PS[bo][:, h * 64:(h + 1) * 64], PT[:, ((h % 2) * 2 + h // 2) * 128:((h % 2) * 2 + h // 2 + 1) * 128],
                              vt[:, j, h * 64:(h + 1) * 64], True, True) for h in range(4)]),
                   R=[ptb[j % 2].k(0), ptb[j % 2].k(1), vtb.k(j)], W=[psk(bo)])
                bxa = [P.bank(), P.bank()]
                for a in range(2):
                    op("pe", MM([(PS[bxa[a]][:, (d_ * 2 + hp) * 64:(d_ * 2 + hp + 1) * 64], qT[a * 64:(a + 1) * 64, hp, jc],
                                  sbf[a * 64:(a + 1) * 64, j, d_, hp, :], True, True) for d_ in range(2) for hp in range(2)]),
                       R=[qTb.k(j), sbb.k((j, 0)), sbb.k((j, 1))], W=[psk(bxa[a])])
                oa = P.f32(ob[j % 2]).rearrange("p (u h n) -> p u h n", u=3, h=4)
                ko = ob[j % 2].k()
                for a in range(2):
                    xif = bc(xz[:, 0, a:a + 1], [[2, 2], [0, 64]])
                    xib = bc(xz[:, 2, a:a + 1], [[2, 2], [0, 64]])
                    cf_ = PS[bxa[a]][:, 0:128].rearrange("p (q n) -> p q n", q=2)
                    cb_ = PS[bxa[a]][:, 128:256].rearrange("p (q n) -> p q n", q=2)
                    o0 = oa[:, 0, a::2, :]
                    o1 = oa[:, 1, a::2, :]
                    op("dve", TT(o0, cf_, xif, ALU.mult), R=[psk(bxa[a]), KT_], W=[ko])
                    op("dve", TT(o1, cb_, xib, ALU.mult), R=[psk(bxa[a]), KT_], W=[ko])
                op("pool", TT(oa[:, 0], oa[:, 0], oa[:, 1], ALU.add), R=[ko], W=[ko])
                op("dve", TT(oa[:, 0], oa[:, 0], PS[bo][:, 0:256].rearrange("p (h n) -> p h n", h=4), ALU.add), R=[ko, psk(bo)], W=[ko])
                g0 = (j % 2) * 16
                sm, ng_, vs, sd, rs_ = (gn[:, g0 + 4 * 0:g0 + 4], gn[:, g0 + 4:g0 + 8], gn[:, g0 + 8:g0 + 12], gn[:, g0 + 12:g0 + 16], None)
                kg = gnb.k(j % 2)
                op("dve", RED(sm, oa[:, 0]), R=[ko], W=[kg])
                op("dve", TS(ng_, sm, -1.0 / 64, ALU.mult), R=[kg], W=[kg])
                op("pool", TT(oa[:, 1], oa[:, 0], bc(ng_, [[1, 4], [0, 64]]), ALU.add), R=[ko, kg], W=[ko])
                op("pool", TT(oa[:, 2], oa[:, 1], oa[:, 1], ALU.mult), R=[ko], W=[ko])
                op("dve", RED(vs, oa[:, 2]), R=[ko], W=[kg])
                op("act", ACT(sd, vs, AF.Sqrt, bias=epsc, scale=1.0 / 64), R=[kg, KC], W=[kg])
                op("dve", RECIP(sm, sd), R=[kg], W=[kg])
                op("pool", TT(oa[:, 2], oa[:, 1], bc(sm, [[1, 4], [0, 64]]), ALU.mult), R=[ko, kg], W=[ko])
                ro = P.bf(rob[j % 2])
                op("dve", TT(ro.rearrange("p (h n) -> p h n", h=4), oa[:, 2], gt[:, j, :].rearrange("p (h n) -> p h n", h=4), ALU.mult),
                   R=[ko, gtb.k(j)], W=[rob[j % 2].k()])
                bt = P.bank()
                op("pe", MM([(PS[bt][:, i * 128:(i + 1) * 128], ro[:, i * 128:(i + 1) * 128], ident, True, True) for i in range(2)]),
                   R=[rob[j % 2].k(), KC], W=[psk(bt)])
                op("act", ACT(catT[:, 0:2, t * 128:(t + 1) * 128], PS[bt][:, 0:256].rearrange("p (q n) -> p q n", q=2), AF.Copy),
                   R=[psk(bt)], W=[catb.k(0), catb.k(1)])
            P.free(qTb, kTb, vtb, gtb, sbb, stf, ptb[0], ptb[1], ob[0], ob[1], gnb, rob[0], rob[1])
        P.free(wAb, rtb)

    def mla_all(l, hT, KH, catT, catb, KMC):
        import os as _os
        ropeb = ROPEB[0]
        wBb = P.alloc("wB", 8 * 416 // 2)
        wB = P.bf(wBb).rearrange("p (k n) -> p k n", k=8)
        op("pool", DMA(wB, T["w_in"][l][:, 1280:1696].rearrange("(k p) n -> p k n", p=128)), W=[wBb.k()], dma=True)
        wqb = P.alloc("wuq", 2 * 384 // 2 + 256)
        wuq = P.bf(wqb)[:, 0:768].rearrange("p (k n) -> p k n", k=2)
        wukv = P.bf(wqb)[:, 768:1280]
        op("pool", DMA(wuq, T["mla_w_uq"][l].rearrange("(k p) n -> p k n", p=128)), W=[wqb.k()], dma=True)
        op("pool", DMA(wukv, T["mla_w_ukv"][l]), W=[wqb.k()], dma=True)
        gnb_ = P.alloc("mla_g", 384)
        gq_b, gkv_b = P.f32(gnb_)[:, 0:256], P.f32(gnb_)[:, 256:384]
        op("sp", DMA(gq_b, bass.AP(T["mla_q_norm"].tensor, l * 256, [[0, 128], [1, 256]])), W=[gnb_.k()], dma=True)
        op("sp", DMA(gkv_b, bass.AP(T["mla_kv_norm"].tensor, l * 128, [[0, 128], [1, 128]])), W=[gnb_.k()], dma=True)
        msb = P.alloc("mla_stats", 16)
        ms = P.f32(msb)
        scale = 96.0 ** -0.5
        _seqs = [SEQS[int(i_)] for i_ in _os.environ.get("DBG_SEQS", "0,1,2").split(",")]
        for (t0, nts, latent) in _seqs:
            L = nts * 128
            nkt = nts + (2 if latent else 0)
            Lk = nkt * 128
            cqTb = P.alloc("cqT", L)
            cqT = P.bf(cqTb).rearrange("p (k n) -> p k n", k=2)
            ckTb = P.alloc("ckvT", Lk // 2)
            ckvT = P.bf(ckTb)
            KTb = P.alloc("KT", 2 * Lk)
            KT = P.bf(KTb).rearrange("p (h n) -> p h n", h=4)
            QTb = P.alloc("QT", 2 * L)
            QT = P.bf(QTb).rearrange("p (h n) -> p h n", h=4)
            Vb = P.alloc("Vaug", nkt * 130)
            Va = P.bf(Vb).rearrange("p (t h n) -> p t h n", t=nkt, h=4)
            atb = P.alloc("att_tok", nts * 128)
            att = P.bf(atb).rearrange("p (t n) -> p t n", t=nts)
            op("pool", MSET(P.bf(Vb), 1.0), W=[Vb.k()])
            kstb = [P.alloc(f"kst{i}", 128) for i in range(2)]
            cqnb = [P.alloc(f"cqn{i}", 128) for i in range(2)]
            ckfb = [P.alloc(f"ckvf{i}", 128 + 32) for i in range(2)]
            for i in range(2):
                op("pool", MSET(P.bf(kstb[i]), 0.0), W=[kstb[i].k()])
            pb = (t0 - 16) // 2
            for j in range(nkt):
                i2 = j % 2
                kst = P.bf(kstb[i2])
                kk = kstb[i2].k()
                kcols = slice(j * 128, (j + 1) * 128)
                if j < nts:
                    t = t0 + j
                    bp = P.bank()
                    op("pe", MM([(PS[bp][:, 0:416], hT[:, k, t * 128:(t + 1) * 128], wB[:, k, :], k == 0, k == 7) for k in range(8)]),
                       R=[KH, wBb.k()], W=[psk(bp)])
                    ssq, ssk, sq1, sq2, rq, rk = (ms[:, i2 * 8 + u:i2 * 8 + u + 1] for u in range(6))
                    kst_ = msb.k(i2)
                    cqn = P.bf(cqnb[i2])
                    ckf = P.f32(ckfb[i2])
                    op("act", ACT(cqn, PS[bp][:, 0:256], AF.Square, accum=ssq), R=[psk(bp)], W=[cqnb[i2].k(), kst_])
                    op("act", ACT(ckf[:, 0:128], PS[bp][:, 256:384], AF.Square, accum=ssk), R=[psk(bp)], W=[ckfb[i2].k(), kst_])
                    op("act", ACT(sq1, ssq, AF.Sqrt, bias=epsc, scale=1.0 / 256), R=[kst_, KC], W=[kst_])
                    op("act", ACT(sq2, ssk, AF.Sqrt, bias=epsc, scale=1.0 / 128), R=[kst_, KC], W=[kst_])
                    op("dve", RECIP(ms[:, i2 * 8 + 4:i2 * 8 + 6], ms[:, i2 * 8 + 2:i2 * 8 + 4]), R=[kst_], W=[kst_])
                    op("dve", STT(cqn, PS[bp][:, 0:256], rq, gq_b, ALU.mult, ALU.mult), R=[psk(bp), kst_, gnb_.k()], W=[cqnb[i2].k()])
                    op("dve", STT(ckf[:, 0:128], PS[bp][:, 256:384], rk, gkv_b, ALU.mult, ALU.mult), R=[psk(bp), kst_, gnb_.k()],
                       W=[ckfb[i2].k()])
                    op("pool", CP(kst[:, 0:128], ckf[:, 0:128]), R=[ckfb[i2].k()], W=[kk])
                    if latent:
                        op("dve", CP(ckf[:, 128:160], PS[bp][:, 384:416]), R=[psk(bp), kst_], W=[ckfb[i2].k("kr")])
                        kr3 = ckf[:, 128:160].rearrange("p (r x j) -> p r x j", r=2, x=2)
                        kd3 = kst[:, 192:224].rearrange("p (r x j) -> p r x j", r=2, x=2)
                        cs = MC["cos_mla"][:, j * 16:(j + 1) * 16].rearrange("p (r j) -> p r j", r=2)
                        sn = MC["sin_mla"][:, j * 16:(j + 1) * 16].rearrange("p (r j) -> p r j", r=2)
                        tmk = ms
                        rtb_ = ckfb[i2].k("rt")
                        tmp4 = P.f32(ropeb)[:, i2 * 64:(i2 + 1) * 64].rearrange("p (u r j) -> p u r j", u=4, r=2)
                        kro = ropeb.k(i2)
                        op("pool", TT(tmp4[:, 0], kr3[:, :, 0, :], cs, ALU.mult), R=[ckfb[i2].k("kr"), KMC], W=[kro])
                        op("pool", TT(tmp4[:, 1], kr3[:, :, 1, :], sn, ALU.mult), R=[ckfb[i2].k("kr"), KMC], W=[kro])
                        op("pool", TT(tmp4[:, 2], kr3[:, :, 1, :], cs, ALU.mult), R=[ckfb[i2].k("kr"), KMC], W=[kro])
                        op("pool", TT(tmp4[:, 3], kr3[:, :, 0, :], sn, ALU.mult), R=[ckfb[i2].k("kr"), KMC], W=[kro])
                        op("pool", TT(kd3[:, :, 0, :], tmp4[:, 0], tmp4[:, 1], ALU.subtract), R=[kro], W=[kk])
                        op("pool", TT(kd3[:, :, 1, :], tmp4[:, 2], tmp4[:, 3], ALU.add), R=[kro], W=[kk])
                    else:
                        op("dve", CP(ckf[:, 128:160], PS[bp][:, 384:416]), R=[psk(bp), kst_], W=[ckfb[i2].k("kr")])
                        op("pool", CP(kst[:, 192:224], ckf[:, 128:160]), R=[ckfb[i2].k("kr")], W=[kk])
                        rows = slice(j * 128, (j + 1) * 128)
                        op("sp", DMA(T["o_ckv"][pb, l, rows, :], ckf[:, 0:128]), R=[ckfb[i2].k()], dma=True)
                        op("sp", DMA(T["o_kr"][pb, l, rows, :], ckf[:, 128:160]), R=[ckfb[i2].k("kr")], dma=True)
                else:
                    rows = slice((j - nts) * 128, (j - nts + 1) * 128)
                    op("pool", DMA(kst[:, 0:128], T["ckv_ctx"][l, rows, :]), W=[kk], dma=True)
                    op("pool", DMA(kst[:, 192:224], T["kr_ctx"][l, rows, :]), W=[kk], dma=True)
                bT = P.bank()
                mms = []
                if j < nts:
                    mms += [(PS[bT][:, i * 128:(i + 1) * 128], P.bf(cqnb[i2])[:, i * 128:(i + 1) * 128], ident, True, True) for i in range(2)]
                mms += [(PS[bT][:, 256:384], kst[:, 0:128], ident, True, True),
                        (PS[bT][0:96, 384:512], kst[:, 128:224], ident, True, True)]
                op("pe", MM(mms), R=[cqnb[i2].k(), kk, KC], W=[psk(bT)])
                ev = "act" if j % 2 == 0 else "dve"
                cpy = (lambda o_, i_: ACT(o_, i_, AF.Copy)) if ev == "act" else CP
                if j < nts:
                    op(ev, cpy(cqT[:, :, j * 128:(j + 1) * 128], PS[bT][:, 0:256].rearrange("p (k n) -> p k n", k=2)), R=[psk(bT)], W=[cqTb.k(j)])
                op(ev, cpy(ckvT[:, kcols], PS[bT][:, 256:384]), R=[psk(bT)], W=[ckTb.k(j)])
                op(ev, cpy(KT[64:96, :, kcols], bc(PS[bT][64:96, 384:512], [[0, 4], [1, 128]])), R=[psk(bT)], W=[KTb.k(("r", j))])
            rawb = [P.alloc(f"qraw{i}", 384) for i in range(2)]
            qtkb = [P.alloc(f"qtok{i}", 192) for i in range(2)]
            for j in range(nts):
                i2 = j % 2
                jc = slice(j * 128, (j + 1) * 128)
                bq = P.bank()
                op("pe", MM([(PS[bq][:, 0:384], cqT[:, k, jc], wuq[:, k, :], k == 0, k == 1) for k in range(2)]),
                   R=[cqTb.k(j), wqb.k()], W=[psk(bq)])
                qtk = P.bf(qtkb[i2])
                kq_ = qtkb[i2].k()
                if latent:
                    raw = P.f32(rawb[i2])
                    op("act", ACT(raw, PS[bq][:, 0:384], AF.Copy), R=[psk(bq)], W=[rawb[i2].k()])
                    r4 = raw.rearrange("p (h n) -> p h n", h=4)
                    q4 = qtk.rearrange("p (h n) -> p h n", h=4)
                    op("dve", CP(q4[:, :, 0:64], r4[:, :, 0:64]), R=[rawb[i2].k()], W=[kq_])
                    def rv(base, off):
                        return bass.AP(base.tensor, base.offset + off, [list(base.ap[0]), [96, 4], [16, 2], [1, 8]])
                    x1, x2 = rv(raw, 64), rv(raw, 72)
                    o1_, o2_ = rv(qtk, 64), rv(qtk, 72)
                    cs = bc(MC["cos_mla"][:, j * 16:(j + 1) * 16], [[0, 4], [8, 2], [1, 8]])
                    sn = bc(MC["sin_mla"][:, j * 16:(j + 1) * 16], [[0, 4], [8, 2], [1, 8]])
                    tq = P.f32(ropeb)[:, 128 + i2 * 256:128 + (i2 + 1) * 256].rearrange("p (u h r j) -> p u h r j", u=4, h=4, r=2)
                    kro = ropeb.k(("q", i2))
                    op("pool", TT(tq[:, 0], x1, cs, ALU.mult), R=[rawb[i2].k(), KMC], W=[kro])
                    op("pool", TT(tq[:, 1], x2, sn, ALU.mult), R=[rawb[i2].k(), KMC], W=[kro])
                    op("pool", TT(tq[:, 2], x2, cs, ALU.mult), R=[rawb[i2].k(), KMC], W=[kro])
                    op("pool", TT(tq[:, 3], x1, sn, ALU.mult), R=[rawb[i2].k(), KMC], W=[kro])
                    op("pool", TT(o1_, tq[:, 0], tq[:, 1], ALU.subtract), R=[kro], W=[kq_])
                    op("pool", TT(o2_, tq[:, 2], tq[:, 3], ALU.add), R=[kro], W=[kq_])
                else:
                    op("act", ACT(qtk, PS[bq][:, 0:384], AF.Copy), R=[psk(bq)], W=[kq_])
                bT = P.bank()
                op("pe", MM([(PS[bT][0:96, h * 128:(h + 1) * 128], qtk[:, h * 96:(h + 1) * 96], ident, True, True) for h in range(4)]),
                   R=[kq_, KC], W=[psk(bT)])
                ev = "act" if j % 2 == 0 else "dve"
                cpy = (lambda o_, i_: ACT(o_, i_, AF.Copy)) if ev == "act" else CP
                op(ev, cpy(QT[0:96, :, jc], PS[bT][0:96, :].rearrange("p (h n) -> p h n", h=4)), R=[psk(bT)], W=[QTb.k(j)])
            P.free(rawb[0], rawb[1], qtkb[0], qtkb[1])
            wk4 = wukv.rearrange("p (h c n) -> p h c n", h=4, c=2)
            for c0 in range(0, Lk, 512):
                cw = min(512, Lk - c0)
                for h in range(4):
                    b = P.bank()
                    op("pe", MM([(PS[b][0:64, 0:cw], wk4[:, h, 0, :], ckvT[:, c0:c0 + cw], True, True)]),
                       R=[wqb.k()] + [ckTb.k(j_) for j_ in range(c0 // 128, (c0 + cw) // 128)], W=[psk(b)])
                    ev = "act" if h % 2 == 0 else "dve"
                    cpy = (lambda o_, i_: ACT(o_, i_, AF.Copy)) if ev == "act" else CP
                    op(ev, cpy(KT[0:64, h, c0:c0 + cw], PS[b][0:64, 0:cw]), R=[psk(b)], W=[KTb.k(("n", h, c0))])
            for kt in range(nkt):
                b = P.bank()
                op("pe", MM([(PS[b][:, 0:256], ckvT[:, kt * 128:(kt + 1) * 128], wk4[:, :, 1, :], True, True)]),
                   R=[wqb.k(), ckTb.k(kt)], W=[psk(b)])
                ev = "act" if kt % 2 == 0 else "dve"
                cpy = (lambda o_, i_: ACT(o_, i_, AF.Copy)) if ev == "act" else CP
                op(ev, cpy(Va[:, kt, :, 0:64], PS[b][:, 0:256].rearrange("p (h n) -> p h n", h=4)), R=[psk(b), Vb.k()], W=[Vb.k(kt)])
            QC = min(512, L)
            nsub = QC // 128
            Eb = [P.alloc(f"E{i}", QC // 2) for i in range(3)]
            rcb = P.alloc("att_rec", 8)
            ei = 0
            allKT = [KTb.k(("r", j_)) for j_ in range(nkt)]
            for h in range(4):
                kth = allKT + [KTb.k(("n", h, c0)) for c0 in range(0, Lk, 512)]
                for qc in range(L // QC):
                    qcols = slice(qc * QC, (qc + 1) * QC)
                    bo = P.bank()
                    for kt in range(nkt):
                        bs = P.bank()
                        if bs == bo:
                            bs = P.bank()
                        op("pe", MM([(PS[bs][:, 0:QC], KT[0:96, h, kt * 128:(kt + 1) * 128], QT[0:96, h, qcols], True, True)]),
                           R=kth + [QTb.k(j_) for j_ in range(qc * nsub, (qc + 1) * nsub)], W=[psk(bs)])
                        E = P.bf(Eb[ei % 3])
                        ke = Eb[ei % 3].k()
                        ei += 1
                        op("act", ACT(E, PS[bs][:, 0:QC], AF.Exp, scale=scale), R=[psk(bs)], W=[ke])
                        op("pe", MM([(PS[bo][:, sub * 65:(sub + 1) * 65], E[:, sub * 128:(sub + 1) * 128], Va[:, kt, h, :],
                                      kt == 0 and sub == 0, kt == nkt - 1 and sub == nsub - 1)
                                     for sub in range(nsub)]), R=[ke, Vb.k(kt), Vb.k()], W=[psk(bo)])
                    o3 = PS[bo][:, 0:nsub * 65].rearrange("p (s n) -> p s n", s=nsub)
                    rec = P.f32(rcb)[:, (qc % 2) * 4:(qc % 2) * 4 + nsub]
                    op("dve", RECIP(rec, o3[:, :, 64]), R=[psk(bo)], W=[rcb.k(qc % 2)])
                    op("dve", TT(att[:, qc * nsub:(qc + 1) * nsub, h * 64:(h + 1) * 64], o3[:, :, 0:64], bc(rec, [[1, nsub], [0, 64]]), ALU.mult),
                       R=[psk(bo), rcb.k(qc % 2)], W=[atb.k((h, qc))])
            for j in range(nts):
                t = t0 + j
                bT = P.bank()
                op("pe", MM([(PS[bT][:, i * 128:(i + 1) * 128], att[:, j, i * 128:(i + 1) * 128], ident, True, True) for i in range(2)]),
                   R=[atb.k((h_, j // nsub)) for h_ in range(4)] + [KC], W=[psk(bT)])
                ev = "act" if j % 2 == 0 else "dve"
                cpy = (lambda o_, i_: ACT(o_, i_, AF.Copy)) if ev == "act" else CP
                op(ev, cpy(catT[:, 4:6, t * 128:(t + 1) * 128], PS[bT][:, 0:256].rearrange("p (q n) -> p q n", q=2)), R=[psk(bT)],
                   W=[catb.k(4), catb.k(5)])
            P.free(cqTb, ckTb, KTb, QTb, Vb, atb, kstb[0], kstb[1], cqnb[0], cqnb[1], ckfb[0], ckfb[1], Eb[0], Eb[1], Eb[2], rcb)
        P.free(wBb, wqb, gnb_, msb)


    def fourier_all(l, hT, KH, catT, catb):
        import os as _os
        wCb = P.alloc("wC", 8 * 256 // 2)
        wC = P.bf(wCb).rearrange("p (k n) -> p k n", k=8)
        op("pool", DMA(wC, T["w_in"][l][:, 1024:1280].rearrange("(k p) n -> p k n", p=128)), W=[wCb.k()], dma=True)
        _seqs = [SEQS[int(i_)] for i_ in _os.environ.get("DBG_SEQS", "0,1,2").split(",")]
        for (t0, nts, latent) in _seqs:
            L = nts * 128
            nj = nts // 2
            H = L // 2
            PB = min(512, H)
            base = t0 * 128
            ftb = P.alloc("f_tok", 2 * nj * 256 // 2)
            ft = P.bf(ftb).rearrange("p (a j n) -> p a j n", a=2, j=nj)
            for a in range(2):
                for j in range(nj):
                    b = P.bank()
                    c0 = base + a + 256 * j
                    op("pe", MM([(PS[b][:, 0:256], hT[:, k, c0:c0 + 255:2], wC[:, k, :], k == 0, k == 7) for k in range(8)]),
                       R=[KH, wCb.k()], W=[psk(b)])
                    ev = "act" if (a * nj + j) % 2 == 0 else "dve"
                    cpy = (lambda o_, i_: ACT(o_, i_, AF.Copy)) if ev == "act" else CP
                    op(ev, cpy(ft[:, a, j, :], PS[b][:, 0:256]), R=[psk(b)], W=[ftb.k((a, j))])
            tabC = T[f"f_cl{L}"].rearrange("p (a j n) -> p a j n", a=2, j=nj)
            tabS = T[f"f_sl{L}"].rearrange("p (a j n) -> p a j n", a=2, j=nj)
            mtb = [P.alloc(f"ftab{i}", 2 * nj * PB // 2) for i in range(2)]
            uvb = [P.alloc(f"fuv{i}", 4 * PB // 2) for i in range(2)]
            zob = P.alloc("fzo", 2 * PB)
            mi = 0
            for ph in range(H // PB):
                zbanks = {}
                for a in range(2):
                    mt = P.bf(mtb[mi % 2]).rearrange("p (q j n) -> p q j n", q=2, j=nj)
                    km = mtb[mi % 2].k()
                    op("sp", DMA(mt[:, 0], tabC[:, a, :, ph * PB:(ph + 1) * PB]), W=[km], dma=True)
                    op("sp", DMA(mt[:, 1], tabS[:, a, :, ph * PB:(ph + 1) * PB]), W=[km], dma=True)
                    uv = P.bf(uvb[mi % 2]).rearrange("p (c q n) -> p c q n", c=2, q=2)
                    ku = uvb[mi % 2].k()
                    mi += 1
                    for cc in range(2):
                        for q in range(2):
                            b = P.bank()
                            op("pe", MM([(PS[b][:, 0:PB], ft[:, a, j, cc * 128:(cc + 1) * 128], mt[:, q, j, :], j == 0, j == nj - 1)
                                         for j in range(nj)]), R=[ftb.k((a, j)) for j in range(nj)] + [km], W=[psk(b)])
                            ev = "act" if q == 0 else "dve"
                            cpy = (lambda o_, i_: ACT(o_, i_, AF.Copy)) if ev == "act" else CP
                            op(ev, cpy(uv[:, cc, q, :], PS[b][:, 0:PB]), R=[psk(b)], W=[uvb[(mi - 1) % 2].k((cc, q))])
                    for cc in range(2):
                        b = P.bank()
                        zbanks[(a, cc)] = b
                        kk_ = uvb[(mi - 1) % 2]
                        op("pe", MM([(PS[b][:, 0:PB], c64, uv[:, cc, 0, :], True, False), (PS[b][:, 0:PB], s64n, uv[:, cc, 1, :], False, True)]),
                           R=[kk_.k((cc, 0)), kk_.k((cc, 1)), KC], W=[psk(b)])
                zo = P.f32(zob).rearrange("p (c n) -> p c n", c=2)
                for cc in range(2):
                    be, bo_ = zbanks[(0, cc)], zbanks[(1, cc)]
                    op("act", ACT(zo[:, cc, :], PS[bo_][:, 0:PB], AF.Copy), R=[psk(bo_)], W=[zob.k(cc)])
                    p0 = base + ph * PB
                    op("dve", TT(catT[:, 2 + cc, p0:p0 + PB], PS[be][:, 0:PB], zo[:, cc, :], ALU.add), R=[psk(be), zob.k(cc)], W=[catb.k(2 + cc)])
                    op("dve", TT(catT[:, 2 + cc, p0 + H:p0 + H + PB], PS[be][:, 0:PB], zo[:, cc, :], ALU.subtract), R=[psk(be), zob.k(cc)],
                       W=[catb.k(2 + cc)])
            P.free(ftb, mtb[0], mtb[1], uvb[0], uvb[1], zob)
        P.free(wCb)


    def hyena_all(l, hT, KH, catT, catb, hTb):
        import os as _os
        TWO_PI = 2.0 * math.pi
        hpb = P.alloc("hy_par", 18 + 6 + 4 + 4 + 64 + 64 + 1024 + 8)
        hp = P.f32(hpb)
        KP = hpb.k()
        shw = hp[:, 0:18].rearrange("p (m k) -> p m k", m=6)
        shb = hp[:, 18:24]
        dsk = hp[:, 24:28].rearrange("p (o c) -> p o c", o=2)
        bcol = hp[:, 28:32]
        w1s = hp[0:33, 32:96]
        w2s = hp[0:64, 96:160]
        w3s = hp[0:64, 160:1184]
        for k in range(3):
            op("sp", DMAS(shw[:, :, k:k + 1], bass.AP(T["hy_short_w"].tensor, (l * 3 + k) * 768, [[1, 128], [128, 6], [1, 1]])), W=[KP], dma=True)
        op("sp", DMAS(shb.rearrange("p (m o) -> p m o", o=1), bass.AP(T["hy_short_b"].tensor, l * 768, [[1, 128], [128, 6], [1, 1]])), W=[KP], dma=True)
        op("sp", DMAS(hp[:, 24:28].rearrange("p (q o) -> p q o", o=1), bass.AP(T["hy_bias"].tensor, l * 512, [[1, 128], [128, 4], [1, 1]])), W=[KP], dma=True)
        op("sp", DMAS(bcol[0:64, 0:1], bass.AP(T["hy_b1"].tensor, l * 64, [[1, 64], [1, 1]])), W=[KP], dma=True)
        op("sp", DMAS(bcol[0:64, 1:2], bass.AP(T["hy_b2"].tensor, l * 64, [[1, 64], [1, 1]])), W=[KP], dma=True)
        op("sp", DMA(w1s, T["hy_w1"][l]), W=[KP], dma=True)
        op("sp", DMA(w2s, T["hy_w2"][l]), W=[KP], dma=True)
        op("sp", DMA(w3s, T["hy_w3"][l]), W=[KP], dma=True)
        bx = hp[:, 1184:1192]
        op("dve", TS(bx[0:64, 0:2], bcol[0:64, 0:2], 0.5, ALU.mult), R=[KP], W=[KP])
        op("dve", TS(bx[0:64, 2:4], bcol[0:64, 0:2], 0.25, ALU.mult), R=[KP], W=[KP])
        wDb = P.alloc("wD", 8 * 768 // 2)
        wD = P.bf(wDb).rearrange("p (k n) -> p k n", k=8)
        op("pool", DMA(wD, T["w_in"][l][:, 1696:2464].rearrange("(k p) n -> p k n", p=128)), W=[wDb.k()], dma=True)
        ucb = P.alloc("ucT", 6 * 2560 // 2)
        ucT = P.bf(ucb).rearrange("p (m n) -> p m n", m=6)
        upb = [P.alloc(f"upad{i}", 2050) for i in range(2)]
        ctb = P.alloc("convtmp", 2048)
        ui = 0
        for (t0, nts, latent) in SEQS:
            L = nts * 128
            base = t0 * 128
            for m in range(6):
                ub = upb[ui % 2]
                up = P.f32(ub)
                ku = ub.k()
                ui += 1
                op("pool", MSET(up[:, 0:1], 0.0), W=[ku])
                op("pool", MSET(up[:, L + 1:L + 2], 0.0), W=[ku])
                for c0 in range(0, L, 512):
                    cw = min(512, L - c0)
                    b = P.bank()
                    op("pe", MM([(PS[b][:, 0:cw], wD[:, k, m * 128:(m + 1) * 128], hT[:, k, base + c0:base + c0 + cw], k == 0, k == 7)
                                 for k in range(8)]), R=[KH, wDb.k()], W=[psk(b)])
                    op("act", ACT(up[:, 1 + c0:1 + c0 + cw], PS[b][:, 0:cw], AF.Copy), R=[psk(b)], W=[ku])
                ct = P.f32(ctb)[:, 0:L]
                eng = "dve"
                op(eng, TS(ct, up[:, 0:L], shw[:, m, 0:1], ALU.mult, shb[:, m:m + 1], ALU.add), R=[ku, KP], W=[ctb.k()])
                op(eng, STT(ct, up[:, 1:L + 1], shw[:, m, 1:2], ct, ALU.mult, ALU.add), R=[ku, KP, ctb.k()], W=[ctb.k()])
                op(eng, STT(ucT[:, m, base:base + L], up[:, 2:L + 2], shw[:, m, 2:3], ct, ALU.mult, ALU.add), R=[ku, KP, ctb.k()],
                   W=[ucb.k((m, t0))])
        P.free(hTb, wDb, upb[0], upb[1], ctb)
        dump(f"ucT{l}", ucT, [ucb.k((m, t0)) for m in range(6) for (t0, _, _) in SEQS])

        for L in (2048, 256):
            seqs = [sq for sq in SEQS if sq[1] * 128 == L]
            nj = L // 256
            ntt = 2 * nj
            NF = L // 256
            HB = L // 2
            TB = min(512, HB)
            zb = P.alloc("hy_z", L)
            h1b = P.alloc("hy_h1", L)
            zT = P.f32(zb)[0:33, :]
            h1T = P.f32(h1b)[0:64, :]
            op("sp", DMA(zT, T[f"h_zT{L}"]), W=[zb.k()], dma=True)
            h2b = P.alloc("hy_h2", L)
            h2T = P.f32(h2b)[0:64, :]
            sinb = P.alloc("hy_sint", 1024)
            for (src_, dst_, w_, kk_, bi_, kr_, kw_) in ((zT, h1T, w1s, 33, 0, zb.k(), h1b.k()), (h1T, h2T, w2s, 64, 1, h1b.k(), h2b.k())):
                for c0 in range(0, L, 512):
                    cw = min(512, L - c0)
                    b = P.bank()
                    op("pe", MM([(PS[b][0:64, 0:cw], w_, src_[:, c0:c0 + cw], True, True)]), R=[KP, kr_], W=[psk(b)])
                    s2 = P.f32(sinb)[0:64, 0:cw]
                    s4 = P.f32(sinb)[0:64, 512:512 + cw]
                    op("act", ACT(s2, PS[b][0:64, 0:cw], AF.Sin, bias=bx[0:64, bi_:bi_ + 1], scale=0.5), R=[psk(b), KP], W=[sinb.k(0)])
                    op("act", ACT(s4, PS[b][0:64, 0:cw], AF.Sin, bias=bx[0:64, 2 + bi_:3 + bi_], scale=0.25), R=[psk(b), KP], W=[sinb.k(1)])
                    op("dve", TT(s4, s4, s4, ALU.mult), R=[sinb.k(1)], W=[sinb.k(1)])
                    op("dve", TS(s4, s4, -2.0, ALU.mult, 1.0, ALU.add), R=[sinb.k(1)], W=[sinb.k(1)])
                    op("dve", STT(dst_[:, c0:c0 + cw], s2, 2.0, s4, ALU.mult, ALU.mult), R=[sinb.k(0), sinb.k(1)], W=[kw_])
            P.free(zb, h1b, sinb)
            for o in range(2):
                winb = P.alloc("hy_win", ntt * 256)
                win = P.f32(winb).rearrange("p (q n) -> p q n", q=ntt)
                op("sp", DMA(P.f32(winb), T[f"h_win{L}"]), W=[winb.k()], dma=True)
                abb = P.alloc("hy_ab", 2 * ntt * 256 // 2)
                ab = P.bf(abb).rearrange("p (s q n) -> p s q n", s=2, q=ntt)
                accb = P.alloc("hy_acc", 256)
                acc = P.f32(accb)
                op("pool", MSET(acc, 0.0), W=[accb.k()])
                g2b = [P.alloc(f"hy_g2{i}", 512) for i in range(2)]
                abs_b = [P.alloc(f"hy_abs{i}", 512) for i in range(2)]
                for q in range(ntt):
                    b = P.bank()
                    op("pe", MM([(PS[b][:, :], h2T[:, q * 128:(q + 1) * 128], w3s[:, o * 512:(o + 1) * 512], True, True)]),
                       R=[h2b.k(), KP], W=[psk(b)])
                    g2 = P.f32(g2b[q % 2]).rearrange("p (d n) -> p d n", d=2)
                    kg = g2b[q % 2].k()
                    op("dve", TT(g2, PS[b][:, :].rearrange("p (d n) -> p d n", d=2), bc(win[:, q, :], [[0, 2], [1, 256]]), ALU.mult),
                       R=[psk(b), winb.k()], W=[kg])
                    if q == 0:
                        op("pool", MSET(g2[0:1, 1, :], 0.0), R=[kg], W=[kg])
                    op("pool", TT(ab[:, 0, q, :], g2[:, 0, :], g2[:, 1, :], ALU.add), R=[kg], W=[abb.k((0, q))])
                    op("pool", TT(ab[:, 1, q, :], g2[:, 0, :], g2[:, 1, :], ALU.subtract), R=[kg], W=[abb.k((1, q))])
                    av = P.f32(abs_b[q % 2])
                    op("act", ACT(av, P.f32(g2b[q % 2]), AF.Abs), R=[kg], W=[abs_b[q % 2].k()])
                    op("dve", TT(acc, acc, av[:, 0:256], ALU.add), R=[abs_b[q % 2].k(), accb.k()], W=[accb.k()])
                    op("dve", TT(acc, acc, av[:, 256:512], ALU.add), R=[abs_b[q % 2].k(), accb.k()], W=[accb.k()])
                P.free(g2b[0], g2b[1], abs_b[0], abs_b[1], winb)
                rnb = P.alloc("hy_rn", 256 + 256)
                rnrow = P.f32(rnb)[0:1, 0:256]
                RN = P.f32(rnb)[:, 256:512]
                b = P.bank()
                op("pe", MM([(PS[b][0:1, 0:256], onesf[:, 0:1], acc, True, True)]), R=[accb.k(), KC], W=[psk(b)])
                op("dve", TS(rnrow, PS[b][0:1, 0:256], EPS, ALU.add), R=[psk(b)], W=[rnb.k("row")])
                op("dve", RECIP(rnrow, rnrow), R=[rnb.k("row")], W=[rnb.k("row")])
                b = P.bank()
                op("pe", MM([(PS[b][:, 0:256], onesf[0:1, 0:128], rnrow, True, True)]), R=[rnb.k("row"), KC], W=[psk(b)])
                op("act", ACT(RN, PS[b][:, 0:256], AF.Copy), R=[psk(b)], W=[rnb.k()])
                P.free(accb)
                hyt[0] = P.alloc("hy_ftmp", 1024)
                Gb = P.alloc("hy_G", NF * 4 * 256 // 2)
                G = P.bf(Gb).rearrange("p (f q n) -> p f q n", f=NF, q=4)
                ftb_ = [P.alloc(f"hy_ft{i}", 2 * ntt * 128 // 2) for i in range(2)]
                osb = P.alloc("hy_osb", 512)
                for fc in range(NF):
                    tb_ = ftb_[fc % 2]
                    tf = P.bf(tb_).rearrange("p (s q n) -> p s q n", s=2, q=ntt)
                    op("sp", DMA(tf[:, 0], T[f"h_cf{L}"][fc].rearrange("p (q n) -> p q n", q=ntt)), W=[tb_.k()], dma=True)
                    op("sp", DMA(tf[:, 1], T[f"h_sf{L}"][fc].rearrange("p (q n) -> p q n", q=ntt)), W=[tb_.k()], dma=True)
                    bE, bO = P.bank(), P.bank()
                    for (bk_, a_) in ((bE, 0), (bO, 1)):
                        mm = []
                        for j in range(nj):
                            q = a_ * nj + j
                            mm.append((PS[bk_][:, 0:256], tf[:, 0, q, :], ab[:, 0, q, :], j == 0, False))
                            mm.append((PS[bk_][:, 256:512], tf[:, 1, q, :], ab[:, 1, q, :], False, j == nj - 1))
                        op("pe", MM(mm), R=[tb_.k()] + [abb.k((s_, a_ * nj + j)) for s_ in range(2) for j in range(nj)], W=[psk(bk_)])
                    osv = P.f32(osb)
                    op("act", ACT(osv, PS[bO][:, :], AF.Copy), R=[psk(bO)], W=[osb.k()])
                    RN2 = bc(RN, [[0, 2], [1, 256]])
                    tsum = P.f32(osb)
                    tmpb_ = hyt[0]
                    tsv = P.f32(tmpb_)[:, 0:512]
                    tdv = P.f32(tmpb_)[:, 512:1024]
                    op("dve", TT(tsv, PS[bE][:, :], osv, ALU.add), R=[psk(bE), osb.k()], W=[tmpb_.k(0)])
                    op("dve", TT(tdv, PS[bE][:, :], osv, ALU.subtract), R=[psk(bE), osb.k()], W=[tmpb_.k(1)])
                    op("pool", TT(G[:, fc, 0:2, :], tsv.rearrange("p (q n) -> p q n", q=2), RN2, ALU.mult), R=[tmpb_.k(0), rnb.k()], W=[Gb.k(fc)])
                    op("pool", TT(G[:, fc, 2:4, :], tdv.rearrange("p (q n) -> p q n", q=2), RN2, ALU.mult), R=[tmpb_.k(1), rnb.k()], W=[Gb.k(fc)])
                P.free(abb, rnb, ftb_[0], ftb_[1], osb, hyt[0])
                dump(f"G{l}_{L}_{o}", G, [Gb.k(fc) for fc in range(NF)])
                for (t0, nts, latent) in seqs:
                    hyena_conv(l, o, L, t0, G, Gb, ucT, ucb, catT, catb, dsk, KP)
                P.free(Gb)
            P.free(h2b)
        P.free(hpb, ucb)
        for k_ in list(Z1.keys()):
            P.free(Z1.pop(k_)[0])

    hyt = [None]
    Z1 = {}

    def hyena_conv(l, o, L, t0, G, Gb, ucT, ucb, catT, catb, dsk, KP):
        nj = L // 256
        ntt = 2 * nj
        NF = L // 256
        HB = L // 2
        TB = min(512, HB)
        base = t0 * 128
        if o == 0:
            zb_ = P.alloc("hy_z1T", 2 * L // 2)
            Z1[t0] = (zb_, P.bf(zb_).rearrange("p (c n) -> p c n", c=2))
            vin = ucT[:, 0:2, base:base + L]
            kvin = [ucb.k((m, t0)) for m in (0, 1)]
            gate = ucT[:, 2:4, base:base + L]
            kgate = [ucb.k((m, t0)) for m in (2, 3)]
            outv = Z1[t0][1]
            kout = [Z1[t0][0].k(0), Z1[t0][0].k(1)]
        else:
            vin = Z1[t0][1]
            kvin = [Z1[t0][0].k(0), Z1[t0][0].k(1)]
            gate = ucT[:, 4:6, base:base + L]
            kgate = [ucb.k((m, t0)) for m in (4, 5)]
            outv = catT[:, 6:8, base:base + L]
            kout = [catb.k(6), catb.k(7)]
        vtb = P.alloc("hy_vtok", ntt * 256 // 2)
        vt = P.bf(vtb).rearrange("p (q n) -> p q n", q=ntt)
        for q in range(0, ntt, 2):
            b = P.bank()
            mm = []
            for qq in (q, q + 1):
                a_, j = qq // nj, qq % nj
                for cc in range(2):
                    mm.append((PS[b][:, (qq - q) * 256 + cc * 128:(qq - q) * 256 + (cc + 1) * 128],
                               vin[:, cc, a_ + 256 * j:a_ + 256 * j + 255:2], ident, True, True))
            op("pe", MM(mm), R=kvin + [KC], W=[psk(b)])
            ev = "act" if (q // 2) % 2 == 0 else "dve"
            cpy = (lambda o_, i_: ACT(o_, i_, AF.Copy)) if ev == "act" else CP
            op(ev, cpy(vt[:, q:q + 2, :], PS[b][:, :].rearrange("p (q n) -> p q n", q=2)), R=[psk(b)], W=[vtb.k(q // 2)])
        pqb = P.alloc("hy_PQ", NF * 4 * 256 // 2)
        PQ = P.bf(pqb).rearrange("p (f q n) -> p f q n", f=NF, q=4)
        ftb_ = [P.alloc(f"hy_ft{i}", 2 * ntt * 128 // 2) for i in range(2)]
        osb = P.alloc("hy_osb", 512)
        tmb = P.alloc("hy_pw", 512 * 8)
        tm = P.f32(tmb)
        SSv, DDv, T1s, T2s, Av, Bv, T1d, T2d = (tm[:, i * 512:(i + 1) * 512] for i in range(8))
        allvt = [vtb.k(i) for i in range(nj)]
        for fc in range(NF):
            tb_ = ftb_[fc % 2]
            tf = P.bf(tb_).rearrange("p (s q n) -> p s q n", s=2, q=ntt)
            op("sp", DMA(tf[:, 0], T[f"h_cf{L}"][fc].rearrange("p (q n) -> p q n", q=ntt)), W=[tb_.k()], dma=True)
            op("sp", DMA(tf[:, 1], T[f"h_sf{L}"][fc].rearrange("p (q n) -> p q n", q=ntt)), W=[tb_.k()], dma=True)
            bE, bO = P.bank(), P.bank()
            for (bk_, a_) in ((bE, 0), (bO, 1)):
                mm = []
                for j in range(nj):
                    q = a_ * nj + j
                    mm.append((PS[bk_][:, 0:256], tf[:, 0, q, :], vt[:, q, :], j == 0, False))
                    mm.append((PS[bk_][:, 256:512], tf[:, 1, q, :], vt[:, q, :], False, j == nj - 1))
                op("pe", MM(mm), R=[tb_.k()] + allvt, W=[psk(bk_)])
            osv = P.f32(osb)
            op("act", ACT(osv, PS[bO][:, :], AF.Copy), R=[psk(bO)], W=[osb.k()])
            op("dve", TT(SSv, PS[bE][:, :], osv, ALU.add), R=[psk(bE), osb.k()], W=[tmb.k("S")])
            op("dve", TT(DDv, PS[bE][:, :], osv, ALU.subtract), R=[psk(bE), osb.k()], W=[tmb.k("D")])
            e1, e2 = ("dve", "pool") if fc % 2 == 0 else ("pool", "dve")
            for (X, gr, gi, dst, kx, e_, T1, T2) in ((SSv, 0, 1, Av, "S", e1, T1s, T2s), (DDv, 2, 3, Bv, "D", e2, T1d, T2d)):
                X2 = X.rearrange("p (q n) -> p q n", q=2)
                op(e_, TT(T1.rearrange("p (q n) -> p q n", q=2), X2, bc(G[:, fc, gr, :], [[0, 2], [1, 256]]), ALU.mult),
                   R=[tmb.k(kx), Gb.k(fc)], W=[tmb.k("T1" + kx)])
                op(e_, TT(T2.rearrange("p (q n) -> p q n", q=2), X2, bc(G[:, fc, gi, :], [[0, 2], [1, 256]]), ALU.mult),
                   R=[tmb.k(kx), Gb.k(fc)], W=[tmb.k("T2" + kx)])
                op(e_, TT(dst[:, 0:256], T1[:, 0:256], T2[:, 256:512], ALU.subtract), R=[tmb.k("T1" + kx), tmb.k("T2" + kx)], W=[tmb.k("A" + kx)])
                op(e_, TT(dst[:, 256:512], T2[:, 0:256], T1[:, 256:512], ALU.add), R=[tmb.k("T1" + kx), tmb.k("T2" + kx)], W=[tmb.k("A" + kx)])
            op("dve", TT(PQ[:, fc, 0:2, :], Av.rearrange("p (q n) -> p q n", q=2), Bv.rearrange("p (q n) -> p q n", q=2), ALU.add),
               R=[tmb.k("AS"), tmb.k("AD")], W=[pqb.k(fc)])
            op("pool", TT(PQ[:, fc, 2:4, :], Av.rearrange("p (q n) -> p q n", q=2), Bv.rearrange("p (q n) -> p q n", q=2), ALU.subtract),
               R=[tmb.k("AS"), tmb.k("AD")], W=[pqb.k(fc)])
        P.free(vtb, ftb_[0], ftb_[1], osb, tmb)
        itb = [P.alloc(f"hy_it{i}", 2 * NF * TB // 2) for i in range(2)]
        ytb = [P.alloc(f"hy_yt{i}", TB) for i in range(2)]
        CI4 = T[f"h_ci{L}"].rearrange("p (f a n) -> p f a n", f=NF, a=2)
        SI4 = T[f"h_si{L}"].rearrange("p (f a n) -> p f a n", f=NF, a=2)
        ii = 0
        allpq = [pqb.k(fc) for fc in range(NF)]
        for a_ in range(2):
            for tch in range(HB // TB):
                ib = itb[ii % 2]
                it = P.bf(ib).rearrange("p (s f n) -> p s f n", s=2, f=NF)
                op("sp", DMA(it[:, 0], CI4[:, :, a_, tch * TB:(tch + 1) * TB]), W=[ib.k()], dma=True)
                op("sp", DMA(it[:, 1], SI4[:, :, a_, tch * TB:(tch + 1) * TB]), W=[ib.k()], dma=True)
                ii += 1
                for cc in range(2):
                    b = P.bank()
                    mm = []
                    for fc in range(NF):
                        mm.append((PS[b][:, 0:TB], PQ[:, fc, 2 * a_, cc * 128:(cc + 1) * 128], it[:, 0, fc, :], fc == 0, False))
                        mm.append((PS[b][:, 0:TB], PQ[:, fc, 2 * a_ + 1, cc * 128:(cc + 1) * 128], it[:, 1, fc, :], False, fc == NF - 1))
                    op("pe", MM(mm), R=allpq + [ib.k()], W=[psk(b)])
                    tstart = a_ + 2 * tch * TB
                    sl = slice(tstart, tstart + 2 * TB - 1, 2)
                    yt = P.f32(ytb[cc])[:, 0:TB]
                    op("dve", STT(yt, vin[:, cc, sl], dsk[:, o, cc:cc + 1], PS[b][:, 0:TB], ALU.mult, ALU.add),
                       R=[psk(b), kvin[cc], KP], W=[ytb[cc].k()])
                    op("pool", TT(outv[:, cc, sl], yt, gate[:, cc, sl], ALU.mult), R=[ytb[cc].k(), kgate[cc]], W=[kout[cc]])
        P.free(pqb, itb[0], itb[1], ytb[0], ytb[1])


    return dict(nc=nc, P=P, T=T, DBG=DBG, layer_mod=layer_mod, ffn=ffn, mixer=mixer, loc=locals())


def build_full(dbg=()):
    B = build(dbg=dbg)
    T = B["T"]
    for l in range(2):
        B["layer_mod"](l)
        B["mixer"](l, T["xin"] if l == 0 else T["xb"], T["xa"])
        B["ffn"](l, T["xa"], T["xb"] if l == 0 else T["y"])
    B["P"].emit()
    return B


_NC_CACHE = {}


def core_inputs(core, inp, consts):
    m = {nm: np.ascontiguousarray(inp[nm], dtype=np.float32) for nm, _ in WEIGHT_SPECS}
    m.update(consts)
    m["xin"] = np.ascontiguousarray(np.concatenate(
        [inp["x_sample"][core], inp["x_prompt"][2 * core], inp["x_prompt"][2 * core + 1]], 0), dtype=np.float32)
    m["ckv_ctx"] = np.ascontiguousarray(inp["cache_ckv"][core], dtype=np.float32)
    m["kr_ctx"] = np.ascontiguousarray(inp["cache_krope"][core], dtype=np.float32)
    m["s0"] = np.ascontiguousarray(inp["state_ret"][core], dtype=np.float32)
    m["cvec"] = np.ascontiguousarray(np.stack([inp["c_ctx"], inp["c"][core]]), dtype=np.float32)
    return m


def kernel(**inputs):
    inp = {k: np.asarray(v) for k, v in inputs.items()}
    consts = get_consts()
    if "nc" not in _NC_CACHE:
        _NC_CACHE["nc"] = build_full()["nc"]
    nc = _NC_CACHE["nc"]
    in_maps = [core_inputs(c, inp, consts) for c in range(8)]
    res = run_bass_kernel_spmd(nc, in_maps, core_ids=list(range(8)))
    R = res.results
    y_prompt = np.zeros((16, 256, 1024), np.float32)
    y_sample = np.zeros((8, 2048, 1024), np.float32)
    new_ckv = np.zeros((16, 2, 256, 128), np.float32)
    new_kr = np.zeros((16, 2, 256, 32), np.float32)
    new_st = np.zeros((16, 2, 2, 4, 64, 64), np.float32)
    for c in range(8):
        y = R[c]["y"]
        y_sample[c] = y[0:2048]
        y_prompt[2 * c] = y[2048:2304]
        y_prompt[2 * c + 1] = y[2304:2560]
        new_ckv[2 * c:2 * c + 2] = R[c]["o_ckv"]
        new_kr[2 * c:2 * c + 2] = R[c]["o_kr"]
        new_st[2 * c:2 * c + 2] = R[c]["o_st"]
    return (y_prompt, y_sample, new_ckv, new_kr, new_st)
```

```python
import contextlib
import math
import numpy as np
import ml_dtypes
import concourse.bass as bass
import concourse.mybir as mybir
from concourse.bass_utils import run_bass_kernel_spmd

F32 = mybir.dt.float32
BF16 = mybir.dt.bfloat16
ALU = mybir.AluOpType
AF = mybir.ActivationFunctionType
AX = mybir.AxisListType
NPBF = ml_dtypes.bfloat16

NDMA_SLOTS = 8
D = 1024
DFF = 2816
NFF = 22
NT = 20
EPS = 1e-6
SEQS = [(0, 16, True), (16, 2, False), (18, 2, False)]


class Buf:
    def __init__(self, name, col0, ncols):
        self.name, self.col0, self.ncols = name, col0, ncols
        self.subs = set()
        self.inherit = set()

    def k(self, sub=None):
        self.subs.add(sub)
        return (self, sub)


class Prog:
    def __init__(self, nc):
        self.nc = nc
        self.ops = []
        self.last_writer = {}
        self.readers = {}
        self.stack = contextlib.ExitStack()
        self.live = []
        self.dead = []
        self.psn = 0

    def make_arena(self, ncols):
        self.arena = self.stack.enter_context(self.nc.sbuf_tensor("arena", [128, ncols], F32))
        self.arena_cols = ncols
        self.ps = [self.stack.enter_context(self.nc.psum_tensor(f"ps{i}", [128, 512], F32)) for i in range(8)]

    def bank(self):
        i = self.psn % 8
        self.psn += 1
        return i

    def alloc(self, name, ncols):
        ncols = int(math.ceil(ncols))
        segs = sorted((b.col0, b.ncols) for b in self.live)
        pos, found = 0, None
        for c0, n in segs:
            if c0 - pos >= ncols:
                found = pos
                break
            pos = max(pos, c0 + n)
        if found is None:
            if self.arena_cols - pos >= ncols:
                found = pos
            else:
                raise RuntimeError(f"arena OOM {name} {ncols}: live={[(b.name, b.ncols) for b in self.live]}")
        b = Buf(name, found, ncols)
        self.live.append(b)
        for ob in self.dead:
            if ob.col0 < found + ncols and found < ob.col0 + ob.ncols:
                for sk in ob.subs:
                    kk = (ob, sk)
                    w = self.last_writer.get(kk)
                    if w is not None:
                        b.inherit.add(w)
                    b.inherit.update(self.readers.get(kk, ()))
                b.inherit.update(ob.inherit)
        return b

    def free(self, *bs):
        for b in bs:
            self.live.remove(b)
            self.dead.append(b)

    def f32(self, b, p0=0, p1=128):
        return self.arena[p0:p1, b.col0:b.col0 + b.ncols]

    def bf(self, b, p0=0, p1=128):
        return self.arena[p0:p1, b.col0:b.col0 + b.ncols].bitcast(BF16)

    def op(self, eng, fn, R=(), W=(), dma=False):
        idx = len(self.ops)
        deps = set()
        for k in list(R) + list(W):
            if isinstance(k, tuple) and isinstance(k[0], Buf) and k[0].inherit:
                deps.update(k[0].inherit)
        for k in R:
            w = self.last_writer.get(k)
            if w is not None:
                deps.add(w)
        for k in W:
            w = self.last_writer.get(k)
            if w is not None:
                deps.add(w)
            deps.update(self.readers.get(k, ()))
        for k in R:
            self.readers.setdefault(k, []).append(idx)
        for k in W:
            self.last_writer[k] = idx
            self.readers[k] = []
        self.ops.append(dict(eng=eng, fn=fn, deps=deps, dma=dma, signal=False))
        return idx

    def emit(self):
        nc, ops = self.nc, self.ops
        engs = ["pe", "act", "dve", "pool", "sp"]
        per = {e: [] for e in engs}
        for i, o in enumerate(ops):
            per[o["eng"]].append(i)
        for e in engs:
            seen = {pe_: -1 for pe_ in engs}
            for i in per[e]:
                o = ops[i]
                need = {}
                o["wdeps"] = []
                for d in o["deps"]:
                    od = ops[d]
                    if od["dma"]:
                        o["wdeps"].append(d)
                    else:
                        need[od["eng"]] = max(need.get(od["eng"], -1), d)
                for pe_, d in need.items():
                    if d > seen[pe_]:
                        seen[pe_] = d
                        o["wdeps"].append(d)
                        ops[d]["signal"] = True
        sems = {e: self.stack.enter_context(nc.semaphore("s_" + e)) for e in engs}
        dsems = {e: [self.stack.enter_context(nc.semaphore(f"d_{e}{i}")) for i in range(NDMA_SLOTS)]
                 for e in ("sp", "pool", "act")}
        cnt = {e: 0 for e in engs}
        dcnt = {e: 0 for e in dsems}
        for o in ops:
            e = o["eng"]
            if o["dma"]:
                j = dcnt[e]
                dcnt[e] += 1
                o["sem"] = dsems[e][j % NDMA_SLOTS]
                o["val"] = 16 * (j // NDMA_SLOTS + 1)
                o["prev"] = 16 * (j // NDMA_SLOTS)
            elif o["signal"]:
                cnt[e] += 1
                o["sem"] = sems[e]
                o["val"] = cnt[e]
        nw = [0]

        def run_engine(ename, eobj):
            waited = {}
            for i in per[ename]:
                o = ops[i]
                wl = {}
                for d in o["wdeps"]:
                    od = ops[d]
                    s = od["sem"]
                    if od["val"] > wl.get(s.name, (s, 0))[1]:
                        wl[s.name] = (s, od["val"])
                if o["dma"] and o["prev"] > 0:
                    s = o["sem"]
                    if o["prev"] > wl.get(s.name, (s, 0))[1]:
                        wl[s.name] = (s, o["prev"])
                for nm, (s, v) in wl.items():
                    if waited.get(nm, 0) >= v:
                        continue
                    eobj.wait_ge(s, v)
                    nw[0] += 1
                    waited[nm] = v
                ins = o["fn"](eobj)
                if o["dma"]:
                    ins.then_inc(o["sem"], 16)
                elif o["signal"]:
                    ins.then_inc(o["sem"], 1)
            if ename in dsems:
                n = dcnt[ename]
                for slot in range(NDMA_SLOTS):
                    k = (n - slot + NDMA_SLOTS - 1) // NDMA_SLOTS
                    if k > 0 and waited.get(dsems[ename][slot].name, 0) < 16 * k:
                        eobj.wait_ge(dsems[ename][slot], 16 * k)

        with nc.Block() as block:
            @block.tensor
            def _(e):
                run_engine("pe", e)

            @block.scalar
            def _(e):
                run_engine("act", e)

            @block.vector
            def _(e):
                run_engine("dve", e)

            @block.gpsimd
            def _(e):
                run_engine("pool", e)

            @block.sync
            def _(e):
                run_engine("sp", e)
        self.stats = dict(nops=len(ops), nwaits=nw[0], cnt=cnt, dcnt=dcnt)


def MM(specs):
    def f(e):
        ins = None
        for (o, l, r, st, sp) in specs:
            ins = e.matmul(o, lhsT=l, rhs=r, start=st, stop=sp)
        return ins
    return f


def ACT(out, in_, func, bias=None, scale=1.0, accum=None):
    def f(e):
        kw = {}
        if bias is not None:
            kw["bias"] = bias
        if accum is not None:
            kw["accum_out"] = accum
        return e.activation(out=out, in_=in_, func=func, scale=scale, **kw)
    return f


def TT(out, a, b, op):
    return lambda e: e.tensor_tensor(out=out, in0=a, in1=b, op=op)


def TS(out, a, s1, op0, s2=None, op1=None):
    if op1 is None:
        return lambda e: e.tensor_scalar(out=out, in0=a, scalar1=s1, scalar2=None, op0=op0)
    return lambda e: e.tensor_scalar(out=out, in0=a, scalar1=s1, scalar2=s2, op0=op0, op1=op1)


def STT(out, a, s, b, op0, op1):
    return lambda e: e.scalar_tensor_tensor(out=out, in0=a, scalar=s, in1=b, op0=op0, op1=op1)


def CP(out, in_):
    return lambda e: e.tensor_copy(out=out, in_=in_)


def RED(out, in_, op=None):
    return lambda e: e.tensor_reduce(out=out, in_=in_, axis=AX.X, op=op or ALU.add)


def RECIP(out, in_):
    return lambda e: e.reciprocal(out=out, in_=in_)


def MSET(out, v):
    return lambda e: e.memset(out, v)


def DMA(out, in_):
    return lambda e: e.dma_start(out=out, in_=in_)


def DMAS(out, in_):
    return lambda e: e.dma_start(out=out, in_=in_, allow_slow_non_contiguous=True)


def bc(ap, dims):
    return bass.AP(ap.tensor, ap.offset, [list(ap.ap[0])] + [list(d) for d in dims])


def parity_perm(L):
    nj = L // 256
    idx = np.zeros((2, nj, 128), np.int64)
    for pi in range(2):
        for j in range(nj):
            idx[pi, j] = pi + 2 * (128 * j + np.arange(128))
    return idx


def host_consts():
    C = {}
    C["ident"] = np.eye(128, dtype=np.float32).astype(NPBF)
    tok = np.arange(2048)
    row, col = tok // 64, tok % 64
    for nm, half in (("ret", 16), ("mla", 8)):
        inv = 10000.0 ** (-np.arange(half, dtype=np.float64) / half)
        ang = np.stack([row[:, None] * inv[None], col[:, None] * inv[None]], axis=1)
        ang = ang.reshape(16, 128, 2, half).transpose(1, 0, 2, 3).reshape(128, 16 * 2 * half)
        C["cos_" + nm] = np.cos(ang).astype(np.float32)
        C["sin_" + nm] = np.sin(ang).astype(np.float32)
    m = np.arange(128)[:, None].astype(np.float64)
    c = np.arange(128)[None, :].astype(np.float64)
    C["ret_dpos"] = np.tile(np.maximum(c - m, 0), (1, 4)).astype(np.float32)
    C["ret_dneg"] = np.tile(np.maximum(m - c, 0), (1, 4)).astype(np.float32)
    C["ret_mge"] = np.tile((c >= m) * 0.125, (1, 4)).astype(np.float32)
    C["ret_mle"] = np.tile((c <= m) * 0.125, (1, 4)).astype(np.float32)
    p = np.arange(128, dtype=np.float64)
    C["ret_cols"] = np.stack([p + 1, 127 - p, 128 - p, p], axis=1).astype(np.float32)
    a = 2 * np.pi * np.outer(np.arange(64), np.arange(64)) / 64
    C64 = np.kron(np.eye(2), np.cos(a))
    S64 = np.kron(np.eye(2), np.sin(a))
    C["f_c64"] = C64.astype(NPBF)
    C["f_s64n"] = (-S64).astype(NPBF)
    for L in (2048, 256):
        idx = parity_perm(L)
        nj = L // 256
        l = idx.astype(np.float64)
        pp = np.arange(L // 2, dtype=np.float64)
        ang = 2 * np.pi * l[..., None] * pp / L
        sc = 1.0 / math.sqrt(L * 64)
        C[f"f_cl{L}"] = (np.cos(ang) * sc).transpose(2, 0, 1, 3).reshape(128, -1).astype(NPBF)
        C[f"f_sl{L}"] = (np.sin(ang) * sc).transpose(2, 0, 1, 3).reshape(128, -1).astype(NPBF)
        pos = idx.reshape(-1).astype(np.float64)
        t = pos / L
        bands = np.arange(1, 17, dtype=np.float64)
        ang2 = (2 * np.pi / L) * pos[:, None] * bands[None]
        z = np.concatenate([t[:, None], np.sin(ang2), np.cos(ang2)], axis=1)
        C[f"h_zT{L}"] = np.ascontiguousarray(z.T).astype(np.float32)
        deltas = np.abs(np.linspace(math.log(1e-2) / 1.5, math.log(1e-2) / 0.3, 256))
        win = np.exp(-t[:, None] * deltas[None])
        C[f"h_win{L}"] = win.reshape(2 * nj, 128, 256).transpose(1, 0, 2).reshape(128, -1).astype(np.float32)
        N = 2 * L
        nf = L // 256 if L >= 256 else 1
        F = L // 2
        f = np.arange(F, dtype=np.float64) + 0.5
        s = idx.astype(np.float64)
        psi = 2 * np.pi * s[..., None] * f / N
        fch = F // 128
        cf = np.cos(psi).reshape(2, nj, 128, fch, 128).transpose(3, 2, 0, 1, 4).reshape(fch, 128, -1)
        sf = (-np.sin(psi)).reshape(2, nj, 128, fch, 128).transpose(3, 2, 0, 1, 4).reshape(fch, 128, -1)
        C[f"h_cf{L}"] = cf.astype(NPBF)
        C[f"h_sf{L}"] = sf.astype(NPBF)
        tt = np.stack([2 * np.arange(L // 2), 2 * np.arange(L // 2) + 1]).astype(np.float64)
        psi2 = 2 * np.pi * f[:, None, None] * tt[None] / N
        ci = (2.0 / N) * np.cos(psi2)
        si = -(2.0 / N) * np.sin(psi2)
        C[f"h_ci{L}"] = ci.reshape(fch, 128, 2, L // 2).transpose(1, 0, 2, 3).reshape(128, -1).astype(NPBF)
        C[f"h_si{L}"] = si.reshape(fch, 128, 2, L // 2).transpose(1, 0, 2, 3).reshape(128, -1).astype(NPBF)
    return C


_CONSTS = None


def get_consts():
    global _CONSTS
    if _CONSTS is None:
        _CONSTS = host_consts()
    return _CONSTS


WEIGHT_SPECS = [
    ("w_ada", (2, 1024, 6144)), ("b_ada", (2, 6144)), ("norm_g", (2, 4, 1024)), ("w_in", (2, 1024, 2464)),
    ("w_out", (2, 1024, 1024)), ("ret_decay", (2, 2, 4)), ("mla_q_norm", (2, 256)), ("mla_kv_norm", (2, 128)),
    ("mla_w_uq", (2, 256, 384)), ("mla_w_ukv", (2, 128, 512)), ("hy_short_w", (2, 3, 768)),
    ("hy_short_b", (2, 768)), ("hy_w1", (2, 33, 64)), ("hy_b1", (2, 64)), ("hy_w2", (2, 64, 64)),
    ("hy_b2", (2, 64)), ("hy_w3", (2, 64, 1024)), ("hy_bias", (2, 2, 256)), ("w_gate", (2, 1024, 2816)),
    ("w_up", (2, 1024, 2816)), ("w_down", (2, 2816, 1024)),
]
CORE_SPECS = [("xin", (2560, 1024)), ("ckv_ctx", (2, 256, 128)), ("kr_ctx", (2, 256, 32)),
              ("s0", (2, 2, 4, 64, 64)), ("cvec", (2, 1024))]
OUT_SPECS = [("y", (2560, 1024)), ("o_ckv", (2, 2, 256, 128)), ("o_kr", (2, 2, 256, 32)),
             ("o_st", (2, 2, 2, 4, 64, 64))]


def build(dbg=(), stop_after=None):
    nc = bass.Bass("TRN2", target_bir_lowering=False)
    P = Prog(nc)
    C = get_consts()
    T = {}
    for nm, shp in WEIGHT_SPECS + CORE_SPECS:
        T[nm] = nc.dram_tensor(nm, list(shp), F32, kind="ExternalInput").ap()
    for nm, arr in C.items():
        T[nm] = nc.dram_tensor(nm, list(arr.shape), BF16 if arr.dtype == NPBF else F32, kind="ExternalInput").ap()
    for nm, shp in OUT_SPECS:
        T[nm] = nc.dram_tensor(nm, list(shp), F32, kind="ExternalOutput").ap()
    T["xa"] = nc.dram_tensor("xa", [2560, 1024], F32, kind="Internal").ap()
    T["xb"] = nc.dram_tensor("xb", [2560, 1024], F32, kind="Internal").ap()
    DBG = {}

    P.make_arena(53184)
    PS = P.ps
    op = P.op

    def psk(i):
        return ("ps", i)

    def dump(name, ap, keys, shape=None):
        if name not in dbg:
            return
        shape = list(shape or ap.shape)
        d = nc.dram_tensor("dbg_" + name, shape, F32, kind="ExternalOutput").ap()
        DBG[name] = d
        if len(shape) == 3:
            for i_ in range(shape[1]):
                op("pool", DMA(d[:, i_, :], ap[:, i_, :]), R=keys, dma=True)
        else:
            op("pool", DMA(d, ap), R=keys, dma=True)

    cb = P.alloc("consts", 64 + 1 + 8 + 64 * 3 + 128)
    cw = P.f32(cb)
    o = [0]

    def take(n, dt=F32, src=None):
        src = cw if src is None else src
        v = src[:, o[0]:o[0] + n]
        o[0] += n
        return v.bitcast(BF16) if dt == BF16 else v
    ident = take(64, BF16)
    epsc = take(1)
    cols8 = take(8)
    c64, s64n, onesb = take(64, BF16), take(64, BF16), take(64, BF16)
    onesf = take(128)
    KC = cb.k()
    for dst, nm in ((ident, "ident"), (c64, "f_c64"), (s64n, "f_s64n")):
        op("sp", DMA(dst, T[nm]), W=[KC], dma=True)
    op("pool", MSET(epsc, EPS), W=[KC])
    op("pool", MSET(cols8[:, 0:1], -math.pi), W=[KC])
    op("pool", MSET(onesb, 1.0), W=[KC])
    op("pool", MSET(onesf, 1.0), W=[KC])
    MC = {}

    def mixer_consts():
        mb = P.alloc("mconsts", 4 * 512 + 4 + 2 * 512 + 2 * 256)
        o[0] = 0
        mw = P.f32(mb)
        for nm, n in (("ret_dpos", 512), ("ret_dneg", 512), ("ret_mge", 512), ("ret_mle", 512), ("ret_cols", 4),
                      ("cos_ret", 512), ("sin_ret", 512), ("cos_mla", 256), ("sin_mla", 256)):
            MC[nm] = take(n, src=mw)
            op("sp", DMA(MC[nm], T[nm]), W=[mb.k()], dma=True)
        MC["buf"] = mb
        MC["key"] = mb.k()

    mcolb = P.alloc("modcols", 2 * 4 * 8)
    mcol = P.f32(mcolb).rearrange("p (r q k) -> p r q k", r=2, q=4)
    gbb = P.alloc("gbc", 4 * 1024)
    gbc = P.f32(gbb).rearrange("p (r q n) -> p r q n", r=2, q=2)
    PR = (0, 32)

    def layer_mod(l):
        rb = P.alloc("rows", 6144 + 4096 + 6144)
        rows = P.f32(rb)[0:33, :]
        m = rows[:, 0:6144]
        ngr = rows[:, 6144:10240]
        rowt = rows[:, 10240:16384]
        KR = rb.k()
        scb = P.alloc("silu_c", 8 * 34 // 2)
        sct = P.bf(scb).rearrange("p (k r) -> p k r", r=34)
        cfb = P.alloc("c_f32", 16)
        cf32 = P.f32(cfb).rearrange("p (k r) -> p k r", r=2)
        for r in range(2):
            op("sp", DMAS(cf32[:, :, r:r + 1], bass.AP(T["cvec"].tensor, r * 1024, [[1, 128], [128, 8], [1, 1]])),
               W=[cfb.k()], dma=True)
        op("pool", MSET(sct, 0.0), W=[scb.k()])
        for r in range(2):
            op("act", ACT(sct[:, :, PR[r]:PR[r] + 1], cf32[:, :, r:r + 1], AF.Silu), R=[cfb.k()], W=[scb.k()])
        op("sp", DMA(ngr, bass.AP(T["norm_g"].tensor, l * 4096, [[0, 33], [1, 4096]])), W=[KR], dma=True)
        wab = [P.alloc(f"wada{i}", 8 * 512 // 2) for i in range(2)]
        badb = P.alloc("bada", 2 * 512)
        for nb in range(12):
            wb_ = wab[nb % 2]
            par = nb % 2
            wv = P.bf(wb_).rearrange("p (k n) -> p k n", k=8)
            op("pool", DMA(wv, T["w_ada"][l][:, nb * 512:(nb + 1) * 512].rearrange("(k p) n -> p k n", p=128)),
               W=[wb_.k()], dma=True)
            bv = P.f32(badb)[0:33, par * 512:par * 512 + 512]
            op("sp", DMA(bv, bass.AP(T["b_ada"].tensor, l * 6144 + nb * 512, [[0, 33], [1, 512]])),
               W=[badb.k(par)], dma=True)
            b = P.bank()
            op("pe", MM([(PS[b][0:33, :], sct[:, k, 0:33], wv[:, k, :], k == 0, k == 7) for k in range(8)]),
               R=[scb.k(), wb_.k()], W=[psk(b)])
            op("dve", TT(m[:, nb * 512:(nb + 1) * 512], PS[b][0:33, :], bv, ALU.add),
               R=[psk(b), badb.k(par)], W=[KR])
        for r in range(2):
            dump(f"mod{l}{r}", m[PR[r]:PR[r] + 1, :], [KR])
        op("dve", STT(rowt[:, 0:1024], m[:, 1024:2048], 1.0, ngr[:, 0:1024], ALU.add, ALU.mult), R=[KR], W=[KR])
        op("dve", CP(rowt[:, 1024:2048], m[:, 0:1024]), R=[KR], W=[KR])
        op("dve", STT(rowt[:, 2048:3072], m[:, 4096:5120], 1.0, ngr[:, 2048:3072], ALU.add, ALU.mult), R=[KR], W=[KR])
        op("dve", CP(rowt[:, 3072:4096], m[:, 3072:4096]), R=[KR], W=[KR])
        op("dve", TT(rowt[:, 4096:5120], m[:, 2048:3072], ngr[:, 1024:2048], ALU.mult), R=[KR], W=[KR])
        op("dve", TT(rowt[:, 5120:6144], m[:, 5120:6144], ngr[:, 3072:4096], ALU.mult), R=[KR], W=[KR])
        for r in range(2):
            pr = PR[r]
            b = P.bank()
            op("pe", MM([(PS[b][:, q * 8 + k:q * 8 + k + 1], rowt[pr:pr + 1, q * 1024 + k * 128:q * 1024 + (k + 1) * 128],
                          onesf[pr:pr + 1, 0:1], True, True) for q in range(4) for k in range(8)]),
               R=[KR, KC], W=[psk(b)])
            op("dve", CP(mcol[:, r], PS[b][:, 0:32].rearrange("p (q k) -> p q k", q=4)), R=[psk(b)], W=[mcolb.k()])
            for q in range(2):
                for hf in range(2):
                    b = P.bank()
                    c0 = (4 + q) * 1024 + hf * 512
                    op("pe", MM([(PS[b][:, :], onesf[pr:pr + 1, 0:128], rowt[pr:pr + 1, c0:c0 + 512], True, True)]),
                       R=[KR, KC], W=[psk(b)])
                    op("act", ACT(gbc[:, r, q, hf * 512:(hf + 1) * 512], PS[b][:, :], AF.Copy), R=[psk(b)], W=[gbb.k()])
        P.free(rb, scb, cfb, wab[0], wab[1], badb)

    def rstd_from_ss(ss, out, n, keys_r, keys_w, tmp):
        op("act", ACT(tmp, ss, AF.Sqrt, bias=epsc[0:ss.shape[0], :], scale=1.0 / n), R=keys_r + [KC], W=[keys_w[1]])
        op("dve", RECIP(out, tmp), R=[keys_w[1]], W=[keys_w[0]])

    xtb = [None, None]
    xnb = [P.alloc(f"xn{i}", 512) for i in range(2)]
    stb = P.alloc("stats", 64)
    stv = P.f32(stb)
    tcount = [0]

    def norm_transpose(src, tile, r, q0, dstT, dcol, xkeep=None):
        i = tcount[0] % 2
        tcount[0] += 1
        xt = P.f32(xtb[i]) if xkeep is None else xkeep[0]
        kx = xtb[i].k() if xkeep is None else xkeep[1]
        op("sp", DMA(xt, src[tile * 128:(tile + 1) * 128, :]), R=[("dram", src.tensor.name, tile)], W=[kx], dma=True)
        ss, rs, tm = stv[:, i * 4:i * 4 + 1], stv[:, i * 4 + 1:i * 4 + 2], stv[:, i * 4 + 2:i * 4 + 3]
        op("act", ACT(P.bf(xnb[i]), xt, AF.Square, accum=ss), R=[kx], W=[xnb[i].k(), stb.k(("ss", i))])
        rstd_from_ss(ss, rs, 1024.0, [stb.k(("ss", i))], [stb.k(("rs", i)), stb.k(("tm", i))], tm)
        xn = P.bf(xnb[i])
        op("dve", TS(xn, xt, rs, ALU.mult), R=[kx, stb.k(("rs", i))], W=[xnb[i].k()])
        for half in range(2):
            b = P.bank()
            op("pe", MM([(PS[b][:, j * 128:(j + 1) * 128], xn[:, (half * 4 + j) * 128:(half * 4 + j + 1) * 128], ident, True, True)
                         for j in range(4)]), R=[xnb[i].k(), KC], W=[psk(b)])
            for j in range(4):
                kc = half * 4 + j
                o_ = dstT[:, kc, dcol:dcol + 128]
                src_ps = PS[b][:, j * 128:(j + 1) * 128]
                A, B = mcol[:, r, q0, kc:kc + 1], mcol[:, r, q0 + 1, kc:kc + 1]
                if half == 0:
                    op("act", ACT(o_, src_ps, AF.Identity, bias=B, scale=A), R=[psk(b), mcolb.k()], W=[dstT_key[0]])
                else:
                    op("dve", TS(o_, src_ps, A, ALU.mult, B, ALU.add), R=[psk(b), mcolb.k()], W=[dstT_key[0]])

    dstT_key = [None]
    J2 = [None]
    ROPEB = [None]

    def resid_update(ps2, tile, r, q, xt, kx, dst, ri):
        junk2b = J2[0]
        ssa, ssb, ss, rs, tm = (stv[:, 16 + ri * 8 + j:16 + ri * 8 + j + 1] for j in range(5))
        kk = stb.k(("ru", ri))
        for hf, sx in ((0, ssa), (1, ssb)):
            op("act", ACT(P.f32(junk2b)[:, hf * 512:(hf + 1) * 512], PS[ps2[hf]][:, :], AF.Square, accum=sx),
               R=[psk(ps2[hf])], W=[junk2b.k(hf), kk])
        op("dve", TT(ss, ssa, ssb, ALU.add), R=[kk], W=[kk])
        op("act", ACT(tm, ss, AF.Sqrt, bias=epsc, scale=1.0 / 1024), R=[kk, KC], W=[kk])
        op("dve", RECIP(rs, tm), R=[kk], W=[kk])
        for hf in range(2):
            tmp = P.f32(junk2b)[:, hf * 512:(hf + 1) * 512]
            op("dve", STT(tmp, PS[ps2[hf]][:, :], rs, gbc[:, r, q, hf * 512:(hf + 1) * 512], ALU.mult, ALU.mult),
               R=[psk(ps2[hf]), kk, gbb.k()], W=[junk2b.k(hf)])
            op("pool", TT(xt[:, hf * 512:(hf + 1) * 512], tmp, xt[:, hf * 512:(hf + 1) * 512], ALU.add),
               R=[junk2b.k(hf), kx], W=[kx])
        op("sp", DMA(dst[tile * 128:(tile + 1) * 128, :], xt), R=[kx], W=[("dram", dst.tensor.name, tile)], dma=True)

    junk2b = None

    def ffn(l, src, dst):
        wgb = P.alloc("wg", 8 * DFF // 2)
        wub = P.alloc("wu", 8 * DFF // 2)
        wdb = P.alloc("wd", NFF * 1024 // 2)
        wg = P.bf(wgb).rearrange("p (k n) -> p k n", k=8)
        wu = P.bf(wub).rearrange("p (k n) -> p k n", k=8)
        wd = P.bf(wdb).rearrange("p (k n) -> p k n", k=NFF)
        FB = 4
        for f0 in range(0, NFF, FB):
            f1 = min(NFF, f0 + FB)
            for (wv_, nm_, wb__) in ((wg, "w_gate", wgb), (wu, "w_up", wub)):
                op("pool", DMA(wv_[:, :, f0 * 128:f1 * 128], T[nm_][l][:, f0 * 128:f1 * 128].rearrange("(k p) n -> p k n", p=128)),
                   W=[wb__.k(f0 // FB)], dma=True)
        for k in range(0, NFF, 2):
            op("pool", DMA(wd[:, k:k + 2, :], T["w_down"][l][k * 128:(k + 2) * 128, :].rearrange("(k p) n -> p k n", p=128)),
               W=[wdb.k(k // 2)], dma=True)
        h2b = P.alloc("h2T", 8 * 512 // 2)
        h2T = P.bf(h2b).rearrange("p (k n) -> p k n", k=8)
        aTb = P.alloc("aT", NFF * 512 // 2)
        aT = P.bf(aTb).rearrange("p (k n) -> p k n", k=NFF)
        xgb = P.alloc("xgrp", 4 * 1024)
        sgb = [P.alloc(f"sg{i}", 256) for i in range(2)]
        J2[0] = P.alloc("junk2", 1024)
        for g in range(5):
            r = 1 if g < 4 else 0
            dstT_key[0] = h2b.k()
            for j in range(4):
                tile = g * 4 + j
                xt = P.f32(xgb)[:, j * 1024:(j + 1) * 1024]
                norm_transpose(src, tile, r, 2, h2T, j * 128, xkeep=(xt, xgb.k(j)))
            for fc in range(NFF):
                bg, bu = P.bank(), P.bank()
                op("pe", MM([(PS[bg][:, :], wg[:, k, fc * 128:(fc + 1) * 128], h2T[:, k, :], k == 0, k == 7) for k in range(8)]),
                   R=[wgb.k(fc // FB), h2b.k()], W=[psk(bg)])
                op("pe", MM([(PS[bu][:, :], wu[:, k, fc * 128:(fc + 1) * 128], h2T[:, k, :], k == 0, k == 7) for k in range(8)]),
                   R=[wub.k(fc // FB), h2b.k()], W=[psk(bu)])
                sg = P.bf(sgb[fc % 2])
                op("act", ACT(sg, PS[bg][:, :], AF.Silu), R=[psk(bg)], W=[sgb[fc % 2].k()])
                op("dve", TT(aT[:, fc, :], sg, PS[bu][:, :], ALU.mult), R=[sgb[fc % 2].k(), psk(bu)], W=[aTb.k(fc)])
            for j in range(4):
                tile = g * 4 + j
                b0, b1 = P.bank(), P.bank()
                for hf, b in ((0, b0), (1, b1)):
                    op("pe", MM([(PS[b][:, :], aT[:, fc, j * 128:(j + 1) * 128], wd[:, fc, hf * 512:(hf + 1) * 512], fc == 0, fc == NFF - 1)
                                 for fc in range(NFF)]), R=[aTb.k(fc) for fc in range(NFF)] + [wdb.k(k_) for k_ in range(NFF // 2)], W=[psk(b)])
                xt = P.f32(xgb)[:, j * 1024:(j + 1) * 1024]
                resid_update((b0, b1), tile, r, 1, xt, xgb.k(j), dst, j % 2)
        P.free(wgb, wub, wdb, h2b, aTb, xgb, sgb[0], sgb[1], J2[0])

    def mixer(l, src, dst, parts=("ret", "mla", "four", "hy"), do_out=True):
        mixer_consts()
        KMC = MC["key"]
        xtb[0], xtb[1] = P.alloc("xt0", 1024), P.alloc("xt1", 1024)
        hTb = P.alloc("hT", 8 * 2560 // 2)
        hT = P.bf(hTb).rearrange("p (k n) -> p k n", k=8)
        catb = P.alloc("catT", 8 * 2560 // 2)
        catT = P.bf(catb).rearrange("p (k n) -> p k n", k=8)
        KH = hTb.k()
        if dbg:
            op("pool", MSET(catT, 0.0), W=[catb.k(c) for c in range(8)])
        dstT_key[0] = KH
        for t in range(NT):
            norm_transpose(src, t, 1 if t < 16 else 0, 0, hT, t * 128)
        dump(f"hT{l}", hT, [KH])
        P.free(xtb[0], xtb[1])
        ROPEB[0] = P.alloc("ropeb", 128 + 512)
        if "ret" in parts:
            retention_all(l, hT, KH, catT, catb, KMC)
        if "mla" in parts:
            mla_all(l, hT, KH, catT, catb, KMC)
        P.free(ROPEB[0], MC["buf"])
        if "four" in parts:
            fourier_all(l, hT, KH, catT, catb)
        if "hy" in parts:
            hyena_all(l, hT, KH, catT, catb, hTb)
        else:
            P.free(hTb)
        dump(f"catT{l}", catT, [catb.k(c) for c in range(8)])
        if do_out:
            xtb[0], xtb[1] = P.alloc("xt0", 1024), P.alloc("xt1", 1024)
            J2[0] = P.alloc("junk2", 1024)
            wob = P.alloc("wout", 8 * 1024 // 2)
            wo = P.bf(wob).rearrange("p (k n) -> p k n", k=8)
            op("pool", DMA(wo, T["w_out"][l].rearrange("(k p) n -> p k n", p=128)), W=[wob.k()], dma=True)
            for t in range(NT):
                r = 1 if t < 16 else 0
                i2 = t % 2
                xt = P.f32(xtb[i2])
                op("sp", DMA(xt, src[t * 128:(t + 1) * 128, :]), R=[("dram", src.tensor.name, t)], W=[xtb[i2].k()], dma=True)
                b0, b1 = P.bank(), P.bank()
                for hf, b in ((0, b0), (1, b1)):
                    op("pe", MM([(PS[b][:, :], catT[:, k, t * 128:(t + 1) * 128], wo[:, k, hf * 512:(hf + 1) * 512], k == 0, k == 7)
                                 for k in range(8)]), R=[catb.k(c) for c in range(8)] + [wob.k()], W=[psk(b)])
                resid_update((b0, b1), t, r, 0, xt, xtb[i2].k(), dst, i2)
            P.free(wob, xtb[0], xtb[1], J2[0])
        P.free(catb)

    def retention_all(l, hT, KH, catT, catb, KMC):
        wAb = P.alloc("wA", 8 * 1024 // 2)
        wA = P.bf(wAb).rearrange("p (k n) -> p k n", k=8)
        op("pool", DMA(wA, T["w_in"][l][:, 0:1024].rearrange("(k p) n -> p k n", p=128)), W=[wAb.k()], dma=True)
        rtb = P.alloc("rtabs", 8 + 8 + 8 + 16 + 512 + 4 + 4)
        rt = P.f32(rtb)
        KT_ = rtb.k()
        decb = rt[:, 0:8]
        lgb = rt[:, 8:16]
        tmp8 = rt[:, 16:24]
        xz = rt[:, 24:40].rearrange("p (q h) -> p q h", q=4)
        dmk = rt[:, 40:552]
        lgsel = rt[:, 552:556].rearrange("p (d q) -> p d q", d=2)
        gsel = rt[:, 556:560].rearrange("p (d q) -> p d q", d=2)
        op("sp", DMA(decb, bass.AP(T["ret_decay"].tensor, l * 8, [[0, 128], [1, 8]])), W=[KT_], dma=True)
        op("act", ACT(tmp8, decb, AF.Exp, scale=-1.0), R=[KT_], W=[KT_])
        op("act", ACT(tmp8, tmp8, AF.Ln, bias=onesf[:, 0:1], scale=1.0), R=[KT_, KC], W=[KT_])
        op("dve", TS(lgb, tmp8, -1.0, ALU.mult), R=[KT_], W=[KT_])
        rc = MC["ret_cols"]
        op("act", ACT(xz[:, 0, :], lgb[:, 0:4], AF.Exp, scale=rc[:, 0:1]), R=[KT_, KMC], W=[KT_])
        op("act", ACT(xz[:, 1, :], lgb[:, 0:4], AF.Exp, scale=rc[:, 1:2]), R=[KT_, KMC], W=[KT_])
        op("act", ACT(xz[:, 2, :], lgb[:, 4:8], AF.Exp, scale=rc[:, 2:3]), R=[KT_, KMC], W=[KT_])
        op("act", ACT(xz[:, 3, :], lgb[:, 4:8], AF.Exp, scale=rc[:, 3:4]), R=[KT_, KMC], W=[KT_])
        for qq in (1, 3):
            op("dve", TS(xz[:, qq, :], xz[:, qq, :], 0.125, ALU.mult), R=[KT_], W=[KT_])
        t1b = P.alloc("rt_tmp", 1024)
        t1 = P.f32(t1b)
        for h in range(4):
            hs = (h % 2) * 2 + h // 2
            sl = slice(hs * 128, (hs + 1) * 128)
            op("act", ACT(t1[:, sl], MC["ret_dpos"][:, sl], AF.Exp, scale=lgb[:, h:h + 1]), R=[KT_, KMC], W=[t1b.k()])
            op("act", ACT(t1[:, 512 + hs * 128:512 + (hs + 1) * 128], MC["ret_dneg"][:, sl], AF.Exp, scale=lgb[:, 4 + h:5 + h]),
               R=[KT_, KMC], W=[t1b.k()])
        op("dve", TT(t1[:, 0:512], t1[:, 0:512], MC["ret_mge"], ALU.mult), R=[t1b.k(), KMC], W=[t1b.k()])
        op("dve", TT(t1[:, 512:1024], t1[:, 512:1024], MC["ret_mle"], ALU.mult), R=[t1b.k(), KMC], W=[t1b.k()])
        op("dve", TT(dmk, t1[:, 0:512], t1[:, 512:1024], ALU.add), R=[t1b.k()], W=[KT_])
        P.free(t1b)
        dv = lgb.rearrange("p (d q a) -> p d q a", d=2, a=2)
        for a in range(2):
            op("dve", CP(lgsel[a * 64:(a + 1) * 64], dv[a * 64:(a + 1) * 64, :, :, a]), R=[KT_], W=[KT_])
        op("act", ACT(gsel, lgsel, AF.Exp, scale=128.0), R=[KT_], W=[KT_])

        import os as _os
        _seqs = [SEQS[int(i_)] for i_ in _os.environ.get("DBG_SEQS", "0,1,2").split(",")]
        _stop = int(_os.environ.get("RET_STOP", "9"))
        if _stop <= 1:
            return
        for (t0, nts, latent) in _seqs:
            if latent:
                nts = int(_os.environ.get("DBG_NT0", nts))
            L = nts * 128
            qTb = P.alloc("qT", 2 * L // 2)
            kTb = P.alloc("kT", 2 * L // 2)
            qT = P.bf(qTb).rearrange("p (q n) -> p q n", q=2)
            kT = P.bf(kTb).rearrange("p (q n) -> p q n", q=2)
            vtb = P.alloc("v_tok", nts * 256 // 2)
            vt = P.bf(vtb).rearrange("p (t n) -> p t n", t=nts)
            gtb = P.alloc("gate_tok", nts * 256 // 2)
            gt = P.bf(gtb).rearrange("p (t n) -> p t n", t=nts)
            kvb = P.alloc("kv_all", nts * 256)
            kva = P.f32(kvb).rearrange("p (t d q n) -> p t d q n", t=nts, d=2, q=2)
            sbb = P.alloc("S_bf", nts * 256 // 2)
            sbf = P.bf(sbb).rearrange("p (t d q n) -> p t d q n", t=nts, d=2, q=2)
            stf = P.alloc("S_f32", 256)
            Sst = P.f32(stf).rearrange("p (d q n) -> p d q n", d=2, q=2)
            qkb = [P.alloc(f"qkrot{i}", 256) for i in range(2)]
            kzb = [P.alloc(f"kz{i}", 256) for i in range(2)]
            rpb = P.alloc("ropetmp", 1024) if latent else None
            for j in range(nts):
                t = t0 + j
                cols = slice(t * 128, (t + 1) * 128)
                bq, bv = P.bank(), P.bank()
                op("pe", MM([(PS[bq][:, :], hT[:, k, cols], wA[:, k, 0:512], k == 0, k == 7) for k in range(8)]),
                   R=[KH, wAb.k()], W=[psk(bq)])
                op("pe", MM([(PS[bv][:, :], hT[:, k, cols], wA[:, k, 512:1024], k == 0, k == 7) for k in range(8)]),
                   R=[KH, wAb.k()], W=[psk(bv)])
                qk = P.bf(qkb[j % 2])
                kq = qkb[j % 2].k()
                if latent:
                    rp = P.f32(rpb)
                    raw = rp[:, 0:512]
                    op("act", ACT(raw, PS[bq][:, :], AF.Copy), R=[psk(bq)], W=[rpb.k(0)])
                    src5 = raw.rearrange("p (h r x j) -> p h r x j", h=8, r=2, x=2)
                    dst5 = qk.rearrange("p (h r x j) -> p h r x j", h=8, r=2, x=2)
                    tm5 = rp[:, 512:1024].rearrange("p (u h r j) -> p u h r j", u=2, h=8, r=2)
                    cs = bc(MC["cos_ret"][:, j * 32:(j + 1) * 32], [[0, 8], [16, 2], [1, 16]])
                    sn = bc(MC["sin_ret"][:, j * 32:(j + 1) * 32], [[0, 8], [16, 2], [1, 16]])
                    x1, x2 = src5[:, :, :, 0, :], src5[:, :, :, 1, :]
                    op("pool", TT(tm5[:, 0], x1, cs, ALU.mult), R=[rpb.k(0), KMC], W=[rpb.k(1)])
                    op("pool", TT(tm5[:, 1], x2, sn, ALU.mult), R=[rpb.k(0), KMC], W=[rpb.k(2)])
                    op("pool", TT(dst5[:, :, :, 0, :], tm5[:, 0], tm5[:, 1], ALU.subtract), R=[rpb.k(1), rpb.k(2)], W=[kq])
                    op("pool", TT(tm5[:, 0], x2, cs, ALU.mult), R=[rpb.k(0), KMC], W=[rpb.k(1)])
                    op("pool", TT(tm5[:, 1], x1, sn, ALU.mult), R=[rpb.k(0), KMC], W=[rpb.k(2)])
                    op("pool", TT(dst5[:, :, :, 1, :], tm5[:, 0], tm5[:, 1], ALU.add), R=[rpb.k(1), rpb.k(2)], W=[kq])
                else:
                    op("act", ACT(qk, PS[bq][:, :], AF.Copy), R=[psk(bq)], W=[kq])
                op("act", ACT(vt[:, j, :], PS[bv][:, 0:256], AF.Copy), R=[psk(bv)], W=[vtb.k(j)])
                op("act", ACT(gt[:, j, :], PS[bv][:, 256:512], AF.Silu), R=[psk(bv)], W=[gtb.k(j)])
                if _os.environ.get("DBG_SKIPKZ"):
                    continue
                kz = P.bf(kzb[j % 2]).rearrange("p (d h n) -> p d h n", d=2, h=4)
                k4 = qk[:, 256:512].rearrange("p (h n) -> p h n", h=4)
                for d_, qq in ((0, 1), (1, 3)):
                    op("pool", TT(kz[:, d_], k4, bc(xz[:, qq, :], [[1, 4], [0, 64]]), ALU.mult), R=[kq, KT_], W=[kzb[j % 2].k(d_)])
                if _os.environ.get("DBG_SKIPT"):
                    continue
                bt, bt2 = P.bank(), P.bank()
                op("pe", MM([(PS[bt][:, i * 128:(i + 1) * 128], qk[:, i * 128:(i + 1) * 128], ident, True, True) for i in range(2)]),
                   R=[kq, KC], W=[psk(bt)])
                op("pe", MM([(PS[bt2][:, i * 128:(i + 1) * 128], qk[:, (2 + i) * 128:(3 + i) * 128], ident, True, True) for i in range(2)]),
                   R=[kq, KC], W=[psk(bt2)])
                jc = slice(j * 128, (j + 1) * 128)
                op("act", ACT(qT[:, :, jc], PS[bt][:, 0:256].rearrange("p (q n) -> p q n", q=2), AF.Copy), R=[psk(bt)], W=[qTb.k(j)])
                op("dve", CP(kT[:, :, jc], PS[bt2][:, 0:256].rearrange("p (q n) -> p q n", q=2)), R=[psk(bt2)], W=[kTb.k(j)])
                if _os.environ.get("DBG_SKIPKV"):
                    continue
                bk = P.bank()
                op("pe", MM([(PS[bk][:, d_ * 256 + hp * 128:d_ * 256 + (hp + 1) * 128], kz[:, d_, 2 * hp:2 * hp + 2, :],
                              vt[:, j, hp * 128:(hp + 1) * 128], True, True) for d_ in range(2) for hp in range(2)]),
                   R=[kzb[j % 2].k(0), kzb[j % 2].k(1), vtb.k(j)], W=[psk(bk)])
                for a in range(2):
                    srcv = bass.AP(PS[bk][:, :].tensor,
                                   PS[bk][a * 64:(a + 1) * 64, a * 64:a * 64 + 1].offset,
                                   [list(PS[bk][a * 64:(a + 1) * 64, :].ap[0]), [256, 2], [128, 2], [1, 64]])
                    op("act" if j % 2 == 0 else "dve",
                       (ACT(kva[a * 64:(a + 1) * 64, j], srcv, AF.Copy) if j % 2 == 0 else CP(kva[a * 64:(a + 1) * 64, j], srcv)),
                       R=[psk(bk)], W=[kvb.k((j, a))])
            if _stop <= 2:
                continue
            KS = stf.k()
            if latent:
                for d_ in range(2):
                    for a in range(2):
                        op("sp", DMA(Sst[a * 64:(a + 1) * 64, d_], bass.AP(T["s0"].tensor, ((l * 2 + d_) * 4 + a) * 4096,
                                                                           [[64, 64], [2 * 4096, 2], [1, 64]])), W=[KS], dma=True)
            else:
                op("pool", MSET(P.f32(stf), 0.0), W=[KS])
            for d_ in range(2):
                order = range(nts) if d_ == 0 else range(nts - 1, -1, -1)
                for j in order:
                    op("act", ACT(sbf[:, j, d_], Sst[:, d_], AF.Copy), R=[KS], W=[sbb.k((j, d_))])
                    for hp in range(2):
                        op("dve", STT(Sst[:, d_, hp, :], Sst[:, d_, hp, :], gsel[:, d_, hp:hp + 1], kva[:, j, d_, hp, :], ALU.mult, ALU.add),
                           R=[KS, KT_, kvb.k((j, 0)), kvb.k((j, 1))], W=[KS])
            if not latent:
                pb = (t0 - 16) // 2
                for d_ in range(2):
                    for a in range(2):
                        op("sp", DMA(bass.AP(T["o_st"].tensor, (((pb * 2 + l) * 2 + d_) * 4 + a) * 4096, [[64, 64], [2 * 4096, 2], [1, 64]]),
                                     Sst[a * 64:(a + 1) * 64, d_]), R=[KS], dma=True)
            if _stop <= 3:
                continue
            P.free(kvb, qkb[0], qkb[1], kzb[0], kzb[1])
            if rpb is not None:
                P.free(rpb)
            ptb = [P.alloc(f"PT{i}", 256) for i in range(2)]
            ob = [P.alloc(f"o_acc{i}", 256 * 3) for i in range(2)]
            gnb = P.alloc("gn", 32)
            gn = P.f32(gnb)
            rob = [P.alloc(f"ret_o{i}", 128) for i in range(2)]
            for j in range(nts):
                t = t0 + j
                jc = slice(j * 128, (j + 1) * 128)
                bsa = [P.bank(), P.bank()]
                PT = P.bf(ptb[j % 2])
                for a in range(2):
                    op("pe", MM([(PS[bsa[a]][:, hp * 128:(hp + 1) * 128], kT[a * 64:(a + 1) * 64, hp, jc],
                                  qT[a * 64:(a + 1) * 64, hp, jc], True, True) for hp in range(2)]),
                       R=[kTb.k(j), qTb.k(j)], W=[psk(bsa[a])])
                    op("dve", TT(PT[:, a * 256:(a + 1) * 256], PS[bsa[a]][:, 0:256], dmk[:, a * 256:(a + 1) * 256], ALU.mult),
                       R=[psk(bsa[a]), KT_], W=[ptb[j % 2].k(a)])
                bo = P.bank()
                op("pe", MM([(PS[bo][:, h * 64:(h + 1) * 64], PT[:, ((h % 2) * 2 + h // 2) * 128:((h % 2) * 2 + h // 2 + 1) * 128],
                              vt[:, j, h * 64:(h + 1) * 64], True, True) for h in range(4)]),
                   R=[ptb[j % 2].k(0), ptb[j % 2].k(1), vtb.k(j)], W=[psk(bo)])
                bxa = [P.bank(), P.bank()]
                for a in range(2):
                    op("pe", MM([(PS[bxa[a]][:, (d_ * 2 + hp) * 64:(d_ * 2 + hp + 1) * 64], qT[a * 64:(a + 1) * 64, hp, jc],
                                  sbf[a * 64:(a + 1) * 64, j, d_, hp, :], True, True) for d_ in range(2) for hp in range(2)]),
                       R=[qTb.k(j), sbb.k((j, 0)), sbb.k((j, 1))], W=[psk(bxa[a])])
                oa = P.f32(ob[j % 2]).rearrange("p (u h n) -> p u h n", u=3, h=4)
                ko = ob[j % 2].k()
                for a in range(2):
                    xif = bc(xz[:, 0, a:a + 1], [[2, 2], [0, 64]])
                    xib = bc(xz[:, 2, a:a + 1], [[2, 2], [0, 64]])
                    cf_ = PS[bxa[a]][:, 0:128].rearrange("p (q n) -> p q n", q=2)
                    cb_ = PS[bxa[a]][:, 128:256].rearrange("p (q n) -> p q n", q=2)
                    o0 = oa[:, 0, a::2, :]
                    o1 = oa[:, 1, a::2, :]
                    op("dve", TT(o0, cf_, xif, ALU.mult), R=[psk(bxa[a]), KT_], W=[ko])
                    op("dve", TT(o1, cb_, xib, ALU.mult), R=[psk(bxa[a]), KT_], W=[ko])
                op("pool", TT(oa[:, 0], oa[:, 0], oa[:, 1], ALU.add), R=[ko], W=[ko])
                op("dve", TT(oa[:, 0], oa[:, 0], PS[bo][:, 0:256].rearrange("p (h n) -> p h n", h=4), ALU.add), R=[ko, psk(bo)], W=[ko])
                g0 = (j % 2) * 16
                sm, ng_, vs, sd, rs_ = (gn[:, g0 + 4 * 0:g0 + 4], gn[:, g0 + 4:g0 + 8], gn[:, g0 + 8:g0 + 12], gn[:, g0 + 12:g0 + 16], None)
                kg = gnb.k(j % 2)
                op("dve", RED(sm, oa[:, 0]), R=[ko], W=[kg])
                op("dve", TS(ng_, sm, -1.0 / 64, ALU.mult), R=[kg], W=[kg])
                op("pool", TT(oa[:, 1], oa[:, 0], bc(ng_, [[1, 4], [0, 64]]), ALU.add), R=[ko, kg], W=[ko])
                op("pool", TT(oa[:, 2], oa[:, 1], oa[:, 1], ALU.mult), R=[ko], W=[ko])
                op("dve", RED(vs, oa[:, 2]), R=[ko], W=[kg])
                op("act", ACT(sd, vs, AF.Sqrt, bias=epsc, scale=1.0 / 64), R=[kg, KC], W=[kg])
                op("dve", RECIP(sm, sd), R=[kg], W=[kg])
                op("pool", TT(oa[:, 2], oa[:, 1], bc(sm, [[1, 4], [0, 64]]), ALU.mult), R=[ko, kg], W=[ko])
                ro = P.bf(rob[j % 2])
                op("dve", TT(ro.rearrange("p (h n) -> p h n", h=4), oa[:, 2], gt[:, j, :].rearrange("p (h n) -> p h n", h=4), ALU.mult),
                   R=[ko, gtb.k(j)], W=[rob[j % 2].k()])
                bt = P.bank()
                op("pe", MM([(PS[bt][:, i * 128:(i + 1) * 128], ro[:, i * 128:(i + 1) * 128], ident, True, True) for i in range(2)]),
                   R=[rob[j % 2].k(), KC], W=[psk(bt)])
                op("act", ACT(catT[:, 0:2, t * 128:(t + 1) * 128], PS[bt][:, 0:256].rearrange("p (q n) -> p q n", q=2), AF.Copy),
                   R=[psk(bt)], W=[catb.k(0), catb.k(1)])
            P.free(qTb, kTb, vtb, gtb, sbb, stf, ptb[0], ptb[1], ob[0], ob[1], gnb, rob[0], rob[1])
        P.free(wAb, rtb)

    def mla_all(l, hT, KH, catT, catb, KMC):
        import os as _os
        ropeb = ROPEB[0]
        wBb = P.alloc("wB", 8 * 416 // 2)
        wB = P.bf(wBb).rearrange("p (k n) -> p k n", k=8)
        op("pool", DMA(wB, T["w_in"][l][:, 1280:1696].rearrange("(k p) n -> p k n", p=128)), W=[wBb.k()], dma=True)
        wqb = P.alloc("wuq", 2 * 384 // 2 + 256)
        wuq = P.bf(wqb)[:, 0:768].rearrange("p (k n) -> p k n", k=2)
        wukv = P.bf(wqb)[:, 768:1280]
        op("pool", DMA(wuq, T["mla_w_uq"][l].rearrange("(k p) n -> p k n", p=128)), W=[wqb.k()], dma=True)
        op("pool", DMA(wukv, T["mla_w_ukv"][l]), W=[wqb.k()], dma=True)
        gnb_ = P.alloc("mla_g", 384)
        gq_b, gkv_b = P.f32(gnb_)[:, 0:256], P.f32(gnb_)[:, 256:384]
        op("sp", DMA(gq_b, bass.AP(T["mla_q_norm"].tensor, l * 256, [[0, 128], [1, 256]])), W=[gnb_.k()], dma=True)
        op("sp", DMA(gkv_b, bass.AP(T["mla_kv_norm"].tensor, l * 128, [[0, 128], [1, 128]])), W=[gnb_.k()], dma=True)
        msb = P.alloc("mla_stats", 16)
        ms = P.f32(msb)
        scale = 96.0 ** -0.5
        _seqs = [SEQS[int(i_)] for i_ in _os.environ.get("DBG_SEQS", "0,1,2").split(",")]
        for (t0, nts, latent) in _seqs:
            L = nts * 128
            nkt = nts + (2 if latent else 0)
            Lk = nkt * 128
            cqTb = P.alloc("cqT", L)
            cqT = P.bf(cqTb).rearrange("p (k n) -> p k n", k=2)
            ckTb = P.alloc("ckvT", Lk // 2)
            ckvT = P.bf(ckTb)
            KTb = P.alloc("KT", 2 * Lk)
            KT = P.bf(KTb).rearrange("p (h n) -> p h n", h=4)
            QTb = P.alloc("QT", 2 * L)
            QT = P.bf(QTb).rearrange("p (h n) -> p h n", h=4)
            Vb = P.alloc("Vaug", nkt * 130)
            Va = P.bf(Vb).rearrange("p (t h n) -> p t h n", t=nkt, h=4)
            atb = P.alloc("att_tok", nts * 128)
            att = P.bf(atb).rearrange("p (t n) -> p t n", t=nts)
            op("pool", MSET(P.bf(Vb), 1.0), W=[Vb.k()])
            kstb = [P.alloc(f"kst{i}", 128) for i in range(2)]
            cqnb = [P.alloc(f"cqn{i}", 128) for i in range(2)]
            ckfb = [P.alloc(f"ckvf{i}", 128 + 32) for i in range(2)]
            for i in range(2):
                op("pool", MSET(P.bf(kstb[i]), 0.0), W=[kstb[i].k()])
            pb = (t0 - 16) // 2
            for j in range(nkt):
                i2 = j % 2
                kst = P.bf(kstb[i2])
                kk = kstb[i2].k()
                kcols = slice(j * 128, (j + 1) * 128)
                if j < nts:
                    t = t0 + j
                    bp = P.bank()
                    op("pe", MM([(PS[bp][:, 0:416], hT[:, k, t * 128:(t + 1) * 128], wB[:, k, :], k == 0, k == 7) for k in range(8)]),
                       R=[KH, wBb.k()], W=[psk(bp)])
                    ssq, ssk, sq1, sq2, rq, rk = (ms[:, i2 * 8 + u:i2 * 8 + u + 1] for u in range(6))
                    kst_ = msb.k(i2)
                    cqn = P.bf(cqnb[i2])
                    ckf = P.f32(ckfb[i2])
                    op("act", ACT(cqn, PS[bp][:, 0:256], AF.Square, accum=ssq), R=[psk(bp)], W=[cqnb[i2].k(), kst_])
                    op("act", ACT(ckf[:, 0:128], PS[bp][:, 256:384], AF.Square, accum=ssk), R=[psk(bp)], W=[ckfb[i2].k(), kst_])
                    op("act", ACT(sq1, ssq, AF.Sqrt, bias=epsc, scale=1.0 / 256), R=[kst_, KC], W=[kst_])
                    op("act", ACT(sq2, ssk, AF.Sqrt, bias=epsc, scale=1.0 / 128), R=[kst_, KC], W=[kst_])
                    op("dve", RECIP(ms[:, i2 * 8 + 4:i2 * 8 + 6], ms[:, i2 * 8 + 2:i2 * 8 + 4]), R=[kst_], W=[kst_])
                    op("dve", STT(cqn, PS[bp][:, 0:256], rq, gq_b, ALU.mult, ALU.mult), R=[psk(bp), kst_, gnb_.k()], W=[cqnb[i2].k()])
                    op("dve", STT(ckf[:, 0:128], PS[bp][:, 256:384], rk, gkv_b, ALU.mult, ALU.mult), R=[psk(bp), kst_, gnb_.k()],
                       W=[ckfb[i2].k()])
                    op("pool", CP(kst[:, 0:128], ckf[:, 0:128]), R=[ckfb[i2].k()], W=[kk])
                    if latent:
                        op("dve", CP(ckf[:, 128:160], PS[bp][:, 384:416]), R=[psk(bp), kst_], W=[ckfb[i2].k("kr")])
                        kr3 = ckf[:, 128:160].rearrange("p (r x j) -> p r x j", r=2, x=2)
                        kd3 = kst[:, 192:224].rearrange("p (r x j) -> p r x j", r=2, x=2)
                        cs = MC["cos_mla"][:, j * 16:(j + 1) * 16].rearrange("p (r j) -> p r j", r=2)
                        sn = MC["sin_mla"][:, j * 16:(j + 1) * 16].rearrange("p (r j) -> p r j", r=2)
                        tmk = ms
                        rtb_ = ckfb[i2].k("rt")
                        tmp4 = P.f32(ropeb)[:, i2 * 64:(i2 + 1) * 64].rearrange("p (u r j) -> p u r j", u=4, r=2)
                        kro = ropeb.k(i2)
                        op("pool", TT(tmp4[:, 0], kr3[:, :, 0, :], cs, ALU.mult), R=[ckfb[i2].k("kr"), KMC], W=[kro])
                        op("pool", TT(tmp4[:, 1], kr3[:, :, 1, :], sn, ALU.mult), R=[ckfb[i2].k("kr"), KMC], W=[kro])
                        op("pool", TT(tmp4[:, 2], kr3[:, :, 1, :], cs, ALU.mult), R=[ckfb[i2].k("kr"), KMC], W=[kro])
                        op("pool", TT(tmp4[:, 3], kr3[:, :, 0, :], sn, ALU.mult), R=[ckfb[i2].k("kr"), KMC], W=[kro])
                        op("pool", TT(kd3[:, :, 0, :], tmp4[:, 0], tmp4[:, 1], ALU.subtract), R=[kro], W=[kk])
                        op("pool", TT(kd3[:, :, 1, :], tmp4[:, 2], tmp4[:, 3], ALU.add), R=[kro], W=[kk])
                    else:
                        op("dve", CP(ckf[:, 128:160], PS[bp][:, 384:416]), R=[psk(bp), kst_], W=[ckfb[i2].k("kr")])
                        op("pool", CP(kst[:, 192:224], ckf[:, 128:160]), R=[ckfb[i2].k("kr")], W=[kk])
                        rows = slice(j * 128, (j + 1) * 128)
                        op("sp", DMA(T["o_ckv"][pb, l, rows, :], ckf[:, 0:128]), R=[ckfb[i2].k()], dma=True)
                        op("sp", DMA(T["o_kr"][pb, l, rows, :], ckf[:, 128:160]), R=[ckfb[i2].k("kr")], dma=True)
                else:
                    rows = slice((j - nts) * 128, (j - nts + 1) * 128)
                    op("pool", DMA(kst[:, 0:128], T["ckv_ctx"][l, rows, :]), W=[kk], dma=True)
                    op("pool", DMA(kst[:, 192:224], T["kr_ctx"][l, rows, :]), W=[kk], dma=True)
                bT = P.bank()
                mms = []
                if j < nts:
                    mms += [(PS[bT][:, i * 128:(i + 1) * 128], P.bf(cqnb[i2])[:, i * 128:(i + 1) * 128], ident, True, True) for i in range(2)]
                mms += [(PS[bT][:, 256:384], kst[:, 0:128], ident, True, True),
                        (PS[bT][0:96, 384:512], kst[:, 128:224], ident, True, True)]
                op("pe", MM(mms), R=[cqnb[i2].k(), kk, KC], W=[psk(bT)])
                ev = "act" if j % 2 == 0 else "dve"
                cpy = (lambda o_, i_: ACT(o_, i_, AF.Copy)) if ev == "act" else CP
                if j < nts:
                    op(ev, cpy(cqT[:, :, j * 128:(j + 1) * 128], PS[bT][:, 0:256].rearrange("p (k n) -> p k n", k=2)), R=[psk(bT)], W=[cqTb.k(j)])
                op(ev, cpy(ckvT[:, kcols], PS[bT][:, 256:384]), R=[psk(bT)], W=[ckTb.k(j)])
                op(ev, cpy(KT[64:96, :, kcols], bc(PS[bT][64:96, 384:512], [[0, 4], [1, 128]])), R=[psk(bT)], W=[KTb.k(("r", j))])
            rawb = [P.alloc(f"qraw{i}", 384) for i in range(2)]
            qtkb = [P.alloc(f"qtok{i}", 192) for i in range(2)]
            for j in range(nts):
                i2 = j % 2
                jc = slice(j * 128, (j + 1) * 128)
                bq = P.bank()
                op("pe", MM([(PS[bq][:, 0:384], cqT[:, k, jc], wuq[:, k, :], k == 0, k == 1) for k in range(2)]),
                   R=[cqTb.k(j), wqb.k()], W=[psk(bq)])
                qtk = P.bf(qtkb[i2])
                kq_ = qtkb[i2].k()
                if latent:
                    raw = P.f32(rawb[i2])
                    op("act", ACT(raw, PS[bq][:, 0:384], AF.Copy), R=[psk(bq)], W=[rawb[i2].k()])
                    r4 = raw.rearrange("p (h n) -> p h n", h=4)
                    q4 = qtk.rearrange("p (h n) -> p h n", h=4)
                    op("dve", CP(q4[:, :, 0:64], r4[:, :, 0:64]), R=[rawb[i2].k()], W=[kq_])
                    def rv(base, off):
                        return bass.AP(base.tensor, base.offset + off, [list(base.ap[0]), [96, 4], [16, 2], [1, 8]])
                    x1, x2 = rv(raw, 64), rv(raw, 72)
                    o1_, o2_ = rv(qtk, 64), rv(qtk, 72)
                    cs = bc(MC["cos_mla"][:, j * 16:(j + 1) * 16], [[0, 4], [8, 2], [1, 8]])
                    sn = bc(MC["sin_mla"][:, j * 16:(j + 1) * 16], [[0, 4], [8, 2], [1, 8]])
                    tq = P.f32(ropeb)[:, 128 + i2 * 256:128 + (i2 + 1) * 256].rearrange("p (u h r j) -> p u h r j", u=4, h=4, r=2)
                    kro = ropeb.k(("q", i2))
                    op("pool", TT(tq[:, 0], x1, cs, ALU.mult), R=[rawb[i2].k(), KMC], W=[kro])
                    op("pool", TT(tq[:, 1], x2, sn, ALU.mult), R=[rawb[i2].k(), KMC], W=[kro])
                    op("pool", TT(tq[:, 2], x2, cs, ALU.mult), R=[rawb[i2].k(), KMC], W=[kro])
                    op("pool", TT(tq[:, 3], x1, sn, ALU.mult), R=[rawb[i2].k(), KMC], W=[kro])
                    op("pool", TT(o1_, tq[:, 0], tq[:, 1], ALU.subtract), R=[kro], W=[kq_])
                    op("pool", TT(o2_, tq[:, 2], tq[:, 3], ALU.add), R=[kro], W=[kq_])
                else:
                    op("act", ACT(qtk, PS[bq][:, 0:384], AF.Copy), R=[psk(bq)], W=[kq_])
                bT = P.bank()
                op("pe", MM([(PS[bT][0:96, h * 128:(h + 1) * 128], qtk[:, h * 96:(h + 1) * 96], ident, True, True) for h in range(4)]),
                   R=[kq_, KC], W=[psk(bT)])
                ev = "act" if j % 2 == 0 else "dve"
                cpy = (lambda o_, i_: ACT(o_, i_, AF.Copy)) if ev == "act" else CP
                op(ev, cpy(QT[0:96, :, jc], PS[bT][0:96, :].rearrange("p (h n) -> p h n", h=4)), R=[psk(bT)], W=[QTb.k(j)])
            P.free(rawb[0], rawb[1], qtkb[0], qtkb[1])
            wk4 = wukv.rearrange("p (h c n) -> p h c n", h=4, c=2)
            for c0 in range(0, Lk, 512):
                cw = min(512, Lk - c0)
                for h in range(4):
                    b = P.bank()
                    op("pe", MM([(PS[b][0:64, 0:cw], wk4[:, h, 0, :], ckvT[:, c0:c0 + cw], True, True)]),
                       R=[wqb.k()] + [ckTb.k(j_) for j_ in range(c0 // 128, (c0 + cw) // 128)], W=[psk(b)])
                    ev = "act" if h % 2 == 0 else "dve"
                    cpy = (lambda o_, i_: ACT(o_, i_, AF.Copy)) if ev == "act" else CP
                    op(ev, cpy(KT[0:64, h, c0:c0 + cw], PS[b][0:64, 0:cw]), R=[psk(b)], W=[KTb.k(("n", h, c0))])
            for kt in range(nkt):
                b = P.bank()
                op("pe", MM([(PS[b][:, 0:256], ckvT[:, kt * 128:(kt + 1) * 128], wk4[:, :, 1, :], True, True)]),
                   R=[wqb.k(), ckTb.k(kt)], W=[psk(b)])
                ev = "act" if kt % 2 == 0 else "dve"
                cpy = (lambda o_, i_: ACT(o_, i_, AF.Copy)) if ev == "act" else CP
                op(ev, cpy(Va[:, kt, :, 0:64], PS[b][:, 0:256].rearrange("p (h n) -> p h n", h=4)), R=[psk(b), Vb.k()], W=[Vb.k(kt)])
            QC = min(512, L)
            nsub = QC // 128
            Eb = [P.alloc(f"E{i}", QC // 2) for i in range(3)]
            rcb = P.alloc("att_rec", 8)
            ei = 0
            allKT = [KTb.k(("r", j_)) for j_ in range(nkt)]
            for h in range(4):
                kth = allKT + [KTb.k(("n", h, c0)) for c0 in range(0, Lk, 512)]
                for qc in range(L // QC):
                    qcols = slice(qc * QC, (qc + 1) * QC)
                    bo = P.bank()
                    qdeps = [QTb.k(j_) for j_ in range(qc * nsub, (qc + 1) * nsub)]

                    def scores(kt):
                        bs = P.bank()
                        if bs == bo:
                            bs = P.bank()
                        op("pe", MM([(PS[bs][:, 0:QC], KT[0:96, h, kt * 128:(kt + 1) * 128], QT[0:96, h, qcols], True, True)]),
                           R=kth + qdeps, W=[psk(bs)])
                        return bs
                    pend = scores(0)
                    for kt in range(nkt):
                        bs = pend
                        if kt + 1 < nkt:
                            pend = scores(kt + 1)
                        E = P.bf(Eb[ei % 3])
                        ke = Eb[ei % 3].k()
                        ei += 1
                        op("act", ACT(E, PS[bs][:, 0:QC], AF.Exp, scale=scale), R=[psk(bs)], W=[ke])
                        op("pe", MM([(PS[bo][:, sub * 65:(sub + 1) * 65], E[:, sub * 128:(sub + 1) * 128], Va[:, kt, h, :],
                                      kt == 0 and sub == 0, kt == nkt - 1 and sub == nsub - 1)
                                     for sub in range(nsub)]), R=[ke, Vb.k(kt), Vb.k()], W=[psk(bo)])
                    o3 = PS[bo][:, 0:nsub * 65].rearrange("p (s n) -> p s n", s=nsub)
                    rec = P.f32(rcb)[:, (qc % 2) * 4:(qc % 2) * 4 + nsub]
                    op("dve", RECIP(rec, o3[:, :, 64]), R=[psk(bo)], W=[rcb.k(qc % 2)])
                    op("dve", TT(att[:, qc * nsub:(qc + 1) * nsub, h * 64:(h + 1) * 64], o3[:, :, 0:64], bc(rec, [[1, nsub], [0, 64]]), ALU.mult),
                       R=[psk(bo), rcb.k(qc % 2)], W=[atb.k((h, qc))])
            for j in range(nts):
                t = t0 + j
                bT = P.bank()
                op("pe", MM([(PS[bT][:, i * 128:(i + 1) * 128], att[:, j, i * 128:(i + 1) * 128], ident, True, True) for i in range(2)]),
                   R=[atb.k((h_, j // nsub)) for h_ in range(4)] + [KC], W=[psk(bT)])
                ev = "act" if j % 2 == 0 else "dve"
                cpy = (lambda o_, i_: ACT(o_, i_, AF.Copy)) if ev == "act" else CP
                op(ev, cpy(catT[:, 4:6, t * 128:(t + 1) * 128], PS[bT][:, 0:256].rearrange("p (q n) -> p q n", q=2)), R=[psk(bT)],
                   W=[catb.k(4), catb.k(5)])
            P.free(cqTb, ckTb, KTb, QTb, Vb, atb, kstb[0], kstb[1], cqnb[0], cqnb[1], ckfb[0], ckfb[1], Eb[0], Eb[1], Eb[2], rcb)
        P.free(wBb, wqb, gnb_, msb)


    def fourier_all(l, hT, KH, catT, catb):
        import os as _os
        wCb = P.alloc("wC", 8 * 256 // 2)
        wC = P.bf(wCb).rearrange("p (k n) -> p k n", k=8)
        op("pool", DMA(wC, T["w_in"][l][:, 1024:1280].rearrange("(k p) n -> p k n", p=128)), W=[wCb.k()], dma=True)
        _seqs = [SEQS[int(i_)] for i_ in _os.environ.get("DBG_SEQS", "0,1,2").split(",")]
        for (t0, nts, latent) in _seqs:
            L = nts * 128
            nj = nts // 2
            H = L // 2
            PB = min(512, H)
            base = t0 * 128
            ftb = P.alloc("f_tok", 2 * nj * 256 // 2)
            ft = P.bf(ftb).rearrange("p (a j n) -> p a j n", a=2, j=nj)
            for a in range(2):
                for j in range(nj):
                    b = P.bank()
                    c0 = base + a + 256 * j
                    op("pe", MM([(PS[b][:, 0:256], hT[:, k, c0:c0 + 255:2], wC[:, k, :], k == 0, k == 7) for k in range(8)]),
                       R=[KH, wCb.k()], W=[psk(b)])
                    ev = "act" if (a * nj + j) % 2 == 0 else "dve"
                    cpy = (lambda o_, i_: ACT(o_, i_, AF.Copy)) if ev == "act" else CP
                    op(ev, cpy(ft[:, a, j, :], PS[b][:, 0:256]), R=[psk(b)], W=[ftb.k((a, j))])
            tabC = T[f"f_cl{L}"].rearrange("p (a j n) -> p a j n", a=2, j=nj)
            tabS = T[f"f_sl{L}"].rearrange("p (a j n) -> p a j n", a=2, j=nj)
            mtb = [P.alloc(f"ftab{i}", 2 * nj * PB // 2) for i in range(2)]
            uvb = [P.alloc(f"fuv{i}", 4 * PB // 2) for i in range(2)]
            zob = P.alloc("fzo", 2 * PB)
            mi = 0
            for ph in range(H // PB):
                zbanks = {}
                for a in range(2):
                    mt = P.bf(mtb[mi % 2]).rearrange("p (q j n) -> p q j n", q=2, j=nj)
                    km = mtb[mi % 2].k()
                    op("sp", DMA(mt[:, 0], tabC[:, a, :, ph * PB:(ph + 1) * PB]), W=[km], dma=True)
                    op("sp", DMA(mt[:, 1], tabS[:, a, :, ph * PB:(ph + 1) * PB]), W=[km], dma=True)
                    uv = P.bf(uvb[mi % 2]).rearrange("p (c q n) -> p c q n", c=2, q=2)
                    ku = uvb[mi % 2].k()
                    mi += 1
                    for cc in range(2):
                        for q in range(2):
                            b = P.bank()
                            op("pe", MM([(PS[b][:, 0:PB], ft[:, a, j, cc * 128:(cc + 1) * 128], mt[:, q, j, :], j == 0, j == nj - 1)
                                         for j in range(nj)]), R=[ftb.k((a, j)) for j in range(nj)] + [km], W=[psk(b)])
                            ev = "act" if q == 0 else "dve"
                            cpy = (lambda o_, i_: ACT(o_, i_, AF.Copy)) if ev == "act" else CP
                            op(ev, cpy(uv[:, cc, q, :], PS[b][:, 0:PB]), R=[psk(b)], W=[uvb[(mi - 1) % 2].k((cc, q))])
                    for cc in range(2):
                        b = P.bank()
                        zbanks[(a, cc)] = b
                        kk_ = uvb[(mi - 1) % 2]
                        op("pe", MM([(PS[b][:, 0:PB], c64, uv[:, cc, 0, :], True, False), (PS[b][:, 0:PB], s64n, uv[:, cc, 1, :], False, True)]),
                           R=[kk_.k((cc, 0)), kk_.k((cc, 1)), KC], W=[psk(b)])
                zo = P.f32(zob).rearrange("p (c n) -> p c n", c=2)
                for cc in range(2):
                    be, bo_ = zbanks[(0, cc)], zbanks[(1, cc)]
                    op("act", ACT(zo[:, cc, :], PS[bo_][:, 0:PB], AF.Copy), R=[psk(bo_)], W=[zob.k(cc)])
                    p0 = base + ph * PB
                    op("dve", TT(catT[:, 2 + cc, p0:p0 + PB], PS[be][:, 0:PB], zo[:, cc, :], ALU.add), R=[psk(be), zob.k(cc)], W=[catb.k(2 + cc)])
                    op("dve", TT(catT[:, 2 + cc, p0 + H:p0 + H + PB], PS[be][:, 0:PB], zo[:, cc, :], ALU.subtract), R=[psk(be), zob.k(cc)],
                       W=[catb.k(2 + cc)])
            P.free(ftb, mtb[0], mtb[1], uvb[0], uvb[1], zob)
        P.free(wCb)


    def hyena_all(l, hT, KH, catT, catb, hTb):
        import os as _os
        TWO_PI = 2.0 * math.pi
        hpb = P.alloc("hy_par", 18 + 6 + 4 + 4 + 64 + 64 + 1024 + 8)
        hp = P.f32(hpb)
        KP = hpb.k()
        shw = hp[:, 0:18].rearrange("p (m k) -> p m k", m=6)
        shb = hp[:, 18:24]
        dsk = hp[:, 24:28].rearrange("p (o c) -> p o c", o=2)
        bcol = hp[:, 28:32]
        w1s = hp[0:33, 32:96]
        w2s = hp[0:64, 96:160]
        w3s = hp[0:64, 160:1184]
        for k in range(3):
            op("sp", DMAS(shw[:, :, k:k + 1], bass.AP(T["hy_short_w"].tensor, (l * 3 + k) * 768, [[1, 128], [128, 6], [1, 1]])), W=[KP], dma=True)
        op("sp", DMAS(shb.rearrange("p (m o) -> p m o", o=1), bass.AP(T["hy_short_b"].tensor, l * 768, [[1, 128], [128, 6], [1, 1]])), W=[KP], dma=True)
        op("sp", DMAS(hp[:, 24:28].rearrange("p (q o) -> p q o", o=1), bass.AP(T["hy_bias"].tensor, l * 512, [[1, 128], [128, 4], [1, 1]])), W=[KP], dma=True)
        op("sp", DMAS(bcol[0:64, 0:1], bass.AP(T["hy_b1"].tensor, l * 64, [[1, 64], [1, 1]])), W=[KP], dma=True)
        op("sp", DMAS(bcol[0:64, 1:2], bass.AP(T["hy_b2"].tensor, l * 64, [[1, 64], [1, 1]])), W=[KP], dma=True)
        op("sp", DMA(w1s, T["hy_w1"][l]), W=[KP], dma=True)
        op("sp", DMA(w2s, T["hy_w2"][l]), W=[KP], dma=True)
        op("sp", DMA(w3s, T["hy_w3"][l]), W=[KP], dma=True)
        bx = hp[:, 1184:1192]
        op("dve", TS(bx[0:64, 0:2], bcol[0:64, 0:2], 0.5, ALU.mult), R=[KP], W=[KP])
        op("dve", TS(bx[0:64, 2:4], bcol[0:64, 0:2], 0.25, ALU.mult), R=[KP], W=[KP])
        wDb = P.alloc("wD", 8 * 768 // 2)
        wD = P.bf(wDb).rearrange("p (k n) -> p k n", k=8)
        op("pool", DMA(wD, T["w_in"][l][:, 1696:2464].rearrange("(k p) n -> p k n", p=128)), W=[wDb.k()], dma=True)
        ucb = P.alloc("ucT", 6 * 2560 // 2)
        ucT = P.bf(ucb).rearrange("p (m n) -> p m n", m=6)
        upb = [P.alloc(f"upad{i}", 2050) for i in range(2)]
        ctb = P.alloc("convtmp", 2048)
        ui = 0
        for (t0, nts, latent) in SEQS:
            L = nts * 128
            base = t0 * 128
            for m in range(6):
                ub = upb[ui % 2]
                up = P.f32(ub)
                ku = ub.k()
                ui += 1
                op("pool", MSET(up[:, 0:1], 0.0), W=[ku])
                op("pool", MSET(up[:, L + 1:L + 2], 0.0), W=[ku])
                for c0 in range(0, L, 512):
                    cw = min(512, L - c0)
                    b = P.bank()
                    op("pe", MM([(PS[b][:, 0:cw], wD[:, k, m * 128:(m + 1) * 128], hT[:, k, base + c0:base + c0 + cw], k == 0, k == 7)
                                 for k in range(8)]), R=[KH, wDb.k()], W=[psk(b)])
                    op("act", ACT(up[:, 1 + c0:1 + c0 + cw], PS[b][:, 0:cw], AF.Copy), R=[psk(b)], W=[ku])
                ct = P.f32(ctb)[:, 0:L]
                eng = "dve"
                op(eng, TS(ct, up[:, 0:L], shw[:, m, 0:1], ALU.mult, shb[:, m:m + 1], ALU.add), R=[ku, KP], W=[ctb.k()])
                op(eng, STT(ct, up[:, 1:L + 1], shw[:, m, 1:2], ct, ALU.mult, ALU.add), R=[ku, KP, ctb.k()], W=[ctb.k()])
                op(eng, STT(ucT[:, m, base:base + L], up[:, 2:L + 2], shw[:, m, 2:3], ct, ALU.mult, ALU.add), R=[ku, KP, ctb.k()],
                   W=[ucb.k((m, t0))])
        P.free(hTb, wDb, upb[0], upb[1], ctb)
        dump(f"ucT{l}", ucT, [ucb.k((m, t0)) for m in range(6) for (t0, _, _) in SEQS])

        for L in (2048, 256):
            seqs = [sq for sq in SEQS if sq[1] * 128 == L]
            nj = L // 256
            ntt = 2 * nj
            NF = L // 256
            HB = L // 2
            TB = min(512, HB)
            zb = P.alloc("hy_z", L)
            h1b = P.alloc("hy_h1", L)
            zT = P.f32(zb)[0:33, :]
            h1T = P.f32(h1b)[0:64, :]
            op("sp", DMA(zT, T[f"h_zT{L}"]), W=[zb.k()], dma=True)
            h2b = P.alloc("hy_h2", L)
            h2T = P.f32(h2b)[0:64, :]
            sinb = P.alloc("hy_sint", 1024)
            for (src_, dst_, w_, kk_, bi_, kr_, kw_) in ((zT, h1T, w1s, 33, 0, zb.k(), h1b.k()), (h1T, h2T, w2s, 64, 1, h1b.k(), h2b.k())):
                for c0 in range(0, L, 512):
                    cw = min(512, L - c0)
                    b = P.bank()
                    op("pe", MM([(PS[b][0:64, 0:cw], w_, src_[:, c0:c0 + cw], True, True)]), R=[KP, kr_], W=[psk(b)])
                    s2 = P.f32(sinb)[0:64, 0:cw]
                    s4 = P.f32(sinb)[0:64, 512:512 + cw]
                    op("act", ACT(s2, PS[b][0:64, 0:cw], AF.Sin, bias=bx[0:64, bi_:bi_ + 1], scale=0.5), R=[psk(b), KP], W=[sinb.k(0)])
                    op("act", ACT(s4, PS[b][0:64, 0:cw], AF.Sin, bias=bx[0:64, 2 + bi_:3 + bi_], scale=0.25), R=[psk(b), KP], W=[sinb.k(1)])
                    op("dve", TT(s4, s4, s4, ALU.mult), R=[sinb.k(1)], W=[sinb.k(1)])
                    op("dve", TS(s4, s4, -2.0, ALU.mult, 1.0, ALU.add), R=[sinb.k(1)], W=[sinb.k(1)])
                    op("dve", STT(dst_[:, c0:c0 + cw], s2, 2.0, s4, ALU.mult, ALU.mult), R=[sinb.k(0), sinb.k(1)], W=[kw_])
            P.free(zb, h1b, sinb)
            for o in range(2):
                winb = P.alloc("hy_win", ntt * 256)
                win = P.f32(winb).rearrange("p (q n) -> p q n", q=ntt)
                op("sp", DMA(P.f32(winb), T[f"h_win{L}"]), W=[winb.k()], dma=True)
                abb = P.alloc("hy_ab", 2 * ntt * 256 // 2)
                ab = P.bf(abb).rearrange("p (s q n) -> p s q n", s=2, q=ntt)
                accb = P.alloc("hy_acc", 256)
                acc = P.f32(accb)
                op("pool", MSET(acc, 0.0), W=[accb.k()])
                g2b = [P.alloc(f"hy_g2{i}", 512) for i in range(2)]
                abs_b = [P.alloc(f"hy_abs{i}", 512) for i in range(2)]
                for q in range(ntt):
                    b = P.bank()
                    op("pe", MM([(PS[b][:, :], h2T[:, q * 128:(q + 1) * 128], w3s[:, o * 512:(o + 1) * 512], True, True)]),
                       R=[h2b.k(), KP], W=[psk(b)])
                    g2 = P.f32(g2b[q % 2]).rearrange("p (d n) -> p d n", d=2)
                    kg = g2b[q % 2].k()
                    op("dve", TT(g2, PS[b][:, :].rearrange("p (d n) -> p d n", d=2), bc(win[:, q, :], [[0, 2], [1, 256]]), ALU.mult),
                       R=[psk(b), winb.k()], W=[kg])
                    if q == 0:
                        op("pool", MSET(g2[0:1, 1, :], 0.0), R=[kg], W=[kg])
                    op("pool", TT(ab[:, 0, q, :], g2[:, 0, :], g2[:, 1, :], ALU.add), R=[kg], W=[abb.k((0, q))])
                    op("pool", TT(ab[:, 1, q, :], g2[:, 0, :], g2[:, 1, :], ALU.subtract), R=[kg], W=[abb.k((1, q))])
                    av = P.f32(abs_b[q % 2])
                    op("act", ACT(av, P.f32(g2b[q % 2]), AF.Abs), R=[kg], W=[abs_b[q % 2].k()])
                    op("dve", TT(acc, acc, av[:, 0:256], ALU.add), R=[abs_b[q % 2].k(), accb.k()], W=[accb.k()])
                    op("dve", TT(acc, acc, av[:, 256:512], ALU.add), R=[abs_b[q % 2].k(), accb.k()], W=[accb.k()])
                P.free(g2b[0], g2b[1], abs_b[0], abs_b[1], winb)
                rnb = P.alloc("hy_rn", 256 + 256)
                rnrow = P.f32(rnb)[0:1, 0:256]
                RN = P.f32(rnb)[:, 256:512]
                b = P.bank()
                op("pe", MM([(PS[b][0:1, 0:256], onesf[:, 0:1], acc, True, True)]), R=[accb.k(), KC], W=[psk(b)])
                op("dve", TS(rnrow, PS[b][0:1, 0:256], EPS, ALU.add), R=[psk(b)], W=[rnb.k("row")])
                op("dve", RECIP(rnrow, rnrow), R=[rnb.k("row")], W=[rnb.k("row")])
                b = P.bank()
                op("pe", MM([(PS[b][:, 0:256], onesf[0:1, 0:128], rnrow, True, True)]), R=[rnb.k("row"), KC], W=[psk(b)])
                op("act", ACT(RN, PS[b][:, 0:256], AF.Copy), R=[psk(b)], W=[rnb.k()])
                P.free(accb)
                hyt[0] = P.alloc("hy_ftmp", 1024)
                Gb = P.alloc("hy_G", NF * 4 * 256 // 2)
                G = P.bf(Gb).rearrange("p (f q n) -> p f q n", f=NF, q=4)
                ftb_ = [P.alloc(f"hy_ft{i}", 2 * ntt * 128 // 2) for i in range(2)]
                osb = P.alloc("hy_osb", 512)
                for fc in range(NF):
                    tb_ = ftb_[fc % 2]
                    tf = P.bf(tb_).rearrange("p (s q n) -> p s q n", s=2, q=ntt)
                    op("sp", DMA(tf[:, 0], T[f"h_cf{L}"][fc].rearrange("p (q n) -> p q n", q=ntt)), W=[tb_.k()], dma=True)
                    op("sp", DMA(tf[:, 1], T[f"h_sf{L}"][fc].rearrange("p (q n) -> p q n", q=ntt)), W=[tb_.k()], dma=True)
                    bE, bO = P.bank(), P.bank()
                    for (bk_, a_) in ((bE, 0), (bO, 1)):
                        mm = []
                        for j in range(nj):
                            q = a_ * nj + j
                            mm.append((PS[bk_][:, 0:256], tf[:, 0, q, :], ab[:, 0, q, :], j == 0, False))
                            mm.append((PS[bk_][:, 256:512], tf[:, 1, q, :], ab[:, 1, q, :], False, j == nj - 1))
                        op("pe", MM(mm), R=[tb_.k()] + [abb.k((s_, a_ * nj + j)) for s_ in range(2) for j in range(nj)], W=[psk(bk_)])
                    osv = P.f32(osb)
                    op("act", ACT(osv, PS[bO][:, :], AF.Copy), R=[psk(bO)], W=[osb.k()])
                    RN2 = bc(RN, [[0, 2], [1, 256]])
                    tsum = P.f32(osb)
                    tmpb_ = hyt[0]
                    tsv = P.f32(tmpb_)[:, 0:512]
                    tdv = P.f32(tmpb_)[:, 512:1024]
                    op("dve", TT(tsv, PS[bE][:, :], osv, ALU.add), R=[psk(bE), osb.k()], W=[tmpb_.k(0)])
                    op("dve", TT(tdv, PS[bE][:, :], osv, ALU.subtract), R=[psk(bE), osb.k()], W=[tmpb_.k(1)])
                    op("pool", TT(G[:, fc, 0:2, :], tsv.rearrange("p (q n) -> p q n", q=2), RN2, ALU.mult), R=[tmpb_.k(0), rnb.k()], W=[Gb.k(fc)])
                    op("pool", TT(G[:, fc, 2:4, :], tdv.rearrange("p (q n) -> p q n", q=2), RN2, ALU.mult), R=[tmpb_.k(1), rnb.k()], W=[Gb.k(fc)])
                P.free(abb, rnb, ftb_[0], ftb_[1], osb, hyt[0])
                dump(f"G{l}_{L}_{o}", G, [Gb.k(fc) for fc in range(NF)])
                for (t0, nts, latent) in seqs:
                    hyena_conv(l, o, L, t0, G, Gb, ucT, ucb, catT, catb, dsk, KP)
                P.free(Gb)
            P.free(h2b)
        P.free(hpb, ucb)
        for k_ in list(Z1.keys()):
            P.free(Z1.pop(k_)[0])

    hyt = [None]
    Z1 = {}

    def hyena_conv(l, o, L, t0, G, Gb, ucT, ucb, catT, catb, dsk, KP):
        nj = L // 256
        ntt = 2 * nj
        NF = L // 256
        HB = L // 2
        TB = min(512, HB)
        base = t0 * 128
        if o == 0:
            zb_ = P.alloc("hy_z1T", 2 * L // 2)
            Z1[t0] = (zb_, P.bf(zb_).rearrange("p (c n) -> p c n", c=2))
            vin = ucT[:, 0:2, base:base + L]
            kvin = [ucb.k((m, t0)) for m in (0, 1)]
            gate = ucT[:, 2:4, base:base + L]
            kgate = [ucb.k((m, t0)) for m in (2, 3)]
            outv = Z1[t0][1]
            kout = [Z1[t0][0].k(0), Z1[t0][0].k(1)]
        else:
            vin = Z1[t0][1]
            kvin = [Z1[t0][0].k(0), Z1[t0][0].k(1)]
            gate = ucT[:, 4:6, base:base + L]
            kgate = [ucb.k((m, t0)) for m in (4, 5)]
            outv = catT[:, 6:8, base:base + L]
            kout = [catb.k(6), catb.k(7)]
        vtb = P.alloc("hy_vtok", ntt * 256 // 2)
        vt = P.bf(vtb).rearrange("p (q n) -> p q n", q=ntt)
        for q in range(0, ntt, 2):
            b = P.bank()
            mm = []
            for qq in (q, q + 1):
                a_, j = qq // nj, qq % nj
                for cc in range(2):
                    mm.append((PS[b][:, (qq - q) * 256 + cc * 128:(qq - q) * 256 + (cc + 1) * 128],
                               vin[:, cc, a_ + 256 * j:a_ + 256 * j + 255:2], ident, True, True))
            op("pe", MM(mm), R=kvin + [KC], W=[psk(b)])
            ev = "act" if (q // 2) % 2 == 0 else "dve"
            cpy = (lambda o_, i_: ACT(o_, i_, AF.Copy)) if ev == "act" else CP
            op(ev, cpy(vt[:, q:q + 2, :], PS[b][:, :].rearrange("p (q n) -> p q n", q=2)), R=[psk(b)], W=[vtb.k(q // 2)])
        pqb = P.alloc("hy_PQ", NF * 4 * 256 // 2)
        PQ = P.bf(pqb).rearrange("p (f q n) -> p f q n", f=NF, q=4)
        ftb_ = [P.alloc(f"hy_ft{i}", 2 * ntt * 128 // 2) for i in range(2)]
        osb = P.alloc("hy_osb", 512)
        tmb = P.alloc("hy_pw", 512 * 8)
        tm = P.f32(tmb)
        SSv, DDv, T1s, T2s, Av, Bv, T1d, T2d = (tm[:, i * 512:(i + 1) * 512] for i in range(8))
        allvt = [vtb.k(i) for i in range(nj)]
        for fc in range(NF):
            tb_ = ftb_[fc % 2]
            tf = P.bf(tb_).rearrange("p (s q n) -> p s q n", s=2, q=ntt)
            op("sp", DMA(tf[:, 0], T[f"h_cf{L}"][fc].rearrange("p (q n) -> p q n", q=ntt)), W=[tb_.k()], dma=True)
            op("sp", DMA(tf[:, 1], T[f"h_sf{L}"][fc].rearrange("p (q n) -> p q n", q=ntt)), W=[tb_.k()], dma=True)
            bE, bO = P.bank(), P.bank()
            for (bk_, a_) in ((bE, 0), (bO, 1)):
                mm = []
                for j in range(nj):
                    q = a_ * nj + j
                    mm.append((PS[bk_][:, 0:256], tf[:, 0, q, :], vt[:, q, :], j == 0, False))
                    mm.append((PS[bk_][:, 256:512], tf[:, 1, q, :], vt[:, q, :], False, j == nj - 1))
                op("pe", MM(mm), R=[tb_.k()] + allvt, W=[psk(bk_)])
            osv = P.f32(osb)
            op("act", ACT(osv, PS[bO][:, :], AF.Copy), R=[psk(bO)], W=[osb.k()])
            op("dve", TT(SSv, PS[bE][:, :], osv, ALU.add), R=[psk(bE), osb.k()], W=[tmb.k("S")])
            op("dve", TT(DDv, PS[bE][:, :], osv, ALU.subtract), R=[psk(bE), osb.k()], W=[tmb.k("D")])
            e1, e2 = ("dve", "pool") if fc % 2 == 0 else ("pool", "dve")
            for (X, gr, gi, dst, kx, e_, T1, T2) in ((SSv, 0, 1, Av, "S", e1, T1s, T2s), (DDv, 2, 3, Bv, "D", e2, T1d, T2d)):
                X2 = X.rearrange("p (q n) -> p q n", q=2)
                op(e_, TT(T1.rearrange("p (q n) -> p q n", q=2), X2, bc(G[:, fc, gr, :], [[0, 2], [1, 256]]), ALU.mult),
                   R=[tmb.k(kx), Gb.k(fc)], W=[tmb.k("T1" + kx)])
                op(e_, TT(T2.rearrange("p (q n) -> p q n", q=2), X2, bc(G[:, fc, gi, :], [[0, 2], [1, 256]]), ALU.mult),
                   R=[tmb.k(kx), Gb.k(fc)], W=[tmb.k("T2" + kx)])
                op(e_, TT(dst[:, 0:256], T1[:, 0:256], T2[:, 256:512], ALU.subtract), R=[tmb.k("T1" + kx), tmb.k("T2" + kx)], W=[tmb.k("A" + kx)])
                op(e_, TT(dst[:, 256:512], T2[:, 0:256], T1[:, 256:512], ALU.add), R=[tmb.k("T1" + kx), tmb.k("T2" + kx)], W=[tmb.k("A" + kx)])
            op("dve", TT(PQ[:, fc, 0:2, :], Av.rearrange("p (q n) -> p q n", q=2), Bv.rearrange("p (q n) -> p q n", q=2), ALU.add),
               R=[tmb.k("AS"), tmb.k("AD")], W=[pqb.k(fc)])
            op("pool", TT(PQ[:, fc, 2:4, :], Av.rearrange("p (q n) -> p q n", q=2), Bv.rearrange("p (q n) -> p q n", q=2), ALU.subtract),
               R=[tmb.k("AS"), tmb.k("AD")], W=[pqb.k(fc)])
        P.free(vtb, ftb_[0], ftb_[1], osb, tmb)
        itb = [P.alloc(f"hy_it{i}", 2 * NF * TB // 2) for i in range(2)]
        ytb = [P.alloc(f"hy_yt{i}", TB) for i in range(2)]
        CI4 = T[f"h_ci{L}"].rearrange("p (f a n) -> p f a n", f=NF, a=2)
        SI4 = T[f"h_si{L}"].rearrange("p (f a n) -> p f a n", f=NF, a=2)
        ii = 0
        allpq = [pqb.k(fc) for fc in range(NF)]
        for a_ in range(2):
            for tch in range(HB // TB):
                ib = itb[ii % 2]
                it = P.bf(ib).rearrange("p (s f n) -> p s f n", s=2, f=NF)
                op("sp", DMA(it[:, 0], CI4[:, :, a_, tch * TB:(tch + 1) * TB]), W=[ib.k()], dma=True)
                op("sp", DMA(it[:, 1], SI4[:, :, a_, tch * TB:(tch + 1) * TB]), W=[ib.k()], dma=True)
                ii += 1
                for cc in range(2):
                    b = P.bank()
                    mm = []
                    for fc in range(NF):
                        mm.append((PS[b][:, 0:TB], PQ[:, fc, 2 * a_, cc * 128:(cc + 1) * 128], it[:, 0, fc, :], fc == 0, False))
                        mm.append((PS[b][:, 0:TB], PQ[:, fc, 2 * a_ + 1, cc * 128:(cc + 1) * 128], it[:, 1, fc, :], False, fc == NF - 1))
                    op("pe", MM(mm), R=allpq + [ib.k()], W=[psk(b)])
                    tstart = a_ + 2 * tch * TB
                    sl = slice(tstart, tstart + 2 * TB - 1, 2)
                    yt = P.f32(ytb[cc])[:, 0:TB]
                    op("dve", STT(yt, vin[:, cc, sl], dsk[:, o, cc:cc + 1], PS[b][:, 0:TB], ALU.mult, ALU.add),
                       R=[psk(b), kvin[cc], KP], W=[ytb[cc].k()])
                    op("pool", TT(outv[:, cc, sl], yt, gate[:, cc, sl], ALU.mult), R=[ytb[cc].k(), kgate[cc]], W=[kout[cc]])
        P.free(pqb, itb[0], itb[1], ytb[0], ytb[1])


    return dict(nc=nc, P=P, T=T, DBG=DBG, layer_mod=layer_mod, ffn=ffn, mixer=mixer, loc=locals())


def build_full(dbg=()):
    B = build(dbg=dbg)
    T = B["T"]
    for l in range(2):
        B["layer_mod"](l)
        B["mixer"](l, T["xin"] if l == 0 else T["xb"], T["xa"])
        B["ffn"](l, T["xa"], T["xb"] if l == 0 else T["y"])
    B["P"].emit()
    return B


_NC_CACHE = {}


def core_inputs(core, inp, consts):
    m = {nm: np.ascontiguousarray(inp[nm], dtype=np.float32) for nm, _ in WEIGHT_SPECS}
    m.update(consts)
    m["xin"] = np.ascontiguousarray(np.concatenate(
        [inp["x_sample"][core], inp["x_prompt"][2 * core], inp["x_prompt"][2 * core + 1]], 0), dtype=np.float32)
    m["ckv_ctx"] = np.ascontiguousarray(inp["cache_ckv"][core], dtype=np.float32)
    m["kr_ctx"] = np.ascontiguousarray(inp["cache_krope"][core], dtype=np.float32)
    m["s0"] = np.ascontiguousarray(inp["state_ret"][core], dtype=np.float32)
    m["cvec"] = np.ascontiguousarray(np.stack([inp["c_ctx"], inp["c"][core]]), dtype=np.float32)
    return m


def kernel(**inputs):
    inp = {k: np.asarray(v) for k, v in inputs.items()}
    consts = get_consts()
    if "nc" not in _NC_CACHE:
        _NC_CACHE["nc"] = build_full()["nc"]
    nc = _NC_CACHE["nc"]
    in_maps = [core_inputs(c, inp, consts) for c in range(8)]
    res = run_bass_kernel_spmd(nc, in_maps, core_ids=list(range(8)))
    R = res.results
    y_prompt = np.zeros((16, 256, 1024), np.float32)
    y_sample = np.zeros((8, 2048, 1024), np.float32)
    new_ckv = np.zeros((16, 2, 256, 128), np.float32)
    new_kr = np.zeros((16, 2, 256, 32), np.float32)
    new_st = np.zeros((16, 2, 2, 4, 64, 64), np.float32)
    for c in range(8):
        y = R[c]["y"]
        y_sample[c] = y[0:2048]
        y_prompt[2 * c] = y[2048:2304]
        y_prompt[2 * c + 1] = y[2304:2560]
        new_ckv[2 * c:2 * c + 2] = R[c]["o_ckv"]
        new_kr[2 * c:2 * c + 2] = R[c]["o_kr"]
        new_st[2 * c:2 * c + 2] = R[c]["o_st"]
    return (y_prompt, y_sample, new_ckv, new_kr, new_st)
```

```python
import contextlib
import math
import numpy as np
import ml_dtypes
import concourse.bass as bass
import concourse.mybir as mybir
from concourse.bass_utils import run_bass_kernel_spmd

F32 = mybir.dt.float32
BF16 = mybir.dt.bfloat16
ALU = mybir.AluOpType
AF = mybir.ActivationFunctionType
AX = mybir.AxisListType
NPBF = ml_dtypes.bfloat16

NDMA_SLOTS = 8
D = 1024
DFF = 2816
NFF = 22
NT = 20
EPS = 1e-6
SEQS = [(0, 16, True), (16, 2, False), (18, 2, False)]


class Buf:
    def __init__(self, name, col0, ncols):
        self.name, self.col0, self.ncols = name, col0, ncols
        self.subs = set()
        self.inherit = set()

    def k(self, sub=None):
        self.subs.add(sub)
        return (self, sub)


class Prog:
    def __init__(self, nc):
        self.nc = nc
        self.ops = []
        self.last_writer = {}
        self.readers = {}
        self.stack = contextlib.ExitStack()
        self.live = []
        self.dead = []
        self.psn = 0

    def make_arena(self, ncols):
        self.arena = self.stack.enter_context(self.nc.sbuf_tensor("arena", [128, ncols], F32))
        self.arena_cols = ncols
        self.ps = [self.stack.enter_context(self.nc.psum_tensor(f"ps{i}", [128, 512], F32)) for i in range(8)]

    def bank(self):
        i = self.psn % 8
        self.psn += 1
        return i

    def alloc(self, name, ncols):
        ncols = int(math.ceil(ncols))
        segs = sorted((b.col0, b.ncols) for b in self.live)
        pos, found = 0, None
        for c0, n in segs:
            if c0 - pos >= ncols:
                found = pos
                break
            pos = max(pos, c0 + n)
        if found is None:
            if self.arena_cols - pos >= ncols:
                found = pos
            else:
                raise RuntimeError(f"arena OOM {name} {ncols}: live={[(b.name, b.ncols) for b in self.live]}")
        b = Buf(name, found, ncols)
        self.live.append(b)
        for ob in self.dead:
            if ob.col0 < found + ncols and found < ob.col0 + ob.ncols:
                for sk in ob.subs:
                    kk = (ob, sk)
                    w = self.last_writer.get(kk)
                    if w is not None:
                        b.inherit.add(w)
                    b.inherit.update(self.readers.get(kk, ()))
                b.inherit.update(ob.inherit)
        return b

    def free(self, *bs):
        for b in bs:
            self.live.remove(b)
            self.dead.append(b)

    def f32(self, b, p0=0, p1=128):
        return self.arena[p0:p1, b.col0:b.col0 + b.ncols]

    def bf(self, b, p0=0, p1=128):
        return self.arena[p0:p1, b.col0:b.col0 + b.ncols].bitcast(BF16)

    def op(self, eng, fn, R=(), W=(), dma=False):
        idx = len(self.ops)
        deps = set()
        for k in list(R) + list(W):
            if isinstance(k, tuple) and isinstance(k[0], Buf) and k[0].inherit:
                deps.update(k[0].inherit)
        for k in R:
            w = self.last_writer.get(k)
            if w is not None:
                deps.add(w)
        for k in W:
            w = self.last_writer.get(k)
            if w is not None:
                deps.add(w)
            deps.update(self.readers.get(k, ()))
        for k in R:
            self.readers.setdefault(k, []).append(idx)
        for k in W:
            self.last_writer[k] = idx
            self.readers[k] = []
        self.ops.append(dict(eng=eng, fn=fn, deps=deps, dma=dma, signal=False))
        return idx

    def emit(self):
        nc, ops = self.nc, self.ops
        engs = ["pe", "act", "dve", "pool", "sp"]
        per = {e: [] for e in engs}
        for i, o in enumerate(ops):
            per[o["eng"]].append(i)
        for e in engs:
            seen = {pe_: -1 for pe_ in engs}
            for i in per[e]:
                o = ops[i]
                need = {}
                o["wdeps"] = []
                for d in o["deps"]:
                    od = ops[d]
                    if od["dma"]:
                        o["wdeps"].append(d)
                    else:
                        need[od["eng"]] = max(need.get(od["eng"], -1), d)
                for pe_, d in need.items():
                    if d > seen[pe_]:
                        seen[pe_] = d
                        o["wdeps"].append(d)
                        ops[d]["signal"] = True
        sems = {e: self.stack.enter_context(nc.semaphore("s_" + e)) for e in engs}
        dsems = {e: [self.stack.enter_context(nc.semaphore(f"d_{e}{i}")) for i in range(NDMA_SLOTS)]
                 for e in ("sp", "pool", "act")}
        cnt = {e: 0 for e in engs}
        dcnt = {e: 0 for e in dsems}
        for o in ops:
            e = o["eng"]
            if o["dma"]:
                j = dcnt[e]
                dcnt[e] += 1
                o["sem"] = dsems[e][j % NDMA_SLOTS]
                o["val"] = 16 * (j // NDMA_SLOTS + 1)
                o["prev"] = 16 * (j // NDMA_SLOTS)
            elif o["signal"]:
                cnt[e] += 1
                o["sem"] = sems[e]
                o["val"] = cnt[e]
        nw = [0]

        def run_engine(ename, eobj):
            waited = {}
            for i in per[ename]:
                o = ops[i]
                wl = {}
                for d in o["wdeps"]:
                    od = ops[d]
                    s = od["sem"]
                    if od["val"] > wl.get(s.name, (s, 0))[1]:
                        wl[s.name] = (s, od["val"])
                if o["dma"] and o["prev"] > 0:
                    s = o["sem"]
                    if o["prev"] > wl.get(s.name, (s, 0))[1]:
                        wl[s.name] = (s, o["prev"])
                for nm, (s, v) in wl.items():
                    if waited.get(nm, 0) >= v:
                        continue
                    eobj.wait_ge(s, v)
                    nw[0] += 1
                    waited[nm] = v
                ins = o["fn"](eobj)
                if o["dma"]:
                    ins.then_inc(o["sem"], 16)
                elif o["signal"]:
                    ins.then_inc(o["sem"], 1)
            if ename in dsems:
                n = dcnt[ename]
                for slot in range(NDMA_SLOTS):
                    k = (n - slot + NDMA_SLOTS - 1) // NDMA_SLOTS
                    if k > 0 and waited.get(dsems[ename][slot].name, 0) < 16 * k:
                        eobj.wait_ge(dsems[ename][slot], 16 * k)

        with nc.Block() as block:
            @block.tensor
            def _(e):
                run_engine("pe", e)

            @block.scalar
            def _(e):
                run_engine("act", e)

            @block.vector
            def _(e):
                run_engine("dve", e)

            @block.gpsimd
            def _(e):
                run_engine("pool", e)

            @block.sync
            def _(e):
                run_engine("sp", e)
        self.stats = dict(nops=len(ops), nwaits=nw[0], cnt=cnt, dcnt=dcnt)


def MM(specs):
    def f(e):
        ins = None
        for (o, l, r, st, sp) in specs:
            ins = e.matmul(o, lhsT=l, rhs=r, start=st, stop=sp)
        return ins
    return f


def ACT(out, in_, func, bias=None, scale=1.0, accum=None):
    def f(e):
        kw = {}
        if bias is not None:
            kw["bias"] = bias
        if accum is not None:
            kw["accum_out"] = accum
        return e.activation(out=out, in_=in_, func=func, scale=scale, **kw)
    return f


def TT(out, a, b, op):
    return lambda e: e.tensor_tensor(out=out, in0=a, in1=b, op=op)


def TS(out, a, s1, op0, s2=None, op1=None):
    if op1 is None:
        return lambda e: e.tensor_scalar(out=out, in0=a, scalar1=s1, scalar2=None, op0=op0)
    return lambda e: e.tensor_scalar(out=out, in0=a, scalar1=s1, scalar2=s2, op0=op0, op1=op1)


def STT(out, a, s, b, op0, op1):
    return lambda e: e.scalar_tensor_tensor(out=out, in0=a, scalar=s, in1=b, op0=op0, op1=op1)


def CP(out, in_):
    return lambda e: e.tensor_copy(out=out, in_=in_)


def RED(out, in_, op=None):
    return lambda e: e.tensor_reduce(out=out, in_=in_, axis=AX.X, op=op or ALU.add)


def RECIP(out, in_):
    return lambda e: e.reciprocal(out=out, in_=in_)


def MSET(out, v):
    return lambda e: e.memset(out, v)


def DMA(out, in_):
    return lambda e: e.dma_start(out=out, in_=in_)


def DMAS(out, in_):
    return lambda e: e.dma_start(out=out, in_=in_, allow_slow_non_contiguous=True)


def bc(ap, dims):
    return bass.AP(ap.tensor, ap.offset, [list(ap.ap[0])] + [list(d) for d in dims])


def parity_perm(L):
    nj = L // 256
    idx = np.zeros((2, nj, 128), np.int64)
    for pi in range(2):
        for j in range(nj):
            idx[pi, j] = pi + 2 * (128 * j + np.arange(128))
    return idx


def host_consts():
    C = {}
    C["ident"] = np.eye(128, dtype=np.float32).astype(NPBF)
    tok = np.arange(2048)
    row, col = tok // 64, tok % 64
    for nm, half in (("ret", 16), ("mla", 8)):
        inv = 10000.0 ** (-np.arange(half, dtype=np.float64) / half)
        ang = np.stack([row[:, None] * inv[None], col[:, None] * inv[None]], axis=1)
        ang = ang.reshape(16, 128, 2, half).transpose(1, 0, 2, 3).reshape(128, 16 * 2 * half)
        C["cos_" + nm] = np.cos(ang).astype(np.float32)
        C["sin_" + nm] = np.sin(ang).astype(np.float32)
    m = np.arange(128)[:, None].astype(np.float64)
    c = np.arange(128)[None, :].astype(np.float64)
    C["ret_dpos"] = np.tile(np.maximum(c - m, 0), (1, 4)).astype(np.float32)
    C["ret_dneg"] = np.tile(np.maximum(m - c, 0), (1, 4)).astype(np.float32)
    C["ret_mge"] = np.tile((c >= m) * 0.125, (1, 4)).astype(np.float32)
    C["ret_mle"] = np.tile((c <= m) * 0.125, (1, 4)).astype(np.float32)
    p = np.arange(128, dtype=np.float64)
    C["ret_cols"] = np.stack([p + 1, 127 - p, 128 - p, p], axis=1).astype(np.float32)
    a = 2 * np.pi * np.outer(np.arange(64), np.arange(64)) / 64
    C64 = np.kron(np.eye(2), np.cos(a))
    S64 = np.kron(np.eye(2), np.sin(a))
    C["f_c64"] = C64.astype(NPBF)
    C["f_s64n"] = (-S64).astype(NPBF)
    for L in (2048, 256):
        idx = parity_perm(L)
        nj = L // 256
        l = idx.astype(np.float64)
        pp = np.arange(L // 2, dtype=np.float64)
        ang = 2 * np.pi * l[..., None] * pp / L
        sc = 1.0 / math.sqrt(L * 64)
        C[f"f_cl{L}"] = (np.cos(ang) * sc).transpose(2, 0, 1, 3).reshape(128, -1).astype(NPBF)
        C[f"f_sl{L}"] = (np.sin(ang) * sc).transpose(2, 0, 1, 3).reshape(128, -1).astype(NPBF)
        pos = idx.reshape(-1).astype(np.float64)
        t = pos / L
        bands = np.arange(1, 17, dtype=np.float64)
        ang2 = (2 * np.pi / L) * pos[:, None] * bands[None]
        z = np.concatenate([t[:, None], np.sin(ang2), np.cos(ang2)], axis=1)
        C[f"h_zT{L}"] = np.ascontiguousarray(z.T).astype(np.float32)
        deltas = np.abs(np.linspace(math.log(1e-2) / 1.5, math.log(1e-2) / 0.3, 256))
        win = np.exp(-t[:, None] * deltas[None])
        C[f"h_win{L}"] = win.reshape(2 * nj, 128, 256).transpose(1, 0, 2).reshape(128, -1).astype(np.float32)
        N = 2 * L
        nf = L // 256 if L >= 256 else 1
        F = L // 2
        f = np.arange(F, dtype=np.float64) + 0.5
        s = idx.astype(np.float64)
        psi = 2 * np.pi * s[..., None] * f / N
        fch = F // 128
        cf = np.cos(psi).reshape(2, nj, 128, fch, 128).transpose(3, 2, 0, 1, 4).reshape(fch, 128, -1)
        sf = (-np.sin(psi)).reshape(2, nj, 128, fch, 128).transpose(3, 2, 0, 1, 4).reshape(fch, 128, -1)
        C[f"h_cf{L}"] = cf.astype(NPBF)
        C[f"h_sf{L}"] = sf.astype(NPBF)
        tt = np.stack([2 * np.arange(L // 2), 2 * np.arange(L // 2) + 1]).astype(np.float64)
        psi2 = 2 * np.pi * f[:, None, None] * tt[None] / N
        ci = (2.0 / N) * np.cos(psi2)
        si = -(2.0 / N) * np.sin(psi2)
        C[f"h_ci{L}"] = ci.reshape(fch, 128, 2, L // 2).transpose(1, 0, 2, 3).reshape(128, -1).astype(NPBF)
        C[f"h_si{L}"] = si.reshape(fch, 128, 2, L // 2).transpose(1, 0, 2, 3).reshape(128, -1).astype(NPBF)
    return C


_CONSTS = None


def get_consts():
    global _CONSTS
    if _CONSTS is None:
        _CONSTS = host_consts()
    return _CONSTS


WEIGHT_SPECS = [
    ("w_ada", (2, 1024, 6144)), ("b_ada", (2, 6144)), ("norm_g", (2, 4, 1024)), ("w_in", (2, 1024, 2464)),
    ("w_out", (2, 1024, 1024)), ("ret_decay", (2, 2, 4)), ("mla_q_norm", (2, 256)), ("mla_kv_norm", (2, 128)),
    ("mla_w_uq", (2, 256, 384)), ("mla_w_ukv", (2, 128, 512)), ("hy_short_w", (2, 3, 768)),
    ("hy_short_b", (2, 768)), ("hy_w1", (2, 33, 64)), ("hy_b1", (2, 64)), ("hy_w2", (2, 64, 64)),
    ("hy_b2", (2, 64)), ("hy_w3", (2, 64, 1024)), ("hy_bias", (2, 2, 256)), ("w_gate", (2, 1024, 2816)),
    ("w_up", (2, 1024, 2816)), ("w_down", (2, 2816, 1024)),
]
CORE_SPECS = [("xin", (2560, 1024)), ("ckv_ctx", (2, 256, 128)), ("kr_ctx", (2, 256, 32)),
              ("s0", (2, 2, 4, 64, 64)), ("cvec", (2, 1024))]
OUT_SPECS = [("y", (2560, 1024)), ("o_ckv", (2, 2, 256, 128)), ("o_kr", (2, 2, 256, 32)),
             ("o_st", (2, 2, 2, 4, 64, 64))]


def build(dbg=(), stop_after=None):
    nc = bass.Bass("TRN2", target_bir_lowering=False)
    P = Prog(nc)
    C = get_consts()
    T = {}
    for nm, shp in WEIGHT_SPECS + CORE_SPECS:
        T[nm] = nc.dram_tensor(nm, list(shp), F32, kind="ExternalInput").ap()
    for nm, arr in C.items():
        T[nm] = nc.dram_tensor(nm, list(arr.shape), BF16 if arr.dtype == NPBF else F32, kind="ExternalInput").ap()
    for nm, shp in OUT_SPECS:
        T[nm] = nc.dram_tensor(nm, list(shp), F32, kind="ExternalOutput").ap()
    T["xa"] = nc.dram_tensor("xa", [2560, 1024], F32, kind="Internal").ap()
    T["xb"] = nc.dram_tensor("xb", [2560, 1024], F32, kind="Internal").ap()
    DBG = {}

    P.make_arena(53184)
    PS = P.ps
    op = P.op

    def psk(i):
        return ("ps", i)

    def dump(name, ap, keys, shape=None):
        if name not in dbg:
            return
        shape = list(shape or ap.shape)
        d = nc.dram_tensor("dbg_" + name, shape, F32, kind="ExternalOutput").ap()
        DBG[name] = d
        if len(shape) == 3:
            for i_ in range(shape[1]):
                op("pool", DMA(d[:, i_, :], ap[:, i_, :]), R=keys, dma=True)
        else:
            op("pool", DMA(d, ap), R=keys, dma=True)

    cb = P.alloc("consts", 64 + 1 + 8 + 64 * 3 + 128)
    cw = P.f32(cb)
    o = [0]

    def take(n, dt=F32, src=None):
        src = cw if src is None else src
        v = src[:, o[0]:o[0] + n]
        o[0] += n
        return v.bitcast(BF16) if dt == BF16 else v
    ident = take(64, BF16)
    epsc = take(1)
    cols8 = take(8)
    c64, s64n, onesb = take(64, BF16), take(64, BF16), take(64, BF16)
    onesf = take(128)
    KC = cb.k()
    for dst, nm in ((ident, "ident"), (c64, "f_c64"), (s64n, "f_s64n")):
        op("sp", DMA(dst, T[nm]), W=[KC], dma=True)
    op("pool", MSET(epsc, EPS), W=[KC])
    op("pool", MSET(cols8[:, 0:1], -math.pi), W=[KC])
    op("pool", MSET(onesb, 1.0), W=[KC])
    op("pool", MSET(onesf, 1.0), W=[KC])
    MC = {}

    def mixer_consts():
        mb = P.alloc("mconsts", 4 * 512 + 4 + 2 * 512 + 2 * 256)
        o[0] = 0
        mw = P.f32(mb)
        for nm, n in (("ret_dpos", 512), ("ret_dneg", 512), ("ret_mge", 512), ("ret_mle", 512), ("ret_cols", 4),
                      ("cos_ret", 512), ("sin_ret", 512), ("cos_mla", 256), ("sin_mla", 256)):
            MC[nm] = take(n, src=mw)
            op("sp", DMA(MC[nm], T[nm]), W=[mb.k()], dma=True)
        MC["buf"] = mb
        MC["key"] = mb.k()

    mcolb = P.alloc("modcols", 2 * 4 * 8)
    mcol = P.f32(mcolb).rearrange("p (r q k) -> p r q k", r=2, q=4)
    gbb = P.alloc("gbc", 4 * 1024)
    gbc = P.f32(gbb).rearrange("p (r q n) -> p r q n", r=2, q=2)
    PR = (0, 32)

    def layer_mod(l):
        rb = P.alloc("rows", 6144 + 4096 + 6144)
        rows = P.f32(rb)[0:33, :]
        m = rows[:, 0:6144]
        ngr = rows[:, 6144:10240]
        rowt = rows[:, 10240:16384]
        KR = rb.k()
        scb = P.alloc("silu_c", 8 * 34 // 2)
        sct = P.bf(scb).rearrange("p (k r) -> p k r", r=34)
        cfb = P.alloc("c_f32", 16)
        cf32 = P.f32(cfb).rearrange("p (k r) -> p k r", r=2)
        for r in range(2):
            op("sp", DMAS(cf32[:, :, r:r + 1], bass.AP(T["cvec"].tensor, r * 1024, [[1, 128], [128, 8], [1, 1]])),
               W=[cfb.k()], dma=True)
        op("pool", MSET(sct, 0.0), W=[scb.k()])
        for r in range(2):
            op("act", ACT(sct[:, :, PR[r]:PR[r] + 1], cf32[:, :, r:r + 1], AF.Silu), R=[cfb.k()], W=[scb.k()])
        op("sp", DMA(ngr, bass.AP(T["norm_g"].tensor, l * 4096, [[0, 33], [1, 4096]])), W=[KR], dma=True)
        wab = [P.alloc(f"wada{i}", 8 * 512 // 2) for i in range(2)]
        badb = P.alloc("bada", 2 * 512)
        for nb in range(12):
            wb_ = wab[nb % 2]
            par = nb % 2
            wv = P.bf(wb_).rearrange("p (k n) -> p k n", k=8)
            op("pool", DMA(wv, T["w_ada"][l][:, nb * 512:(nb + 1) * 512].rearrange("(k p) n -> p k n", p=128)),
               W=[wb_.k()], dma=True)
            bv = P.f32(badb)[0:33, par * 512:par * 512 + 512]
            op("sp", DMA(bv, bass.AP(T["b_ada"].tensor, l * 6144 + nb * 512, [[0, 33], [1, 512]])),
               W=[badb.k(par)], dma=True)
            b = P.bank()
            op("pe", MM([(PS[b][0:33, :], sct[:, k, 0:33], wv[:, k, :], k == 0, k == 7) for k in range(8)]),
               R=[scb.k(), wb_.k()], W=[psk(b)])
            op("dve", TT(m[:, nb * 512:(nb + 1) * 512], PS[b][0:33, :], bv, ALU.add),
               R=[psk(b), badb.k(par)], W=[KR])
        for r in range(2):
            dump(f"mod{l}{r}", m[PR[r]:PR[r] + 1, :], [KR])
        op("dve", STT(rowt[:, 0:1024], m[:, 1024:2048], 1.0, ngr[:, 0:1024], ALU.add, ALU.mult), R=[KR], W=[KR])
        op("dve", CP(rowt[:, 1024:2048], m[:, 0:1024]), R=[KR], W=[KR])
        op("dve", STT(rowt[:, 2048:3072], m[:, 4096:5120], 1.0, ngr[:, 2048:3072], ALU.add, ALU.mult), R=[KR], W=[KR])
        op("dve", CP(rowt[:, 3072:4096], m[:, 3072:4096]), R=[KR], W=[KR])
        op("dve", TT(rowt[:, 4096:5120], m[:, 2048:3072], ngr[:, 1024:2048], ALU.mult), R=[KR], W=[KR])
        op("dve", TT(rowt[:, 5120:6144], m[:, 5120:6144], ngr[:, 3072:4096], ALU.mult), R=[KR], W=[KR])
        for r in range(2):
            pr = PR[r]
            b = P.bank()
            op("pe", MM([(PS[b][:, q * 8 + k:q * 8 + k + 1], rowt[pr:pr + 1, q * 1024 + k * 128:q * 1024 + (k + 1) * 128],
                          onesf[pr:pr + 1, 0:1], True, True) for q in range(4) for k in range(8)]),
               R=[KR, KC], W=[psk(b)])
            op("dve", CP(mcol[:, r], PS[b][:, 0:32].rearrange("p (q k) -> p q k", q=4)), R=[psk(b)], W=[mcolb.k()])
            for q in range(2):
                for hf in range(2):
                    b = P.bank()
                    c0 = (4 + q) * 1024 + hf * 512
                    op("pe", MM([(PS[b][:, :], onesf[pr:pr + 1, 0:128], rowt[pr:pr + 1, c0:c0 + 512], True, True)]),
                       R=[KR, KC], W=[psk(b)])
                    op("act", ACT(gbc[:, r, q, hf * 512:(hf + 1) * 512], PS[b][:, :], AF.Copy), R=[psk(b)], W=[gbb.k()])
        P.free(rb, scb, cfb, wab[0], wab[1], badb)

    def rstd_from_ss(ss, out, n, keys_r, keys_w, tmp):
        op("act", ACT(tmp, ss, AF.Sqrt, bias=epsc[0:ss.shape[0], :], scale=1.0 / n), R=keys_r + [KC], W=[keys_w[1]])
        op("dve", RECIP(out, tmp), R=[keys_w[1]], W=[keys_w[0]])

    xtb = [None, None]
    xnb = [P.alloc(f"xn{i}", 512) for i in range(2)]
    stb = P.alloc("stats", 64)
    stv = P.f32(stb)
    tcount = [0]

    def norm_transpose(src, tile, r, q0, dstT, dcol, xkeep=None):
        i = tcount[0] % 2
        tcount[0] += 1
        xt = P.f32(xtb[i]) if xkeep is None else xkeep[0]
        kx = xtb[i].k() if xkeep is None else xkeep[1]
        op("sp", DMA(xt, src[tile * 128:(tile + 1) * 128, :]), R=[("dram", src.tensor.name, tile)], W=[kx], dma=True)
        ss, rs, tm = stv[:, i * 4:i * 4 + 1], stv[:, i * 4 + 1:i * 4 + 2], stv[:, i * 4 + 2:i * 4 + 3]
        op("act", ACT(P.bf(xnb[i]), xt, AF.Square, accum=ss), R=[kx], W=[xnb[i].k(), stb.k(("ss", i))])
        rstd_from_ss(ss, rs, 1024.0, [stb.k(("ss", i))], [stb.k(("rs", i)), stb.k(("tm", i))], tm)
        xn = P.bf(xnb[i])
        op("dve", TS(xn, xt, rs, ALU.mult), R=[kx, stb.k(("rs", i))], W=[xnb[i].k()])
        for half in range(2):
            b = P.bank()
            op("pe", MM([(PS[b][:, j * 128:(j + 1) * 128], xn[:, (half * 4 + j) * 128:(half * 4 + j + 1) * 128], ident, True, True)
                         for j in range(4)]), R=[xnb[i].k(), KC], W=[psk(b)])
            for j in range(4):
                kc = half * 4 + j
                o_ = dstT[:, kc, dcol:dcol + 128]
                src_ps = PS[b][:, j * 128:(j + 1) * 128]
                A, B = mcol[:, r, q0, kc:kc + 1], mcol[:, r, q0 + 1, kc:kc + 1]
                if half == 0:
                    op("act", ACT(o_, src_ps, AF.Identity, bias=B, scale=A), R=[psk(b), mcolb.k()], W=[dstT_key[0]])
                else:
                    op("dve", TS(o_, src_ps, A, ALU.mult, B, ALU.add), R=[psk(b), mcolb.k()], W=[dstT_key[0]])

    dstT_key = [None]
    J2 = [None]
    ROPEB = [None]

    def resid_update(ps2, tile, r, q, xt, kx, dst, ri):
        junk2b = J2[0]
        ssa, ssb, ss, rs, tm = (stv[:, 16 + ri * 8 + j:16 + ri * 8 + j + 1] for j in range(5))
        kk = stb.k(("ru", ri))
        for hf, sx in ((0, ssa), (1, ssb)):
            op("act", ACT(P.f32(junk2b)[:, hf * 512:(hf + 1) * 512], PS[ps2[hf]][:, :], AF.Square, accum=sx),
               R=[psk(ps2[hf])], W=[junk2b.k(hf), kk])
        op("dve", TT(ss, ssa, ssb, ALU.add), R=[kk], W=[kk])
        op("act", ACT(tm, ss, AF.Sqrt, bias=epsc, scale=1.0 / 1024), R=[kk, KC], W=[kk])
        op("dve", RECIP(rs, tm), R=[kk], W=[kk])
        for hf in range(2):
            tmp = P.f32(junk2b)[:, hf * 512:(hf + 1) * 512]
            op("dve", STT(tmp, PS[ps2[hf]][:, :], rs, gbc[:, r, q, hf * 512:(hf + 1) * 512], ALU.mult, ALU.mult),
               R=[psk(ps2[hf]), kk, gbb.k()], W=[junk2b.k(hf)])
            op("pool", TT(xt[:, hf * 512:(hf + 1) * 512], tmp, xt[:, hf * 512:(hf + 1) * 512], ALU.add),
               R=[junk2b.k(hf), kx], W=[kx])
        op("sp", DMA(dst[tile * 128:(tile + 1) * 128, :], xt), R=[kx], W=[("dram", dst.tensor.name, tile)], dma=True)

    junk2b = None

    def ffn(l, src, dst):
        wgb = P.alloc("wg", 8 * DFF // 2)
        wub = P.alloc("wu", 8 * DFF // 2)
        wdb = P.alloc("wd", NFF * 1024 // 2)
        wg = P.bf(wgb).rearrange("p (k n) -> p k n", k=8)
        wu = P.bf(wub).rearrange("p (k n) -> p k n", k=8)
        wd = P.bf(wdb).rearrange("p (k n) -> p k n", k=NFF)
        FB = 4
        for f0 in range(0, NFF, FB):
            f1 = min(NFF, f0 + FB)
            for (wv_, nm_, wb__) in ((wg, "w_gate", wgb), (wu, "w_up", wub)):
                op("pool", DMA(wv_[:, :, f0 * 128:f1 * 128], T[nm_][l][:, f0 * 128:f1 * 128].rearrange("(k p) n -> p k n", p=128)),
                   W=[wb__.k(f0 // FB)], dma=True)
        for k in range(0, NFF, 2):
            op("pool", DMA(wd[:, k:k + 2, :], T["w_down"][l][k * 128:(k + 2) * 128, :].rearrange("(k p) n -> p k n", p=128)),
               W=[wdb.k(k // 2)], dma=True)
        h2b = P.alloc("h2T", 8 * 512 // 2)
        h2T = P.bf(h2b).rearrange("p (k n) -> p k n", k=8)
        aTb = P.alloc("aT", NFF * 512 // 2)
        aT = P.bf(aTb).rearrange("p (k n) -> p k n", k=NFF)
        xgb = P.alloc("xgrp", 4 * 1024)
        sgb = [P.alloc(f"sg{i}", 256) for i in range(2)]
        J2[0] = P.alloc("junk2", 1024)
        for g in range(5):
            r = 1 if g < 4 else 0
            dstT_key[0] = h2b.k()
            for j in range(4):
                tile = g * 4 + j
                xt = P.f32(xgb)[:, j * 1024:(j + 1) * 1024]
                norm_transpose(src, tile, r, 2, h2T, j * 128, xkeep=(xt, xgb.k(j)))
            for fc in range(NFF):
                bg, bu = P.bank(), P.bank()
                op("pe", MM([(PS[bg][:, :], wg[:, k, fc * 128:(fc + 1) * 128], h2T[:, k, :], k == 0, k == 7) for k in range(8)]),
                   R=[wgb.k(fc // FB), h2b.k()], W=[psk(bg)])
                op("pe", MM([(PS[bu][:, :], wu[:, k, fc * 128:(fc + 1) * 128], h2T[:, k, :], k == 0, k == 7) for k in range(8)]),
                   R=[wub.k(fc // FB), h2b.k()], W=[psk(bu)])
                sg = P.bf(sgb[fc % 2])
                op("act", ACT(sg, PS[bg][:, :], AF.Silu), R=[psk(bg)], W=[sgb[fc % 2].k()])
                op("dve", TT(aT[:, fc, :], sg, PS[bu][:, :], ALU.mult), R=[sgb[fc % 2].k(), psk(bu)], W=[aTb.k(fc)])
            for j in range(4):
                tile = g * 4 + j
                b0, b1 = P.bank(), P.bank()
                for hf, b in ((0, b0), (1, b1)):
                    op("pe", MM([(PS[b][:, :], aT[:, fc, j * 128:(j + 1) * 128], wd[:, fc, hf * 512:(hf + 1) * 512], fc == 0, fc == NFF - 1)
                                 for fc in range(NFF)]), R=[aTb.k(fc) for fc in range(NFF)] + [wdb.k(k_) for k_ in range(NFF // 2)], W=[psk(b)])
                xt = P.f32(xgb)[:, j * 1024:(j + 1) * 1024]
                resid_update((b0, b1), tile, r, 1, xt, xgb.k(j), dst, j % 2)
        P.free(wgb, wub, wdb, h2b, aTb, xgb, sgb[0], sgb[1], J2[0])

    def mixer(l, src, dst, parts=("ret", "mla", "four", "hy"), do_out=True):
        mixer_consts()
        KMC = MC["key"]
        xtb[0], xtb[1] = P.alloc("xt0", 1024), P.alloc("xt1", 1024)
        hTb = P.alloc("hT", 8 * 2560 // 2)
        hT = P.bf(hTb).rearrange("p (k n) -> p k n", k=8)
        catb = P.alloc("catT", 8 * 2560 // 2)
        catT = P.bf(catb).rearrange("p (k n) -> p k n", k=8)
        KH = hTb.k()
        if dbg:
            op("pool", MSET(catT, 0.0), W=[catb.k(c) for c in range(8)])
        dstT_key[0] = KH
        for t in range(NT):
            norm_transpose(src, t, 1 if t < 16 else 0, 0, hT, t * 128)
        dump(f"hT{l}", hT, [KH])
        P.free(xtb[0], xtb[1])
        ROPEB[0] = P.alloc("ropeb", 128 + 512)
        if "ret" in parts:
            retention_all(l, hT, KH, catT, catb, KMC)
        if "mla" in parts:
            mla_all(l, hT, KH, catT, catb, KMC)
        P.free(ROPEB[0], MC["buf"])
        if "four" in parts:
            fourier_all(l, hT, KH, catT, catb)
        if "hy" in parts:
            hyena_all(l, hT, KH, catT, catb, hTb)
        else:
            P.free(hTb)
        dump(f"catT{l}", catT, [catb.k(c) for c in range(8)])
        if do_out:
            xtb[0], xtb[1] = P.alloc("xt0", 1024), P.alloc("xt1", 1024)
            J2[0] = P.alloc("junk2", 1024)
            wob = P.alloc("wout", 8 * 1024 // 2)
            wo = P.bf(wob).rearrange("p (k n) -> p k n", k=8)
            op("pool", DMA(wo, T["w_out"][l].rearrange("(k p) n -> p k n", p=128)), W=[wob.k()], dma=True)
            for t in range(NT):
                r = 1 if t < 16 else 0
                i2 = t % 2
                xt = P.f32(xtb[i2])
                op("sp", DMA(xt, src[t * 128:(t + 1) * 128, :]), R=[("dram", src.tensor.name, t)], W=[xtb[i2].k()], dma=True)
                b0, b1 = P.bank(), P.bank()
                for hf, b in ((0, b0), (1, b1)):
                    op("pe", MM([(PS[b][:, :], catT[:, k, t * 128:(t + 1) * 128], wo[:, k, hf * 512:(hf + 1) * 512], k == 0, k == 7)
                                 for k in range(8)]), R=[catb.k(c) for c in range(8)] + [wob.k()], W=[psk(b)])
                resid_update((b0, b1), t, r, 0, xt, xtb[i2].k(), dst, i2)
            P.free(wob, xtb[0], xtb[1], J2[0])
        P.free(catb)

    def retention_all(l, hT, KH, catT, catb, KMC):
        wAb = P.alloc("wA", 8 * 1024 // 2)
        wA = P.bf(wAb).rearrange("p (k n) -> p k n", k=8)
        op("pool", DMA(wA, T["w_in"][l][:, 0:1024].rearrange("(k p) n -> p k n", p=128)), W=[wAb.k()], dma=True)
        rtb = P.alloc("rtabs", 8 + 8 + 8 + 16 + 512 + 4 + 4)
        rt = P.f32(rtb)
        KT_ = rtb.k()
        decb = rt[:, 0:8]
        lgb = rt[:, 8:16]
        tmp8 = rt[:, 16:24]
        xz = rt[:, 24:40].rearrange("p (q h) -> p q h", q=4)
        dmk = rt[:, 40:552]
        lgsel = rt[:, 552:556].rearrange("p (d q) -> p d q", d=2)
        gsel = rt[:, 556:560].rearrange("p (d q) -> p d q", d=2)
        op("sp", DMA(decb, bass.AP(T["ret_decay"].tensor, l * 8, [[0, 128], [1, 8]])), W=[KT_], dma=True)
        op("act", ACT(tmp8, decb, AF.Exp, scale=-1.0), R=[KT_], W=[KT_])
        op("act", ACT(tmp8, tmp8, AF.Ln, bias=onesf[:, 0:1], scale=1.0), R=[KT_, KC], W=[KT_])
        op("dve", TS(lgb, tmp8, -1.0, ALU.mult), R=[KT_], W=[KT_])
        rc = MC["ret_cols"]
        op("act", ACT(xz[:, 0, :], lgb[:, 0:4], AF.Exp, scale=rc[:, 0:1]), R=[KT_, KMC], W=[KT_])
        op("act", ACT(xz[:, 1, :], lgb[:, 0:4], AF.Exp, scale=rc[:, 1:2]), R=[KT_, KMC], W=[KT_])
        op("act", ACT(xz[:, 2, :], lgb[:, 4:8], AF.Exp, scale=rc[:, 2:3]), R=[KT_, KMC], W=[KT_])
        op("act", ACT(xz[:, 3, :], lgb[:, 4:8], AF.Exp, scale=rc[:, 3:4]), R=[KT_, KMC], W=[KT_])
        for qq in (1, 3):
            op("dve", TS(xz[:, qq, :], xz[:, qq, :], 0.125, ALU.mult), R=[KT_], W=[KT_])
        t1b = P.alloc("rt_tmp", 1024)
        t1 = P.f32(t1b)
        for h in range(4):
            hs = (h % 2) * 2 + h // 2
            sl = slice(hs * 128, (hs + 1) * 128)
            op("act", ACT(t1[:, sl], MC["ret_dpos"][:, sl], AF.Exp, scale=lgb[:, h:h + 1]), R=[KT_, KMC], W=[t1b.k()])
            op("act", ACT(t1[:, 512 + hs * 128:512 + (hs + 1) * 128], MC["ret_dneg"][:, sl], AF.Exp, scale=lgb[:, 4 + h:5 + h]),
               R=[KT_, KMC], W=[t1b.k()])
        op("dve", TT(t1[:, 0:512], t1[:, 0:512], MC["ret_mge"], ALU.mult), R=[t1b.k(), KMC], W=[t1b.k()])
        op("dve", TT(t1[:, 512:1024], t1[:, 512:1024], MC["ret_mle"], ALU.mult), R=[t1b.k(), KMC], W=[t1b.k()])
        op("dve", TT(dmk, t1[:, 0:512], t1[:, 512:1024], ALU.add), R=[t1b.k()], W=[KT_])
        P.free(t1b)
        dv = lgb.rearrange("p (d q a) -> p d q a", d=2, a=2)
        for a in range(2):
            op("dve", CP(lgsel[a * 64:(a + 1) * 64], dv[a * 64:(a + 1) * 64, :, :, a]), R=[KT_], W=[KT_])
        op("act", ACT(gsel, lgsel, AF.Exp, scale=128.0), R=[KT_], W=[KT_])

        import os as _os
        _seqs = [SEQS[int(i_)] for i_ in _os.environ.get("DBG_SEQS", "0,1,2").split(",")]
        _stop = int(_os.environ.get("RET_STOP", "9"))
        if _stop <= 1:
            return
        for (t0, nts, latent) in _seqs:
            if latent:
                nts = int(_os.environ.get("DBG_NT0", nts))
            L = nts * 128
            qTb = P.alloc("qT", 2 * L // 2)
            kTb = P.alloc("kT", 2 * L // 2)
            qT = P.bf(qTb).rearrange("p (q n) -> p q n", q=2)
            kT = P.bf(kTb).rearrange("p (q n) -> p q n", q=2)
            vtb = P.alloc("v_tok", nts * 256 // 2)
            vt = P.bf(vtb).rearrange("p (t n) -> p t n", t=nts)
            gtb = P.alloc("gate_tok", nts * 256 // 2)
            gt = P.bf(gtb).rearrange("p (t n) -> p t n", t=nts)
            kvb = P.alloc("kv_all", nts * 256)
            kva = P.f32(kvb).rearrange("p (t d q n) -> p t d q n", t=nts, d=2, q=2)
            sbb = P.alloc("S_bf", nts * 256 // 2)
            sbf = P.bf(sbb).rearrange("p (t d q n) -> p t d q n", t=nts, d=2, q=2)
            stf = P.alloc("S_f32", 256)
            Sst = P.f32(stf).rearrange("p (d q n) -> p d q n", d=2, q=2)
            qkb = [P.alloc(f"qkrot{i}", 256) for i in range(2)]
            kzb = [P.alloc(f"kz{i}", 256) for i in range(2)]
            rpbs = [P.alloc(f"ropetmp{i}", 1024) for i in range(2)] if latent else None

            def stageA(j):
                t = t0 + j
                cols = slice(t * 128, (t + 1) * 128)
                bq, bv = P.bank(), P.bank()
                op("pe", MM([(PS[bq][:, :], hT[:, k, cols], wA[:, k, 0:512], k == 0, k == 7) for k in range(8)]),
                   R=[KH, wAb.k()], W=[psk(bq)])
                op("pe", MM([(PS[bv][:, :], hT[:, k, cols], wA[:, k, 512:1024], k == 0, k == 7) for k in range(8)]),
                   R=[KH, wAb.k()], W=[psk(bv)])
                qk = P.bf(qkb[j % 2])
                kq = qkb[j % 2].k()
                if latent:
                    rpb = rpbs[j % 2]
                    rp = P.f32(rpb)
                    raw = rp[:, 0:512]
                    op("act", ACT(raw, PS[bq][:, :], AF.Copy), R=[psk(bq)], W=[rpb.k(0)])
                    src5 = raw.rearrange("p (h r x j) -> p h r x j", h=8, r=2, x=2)
                    dst5 = qk.rearrange("p (h r x j) -> p h r x j", h=8, r=2, x=2)
                    tm5 = rp[:, 512:1024].rearrange("p (u h r j) -> p u h r j", u=2, h=8, r=2)
                    cs = bc(MC["cos_ret"][:, j * 32:(j + 1) * 32], [[0, 8], [16, 2], [1, 16]])
                    sn = bc(MC["sin_ret"][:, j * 32:(j + 1) * 32], [[0, 8], [16, 2], [1, 16]])
                    x1, x2 = src5[:, :, :, 0, :], src5[:, :, :, 1, :]
                    op("pool", TT(tm5[:, 0], x1, cs, ALU.mult), R=[rpb.k(0), KMC], W=[rpb.k(1)])
                    op("pool", TT(tm5[:, 1], x2, sn, ALU.mult), R=[rpb.k(0), KMC], W=[rpb.k(2)])
                    op("pool", TT(dst5[:, :, :, 0, :], tm5[:, 0], tm5[:, 1], ALU.subtract), R=[rpb.k(1), rpb.k(2)], W=[kq])
                    op("pool", TT(tm5[:, 0], x2, cs, ALU.mult), R=[rpb.k(0), KMC], W=[rpb.k(1)])
                    op("pool", TT(tm5[:, 1], x1, sn, ALU.mult), R=[rpb.k(0), KMC], W=[rpb.k(2)])
                    op("pool", TT(dst5[:, :, :, 1, :], tm5[:, 0], tm5[:, 1], ALU.add), R=[rpb.k(1), rpb.k(2)], W=[kq])
                else:
                    op("act", ACT(qk, PS[bq][:, :], AF.Copy), R=[psk(bq)], W=[kq])
                op("act", ACT(vt[:, j, :], PS[bv][:, 0:256], AF.Copy), R=[psk(bv)], W=[vtb.k(j)])
                op("act", ACT(gt[:, j, :], PS[bv][:, 256:512], AF.Silu), R=[psk(bv)], W=[gtb.k(j)])
                kz = P.bf(kzb[j % 2]).rearrange("p (d h n) -> p d h n", d=2, h=4)
                k4 = qk[:, 256:512].rearrange("p (h n) -> p h n", h=4)
                for d_, qq in ((0, 1), (1, 3)):
                    op("pool", TT(kz[:, d_], k4, bc(xz[:, qq, :], [[1, 4], [0, 64]]), ALU.mult), R=[kq, KT_], W=[kzb[j % 2].k(d_)])

            def stageB(j):
                qk = P.bf(qkb[j % 2])
                kq = qkb[j % 2].k()
                kz = P.bf(kzb[j % 2]).rearrange("p (d h n) -> p d h n", d=2, h=4)
                bt, bt2 = P.bank(), P.bank()
                op("pe", MM([(PS[bt][:, i * 128:(i + 1) * 128], qk[:, i * 128:(i + 1) * 128], ident, True, True) for i in range(2)]),
                   R=[kq, KC], W=[psk(bt)])
                op("pe", MM([(PS[bt2][:, i * 128:(i + 1) * 128], qk[:, (2 + i) * 128:(3 + i) * 128], ident, True, True) for i in range(2)]),
                   R=[kq, KC], W=[psk(bt2)])
                jc = slice(j * 128, (j + 1) * 128)
                op("act", ACT(qT[:, :, jc], PS[bt][:, 0:256].rearrange("p (q n) -> p q n", q=2), AF.Copy), R=[psk(bt)], W=[qTb.k(j)])
                op("dve", CP(kT[:, :, jc], PS[bt2][:, 0:256].rearrange("p (q n) -> p q n", q=2)), R=[psk(bt2)], W=[kTb.k(j)])
                bk = P.bank()
                op("pe", MM([(PS[bk][:, d_ * 256 + hp * 128:d_ * 256 + (hp + 1) * 128], kz[:, d_, 2 * hp:2 * hp + 2, :],
                              vt[:, j, hp * 128:(hp + 1) * 128], True, True) for d_ in range(2) for hp in range(2)]),
                   R=[kzb[j % 2].k(0), kzb[j % 2].k(1), vtb.k(j)], W=[psk(bk)])
                for a in range(2):
                    srcv = bass.AP(PS[bk][:, :].tensor,
                                   PS[bk][a * 64:(a + 1) * 64, a * 64:a * 64 + 1].offset,
                                   [list(PS[bk][a * 64:(a + 1) * 64, :].ap[0]), [256, 2], [128, 2], [1, 64]])
                    op("act" if j % 2 == 0 else "dve",
                       (ACT(kva[a * 64:(a + 1) * 64, j], srcv, AF.Copy) if j % 2 == 0 else CP(kva[a * 64:(a + 1) * 64, j], srcv)),
                       R=[psk(bk)], W=[kvb.k((j, a))])

            stageA(0)
            for j in range(nts):
                if j + 1 < nts:
                    stageA(j + 1)
                stageB(j)
            rpb = None
            if _stop <= 2:
                continue
            KS = stf.k()
            if latent:
                for d_ in range(2):
                    for a in range(2):
                        op("sp", DMA(Sst[a * 64:(a + 1) * 64, d_], bass.AP(T["s0"].tensor, ((l * 2 + d_) * 4 + a) * 4096,
                                                                           [[64, 64], [2 * 4096, 2], [1, 64]])), W=[KS], dma=True)
            else:
                op("pool", MSET(P.f32(stf), 0.0), W=[KS])
            for d_ in range(2):
                order = range(nts) if d_ == 0 else range(nts - 1, -1, -1)
                for j in order:
                    op("act", ACT(sbf[:, j, d_], Sst[:, d_], AF.Copy), R=[KS], W=[sbb.k((j, d_))])
                    for hp in range(2):
                        op("dve", STT(Sst[:, d_, hp, :], Sst[:, d_, hp, :], gsel[:, d_, hp:hp + 1], kva[:, j, d_, hp, :], ALU.mult, ALU.add),
                           R=[KS, KT_, kvb.k((j, 0)), kvb.k((j, 1))], W=[KS])
            if not latent:
                pb = (t0 - 16) // 2
                for d_ in range(2):
                    for a in range(2):
                        op("sp", DMA(bass.AP(T["o_st"].tensor, (((pb * 2 + l) * 2 + d_) * 4 + a) * 4096, [[64, 64], [2 * 4096, 2], [1, 64]]),
                                     Sst[a * 64:(a + 1) * 64, d_]), R=[KS], dma=True)
            if _stop <= 3:
                continue
            P.free(kvb, qkb[0], qkb[1], kzb[0], kzb[1])
            if rpbs is not None:
                P.free(rpbs[0], rpbs[1])
            G_ = 4
            ptb = [P.alloc(f"PT{i}", 256) for i in range(G_)]
            ob = [P.alloc(f"o_acc{i}", 256 * 3) for i in range(G_)]
            gnb = P.alloc("gn", 16 * G_)
            gn = P.f32(gnb)
            rob = [P.alloc(f"ret_o{i}", 128) for i in range(G_)]
            st_banks = {}

            def so1(j):
                jc = slice(j * 128, (j + 1) * 128)
                bsa = [P.bank(), P.bank()]
                PT = P.bf(ptb[j % G_])
                for a in range(2):
                    op("pe", MM([(PS[bsa[a]][:, hp * 128:(hp + 1) * 128], kT[a * 64:(a + 1) * 64, hp, jc],
                                  qT[a * 64:(a + 1) * 64, hp, jc], True, True) for hp in range(2)]),
                       R=[kTb.k(j), qTb.k(j)], W=[psk(bsa[a])])
                    op("dve", TT(PT[:, a * 256:(a + 1) * 256], PS[bsa[a]][:, 0:256], dmk[:, a * 256:(a + 1) * 256], ALU.mult),
                       R=[psk(bsa[a]), KT_], W=[ptb[j % G_].k(a)])

            def so2(j):
                jc = slice(j * 128, (j + 1) * 128)
                PT = P.bf(ptb[j % G_])
                bo = P.bank()
                op("pe", MM([(PS[bo][:, h * 64:(h + 1) * 64], PT[:, ((h % 2) * 2 + h // 2) * 128:((h % 2) * 2 + h // 2 + 1) * 128],
                              vt[:, j, h * 64:(h + 1) * 64], True, True) for h in range(4)]),
                   R=[ptb[j % G_].k(0), ptb[j % G_].k(1), vtb.k(j)], W=[psk(bo)])
                bxa = [P.bank(), P.bank()]
                for a in range(2):
                    op("pe", MM([(PS[bxa[a]][:, (d_ * 2 + hp) * 64:(d_ * 2 + hp + 1) * 64], qT[a * 64:(a + 1) * 64, hp, jc],
                                  sbf[a * 64:(a + 1) * 64, j, d_, hp, :], True, True) for d_ in range(2) for hp in range(2)]),
                       R=[qTb.k(j), sbb.k((j, 0)), sbb.k((j, 1))], W=[psk(bxa[a])])
                st_banks[j] = (bo, bxa)

            def so3(j):
                bo, bxa = st_banks[j]
                oa = P.f32(ob[j % G_]).rearrange("p (u h n) -> p u h n", u=3, h=4)
                ko = ob[j % G_].k()
                for a in range(2):
                    xif = bc(xz[:, 0, a:a + 1], [[2, 2], [0, 64]])
                    xib = bc(xz[:, 2, a:a + 1], [[2, 2], [0, 64]])
                    cf_ = PS[bxa[a]][:, 0:128].rearrange("p (q n) -> p q n", q=2)
                    cb_ = PS[bxa[a]][:, 128:256].rearrange("p (q n) -> p q n", q=2)
                    op("dve", TT(oa[:, 0, a::2, :], cf_, xif, ALU.mult), R=[psk(bxa[a]), KT_], W=[ko])
                    op("dve", TT(oa[:, 1, a::2, :], cb_, xib, ALU.mult), R=[psk(bxa[a]), KT_], W=[ko])
                op("pool", TT(oa[:, 0], oa[:, 0], oa[:, 1], ALU.add), R=[ko], W=[ko])
                op("dve", TT(oa[:, 0], oa[:, 0], PS[bo][:, 0:256].rearrange("p (h n) -> p h n", h=4), ALU.add), R=[ko, psk(bo)], W=[ko])

            def gnv(j):
                g0 = (j % G_) * 16
                return (gn[:, g0:g0 + 4], gn[:, g0 + 4:g0 + 8], gn[:, g0 + 8:g0 + 12], gn[:, g0 + 12:g0 + 16], gnb.k(j % G_))

            def so4(j):
                oa = P.f32(ob[j % G_]).rearrange("p (u h n) -> p u h n", u=3, h=4)
                ko = ob[j % G_].k()
                sm, ng_, vs, sd, kg = gnv(j)
                op("dve", RED(sm, oa[:, 0]), R=[ko], W=[kg])
                op("dve", TS(ng_, sm, -1.0 / 64, ALU.mult), R=[kg], W=[kg])
                op("pool", TT(oa[:, 1], oa[:, 0], bc(ng_, [[1, 4], [0, 64]]), ALU.add), R=[ko, kg], W=[ko])
                op("pool", TT(oa[:, 2], oa[:, 1], oa[:, 1], ALU.mult), R=[ko], W=[ko])

            def so5(j):
                oa = P.f32(ob[j % G_]).rearrange("p (u h n) -> p u h n", u=3, h=4)
                ko = ob[j % G_].k()
                sm, ng_, vs, sd, kg = gnv(j)
                op("dve", RED(vs, oa[:, 2]), R=[ko], W=[kg])
                op("act", ACT(sd, vs, AF.Sqrt, bias=epsc, scale=1.0 / 64), R=[kg, KC], W=[kg])
                op("dve", RECIP(sm, sd), R=[kg], W=[kg])

            def so6(j):
                t = t0 + j
                oa = P.f32(ob[j % G_]).rearrange("p (u h n) -> p u h n", u=3, h=4)
                ko = ob[j % G_].k()
                sm, ng_, vs, sd, kg = gnv(j)
                op("pool", TT(oa[:, 2], oa[:, 1], bc(sm, [[1, 4], [0, 64]]), ALU.mult), R=[ko, kg], W=[ko])
                ro = P.bf(rob[j % G_])
                op("dve", TT(ro.rearrange("p (h n) -> p h n", h=4), oa[:, 2], gt[:, j, :].rearrange("p (h n) -> p h n", h=4), ALU.mult),
                   R=[ko, gtb.k(j)], W=[rob[j % G_].k()])
                bt = P.bank()
                op("pe", MM([(PS[bt][:, i * 128:(i + 1) * 128], ro[:, i * 128:(i + 1) * 128], ident, True, True) for i in range(2)]),
                   R=[rob[j % G_].k(), KC], W=[psk(bt)])
                op("act", ACT(catT[:, 0:2, t * 128:(t + 1) * 128], PS[bt][:, 0:256].rearrange("p (q n) -> p q n", q=2), AF.Copy),
                   R=[psk(bt)], W=[catb.k(0), catb.k(1)])

            for j0 in range(0, nts, 2):
                js = list(range(j0, min(nts, j0 + 2)))
                for stage in (so1, so2, so3, so4, so5, so6):
                    for j in js:
                        stage(j)
            P.free(*ptb[2:], *ob[2:], *rob[2:])
            P.free(qTb, kTb, vtb, gtb, sbb, stf, ptb[0], ptb[1], ob[0], ob[1], gnb, rob[0], rob[1])
        P.free(wAb, rtb)

    def mla_all(l, hT, KH, catT, catb, KMC):
        import os as _os
        ropeb = ROPEB[0]
        wBb = P.alloc("wB", 8 * 416 // 2)
        wB = P.bf(wBb).rearrange("p (k n) -> p k n", k=8)
        op("pool", DMA(wB, T["w_in"][l][:, 1280:1696].rearrange("(k p) n -> p k n", p=128)), W=[wBb.k()], dma=True)
        wqb = P.alloc("wuq", 2 * 384 // 2 + 256)
        wuq = P.bf(wqb)[:, 0:768].rearrange("p (k n) -> p k n", k=2)
        wukv = P.bf(wqb)[:, 768:1280]
        op("pool", DMA(wuq, T["mla_w_uq"][l].rearrange("(k p) n -> p k n", p=128)), W=[wqb.k()], dma=True)
        op("pool", DMA(wukv, T["mla_w_ukv"][l]), W=[wqb.k()], dma=True)
        gnb_ = P.alloc("mla_g", 384)
        gq_b, gkv_b = P.f32(gnb_)[:, 0:256], P.f32(gnb_)[:, 256:384]
        op("sp", DMA(gq_b, bass.AP(T["mla_q_norm"].tensor, l * 256, [[0, 128], [1, 256]])), W=[gnb_.k()], dma=True)
        op("sp", DMA(gkv_b, bass.AP(T["mla_kv_norm"].tensor, l * 128, [[0, 128], [1, 128]])), W=[gnb_.k()], dma=True)
        msb = P.alloc("mla_stats", 16)
        ms = P.f32(msb)
        scale = 96.0 ** -0.5
        _seqs = [SEQS[int(i_)] for i_ in _os.environ.get("DBG_SEQS", "0,1,2").split(",")]
        for (t0, nts, latent) in _seqs:
            L = nts * 128
            nkt = nts + (2 if latent else 0)
            Lk = nkt * 128
            cqTb = P.alloc("cqT", L)
            cqT = P.bf(cqTb).rearrange("p (k n) -> p k n", k=2)
            ckTb = P.alloc("ckvT", Lk // 2)
            ckvT = P.bf(ckTb)
            KTb = P.alloc("KT", 2 * Lk)
            KT = P.bf(KTb).rearrange("p (h n) -> p h n", h=4)
            QTb = P.alloc("QT", 2 * L)
            QT = P.bf(QTb).rearrange("p (h n) -> p h n", h=4)
            Vb = P.alloc("Vaug", nkt * 130)
            Va = P.bf(Vb).rearrange("p (t h n) -> p t h n", t=nkt, h=4)
            atb = P.alloc("att_tok", nts * 128)
            att = P.bf(atb).rearrange("p (t n) -> p t n", t=nts)
            op("pool", MSET(P.bf(Vb), 1.0), W=[Vb.k()])
            kstb = [P.alloc(f"kst{i}", 128) for i in range(2)]
            cqnb = [P.alloc(f"cqn{i}", 128) for i in range(2)]
            ckfb = [P.alloc(f"ckvf{i}", 128 + 32) for i in range(2)]
            for i in range(2):
                op("pool", MSET(P.bf(kstb[i]), 0.0), W=[kstb[i].k()])
            pb = (t0 - 16) // 2
            for j in range(nkt):
                i2 = j % 2
                kst = P.bf(kstb[i2])
                kk = kstb[i2].k()
                kcols = slice(j * 128, (j + 1) * 128)
                if j < nts:
                    t = t0 + j
                    bp = P.bank()
                    op("pe", MM([(PS[bp][:, 0:416], hT[:, k, t * 128:(t + 1) * 128], wB[:, k, :], k == 0, k == 7) for k in range(8)]),
                       R=[KH, wBb.k()], W=[psk(bp)])
                    ssq, ssk, sq1, sq2, rq, rk = (ms[:, i2 * 8 + u:i2 * 8 + u + 1] for u in range(6))
                    kst_ = msb.k(i2)
                    cqn = P.bf(cqnb[i2])
                    ckf = P.f32(ckfb[i2])
                    op("act", ACT(cqn, PS[bp][:, 0:256], AF.Square, accum=ssq), R=[psk(bp)], W=[cqnb[i2].k(), kst_])
                    op("act", ACT(ckf[:, 0:128], PS[bp][:, 256:384], AF.Square, accum=ssk), R=[psk(bp)], W=[ckfb[i2].k(), kst_])
                    op("act", ACT(sq1, ssq, AF.Sqrt, bias=epsc, scale=1.0 / 256), R=[kst_, KC], W=[kst_])
                    op("act", ACT(sq2, ssk, AF.Sqrt, bias=epsc, scale=1.0 / 128), R=[kst_, KC], W=[kst_])
                    op("dve", RECIP(ms[:, i2 * 8 + 4:i2 * 8 + 6], ms[:, i2 * 8 + 2:i2 * 8 + 4]), R=[kst_], W=[kst_])
                    op("dve", STT(cqn, PS[bp][:, 0:256], rq, gq_b, ALU.mult, ALU.mult), R=[psk(bp), kst_, gnb_.k()], W=[cqnb[i2].k()])
                    op("dve", STT(ckf[:, 0:128], PS[bp][:, 256:384], rk, gkv_b, ALU.mult, ALU.mult), R=[psk(bp), kst_, gnb_.k()],
                       W=[ckfb[i2].k()])
                    op("pool", CP(kst[:, 0:128], ckf[:, 0:128]), R=[ckfb[i2].k()], W=[kk])
                    if latent:
                        op("dve", CP(ckf[:, 128:160], PS[bp][:, 384:416]), R=[psk(bp), kst_], W=[ckfb[i2].k("kr")])
                        kr3 = ckf[:, 128:160].rearrange("p (r x j) -> p r x j", r=2, x=2)
                        kd3 = kst[:, 192:224].rearrange("p (r x j) -> p r x j", r=2, x=2)
                        cs = MC["cos_mla"][:, j * 16:(j + 1) * 16].rearrange("p (r j) -> p r j", r=2)
                        sn = MC["sin_mla"][:, j * 16:(j + 1) * 16].rearrange("p (r j) -> p r j", r=2)
                        tmk = ms
                        rtb_ = ckfb[i2].k("rt")
                        tmp4 = P.f32(ropeb)[:, i2 * 64:(i2 + 1) * 64].rearrange("p (u r j) -> p u r j", u=4, r=2)
                        kro = ropeb.k(i2)
                        op("pool", TT(tmp4[:, 0], kr3[:, :, 0, :], cs, ALU.mult), R=[ckfb[i2].k("kr"), KMC], W=[kro])
                        op("pool", TT(tmp4[:, 1], kr3[:, :, 1, :], sn, ALU.mult), R=[ckfb[i2].k("kr"), KMC], W=[kro])
                        op("pool", TT(tmp4[:, 2], kr3[:, :, 1, :], cs, ALU.mult), R=[ckfb[i2].k("kr"), KMC], W=[kro])
                        op("pool", TT(tmp4[:, 3], kr3[:, :, 0, :], sn, ALU.mult), R=[ckfb[i2].k("kr"), KMC], W=[kro])
                        op("pool", TT(kd3[:, :, 0, :], tmp4[:, 0], tmp4[:, 1], ALU.subtract), R=[kro], W=[kk])
                        op("pool", TT(kd3[:, :, 1, :], tmp4[:, 2], tmp4[:, 3], ALU.add), R=[kro], W=[kk])
                    else:
                        op("dve", CP(ckf[:, 128:160], PS[bp][:, 384:416]), R=[psk(bp), kst_], W=[ckfb[i2].k("kr")])
                        op("pool", CP(kst[:, 192:224], ckf[:, 128:160]), R=[ckfb[i2].k("kr")], W=[kk])
                        rows = slice(j * 128, (j + 1) * 128)
                        op("sp", DMA(T["o_ckv"][pb, l, rows, :], ckf[:, 0:128]), R=[ckfb[i2].k()], dma=True)
                        op("sp", DMA(T["o_kr"][pb, l, rows, :], ckf[:, 128:160]), R=[ckfb[i2].k("kr")], dma=True)
                else:
                    rows = slice((j - nts) * 128, (j - nts + 1) * 128)
                    op("pool", DMA(kst[:, 0:128], T["ckv_ctx"][l, rows, :]), W=[kk], dma=True)
                    op("pool", DMA(kst[:, 192:224], T["kr_ctx"][l, rows, :]), W=[kk], dma=True)
                bT = P.bank()
                mms = []
                if j < nts:
                    mms += [(PS[bT][:, i * 128:(i + 1) * 128], P.bf(cqnb[i2])[:, i * 128:(i + 1) * 128], ident, True, True) for i in range(2)]
                mms += [(PS[bT][:, 256:384], kst[:, 0:128], ident, True, True),
                        (PS[bT][0:96, 384:512], kst[:, 128:224], ident, True, True)]
                op("pe", MM(mms), R=[cqnb[i2].k(), kk, KC], W=[psk(bT)])
                ev = "act" if j % 2 == 0 else "dve"
                cpy = (lambda o_, i_: ACT(o_, i_, AF.Copy)) if ev == "act" else CP
                if j < nts:
                    op(ev, cpy(cqT[:, :, j * 128:(j + 1) * 128], PS[bT][:, 0:256].rearrange("p (k n) -> p k n", k=2)), R=[psk(bT)], W=[cqTb.k(j)])
                op(ev, cpy(ckvT[:, kcols], PS[bT][:, 256:384]), R=[psk(bT)], W=[ckTb.k(j)])
                op(ev, cpy(KT[64:96, :, kcols], bc(PS[bT][64:96, 384:512], [[0, 4], [1, 128]])), R=[psk(bT)], W=[KTb.k(("r", j))])
            rawb = [P.alloc(f"qraw{i}", 384) for i in range(2)]
            qtkb = [P.alloc(f"qtok{i}", 192) for i in range(2)]
            for j in range(nts):
                i2 = j % 2
                jc = slice(j * 128, (j + 1) * 128)
                bq = P.bank()
                op("pe", MM([(PS[bq][:, 0:384], cqT[:, k, jc], wuq[:, k, :], k == 0, k == 1) for k in range(2)]),
                   R=[cqTb.k(j), wqb.k()], W=[psk(bq)])
                qtk = P.bf(qtkb[i2])
                kq_ = qtkb[i2].k()
                if latent:
                    raw = P.f32(rawb[i2])
                    op("act", ACT(raw, PS[bq][:, 0:384], AF.Copy), R=[psk(bq)], W=[rawb[i2].k()])
                    r4 = raw.rearrange("p (h n) -> p h n", h=4)
                    q4 = qtk.rearrange("p (h n) -> p h n", h=4)
                    op("dve", CP(q4[:, :, 0:64], r4[:, :, 0:64]), R=[rawb[i2].k()], W=[kq_])
                    def rv(base, off):
                        return bass.AP(base.tensor, base.offset + off, [list(base.ap[0]), [96, 4], [16, 2], [1, 8]])
                    x1, x2 = rv(raw, 64), rv(raw, 72)
                    o1_, o2_ = rv(qtk, 64), rv(qtk, 72)
                    cs = bc(MC["cos_mla"][:, j * 16:(j + 1) * 16], [[0, 4], [8, 2], [1, 8]])
                    sn = bc(MC["sin_mla"][:, j * 16:(j + 1) * 16], [[0, 4], [8, 2], [1, 8]])
                    tq = P.f32(ropeb)[:, 128 + i2 * 256:128 + (i2 + 1) * 256].rearrange("p (u h r j) -> p u h r j", u=4, h=4, r=2)
                    kro = ropeb.k(("q", i2))
                    op("pool", TT(tq[:, 0], x1, cs, ALU.mult), R=[rawb[i2].k(), KMC], W=[kro])
                    op("pool", TT(tq[:, 1], x2, sn, ALU.mult), R=[rawb[i2].k(), KMC], W=[kro])
                    op("pool", TT(tq[:, 2], x2, cs, ALU.mult), R=[rawb[i2].k(), KMC], W=[kro])
                    op("pool", TT(tq[:, 3], x1, sn, ALU.mult), R=[rawb[i2].k(), KMC], W=[kro])
                    op("pool", TT(o1_, tq[:, 0], tq[:, 1], ALU.subtract), R=[kro], W=[kq_])
                    op("pool", TT(o2_, tq[:, 2], tq[:, 3], ALU.add), R=[kro], W=[kq_])
                else:
                    op("act", ACT(qtk, PS[bq][:, 0:384], AF.Copy), R=[psk(bq)], W=[kq_])
                bT = P.bank()
                op("pe", MM([(PS[bT][0:96, h * 128:(h + 1) * 128], qtk[:, h * 96:(h + 1) * 96], ident, True, True) for h in range(4)]),
                   R=[kq_, KC], W=[psk(bT)])
                ev = "act" if j % 2 == 0 else "dve"
                cpy = (lambda o_, i_: ACT(o_, i_, AF.Copy)) if ev == "act" else CP
                op(ev, cpy(QT[0:96, :, jc], PS[bT][0:96, :].rearrange("p (h n) -> p h n", h=4)), R=[psk(bT)], W=[QTb.k(j)])
            P.free(rawb[0], rawb[1], qtkb[0], qtkb[1])
            wk4 = wukv.rearrange("p (h c n) -> p h c n", h=4, c=2)
            for c0 in range(0, Lk, 512):
                cw = min(512, Lk - c0)
                for h in range(4):
                    b = P.bank()
                    op("pe", MM([(PS[b][0:64, 0:cw], wk4[:, h, 0, :], ckvT[:, c0:c0 + cw], True, True)]),
                       R=[wqb.k()] + [ckTb.k(j_) for j_ in range(c0 // 128, (c0 + cw) // 128)], W=[psk(b)])
                    ev = "act" if h % 2 == 0 else "dve"
                    cpy = (lambda o_, i_: ACT(o_, i_, AF.Copy)) if ev == "act" else CP
                    op(ev, cpy(KT[0:64, h, c0:c0 + cw], PS[b][0:64, 0:cw]), R=[psk(b)], W=[KTb.k(("n", h, c0))])
            for kt in range(nkt):
                b = P.bank()
                op("pe", MM([(PS[b][:, 0:256], ckvT[:, kt * 128:(kt + 1) * 128], wk4[:, :, 1, :], True, True)]),
                   R=[wqb.k(), ckTb.k(kt)], W=[psk(b)])
                ev = "act" if kt % 2 == 0 else "dve"
                cpy = (lambda o_, i_: ACT(o_, i_, AF.Copy)) if ev == "act" else CP
                op(ev, cpy(Va[:, kt, :, 0:64], PS[b][:, 0:256].rearrange("p (h n) -> p h n", h=4)), R=[psk(b), Vb.k()], W=[Vb.k(kt)])
            QC = min(512, L)
            nsub = QC // 128
            Eb = [P.alloc(f"E{i}", QC // 2) for i in range(3)]
            rcb = P.alloc("att_rec", 8)
            ei = 0
            allKT = [KTb.k(("r", j_)) for j_ in range(nkt)]
            for h in range(4):
                kth = allKT + [KTb.k(("n", h, c0)) for c0 in range(0, Lk, 512)]
                for qc in range(L // QC):
                    qcols = slice(qc * QC, (qc + 1) * QC)
                    bo = P.bank()
                    qdeps = [QTb.k(j_) for j_ in range(qc * nsub, (qc + 1) * nsub)]

                    def scores(kt):
                        bs = P.bank()
                        if bs == bo:
                            bs = P.bank()
                        op("pe", MM([(PS[bs][:, 0:QC], KT[0:96, h, kt * 128:(kt + 1) * 128], QT[0:96, h, qcols], True, True)]),
                           R=kth + qdeps, W=[psk(bs)])
                        return bs
                    pend = scores(0)
                    for kt in range(nkt):
                        bs = pend
                        if kt + 1 < nkt:
                            pend = scores(kt + 1)
                        E = P.bf(Eb[ei % 3])
                        ke = Eb[ei % 3].k()
                        ei += 1
                        op("act", ACT(E, PS[bs][:, 0:QC], AF.Exp, scale=scale), R=[psk(bs)], W=[ke])
                        op("pe", MM([(PS[bo][:, sub * 65:(sub + 1) * 65], E[:, sub * 128:(sub + 1) * 128], Va[:, kt, h, :],
                                      kt == 0 and sub == 0, kt == nkt - 1 and sub == nsub - 1)
                                     for sub in range(nsub)]), R=[ke, Vb.k(kt), Vb.k()], W=[psk(bo)])
                    o3 = PS[bo][:, 0:nsub * 65].rearrange("p (s n) -> p s n", s=nsub)
                    rec = P.f32(rcb)[:, (qc % 2) * 4:(qc % 2) * 4 + nsub]
                    op("dve", RECIP(rec, o3[:, :, 64]), R=[psk(bo)], W=[rcb.k(qc % 2)])
                    op("dve", TT(att[:, qc * nsub:(qc + 1) * nsub, h * 64:(h + 1) * 64], o3[:, :, 0:64], bc(rec, [[1, nsub], [0, 64]]), ALU.mult),
                       R=[psk(bo), rcb.k(qc % 2)], W=[atb.k((h, qc))])
            for j in range(nts):
                t = t0 + j
                bT = P.bank()
                op("pe", MM([(PS[bT][:, i * 128:(i + 1) * 128], att[:, j, i * 128:(i + 1) * 128], ident, True, True) for i in range(2)]),
                   R=[atb.k((h_, j // nsub)) for h_ in range(4)] + [KC], W=[psk(bT)])
                ev = "act" if j % 2 == 0 else "dve"
                cpy = (lambda o_, i_: ACT(o_, i_, AF.Copy)) if ev == "act" else CP
                op(ev, cpy(catT[:, 4:6, t * 128:(t + 1) * 128], PS[bT][:, 0:256].rearrange("p (q n) -> p q n", q=2)), R=[psk(bT)],
                   W=[catb.k(4), catb.k(5)])
            P.free(cqTb, ckTb, KTb, QTb, Vb, atb, kstb[0], kstb[1], cqnb[0], cqnb[1], ckfb[0], ckfb[1], Eb[0], Eb[1], Eb[2], rcb)
        P.free(wBb, wqb, gnb_, msb)


    def fourier_all(l, hT, KH, catT, catb):
        import os as _os
        wCb = P.alloc("wC", 8 * 256 // 2)
        wC = P.bf(wCb).rearrange("p (k n) -> p k n", k=8)
        op("pool", DMA(wC, T["w_in"][l][:, 1024:1280].rearrange("(k p) n -> p k n", p=128)), W=[wCb.k()], dma=True)
        _seqs = [SEQS[int(i_)] for i_ in _os.environ.get("DBG_SEQS", "0,1,2").split(",")]
        for (t0, nts, latent) in _seqs:
            L = nts * 128
            nj = nts // 2
            H = L // 2
            PB = min(512, H)
            base = t0 * 128
            ftb = P.alloc("f_tok", 2 * nj * 256 // 2)
            ft = P.bf(ftb).rearrange("p (a j n) -> p a j n", a=2, j=nj)
            for a in range(2):
                for j in range(nj):
                    b = P.bank()
                    c0 = base + a + 256 * j
                    op("pe", MM([(PS[b][:, 0:256], hT[:, k, c0:c0 + 255:2], wC[:, k, :], k == 0, k == 7) for k in range(8)]),
                       R=[KH, wCb.k()], W=[psk(b)])
                    ev = "act" if (a * nj + j) % 2 == 0 else "dve"
                    cpy = (lambda o_, i_: ACT(o_, i_, AF.Copy)) if ev == "act" else CP
                    op(ev, cpy(ft[:, a, j, :], PS[b][:, 0:256]), R=[psk(b)], W=[ftb.k((a, j))])
            tabC = T[f"f_cl{L}"].rearrange("p (a j n) -> p a j n", a=2, j=nj)
            tabS = T[f"f_sl{L}"].rearrange("p (a j n) -> p a j n", a=2, j=nj)
            mtb = [P.alloc(f"ftab{i}", 2 * nj * PB // 2) for i in range(2)]
            uvb = [P.alloc(f"fuv{i}", 4 * PB // 2) for i in range(2)]
            zob = P.alloc("fzo", 2 * PB)
            mi = 0
            for ph in range(H // PB):
                zbanks = {}
                for a in range(2):
                    mt = P.bf(mtb[mi % 2]).rearrange("p (q j n) -> p q j n", q=2, j=nj)
                    km = mtb[mi % 2].k()
                    op("sp", DMA(mt[:, 0], tabC[:, a, :, ph * PB:(ph + 1) * PB]), W=[km], dma=True)
                    op("sp", DMA(mt[:, 1], tabS[:, a, :, ph * PB:(ph + 1) * PB]), W=[km], dma=True)
                    uv = P.bf(uvb[mi % 2]).rearrange("p (c q n) -> p c q n", c=2, q=2)
                    ku = uvb[mi % 2].k()
                    mi += 1
                    for cc in range(2):
                        for q in range(2):
                            b = P.bank()
                            op("pe", MM([(PS[b][:, 0:PB], ft[:, a, j, cc * 128:(cc + 1) * 128], mt[:, q, j, :], j == 0, j == nj - 1)
                                         for j in range(nj)]), R=[ftb.k((a, j)) for j in range(nj)] + [km], W=[psk(b)])
                            ev = "act" if q == 0 else "dve"
                            cpy = (lambda o_, i_: ACT(o_, i_, AF.Copy)) if ev == "act" else CP
                            op(ev, cpy(uv[:, cc, q, :], PS[b][:, 0:PB]), R=[psk(b)], W=[uvb[(mi - 1) % 2].k((cc, q))])
                    for cc in range(2):
                        b = P.bank()
                        zbanks[(a, cc)] = b
                        kk_ = uvb[(mi - 1) % 2]
                        op("pe", MM([(PS[b][:, 0:PB], c64, uv[:, cc, 0, :], True, False), (PS[b][:, 0:PB], s64n, uv[:, cc, 1, :], False, True)]),
                           R=[kk_.k((cc, 0)), kk_.k((cc, 1)), KC], W=[psk(b)])
                zo = P.f32(zob).rearrange("p (c n) -> p c n", c=2)
                for cc in range(2):
                    be, bo_ = zbanks[(0, cc)], zbanks[(1, cc)]
                    op("act", ACT(zo[:, cc, :], PS[bo_][:, 0:PB], AF.Copy), R=[psk(bo_)], W=[zob.k(cc)])
                    p0 = base + ph * PB
                    op("dve", TT(catT[:, 2 + cc, p0:p0 + PB], PS[be][:, 0:PB], zo[:, cc, :], ALU.add), R=[psk(be), zob.k(cc)], W=[catb.k(2 + cc)])
                    op("dve", TT(catT[:, 2 + cc, p0 + H:p0 + H + PB], PS[be][:, 0:PB], zo[:, cc, :], ALU.subtract), R=[psk(be), zob.k(cc)],
                       W=[catb.k(2 + cc)])
            P.free(ftb, mtb[0], mtb[1], uvb[0], uvb[1], zob)
        P.free(wCb)


    def hyena_all(l, hT, KH, catT, catb, hTb):
        import os as _os
        TWO_PI = 2.0 * math.pi
        hpb = P.alloc("hy_par", 18 + 6 + 4 + 4 + 64 + 64 + 1024 + 8)
        hp = P.f32(hpb)
        KP = hpb.k()
        shw = hp[:, 0:18].rearrange("p (m k) -> p m k", m=6)
        shb = hp[:, 18:24]
        dsk = hp[:, 24:28].rearrange("p (o c) -> p o c", o=2)
        bcol = hp[:, 28:32]
        w1s = hp[0:33, 32:96]
        w2s = hp[0:64, 96:160]
        w3s = hp[0:64, 160:1184]
        for k in range(3):
            op("sp", DMAS(shw[:, :, k:k + 1], bass.AP(T["hy_short_w"].tensor, (l * 3 + k) * 768, [[1, 128], [128, 6], [1, 1]])), W=[KP], dma=True)
        op("sp", DMAS(shb.rearrange("p (m o) -> p m o", o=1), bass.AP(T["hy_short_b"].tensor, l * 768, [[1, 128], [128, 6], [1, 1]])), W=[KP], dma=True)
        op("sp", DMAS(hp[:, 24:28].rearrange("p (q o) -> p q o", o=1), bass.AP(T["hy_bias"].tensor, l * 512, [[1, 128], [128, 4], [1, 1]])), W=[KP], dma=True)
        op("sp", DMAS(bcol[0:64, 0:1], bass.AP(T["hy_b1"].tensor, l * 64, [[1, 64], [1, 1]])), W=[KP], dma=True)
        op("sp", DMAS(bcol[0:64, 1:2], bass.AP(T["hy_b2"].tensor, l * 64, [[1, 64], [1, 1]])), W=[KP], dma=True)
        op("sp", DMA(w1s, T["hy_w1"][l]), W=[KP], dma=True)
        op("sp", DMA(w2s, T["hy_w2"][l]), W=[KP], dma=True)
        op("sp", DMA(w3s, T["hy_w3"][l]), W=[KP], dma=True)
        bx = hp[:, 1184:1192]
        op("dve", TS(bx[0:64, 0:2], bcol[0:64, 0:2], 0.5, ALU.mult), R=[KP], W=[KP])
        op("dve", TS(bx[0:64, 2:4], bcol[0:64, 0:2], 0.25, ALU.mult), R=[KP], W=[KP])
        wDb = P.alloc("wD", 8 * 768 // 2)
        wD = P.bf(wDb).rearrange("p (k n) -> p k n", k=8)
        op("pool", DMA(wD, T["w_in"][l][:, 1696:2464].rearrange("(k p) n -> p k n", p=128)), W=[wDb.k()], dma=True)
        ucb = P.alloc("ucT", 6 * 2560 // 2)
        ucT = P.bf(ucb).rearrange("p (m n) -> p m n", m=6)
        upb = [P.alloc(f"upad{i}", 2050) for i in range(2)]
        ctb = P.alloc("convtmp", 2048)
        ui = 0
        for (t0, nts, latent) in SEQS:
            L = nts * 128
            base = t0 * 128
            for m in range(6):
                ub = upb[ui % 2]
                up = P.f32(ub)
                ku = ub.k()
                ui += 1
                op("pool", MSET(up[:, 0:1], 0.0), W=[ku])
                op("pool", MSET(up[:, L + 1:L + 2], 0.0), W=[ku])
                for c0 in range(0, L, 512):
                    cw = min(512, L - c0)
                    b = P.bank()
                    op("pe", MM([(PS[b][:, 0:cw], wD[:, k, m * 128:(m + 1) * 128], hT[:, k, base + c0:base + c0 + cw], k == 0, k == 7)
                                 for k in range(8)]), R=[KH, wDb.k()], W=[psk(b)])
                    op("act", ACT(up[:, 1 + c0:1 + c0 + cw], PS[b][:, 0:cw], AF.Copy), R=[psk(b)], W=[ku])
                ct = P.f32(ctb)[:, 0:L]
                eng = "dve"
                op(eng, TS(ct, up[:, 0:L], shw[:, m, 0:1], ALU.mult, shb[:, m:m + 1], ALU.add), R=[ku, KP], W=[ctb.k()])
                op(eng, STT(ct, up[:, 1:L + 1], shw[:, m, 1:2], ct, ALU.mult, ALU.add), R=[ku, KP, ctb.k()], W=[ctb.k()])
                op(eng, STT(ucT[:, m, base:base + L], up[:, 2:L + 2], shw[:, m, 2:3], ct, ALU.mult, ALU.add), R=[ku, KP, ctb.k()],
                   W=[ucb.k((m, t0))])
        P.free(hTb, wDb, upb[0], upb[1], ctb)
        dump(f"ucT{l}", ucT, [ucb.k((m, t0)) for m in range(6) for (t0, _, _) in SEQS])

        for L in (2048, 256):
            seqs = [sq for sq in SEQS if sq[1] * 128 == L]
            nj = L // 256
            ntt = 2 * nj
            NF = L // 256
            HB = L // 2
            TB = min(512, HB)
            zb = P.alloc("hy_z", L)
            h1b = P.alloc("hy_h1", L)
            zT = P.f32(zb)[0:33, :]
            h1T = P.f32(h1b)[0:64, :]
            op("sp", DMA(zT, T[f"h_zT{L}"]), W=[zb.k()], dma=True)
            h2b = P.alloc("hy_h2", L)
            h2T = P.f32(h2b)[0:64, :]
            sinb = P.alloc("hy_sint", 1024)
            for (src_, dst_, w_, kk_, bi_, kr_, kw_) in ((zT, h1T, w1s, 33, 0, zb.k(), h1b.k()), (h1T, h2T, w2s, 64, 1, h1b.k(), h2b.k())):
                for c0 in range(0, L, 512):
                    cw = min(512, L - c0)
                    b = P.bank()
                    op("pe", MM([(PS[b][0:64, 0:cw], w_, src_[:, c0:c0 + cw], True, True)]), R=[KP, kr_], W=[psk(b)])
                    s2 = P.f32(sinb)[0:64, 0:cw]
                    s4 = P.f32(sinb)[0:64, 512:512 + cw]
                    op("act", ACT(s2, PS[b][0:64, 0:cw], AF.Sin, bias=bx[0:64, bi_:bi_ + 1], scale=0.5), R=[psk(b), KP], W=[sinb.k(0)])
                    op("act", ACT(s4, PS[b][0:64, 0:cw], AF.Sin, bias=bx[0:64, 2 + bi_:3 + bi_], scale=0.25), R=[psk(b), KP], W=[sinb.k(1)])
                    op("dve", TT(s4, s4, s4, ALU.mult), R=[sinb.k(1)], W=[sinb.k(1)])
                    op("dve", TS(s4, s4, -2.0, ALU.mult, 1.0, ALU.add), R=[sinb.k(1)], W=[sinb.k(1)])
                    op("dve", STT(dst_[:, c0:c0 + cw], s2, 2.0, s4, ALU.mult, ALU.mult), R=[sinb.k(0), sinb.k(1)], W=[kw_])
            P.free(zb, h1b, sinb)
            for o in range(2):
                winb = P.alloc("hy_win", ntt * 256)
                win = P.f32(winb).rearrange("p (q n) -> p q n", q=ntt)
                op("sp", DMA(P.f32(winb), T[f"h_win{L}"]), W=[winb.k()], dma=True)
                abb = P.alloc("hy_ab", 2 * ntt * 256 // 2)
                ab = P.bf(abb).rearrange("p (s q n) -> p s q n", s=2, q=ntt)
                accb = P.alloc("hy_acc", 256)
                acc = P.f32(accb)
                op("pool", MSET(acc, 0.0), W=[accb.k()])
                g2b = [P.alloc(f"hy_g2{i}", 512) for i in range(2)]
                abs_b = [P.alloc(f"hy_abs{i}", 512) for i in range(2)]
                for q in range(ntt):
                    b = P.bank()
                    op("pe", MM([(PS[b][:, :], h2T[:, q * 128:(q + 1) * 128], w3s[:, o * 512:(o + 1) * 512], True, True)]),
                       R=[h2b.k(), KP], W=[psk(b)])
                    g2 = P.f32(g2b[q % 2]).rearrange("p (d n) -> p d n", d=2)
                    kg = g2b[q % 2].k()
                    op("dve", TT(g2, PS[b][:, :].rearrange("p (d n) -> p d n", d=2), bc(win[:, q, :], [[0, 2], [1, 256]]), ALU.mult),
                       R=[psk(b), winb.k()], W=[kg])
                    if q == 0:
                        op("pool", MSET(g2[0:1, 1, :], 0.0), R=[kg], W=[kg])
                    op("pool", TT(ab[:, 0, q, :], g2[:, 0, :], g2[:, 1, :], ALU.add), R=[kg], W=[abb.k((0, q))])
                    op("pool", TT(ab[:, 1, q, :], g2[:, 0, :], g2[:, 1, :], ALU.subtract), R=[kg], W=[abb.k((1, q))])
                    av = P.f32(abs_b[q % 2])
                    op("act", ACT(av, P.f32(g2b[q % 2]), AF.Abs), R=[kg], W=[abs_b[q % 2].k()])
                    op("dve", TT(acc, acc, av[:, 0:256], ALU.add), R=[abs_b[q % 2].k(), accb.k()], W=[accb.k()])
                    op("dve", TT(acc, acc, av[:, 256:512], ALU.add), R=[abs_b[q % 2].k(), accb.k()], W=[accb.k()])
                P.free(g2b[0], g2b[1], abs_b[0], abs_b[1], winb)
                rnb = P.alloc("hy_rn", 256 + 256)
                rnrow = P.f32(rnb)[0:1, 0:256]
                RN = P.f32(rnb)[:, 256:512]
                b = P.bank()
                op("pe", MM([(PS[b][0:1, 0:256], onesf[:, 0:1], acc, True, True)]), R=[accb.k(), KC], W=[psk(b)])
                op("dve", TS(rnrow, PS[b][0:1, 0:256], EPS, ALU.add), R=[psk(b)], W=[rnb.k("row")])
                op("dve", RECIP(rnrow, rnrow), R=[rnb.k("row")], W=[rnb.k("row")])
                b = P.bank()
                op("pe", MM([(PS[b][:, 0:256], onesf[0:1, 0:128], rnrow, True, True)]), R=[rnb.k("row"), KC], W=[psk(b)])
                op("act", ACT(RN, PS[b][:, 0:256], AF.Copy), R=[psk(b)], W=[rnb.k()])
                P.free(accb)
                hyt[0] = P.alloc("hy_ftmp", 1024)
                Gb = P.alloc("hy_G", NF * 4 * 256 // 2)
                G = P.bf(Gb).rearrange("p (f q n) -> p f q n", f=NF, q=4)
                ftb_ = [P.alloc(f"hy_ft{i}", 2 * ntt * 128 // 2) for i in range(2)]
                osb = P.alloc("hy_osb", 512)
                for fc in range(NF):
                    tb_ = ftb_[fc % 2]
                    tf = P.bf(tb_).rearrange("p (s q n) -> p s q n", s=2, q=ntt)
                    op("sp", DMA(tf[:, 0], T[f"h_cf{L}"][fc].rearrange("p (q n) -> p q n", q=ntt)), W=[tb_.k()], dma=True)
                    op("sp", DMA(tf[:, 1], T[f"h_sf{L}"][fc].rearrange("p (q n) -> p q n", q=ntt)), W=[tb_.k()], dma=True)
                    bE, bO = P.bank(), P.bank()
                    for (bk_, a_) in ((bE, 0), (bO, 1)):
                        mm = []
                        for j in range(nj):
                            q = a_ * nj + j
                            mm.append((PS[bk_][:, 0:256], tf[:, 0, q, :], ab[:, 0, q, :], j == 0, False))
                            mm.append((PS[bk_][:, 256:512], tf[:, 1, q, :], ab[:, 1, q, :], False, j == nj - 1))
                        op("pe", MM(mm), R=[tb_.k()] + [abb.k((s_, a_ * nj + j)) for s_ in range(2) for j in range(nj)], W=[psk(bk_)])
                    osv = P.f32(osb)
                    op("act", ACT(osv, PS[bO][:, :], AF.Copy), R=[psk(bO)], W=[osb.k()])
                    RN2 = bc(RN, [[0, 2], [1, 256]])
                    tsum = P.f32(osb)
                    tmpb_ = hyt[0]
                    tsv = P.f32(tmpb_)[:, 0:512]
                    tdv = P.f32(tmpb_)[:, 512:1024]
                    op("dve", TT(tsv, PS[bE][:, :], osv, ALU.add), R=[psk(bE), osb.k()], W=[tmpb_.k(0)])
                    op("dve", TT(tdv, PS[bE][:, :], osv, ALU.subtract), R=[psk(bE), osb.k()], W=[tmpb_.k(1)])
                    op("pool", TT(G[:, fc, 0:2, :], tsv.rearrange("p (q n) -> p q n", q=2), RN2, ALU.mult), R=[tmpb_.k(0), rnb.k()], W=[Gb.k(fc)])
                    op("pool", TT(G[:, fc, 2:4, :], tdv.rearrange("p (q n) -> p q n", q=2), RN2, ALU.mult), R=[tmpb_.k(1), rnb.k()], W=[Gb.k(fc)])
                P.free(abb, rnb, ftb_[0], ftb_[1], osb, hyt[0])
                dump(f"G{l}_{L}_{o}", G, [Gb.k(fc) for fc in range(NF)])
                for (t0, nts, latent) in seqs:
                    hyena_conv(l, o, L, t0, G, Gb, ucT, ucb, catT, catb, dsk, KP)
                P.free(Gb)
            P.free(h2b)
        P.free(hpb, ucb)
        for k_ in list(Z1.keys()):
            P.free(Z1.pop(k_)[0])

    hyt = [None]
    Z1 = {}

    def hyena_conv(l, o, L, t0, G, Gb, ucT, ucb, catT, catb, dsk, KP):
        nj = L // 256
        ntt = 2 * nj
        NF = L // 256
        HB = L // 2
        TB = min(512, HB)
        base = t0 * 128
        if o == 0:
            zb_ = P.alloc("hy_z1T", 2 * L // 2)
            Z1[t0] = (zb_, P.bf(zb_).rearrange("p (c n) -> p c n", c=2))
            vin = ucT[:, 0:2, base:base + L]
            kvin = [ucb.k((m, t0)) for m in (0, 1)]
            gate = ucT[:, 2:4, base:base + L]
            kgate = [ucb.k((m, t0)) for m in (2, 3)]
            outv = Z1[t0][1]
            kout = [Z1[t0][0].k(0), Z1[t0][0].k(1)]
        else:
            vin = Z1[t0][1]
            kvin = [Z1[t0][0].k(0), Z1[t0][0].k(1)]
            gate = ucT[:, 4:6, base:base + L]
            kgate = [ucb.k((m, t0)) for m in (4, 5)]
            outv = catT[:, 6:8, base:base + L]
            kout = [catb.k(6), catb.k(7)]
        vtb = P.alloc("hy_vtok", ntt * 256 // 2)
        vt = P.bf(vtb).rearrange("p (q n) -> p q n", q=ntt)
        for q in range(0, ntt, 2):
            b = P.bank()
            mm = []
            for qq in (q, q + 1):
                a_, j = qq // nj, qq % nj
                for cc in range(2):
                    mm.append((PS[b][:, (qq - q) * 256 + cc * 128:(qq - q) * 256 + (cc + 1) * 128],
                               vin[:, cc, a_ + 256 * j:a_ + 256 * j + 255:2], ident, True, True))
            op("pe", MM(mm), R=kvin + [KC], W=[psk(b)])
            ev = "act" if (q // 2) % 2 == 0 else "dve"
            cpy = (lambda o_, i_: ACT(o_, i_, AF.Copy)) if ev == "act" else CP
            op(ev, cpy(vt[:, q:q + 2, :], PS[b][:, :].rearrange("p (q n) -> p q n", q=2)), R=[psk(b)], W=[vtb.k(q // 2)])
        pqb = P.alloc("hy_PQ", NF * 4 * 256 // 2)
        PQ = P.bf(pqb).rearrange("p (f q n) -> p f q n", f=NF, q=4)
        ftb_ = [P.alloc(f"hy_ft{i}", 2 * ntt * 128 // 2) for i in range(2)]
        osb = P.alloc("hy_osb", 512)
        tmb = P.alloc("hy_pw", 512 * 8)
        tm = P.f32(tmb)
        SSv, DDv, T1s, T2s, Av, Bv, T1d, T2d = (tm[:, i * 512:(i + 1) * 512] for i in range(8))
        allvt = [vtb.k(i) for i in range(nj)]
        for fc in range(NF):
            tb_ = ftb_[fc % 2]
            tf = P.bf(tb_).rearrange("p (s q n) -> p s q n", s=2, q=ntt)
            op("sp", DMA(tf[:, 0], T[f"h_cf{L}"][fc].rearrange("p (q n) -> p q n", q=ntt)), W=[tb_.k()], dma=True)
            op("sp", DMA(tf[:, 1], T[f"h_sf{L}"][fc].rearrange("p (q n) -> p q n", q=ntt)), W=[tb_.k()], dma=True)
            bE, bO = P.bank(), P.bank()
            for (bk_, a_) in ((bE, 0), (bO, 1)):
                mm = []
                for j in range(nj):
                    q = a_ * nj + j
                    mm.append((PS[bk_][:, 0:256], tf[:, 0, q, :], vt[:, q, :], j == 0, False))
                    mm.append((PS[bk_][:, 256:512], tf[:, 1, q, :], vt[:, q, :], False, j == nj - 1))
                op("pe", MM(mm), R=[tb_.k()] + allvt, W=[psk(bk_)])
            osv = P.f32(osb)
            op("act", ACT(osv, PS[bO][:, :], AF.Copy), R=[psk(bO)], W=[osb.k()])
            op("dve", TT(SSv, PS[bE][:, :], osv, ALU.add), R=[psk(bE), osb.k()], W=[tmb.k("S")])
            op("dve", TT(DDv, PS[bE][:, :], osv, ALU.subtract), R=[psk(bE), osb.k()], W=[tmb.k("D")])
            e1, e2 = ("dve", "pool") if fc % 2 == 0 else ("pool", "dve")
            for (X, gr, gi, dst, kx, e_, T1, T2) in ((SSv, 0, 1, Av, "S", e1, T1s, T2s), (DDv, 2, 3, Bv, "D", e2, T1d, T2d)):
                X2 = X.rearrange("p (q n) -> p q n", q=2)
                op(e_, TT(T1.rearrange("p (q n) -> p q n", q=2), X2, bc(G[:, fc, gr, :], [[0, 2], [1, 256]]), ALU.mult),
                   R=[tmb.k(kx), Gb.k(fc)], W=[tmb.k("T1" + kx)])
                op(e_, TT(T2.rearrange("p (q n) -> p q n", q=2), X2, bc(G[:, fc, gi, :], [[0, 2], [1, 256]]), ALU.mult),
                   R=[tmb.k(kx), Gb.k(fc)], W=[tmb.k("T2" + kx)])
                op(e_, TT(dst[:, 0:256], T1[:, 0:256], T2[:, 256:512], ALU.subtract), R=[tmb.k("T1" + kx), tmb.k("T2" + kx)], W=[tmb.k("A" + kx)])
                op(e_, TT(dst[:, 256:512], T2[:, 0:256], T1[:, 256:512], ALU.add), R=[tmb.k("T1" + kx), tmb.k("T2" + kx)], W=[tmb.k("A" + kx)])
            op("dve", TT(PQ[:, fc, 0:2, :], Av.rearrange("p (q n) -> p q n", q=2), Bv.rearrange("p (q n) -> p q n", q=2), ALU.add),
               R=[tmb.k("AS"), tmb.k("AD")], W=[pqb.k(fc)])
            op("pool", TT(PQ[:, fc, 2:4, :], Av.rearrange("p (q n) -> p q n", q=2), Bv.rearrange("p (q n) -> p q n", q=2), ALU.subtract),
               R=[tmb.k("AS"), tmb.k("AD")], W=[pqb.k(fc)])
        P.free(vtb, ftb_[0], ftb_[1], osb, tmb)
        itb = [P.alloc(f"hy_it{i}", 2 * NF * TB // 2) for i in range(2)]
        ytb = [P.alloc(f"hy_yt{i}", TB) for i in range(2)]
        CI4 = T[f"h_ci{L}"].rearrange("p (f a n) -> p f a n", f=NF, a=2)
        SI4 = T[f"h_si{L}"].rearrange("p (f a n) -> p f a n", f=NF, a=2)
        ii = 0
        allpq = [pqb.k(fc) for fc in range(NF)]
        for a_ in range(2):
            for tch in range(HB // TB):
                ib = itb[ii % 2]
                it = P.bf(ib).rearrange("p (s f n) -> p s f n", s=2, f=NF)
                op("sp", DMA(it[:, 0], CI4[:, :, a_, tch * TB:(tch + 1) * TB]), W=[ib.k()], dma=True)
                op("sp", DMA(it[:, 1], SI4[:, :, a_, tch * TB:(tch + 1) * TB]), W=[ib.k()], dma=True)
                ii += 1
                for cc in range(2):
                    b = P.bank()
                    mm = []
                    for fc in range(NF):
                        mm.append((PS[b][:, 0:TB], PQ[:, fc, 2 * a_, cc * 128:(cc + 1) * 128], it[:, 0, fc, :], fc == 0, False))
                        mm.append((PS[b][:, 0:TB], PQ[:, fc, 2 * a_ + 1, cc * 128:(cc + 1) * 128], it[:, 1, fc, :], False, fc == NF - 1))
                    op("pe", MM(mm), R=allpq + [ib.k()], W=[psk(b)])
                    tstart = a_ + 2 * tch * TB
                    sl = slice(tstart, tstart + 2 * TB - 1, 2)
                    yt = P.f32(ytb[cc])[:, 0:TB]
                    op("dve", STT(yt, vin[:, cc, sl], dsk[:, o, cc:cc + 1], PS[b][:, 0:TB], ALU.mult, ALU.add),
                       R=[psk(b), kvin[cc], KP], W=[ytb[cc].k()])
                    op("pool", TT(outv[:, cc, sl], yt, gate[:, cc, sl], ALU.mult), R=[ytb[cc].k(), kgate[cc]], W=[kout[cc]])
        P.free(pqb, itb[0], itb[1], ytb[0], ytb[1])


    return dict(nc=nc, P=P, T=T, DBG=DBG, layer_mod=layer_mod, ffn=ffn, mixer=mixer, loc=locals())


def build_full(dbg=()):
    B = build(dbg=dbg)
    T = B["T"]
    for l in range(2):
        B["layer_mod"](l)
        B["mixer"](l, T["xin"] if l == 0 else T["xb"], T["xa"])
        B["ffn"](l, T["xa"], T["xb"] if l == 0 else T["y"])
    B["P"].emit()
    return B


_NC_CACHE = {}


def core_inputs(core, inp, consts):
    m = {nm: np.ascontiguousarray(inp[nm], dtype=np.float32) for nm, _ in WEIGHT_SPECS}
    m.update(consts)
    m["xin"] = np.ascontiguousarray(np.concatenate(
        [inp["x_sample"][core], inp["x_prompt"][2 * core], inp["x_prompt"][2 * core + 1]], 0), dtype=np.float32)
    m["ckv_ctx"] = np.ascontiguousarray(inp["cache_ckv"][core], dtype=np.float32)
    m["kr_ctx"] = np.ascontiguousarray(inp["cache_krope"][core], dtype=np.float32)
    m["s0"] = np.ascontiguousarray(inp["state_ret"][core], dtype=np.float32)
    m["cvec"] = np.ascontiguousarray(np.stack([inp["c_ctx"], inp["c"][core]]), dtype=np.float32)
    return m


def kernel(**inputs):
    inp = {k: np.asarray(v) for k, v in inputs.items()}
    consts = get_consts()
    if "nc" not in _NC_CACHE:
        _NC_CACHE["nc"] = build_full()["nc"]
    nc = _NC_CACHE["nc"]
    in_maps = [core_inputs(c, inp, consts) for c in range(8)]
    res = run_bass_kernel_spmd(nc, in_maps, core_ids=list(range(8)))
    R = res.results
    y_prompt = np.zeros((16, 256, 1024), np.float32)
    y_sample = np.zeros((8, 2048, 1024), np.float32)
    new_ckv = np.zeros((16, 2, 256, 128), np.float32)
    new_kr = np.zeros((16, 2, 256, 32), np.float32)
    new_st = np.zeros((16, 2, 2, 4, 64, 64), np.float32)
    for c in range(8):
        y = R[c]["y"]
        y_sample[c] = y[0:2048]
        y_prompt[2 * c] = y[2048:2304]
        y_prompt[2 * c + 1] = y[2304:2560]
        new_ckv[2 * c:2 * c + 2] = R[c]["o_ckv"]
        new_kr[2 * c:2 * c + 2] = R[c]["o_kr"]
        new_st[2 * c:2 * c + 2] = R[c]["o_st"]
    return (y_prompt, y_sample, new_ckv, new_kr, new_st)
```

```python
import contextlib
import math
import numpy as np
import ml_dtypes
import concourse.bass as bass
import concourse.mybir as mybir
from concourse.bass_utils import run_bass_kernel_spmd

F32 = mybir.dt.float32
BF16 = mybir.dt.bfloat16
ALU = mybir.AluOpType
AF = mybir.ActivationFunctionType
AX = mybir.AxisListType
NPBF = ml_dtypes.bfloat16

NDMA_SLOTS = 8
import os as _osmod
PROFILE_SCOPES = bool(_osmod.environ.get("KPROFILE"))
SCHEDULE = _osmod.environ.get("KSCHED", "1") == "1"
SCHED_WINDOW = int(_osmod.environ.get("KWIN", "600"))
D = 1024
DFF = 2816
NFF = 22
NT = 20
EPS = 1e-6
SEQS = [(0, 16, True), (16, 2, False), (18, 2, False)]


class Buf:
    def __init__(self, name, col0, ncols):
        self.name, self.col0, self.ncols = name, col0, ncols
        self.subs = set()
        self.inherit = set()

    def k(self, sub=None):
        self.subs.add(sub)
        return (self, sub)


class Prog:
    def __init__(self, nc):
        self.nc = nc
        self.ops = []
        self.last_writer = {}
        self.readers = {}
        self.stack = contextlib.ExitStack()
        self.live = []
        self.dead = []
        self.psn = 0

    def make_arena(self, ncols):
        self.arena = self.stack.enter_context(self.nc.sbuf_tensor("arena", [128, ncols], F32))
        self.arena_cols = ncols
        self.ps = [self.stack.enter_context(self.nc.psum_tensor(f"ps{i}", [128, 512], F32)) for i in range(8)]

    def bank(self):
        i = self.psn % 8
        self.psn += 1
        return i

    def alloc(self, name, ncols):
        ncols = int(math.ceil(ncols))
        segs = sorted((b.col0, b.ncols) for b in self.live)
        pos, found = 0, None
        for c0, n in segs:
            if c0 - pos >= ncols:
                found = pos
                break
            pos = max(pos, c0 + n)
        if found is None:
            if self.arena_cols - pos >= ncols:
                found = pos
            else:
                raise RuntimeError(f"arena OOM {name} {ncols}: live={[(b.name, b.ncols) for b in self.live]}")
        b = Buf(name, found, ncols)
        self.live.append(b)
        for ob in self.dead:
            if ob.col0 < found + ncols and found < ob.col0 + ob.ncols:
                for sk in ob.subs:
                    kk = (ob, sk)
                    w = self.last_writer.get(kk)
                    if w is not None:
                        b.inherit.add(w)
                    b.inherit.update(self.readers.get(kk, ()))
                b.inherit.update(ob.inherit)
        return b

    def free(self, *bs):
        for b in bs:
            self.live.remove(b)
            self.dead.append(b)

    def f32(self, b, p0=0, p1=128):
        return self.arena[p0:p1, b.col0:b.col0 + b.ncols]

    def bf(self, b, p0=0, p1=128):
        return self.arena[p0:p1, b.col0:b.col0 + b.ncols].bitcast(BF16)

    def op(self, eng, fn, R=(), W=(), dma=False):
        idx = len(self.ops)
        deps = set()
        for k in list(R) + list(W):
            if isinstance(k, tuple) and isinstance(k[0], Buf) and k[0].inherit:
                deps.update(k[0].inherit)
        for k in R:
            w = self.last_writer.get(k)
            if w is not None:
                deps.add(w)
        for k in W:
            w = self.last_writer.get(k)
            if w is not None:
                deps.add(w)
            deps.update(self.readers.get(k, ()))
        for k in R:
            self.readers.setdefault(k, []).append(idx)
        for k in W:
            self.last_writer[k] = idx
            self.readers[k] = []
        self.ops.append(dict(eng=eng, fn=fn, deps=deps, dma=dma, signal=False, cost=getattr(fn, "cost", 0.5), phase=getattr(self, "phase", "x") + str(getattr(self, "layer", ""))))
        return idx

    def schedule(self, engs):
        ops = self.ops
        n = len(ops)
        LAT = 1.2
        succ = [[] for _ in range(n)]
        indeg = [0] * n
        for i, o in enumerate(ops):
            for d in o["deps"]:
                succ[d].append(i)
            indeg[i] = len(o["deps"])
        rt = [0.0] * n
        fin = [0.0] * n
        avail = {e: [] for e in engs}
        for i, o in enumerate(ops):
            if indeg[i] == 0:
                avail[o["eng"]].append(i)
        clock = {e: 0.0 for e in engs}
        per = {e: [] for e in engs}
        done = 0
        WIN = SCHED_WINDOW
        while done < n:
            best = None
            for e in engs:
                av = avail[e]
                if not av:
                    continue
                c = clock[e]
                lo = min(av)
                cand = None
                for i in av:
                    if i - lo > WIN:
                        continue
                    if rt[i] <= c and (cand is None or i < cand):
                        cand = i
                if cand is None:
                    cand = min((i for i in av if i - lo <= WIN), key=lambda i: (rt[i], i))
                st = max(c, rt[cand])
                if best is None or (st, cand) < (best[0], best[1]):
                    best = (st, cand, e)
            st, i, e = best
            o = ops[i]
            avail[e].remove(i)
            per[e].append(i)
            if o["dma"]:
                clock[e] = st + 0.15
                fin[i] = st + o["cost"]
            else:
                fin[i] = st + o["cost"]
                clock[e] = fin[i]
            done += 1
            for s_ in succ[i]:
                indeg[s_] -= 1
                lat = LAT if (ops[s_]["eng"] != e or o["dma"]) else 0.25
                rt[s_] = max(rt[s_], fin[i] + lat)
                if indeg[s_] == 0:
                    avail[ops[s_]["eng"]].append(s_)
        self.est_us = max(fin)
        return per

    def emit(self):
        nc, ops = self.nc, self.ops
        engs = ["pe", "act", "dve", "pool", "sp"]
        per = self.schedule(engs) if SCHEDULE else None
        if per is None:
            per = {e: [] for e in engs}
            for i, o in enumerate(ops):
                per[o["eng"]].append(i)
        pos = {}
        for e in engs:
            for p_, i in enumerate(per[e]):
                pos[i] = p_
        for e in engs:
            seen = {pe_: -1 for pe_ in engs}
            for i in per[e]:
                o = ops[i]
                need = {}
                o["wdeps"] = []
                for d in o["deps"]:
                    od = ops[d]
                    if od["dma"]:
                        o["wdeps"].append(d)
                    else:
                        if od["eng"] not in need or pos[d] > pos[need[od["eng"]]]:
                            need[od["eng"]] = d
                for pe_, d in need.items():
                    if pos[d] > seen[pe_]:
                        seen[pe_] = pos[d]
                        o["wdeps"].append(d)
                        ops[d]["signal"] = True
        sems = {e: self.stack.enter_context(nc.semaphore("s_" + e)) for e in engs}
        dsems = {e: [self.stack.enter_context(nc.semaphore(f"d_{e}{i}")) for i in range(NDMA_SLOTS)]
                 for e in ("sp", "pool", "act")}
        cnt = {e: 0 for e in engs}
        dcnt = {e: 0 for e in dsems}
        for e in engs:
            for i in per[e]:
                o = ops[i]
                if o["dma"]:
                    j = dcnt[e]
                    dcnt[e] += 1
                    o["sem"] = dsems[e][j % NDMA_SLOTS]
                    o["val"] = 16 * (j // NDMA_SLOTS + 1)
                    o["prev"] = 16 * (j // NDMA_SLOTS)
                elif o["signal"]:
                    cnt[e] += 1
                    o["sem"] = sems[e]
                    o["val"] = cnt[e]
        nw = [0]

        def run_engine(ename, eobj):
            waited = {}
            for i in per[ename]:
                o = ops[i]
                wl = {}
                for d in o["wdeps"]:
                    od = ops[d]
                    s = od["sem"]
                    if od["val"] > wl.get(s.name, (s, 0))[1]:
                        wl[s.name] = (s, od["val"])
                if o["dma"] and o["prev"] > 0:
                    s = o["sem"]
                    if o["prev"] > wl.get(s.name, (s, 0))[1]:
                        wl[s.name] = (s, o["prev"])
                for nm, (s, v) in wl.items():
                    if waited.get(nm, 0) >= v:
                        continue
                    eobj.wait_ge(s, v)
                    nw[0] += 1
                    waited[nm] = v
                if PROFILE_SCOPES:
                    with nc.named_scope(o["phase"]):
                        ins = o["fn"](eobj)
                else:
                    ins = o["fn"](eobj)
                if o["dma"]:
                    ins.then_inc(o["sem"], 16)
                elif o["signal"]:
                    ins.then_inc(o["sem"], 1)
            if ename in dsems:
                n = dcnt[ename]
                for slot in range(NDMA_SLOTS):
                    k = (n - slot + NDMA_SLOTS - 1) // NDMA_SLOTS
                    if k > 0 and waited.get(dsems[ename][slot].name, 0) < 16 * k:
                        eobj.wait_ge(dsems[ename][slot], 16 * k)

        with nc.Block() as block:
            @block.tensor
            def _(e):
                run_engine("pe", e)

            @block.scalar
            def _(e):
                run_engine("act", e)

            @block.vector
            def _(e):
                run_engine("dve", e)

            @block.gpsimd
            def _(e):
                run_engine("pool", e)

            @block.sync
            def _(e):
                run_engine("sp", e)
        self.stats = dict(nops=len(ops), nwaits=nw[0], cnt=cnt, dcnt=dcnt, est_us=getattr(self, "est_us", None))


def _n(ap):
    n = 1
    for d in ap.shape[1:]:
        n *= d
    return n


def MM(specs):
    def f(e):
        ins = None
        for (o, l, r, st, sp) in specs:
            ins = e.matmul(o, lhsT=l, rhs=r, start=st, stop=sp)
        return ins
    f.cost = sum(max(_n(o), 64) / 2300.0 + 0.005 for (o, l, r, st, sp) in specs)
    return f


def ACT(out, in_, func, bias=None, scale=1.0, accum=None):
    def f(e):
        kw = {}
        if bias is not None:
            kw["bias"] = bias
        if accum is not None:
            kw["accum_out"] = accum
        return e.activation(out=out, in_=in_, func=func, scale=scale, **kw)
    f.cost = 0.19 + _n(in_) / 1200.0
    return f


def _v(fn, n, rate=960.0):
    fn.cost = 0.12 + n / rate
    return fn


def TT(out, a, b, op):
    return _v(lambda e: e.tensor_tensor(out=out, in0=a, in1=b, op=op), _n(out))


def TS(out, a, s1, op0, s2=None, op1=None):
    if op1 is None:
        return _v(lambda e: e.tensor_scalar(out=out, in0=a, scalar1=s1, scalar2=None, op0=op0), _n(out))
    return _v(lambda e: e.tensor_scalar(out=out, in0=a, scalar1=s1, scalar2=s2, op0=op0, op1=op1), _n(out))


def STT(out, a, s, b, op0, op1):
    return _v(lambda e: e.scalar_tensor_tensor(out=out, in0=a, scalar=s, in1=b, op0=op0, op1=op1), _n(out))


def CP(out, in_):
    return _v(lambda e: e.tensor_copy(out=out, in_=in_), _n(out))


def RED(out, in_, op=None):
    return _v(lambda e: e.tensor_reduce(out=out, in_=in_, axis=AX.X, op=op or ALU.add), _n(in_))


def RECIP(out, in_):
    return _v(lambda e: e.reciprocal(out=out, in_=in_), _n(out))


def MSET(out, v):
    return _v(lambda e: e.memset(out, v), _n(out), 2400.0)


def DMA(out, in_):
    f = lambda e: e.dma_start(out=out, in_=in_)
    f.cost = 2.0 + _n(out) * out.shape[0] * 4 / 200000.0
    return f


def DMAS(out, in_):
    f = lambda e: e.dma_start(out=out, in_=in_, allow_slow_non_contiguous=True)
    f.cost = 4.0
    return f


def bc(ap, dims):
    return bass.AP(ap.tensor, ap.offset, [list(ap.ap[0])] + [list(d) for d in dims])


def parity_perm(L):
    nj = L // 256
    idx = np.zeros((2, nj, 128), np.int64)
    for pi in range(2):
        for j in range(nj):
            idx[pi, j] = pi + 2 * (128 * j + np.arange(128))
    return idx


def host_consts():
    C = {}
    C["ident"] = np.eye(128, dtype=np.float32).astype(NPBF)
    tok = np.arange(2048)
    row, col = tok // 64, tok % 64
    for nm, half in (("ret", 16), ("mla", 8)):
        inv = 10000.0 ** (-np.arange(half, dtype=np.float64) / half)
        ang = np.stack([row[:, None] * inv[None], col[:, None] * inv[None]], axis=1)
        ang = ang.reshape(16, 128, 2, half).transpose(1, 0, 2, 3).reshape(128, 16 * 2 * half)
        C["cos_" + nm] = np.cos(ang).astype(np.float32)
        C["sin_" + nm] = np.sin(ang).astype(np.float32)
    m = np.arange(128)[:, None].astype(np.float64)
    c = np.arange(128)[None, :].astype(np.float64)
    C["ret_dpos"] = np.tile(np.maximum(c - m, 0), (1, 4)).astype(np.float32)
    C["ret_dneg"] = np.tile(np.maximum(m - c, 0), (1, 4)).astype(np.float32)
    C["ret_mge"] = np.tile((c >= m) * 0.125, (1, 4)).astype(np.float32)
    C["ret_mle"] = np.tile((c <= m) * 0.125, (1, 4)).astype(np.float32)
    p = np.arange(128, dtype=np.float64)
    C["ret_cols"] = np.stack([p + 1, 127 - p, 128 - p, p], axis=1).astype(np.float32)
    a = 2 * np.pi * np.outer(np.arange(64), np.arange(64)) / 64
    C64 = np.kron(np.eye(2), np.cos(a))
    S64 = np.kron(np.eye(2), np.sin(a))
    C["f_c64"] = C64.astype(NPBF)
    C["f_s64n"] = (-S64).astype(NPBF)
    for L in (2048, 256):
        idx = parity_perm(L)
        nj = L // 256
        l = idx.astype(np.float64)
        pp = np.arange(L // 2, dtype=np.float64)
        ang = 2 * np.pi * l[..., None] * pp / L
        sc = 1.0 / math.sqrt(L * 64)
        C[f"f_cl{L}"] = (np.cos(ang) * sc).transpose(2, 0, 1, 3).reshape(128, -1).astype(NPBF)
        C[f"f_sl{L}"] = (np.sin(ang) * sc).transpose(2, 0, 1, 3).reshape(128, -1).astype(NPBF)
        pos = idx.reshape(-1).astype(np.float64)
        t = pos / L
        bands = np.arange(1, 17, dtype=np.float64)
        ang2 = (2 * np.pi / L) * pos[:, None] * bands[None]
        z = np.concatenate([t[:, None], np.sin(ang2), np.cos(ang2)], axis=1)
        C[f"h_zT{L}"] = np.ascontiguousarray(z.T).astype(np.float32)
        deltas = np.abs(np.linspace(math.log(1e-2) / 1.5, math.log(1e-2) / 0.3, 256))
        win = np.exp(-t[:, None] * deltas[None])
        C[f"h_win{L}"] = win.reshape(2 * nj, 128, 256).transpose(1, 0, 2).reshape(128, -1).astype(np.float32)
        N = 2 * L
        nf = L // 256 if L >= 256 else 1
        F = L // 2
        f = np.arange(F, dtype=np.float64) + 0.5
        s = idx.astype(np.float64)
        psi = 2 * np.pi * s[..., None] * f / N
        fch = F // 128
        cf = np.cos(psi).reshape(2, nj, 128, fch, 128).transpose(3, 2, 0, 1, 4).reshape(fch, 128, -1)
        sf = (-np.sin(psi)).reshape(2, nj, 128, fch, 128).transpose(3, 2, 0, 1, 4).reshape(fch, 128, -1)
        C[f"h_cf{L}"] = cf.astype(NPBF)
        C[f"h_sf{L}"] = sf.astype(NPBF)
        tt = np.stack([2 * np.arange(L // 2), 2 * np.arange(L // 2) + 1]).astype(np.float64)
        psi2 = 2 * np.pi * f[:, None, None] * tt[None] / N
        ci = (2.0 / N) * np.cos(psi2)
        si = -(2.0 / N) * np.sin(psi2)
        C[f"h_ci{L}"] = ci.reshape(fch, 128, 2, L // 2).transpose(1, 0, 2, 3).reshape(128, -1).astype(NPBF)
        C[f"h_si{L}"] = si.reshape(fch, 128, 2, L // 2).transpose(1, 0, 2, 3).reshape(128, -1).astype(NPBF)
    return C


_CONSTS = None


def get_consts():
    global _CONSTS
    if _CONSTS is None:
        _CONSTS = host_consts()
    return _CONSTS


WEIGHT_SPECS = [
    ("w_ada", (2, 1024, 6144)), ("b_ada", (2, 6144)), ("norm_g", (2, 4, 1024)), ("w_in", (2, 1024, 2464)),
    ("w_out", (2, 1024, 1024)), ("ret_decay", (2, 2, 4)), ("mla_q_norm", (2, 256)), ("mla_kv_norm", (2, 128)),
    ("mla_w_uq", (2, 256, 384)), ("mla_w_ukv", (2, 128, 512)), ("hy_short_w", (2, 3, 768)),
    ("hy_short_b", (2, 768)), ("hy_w1", (2, 33, 64)), ("hy_b1", (2, 64)), ("hy_w2", (2, 64, 64)),
    ("hy_b2", (2, 64)), ("hy_w3", (2, 64, 1024)), ("hy_bias", (2, 2, 256)), ("w_gate", (2, 1024, 2816)),
    ("w_up", (2, 1024, 2816)), ("w_down", (2, 2816, 1024)),
]
CORE_SPECS = [("xin", (2560, 1024)), ("ckv_ctx", (2, 256, 128)), ("kr_ctx", (2, 256, 32)),
              ("s0", (2, 2, 4, 64, 64)), ("cvec", (2, 1024))]
OUT_SPECS = [("y", (2560, 1024)), ("o_ckv", (2, 2, 256, 128)), ("o_kr", (2, 2, 256, 32)),
             ("o_st", (2, 2, 2, 4, 64, 64))]


def build(dbg=(), stop_after=None):
    nc = bass.Bass("TRN2", target_bir_lowering=False)
    P = Prog(nc)
    C = get_consts()
    T = {}
    for nm, shp in WEIGHT_SPECS + CORE_SPECS:
        T[nm] = nc.dram_tensor(nm, list(shp), F32, kind="ExternalInput").ap()
    for nm, arr in C.items():
        T[nm] = nc.dram_tensor(nm, list(arr.shape), BF16 if arr.dtype == NPBF else F32, kind="ExternalInput").ap()
    for nm, shp in OUT_SPECS:
        T[nm] = nc.dram_tensor(nm, list(shp), F32, kind="ExternalOutput").ap()
    T["xa"] = nc.dram_tensor("xa", [2560, 1024], F32, kind="Internal").ap()
    T["xb"] = nc.dram_tensor("xb", [2560, 1024], F32, kind="Internal").ap()
    DBG = {}

    P.make_arena(53184)
    PS = P.ps
    op = P.op

    def psk(i):
        return ("ps", i)

    def dump(name, ap, keys, shape=None):
        if name not in dbg:
            return
        shape = list(shape or ap.shape)
        d = nc.dram_tensor("dbg_" + name, shape, F32, kind="ExternalOutput").ap()
        DBG[name] = d
        if len(shape) == 3:
            for i_ in range(shape[1]):
                op("pool", DMA(d[:, i_, :], ap[:, i_, :]), R=keys, dma=True)
        else:
            op("pool", DMA(d, ap), R=keys, dma=True)

    cb = P.alloc("consts", 64 + 1 + 8 + 64 * 3 + 128)
    cw = P.f32(cb)
    o = [0]

    def take(n, dt=F32, src=None):
        src = cw if src is None else src
        v = src[:, o[0]:o[0] + n]
        o[0] += n
        return v.bitcast(BF16) if dt == BF16 else v
    ident = take(64, BF16)
    epsc = take(1)
    cols8 = take(8)
    c64, s64n, onesb = take(64, BF16), take(64, BF16), take(64, BF16)
    onesf = take(128)
    KC = cb.k()
    for dst, nm in ((ident, "ident"), (c64, "f_c64"), (s64n, "f_s64n")):
        op("sp", DMA(dst, T[nm]), W=[KC], dma=True)
    op("pool", MSET(epsc, EPS), W=[KC])
    op("pool", MSET(cols8[:, 0:1], -math.pi), W=[KC])
    op("pool", MSET(onesb, 1.0), W=[KC])
    op("pool", MSET(onesf, 1.0), W=[KC])
    MC = {}

    def mixer_consts():
        mb = P.alloc("mconsts", 4 * 512 + 4 + 2 * 512 + 2 * 256)
        o[0] = 0
        mw = P.f32(mb)
        for nm, n in (("ret_dpos", 512), ("ret_dneg", 512), ("ret_mge", 512), ("ret_mle", 512), ("ret_cols", 4),
                      ("cos_ret", 512), ("sin_ret", 512), ("cos_mla", 256), ("sin_mla", 256)):
            MC[nm] = take(n, src=mw)
            op("sp", DMA(MC[nm], T[nm]), W=[mb.k()], dma=True)
        MC["buf"] = mb
        MC["key"] = mb.k()

    mcolb = P.alloc("modcols", 2 * 4 * 8)
    mcol = P.f32(mcolb).rearrange("p (r q k) -> p r q k", r=2, q=4)
    gbb = P.alloc("gbc", 4 * 1024)
    gbc = P.f32(gbb).rearrange("p (r q n) -> p r q n", r=2, q=2)
    PR = (0, 32)

    def layer_mod(l):
        P.phase = "mod"
        rb = P.alloc("rows", 6144 + 4096 + 6144)
        rows = P.f32(rb)[0:33, :]
        m = rows[:, 0:6144]
        ngr = rows[:, 6144:10240]
        rowt = rows[:, 10240:16384]
        KR = rb.k()
        scb = P.alloc("silu_c", 8 * 34 // 2)
        sct = P.bf(scb).rearrange("p (k r) -> p k r", r=34)
        cfb = P.alloc("c_f32", 16)
        cf32 = P.f32(cfb).rearrange("p (k r) -> p k r", r=2)
        for r in range(2):
            op("sp", DMAS(cf32[:, :, r:r + 1], bass.AP(T["cvec"].tensor, r * 1024, [[1, 128], [128, 8], [1, 1]])),
               W=[cfb.k()], dma=True)
        op("pool", MSET(sct, 0.0), W=[scb.k()])
        for r in range(2):
            op("act", ACT(sct[:, :, PR[r]:PR[r] + 1], cf32[:, :, r:r + 1], AF.Silu), R=[cfb.k()], W=[scb.k()])
        op("sp", DMA(ngr, bass.AP(T["norm_g"].tensor, l * 4096, [[0, 33], [1, 4096]])), W=[KR], dma=True)
        wab = [P.alloc(f"wada{i}", 8 * 512 // 2) for i in range(2)]
        badb = P.alloc("bada", 2 * 512)
        for nb in range(12):
            wb_ = wab[nb % 2]
            par = nb % 2
            wv = P.bf(wb_).rearrange("p (k n) -> p k n", k=8)
            op("pool", DMA(wv, T["w_ada"][l][:, nb * 512:(nb + 1) * 512].rearrange("(k p) n -> p k n", p=128)),
               W=[wb_.k()], dma=True)
            bv = P.f32(badb)[0:33, par * 512:par * 512 + 512]
            op("sp", DMA(bv, bass.AP(T["b_ada"].tensor, l * 6144 + nb * 512, [[0, 33], [1, 512]])),
               W=[badb.k(par)], dma=True)
            b = P.bank()
            op("pe", MM([(PS[b][0:33, :], sct[:, k, 0:33], wv[:, k, :], k == 0, k == 7) for k in range(8)]),
               R=[scb.k(), wb_.k()], W=[psk(b)])
            op("dve", TT(m[:, nb * 512:(nb + 1) * 512], PS[b][0:33, :], bv, ALU.add),
               R=[psk(b), badb.k(par)], W=[KR])
        for r in range(2):
            dump(f"mod{l}{r}", m[PR[r]:PR[r] + 1, :], [KR])
        op("dve", STT(rowt[:, 0:1024], m[:, 1024:2048], 1.0, ngr[:, 0:1024], ALU.add, ALU.mult), R=[KR], W=[KR])
        op("dve", CP(rowt[:, 1024:2048], m[:, 0:1024]), R=[KR], W=[KR])
        op("dve", STT(rowt[:, 2048:3072], m[:, 4096:5120], 1.0, ngr[:, 2048:3072], ALU.add, ALU.mult), R=[KR], W=[KR])
        op("dve", CP(rowt[:, 3072:4096], m[:, 3072:4096]), R=[KR], W=[KR])
        op("dve", TT(rowt[:, 4096:5120], m[:, 2048:3072], ngr[:, 1024:2048], ALU.mult), R=[KR], W=[KR])
        op("dve", TT(rowt[:, 5120:6144], m[:, 5120:6144], ngr[:, 3072:4096], ALU.mult), R=[KR], W=[KR])
        for r in range(2):
            pr = PR[r]
            b = P.bank()
            op("pe", MM([(PS[b][:, q * 8 + k:q * 8 + k + 1], rowt[pr:pr + 1, q * 1024 + k * 128:q * 1024 + (k + 1) * 128],
                          onesf[pr:pr + 1, 0:1], True, True) for q in range(4) for k in range(8)]),
               R=[KR, KC], W=[psk(b)])
            op("dve", CP(mcol[:, r], PS[b][:, 0:32].rearrange("p (q k) -> p q k", q=4)), R=[psk(b)], W=[mcolb.k()])
            for q in range(2):
                for hf in range(2):
                    b = P.bank()
                    c0 = (4 + q) * 1024 + hf * 512
                    op("pe", MM([(PS[b][:, :], onesf[pr:pr + 1, 0:128], rowt[pr:pr + 1, c0:c0 + 512], True, True)]),
                       R=[KR, KC], W=[psk(b)])
                    op("act", ACT(gbc[:, r, q, hf * 512:(hf + 1) * 512], PS[b][:, :], AF.Copy), R=[psk(b)], W=[gbb.k()])
        P.free(rb, scb, cfb, wab[0], wab[1], badb)

    def rstd_from_ss(ss, out, n, keys_r, keys_w, tmp):
        op("act", ACT(tmp, ss, AF.Sqrt, bias=epsc[0:ss.shape[0], :], scale=1.0 / n), R=keys_r + [KC], W=[keys_w[1]])
        op("dve", RECIP(out, tmp), R=[keys_w[1]], W=[keys_w[0]])

    xtb = [None, None]
    xnb = [P.alloc(f"xn{i}", 512) for i in range(2)]
    stb = P.alloc("stats", 64)
    stv = P.f32(stb)
    tcount = [0]

    def norm_transpose(src, tile, r, q0, dstT, dcol, xkeep=None):
        i = tcount[0] % 2
        tcount[0] += 1
        xt = P.f32(xtb[i]) if xkeep is None else xkeep[0]
        kx = xtb[i].k() if xkeep is None else xkeep[1]
        op("sp", DMA(xt, src[tile * 128:(tile + 1) * 128, :]), R=[("dram", src.tensor.name, tile)], W=[kx], dma=True)
        ss, rs, tm = stv[:, i * 4:i * 4 + 1], stv[:, i * 4 + 1:i * 4 + 2], stv[:, i * 4 + 2:i * 4 + 3]
        op("act", ACT(P.bf(xnb[i]), xt, AF.Square, accum=ss), R=[kx], W=[xnb[i].k(), stb.k(("ss", i))])
        rstd_from_ss(ss, rs, 1024.0, [stb.k(("ss", i))], [stb.k(("rs", i)), stb.k(("tm", i))], tm)
        xn = P.bf(xnb[i])
        op("dve", TS(xn, xt, rs, ALU.mult), R=[kx, stb.k(("rs", i))], W=[xnb[i].k()])
        for half in range(2):
            b = P.bank()
            op("pe", MM([(PS[b][:, j * 128:(j + 1) * 128], xn[:, (half * 4 + j) * 128:(half * 4 + j + 1) * 128], ident, True, True)
                         for j in range(4)]), R=[xnb[i].k(), KC], W=[psk(b)])
            for j in range(4):
                kc = half * 4 + j
                o_ = dstT[:, kc, dcol:dcol + 128]
                src_ps = PS[b][:, j * 128:(j + 1) * 128]
                A, B = mcol[:, r, q0, kc:kc + 1], mcol[:, r, q0 + 1, kc:kc + 1]
                if half == 0:
                    op("act", ACT(o_, src_ps, AF.Identity, bias=B, scale=A), R=[psk(b), mcolb.k()], W=[dstT_key[0]])
                else:
                    op("dve", TS(o_, src_ps, A, ALU.mult, B, ALU.add), R=[psk(b), mcolb.k()], W=[dstT_key[0]])

    dstT_key = [None]
    J2 = [None]
    ROPEB = [None]

    def resid_update(ps2, tile, r, q, xt, kx, dst, ri):
        junk2b = J2[0]
        ssa, ssb, ss, rs, tm = (stv[:, 16 + ri * 8 + j:16 + ri * 8 + j + 1] for j in range(5))
        kk = stb.k(("ru", ri))
        for hf, sx in ((0, ssa), (1, ssb)):
            op("act", ACT(P.f32(junk2b)[:, hf * 512:(hf + 1) * 512], PS[ps2[hf]][:, :], AF.Square, accum=sx),
               R=[psk(ps2[hf])], W=[junk2b.k(hf), kk])
        op("dve", TT(ss, ssa, ssb, ALU.add), R=[kk], W=[kk])
        op("act", ACT(tm, ss, AF.Sqrt, bias=epsc, scale=1.0 / 1024), R=[kk, KC], W=[kk])
        op("dve", RECIP(rs, tm), R=[kk], W=[kk])
        for hf in range(2):
            tmp = P.f32(junk2b)[:, hf * 512:(hf + 1) * 512]
            op("dve", STT(tmp, PS[ps2[hf]][:, :], rs, gbc[:, r, q, hf * 512:(hf + 1) * 512], ALU.mult, ALU.mult),
               R=[psk(ps2[hf]), kk, gbb.k()], W=[junk2b.k(hf)])
            op("pool", TT(xt[:, hf * 512:(hf + 1) * 512], tmp, xt[:, hf * 512:(hf + 1) * 512], ALU.add),
               R=[junk2b.k(hf), kx], W=[kx])
        op("sp", DMA(dst[tile * 128:(tile + 1) * 128, :], xt), R=[kx], W=[("dram", dst.tensor.name, tile)], dma=True)

    junk2b = None

    def ffn(l, src, dst):
        P.phase = "ffn"
        wgb = P.alloc("wg", 8 * DFF // 2)
        wub = P.alloc("wu", 8 * DFF // 2)
        wdb = P.alloc("wd", NFF * 1024 // 2)
        wg = P.bf(wgb).rearrange("p (k n) -> p k n", k=8)
        wu = P.bf(wub).rearrange("p (k n) -> p k n", k=8)
        wd = P.bf(wdb).rearrange("p (k n) -> p k n", k=NFF)
        FB = 4
        for f0 in range(0, NFF, FB):
            f1 = min(NFF, f0 + FB)
            for (wv_, nm_, wb__) in ((wg, "w_gate", wgb), (wu, "w_up", wub)):
                op("pool", DMA(wv_[:, :, f0 * 128:f1 * 128], T[nm_][l][:, f0 * 128:f1 * 128].rearrange("(k p) n -> p k n", p=128)),
                   W=[wb__.k(f0 // FB)], dma=True)
        for k in range(0, NFF, 2):
            op("pool", DMA(wd[:, k:k + 2, :], T["w_down"][l][k * 128:(k + 2) * 128, :].rearrange("(k p) n -> p k n", p=128)),
               W=[wdb.k(k // 2)], dma=True)
        h2b = P.alloc("h2T", 8 * 512 // 2)
        h2T = P.bf(h2b).rearrange("p (k n) -> p k n", k=8)
        aTb = P.alloc("aT", NFF * 512 // 2)
        aT = P.bf(aTb).rearrange("p (k n) -> p k n", k=NFF)
        xgb = P.alloc("xgrp", 4 * 1024)
        sgb = [P.alloc(f"sg{i}", 256) for i in range(2)]
        J2[0] = P.alloc("junk2", 1024)
        for g in range(5):
            r = 1 if g < 4 else 0
            dstT_key[0] = h2b.k()
            for j in range(4):
                tile = g * 4 + j
                xt = P.f32(xgb)[:, j * 1024:(j + 1) * 1024]
                norm_transpose(src, tile, r, 2, h2T, j * 128, xkeep=(xt, xgb.k(j)))
            for fc in range(NFF):
                bg, bu = P.bank(), P.bank()
                op("pe", MM([(PS[bg][:, :], wg[:, k, fc * 128:(fc + 1) * 128], h2T[:, k, :], k == 0, k == 7) for k in range(8)]),
                   R=[wgb.k(fc // FB), h2b.k()], W=[psk(bg)])
                op("pe", MM([(PS[bu][:, :], wu[:, k, fc * 128:(fc + 1) * 128], h2T[:, k, :], k == 0, k == 7) for k in range(8)]),
                   R=[wub.k(fc // FB), h2b.k()], W=[psk(bu)])
                sg = P.bf(sgb[fc % 2])
                op("act", ACT(sg, PS[bg][:, :], AF.Silu), R=[psk(bg)], W=[sgb[fc % 2].k()])
                op("dve", TT(aT[:, fc, :], sg, PS[bu][:, :], ALU.mult), R=[sgb[fc % 2].k(), psk(bu)], W=[aTb.k(fc)])
            for j in range(4):
                tile = g * 4 + j
                b0, b1 = P.bank(), P.bank()
                for hf, b in ((0, b0), (1, b1)):
                    op("pe", MM([(PS[b][:, :], aT[:, fc, j * 128:(j + 1) * 128], wd[:, fc, hf * 512:(hf + 1) * 512], fc == 0, fc == NFF - 1)
                                 for fc in range(NFF)]), R=[aTb.k(fc) for fc in range(NFF)] + [wdb.k(k_) for k_ in range(NFF // 2)], W=[psk(b)])
                xt = P.f32(xgb)[:, j * 1024:(j + 1) * 1024]
                resid_update((b0, b1), tile, r, 1, xt, xgb.k(j), dst, j % 2)
        P.free(wgb, wub, wdb, h2b, aTb, xgb, sgb[0], sgb[1], J2[0])

    def mixer(l, src, dst, parts=("ret", "mla", "four", "hy"), do_out=True):
        mixer_consts()
        KMC = MC["key"]
        xtb[0], xtb[1] = P.alloc("xt0", 1024), P.alloc("xt1", 1024)
        hTb = P.alloc("hT", 8 * 2560 // 2)
        hT = P.bf(hTb).rearrange("p (k n) -> p k n", k=8)
        catb = P.alloc("catT", 8 * 2560 // 2)
        catT = P.bf(catb).rearrange("p (k n) -> p k n", k=8)
        KH = hTb.k()
        if dbg:
            op("pool", MSET(catT, 0.0), W=[catb.k(c) for c in range(8)])
        dstT_key[0] = KH
        P.phase = "phaseA"
        for t in range(NT):
            norm_transpose(src, t, 1 if t < 16 else 0, 0, hT, t * 128)
        dump(f"hT{l}", hT, [KH])
        P.free(xtb[0], xtb[1])
        ROPEB[0] = P.alloc("ropeb", 128 + 512)
        if "ret" in parts:
            retention_all(l, hT, KH, catT, catb, KMC)
        if "mla" in parts:
            mla_all(l, hT, KH, catT, catb, KMC)
        P.free(ROPEB[0], MC["buf"])
        if "four" in parts:
            fourier_all(l, hT, KH, catT, catb)
        if "hy" in parts:
            hyena_all(l, hT, KH, catT, catb, hTb)
        else:
            P.free(hTb)
        dump(f"catT{l}", catT, [catb.k(c) for c in range(8)])
        if do_out:
            P.phase = "phaseC"
            xtb[0], xtb[1] = P.alloc("xt0", 1024), P.alloc("xt1", 1024)
            J2[0] = P.alloc("junk2", 1024)
            wob = P.alloc("wout", 8 * 1024 // 2)
            wo = P.bf(wob).rearrange("p (k n) -> p k n", k=8)
            op("pool", DMA(wo, T["w_out"][l].rearrange("(k p) n -> p k n", p=128)), W=[wob.k()], dma=True)
            for t in range(NT):
                r = 1 if t < 16 else 0
                i2 = t % 2
                xt = P.f32(xtb[i2])
                op("sp", DMA(xt, src[t * 128:(t + 1) * 128, :]), R=[("dram", src.tensor.name, t)], W=[xtb[i2].k()], dma=True)
                b0, b1 = P.bank(), P.bank()
                for hf, b in ((0, b0), (1, b1)):
                    op("pe", MM([(PS[b][:, :], catT[:, k, t * 128:(t + 1) * 128], wo[:, k, hf * 512:(hf + 1) * 512], k == 0, k == 7)
                                 for k in range(8)]), R=[catb.k(c) for c in range(8)] + [wob.k()], W=[psk(b)])
                resid_update((b0, b1), t, r, 0, xt, xtb[i2].k(), dst, i2)
            P.free(wob, xtb[0], xtb[1], J2[0])
        P.free(catb)

    def retention_all(l, hT, KH, catT, catb, KMC):
        P.phase = "ret_tab"
        wAb = P.alloc("wA", 8 * 1024 // 2)
        wA = P.bf(wAb).rearrange("p (k n) -> p k n", k=8)
        op("pool", DMA(wA, T["w_in"][l][:, 0:1024].rearrange("(k p) n -> p k n", p=128)), W=[wAb.k()], dma=True)
        rtb = P.alloc("rtabs", 8 + 8 + 8 + 16 + 512 + 4 + 4)
        rt = P.f32(rtb)
        KT_ = rtb.k()
        decb = rt[:, 0:8]
        lgb = rt[:, 8:16]
        tmp8 = rt[:, 16:24]
        xz = rt[:, 24:40].rearrange("p (q h) -> p q h", q=4)
        dmk = rt[:, 40:552]
        lgsel = rt[:, 552:556].rearrange("p (d q) -> p d q", d=2)
        gsel = rt[:, 556:560].rearrange("p (d q) -> p d q", d=2)
        op("sp", DMA(decb, bass.AP(T["ret_decay"].tensor, l * 8, [[0, 128], [1, 8]])), W=[KT_], dma=True)
        op("act", ACT(tmp8, decb, AF.Exp, scale=-1.0), R=[KT_], W=[KT_])
        op("act", ACT(tmp8, tmp8, AF.Ln, bias=onesf[:, 0:1], scale=1.0), R=[KT_, KC], W=[KT_])
        op("dve", TS(lgb, tmp8, -1.0, ALU.mult), R=[KT_], W=[KT_])
        rc = MC["ret_cols"]
        op("act", ACT(xz[:, 0, :], lgb[:, 0:4], AF.Exp, scale=rc[:, 0:1]), R=[KT_, KMC], W=[KT_])
        op("act", ACT(xz[:, 1, :], lgb[:, 0:4], AF.Exp, scale=rc[:, 1:2]), R=[KT_, KMC], W=[KT_])
        op("act", ACT(xz[:, 2, :], lgb[:, 4:8], AF.Exp, scale=rc[:, 2:3]), R=[KT_, KMC], W=[KT_])
        op("act", ACT(xz[:, 3, :], lgb[:, 4:8], AF.Exp, scale=rc[:, 3:4]), R=[KT_, KMC], W=[KT_])
        for qq in (1, 3):
            op("dve", TS(xz[:, qq, :], xz[:, qq, :], 0.125, ALU.mult), R=[KT_], W=[KT_])
        t1b = P.alloc("rt_tmp", 1024)
        t1 = P.f32(t1b)
        for h in range(4):
            hs = (h % 2) * 2 + h // 2
            sl = slice(hs * 128, (hs + 1) * 128)
            op("act", ACT(t1[:, sl], MC["ret_dpos"][:, sl], AF.Exp, scale=lgb[:, h:h + 1]), R=[KT_, KMC], W=[t1b.k()])
            op("act", ACT(t1[:, 512 + hs * 128:512 + (hs + 1) * 128], MC["ret_dneg"][:, sl], AF.Exp, scale=lgb[:, 4 + h:5 + h]),
               R=[KT_, KMC], W=[t1b.k()])
        op("dve", TT(t1[:, 0:512], t1[:, 0:512], MC["ret_mge"], ALU.mult), R=[t1b.k(), KMC], W=[t1b.k()])
        op("dve", TT(t1[:, 512:1024], t1[:, 512:1024], MC["ret_mle"], ALU.mult), R=[t1b.k(), KMC], W=[t1b.k()])
        op("dve", TT(dmk, t1[:, 0:512], t1[:, 512:1024], ALU.add), R=[t1b.k()], W=[KT_])
        P.free(t1b)
        dv = lgb.rearrange("p (d q a) -> p d q a", d=2, a=2)
        for a in range(2):
            op("dve", CP(lgsel[a * 64:(a + 1) * 64], dv[a * 64:(a + 1) * 64, :, :, a]), R=[KT_], W=[KT_])
        op("act", ACT(gsel, lgsel, AF.Exp, scale=128.0), R=[KT_], W=[KT_])

        import os as _os
        _seqs = [SEQS[int(i_)] for i_ in _os.environ.get("DBG_SEQS", "0,1,2").split(",")]
        _stop = int(_os.environ.get("RET_STOP", "9"))
        if _stop <= 1:
            return
        for (t0, nts, latent) in _seqs:
            if latent:
                nts = int(_os.environ.get("DBG_NT0", nts))
            L = nts * 128
            qTb = P.alloc("qT", 2 * L // 2)
            kTb = P.alloc("kT", 2 * L // 2)
            qT = P.bf(qTb).rearrange("p (q n) -> p q n", q=2)
            kT = P.bf(kTb).rearrange("p (q n) -> p q n", q=2)
            vtb = P.alloc("v_tok", nts * 256 // 2)
            vt = P.bf(vtb).rearrange("p (t n) -> p t n", t=nts)
            gtb = P.alloc("gate_tok", nts * 256 // 2)
            gt = P.bf(gtb).rearrange("p (t n) -> p t n", t=nts)
            kvb = P.alloc("kv_all", nts * 256)
            kva = P.f32(kvb).rearrange("p (t d q n) -> p t d q n", t=nts, d=2, q=2)
            sbb = P.alloc("S_bf", nts * 256 // 2)
            sbf = P.bf(sbb).rearrange("p (t d q n) -> p t d q n", t=nts, d=2, q=2)
            stf = P.alloc("S_f32", 256)
            Sst = P.f32(stf).rearrange("p (d q n) -> p d q n", d=2, q=2)
            qkb = [P.alloc(f"qkrot{i}", 256) for i in range(2)]
            kzb = [P.alloc(f"kz{i}", 256) for i in range(2)]
            rpbs = [P.alloc(f"ropetmp{i}", 1024) for i in range(2)] if latent else None

            def stageA(j):
                t = t0 + j
                cols = slice(t * 128, (t + 1) * 128)
                bq, bv = P.bank(), P.bank()
                op("pe", MM([(PS[bq][:, :], hT[:, k, cols], wA[:, k, 0:512], k == 0, k == 7) for k in range(8)]),
                   R=[KH, wAb.k()], W=[psk(bq)])
                op("pe", MM([(PS[bv][:, :], hT[:, k, cols], wA[:, k, 512:1024], k == 0, k == 7) for k in range(8)]),
                   R=[KH, wAb.k()], W=[psk(bv)])
                qk = P.bf(qkb[j % 2])
                kq = qkb[j % 2].k()
                if latent:
                    rpb = rpbs[j % 2]
                    rp = P.f32(rpb)
                    raw = rp[:, 0:512]
                    op("act", ACT(raw, PS[bq][:, :], AF.Copy), R=[psk(bq)], W=[rpb.k(0)])
                    src5 = raw.rearrange("p (h r x j) -> p h r x j", h=8, r=2, x=2)
                    dst5 = qk.rearrange("p (h r x j) -> p h r x j", h=8, r=2, x=2)
                    tm5 = rp[:, 512:1024].rearrange("p (u h r j) -> p u h r j", u=2, h=8, r=2)
                    cs = bc(MC["cos_ret"][:, j * 32:(j + 1) * 32], [[0, 8], [16, 2], [1, 16]])
                    sn = bc(MC["sin_ret"][:, j * 32:(j + 1) * 32], [[0, 8], [16, 2], [1, 16]])
                    x1, x2 = src5[:, :, :, 0, :], src5[:, :, :, 1, :]
                    op("pool", TT(tm5[:, 0], x1, cs, ALU.mult), R=[rpb.k(0), KMC], W=[rpb.k(1)])
                    op("pool", TT(tm5[:, 1], x2, sn, ALU.mult), R=[rpb.k(0), KMC], W=[rpb.k(2)])
                    op("pool", TT(dst5[:, :, :, 0, :], tm5[:, 0], tm5[:, 1], ALU.subtract), R=[rpb.k(1), rpb.k(2)], W=[kq])
                    op("pool", TT(tm5[:, 0], x2, cs, ALU.mult), R=[rpb.k(0), KMC], W=[rpb.k(1)])
                    op("pool", TT(tm5[:, 1], x1, sn, ALU.mult), R=[rpb.k(0), KMC], W=[rpb.k(2)])
                    op("pool", TT(dst5[:, :, :, 1, :], tm5[:, 0], tm5[:, 1], ALU.add), R=[rpb.k(1), rpb.k(2)], W=[kq])
                else:
                    op("act", ACT(qk, PS[bq][:, :], AF.Copy), R=[psk(bq)], W=[kq])
                op("act", ACT(vt[:, j, :], PS[bv][:, 0:256], AF.Copy), R=[psk(bv)], W=[vtb.k(j)])
                op("act", ACT(gt[:, j, :], PS[bv][:, 256:512], AF.Silu), R=[psk(bv)], W=[gtb.k(j)])
                kz = P.bf(kzb[j % 2]).rearrange("p (d h n) -> p d h n", d=2, h=4)
                k4 = qk[:, 256:512].rearrange("p (h n) -> p h n", h=4)
                for d_, qq in ((0, 1), (1, 3)):
                    op("pool", TT(kz[:, d_], k4, bc(xz[:, qq, :], [[1, 4], [0, 64]]), ALU.mult), R=[kq, KT_], W=[kzb[j % 2].k(d_)])

            def stageB(j):
                qk = P.bf(qkb[j % 2])
                kq = qkb[j % 2].k()
                kz = P.bf(kzb[j % 2]).rearrange("p (d h n) -> p d h n", d=2, h=4)
                bt, bt2 = P.bank(), P.bank()
                op("pe", MM([(PS[bt][:, i * 128:(i + 1) * 128], qk[:, i * 128:(i + 1) * 128], ident, True, True) for i in range(2)]),
                   R=[kq, KC], W=[psk(bt)])
                op("pe", MM([(PS[bt2][:, i * 128:(i + 1) * 128], qk[:, (2 + i) * 128:(3 + i) * 128], ident, True, True) for i in range(2)]),
                   R=[kq, KC], W=[psk(bt2)])
                jc = slice(j * 128, (j + 1) * 128)
                op("act", ACT(qT[:, :, jc], PS[bt][:, 0:256].rearrange("p (q n) -> p q n", q=2), AF.Copy), R=[psk(bt)], W=[qTb.k(j)])
                op("dve", CP(kT[:, :, jc], PS[bt2][:, 0:256].rearrange("p (q n) -> p q n", q=2)), R=[psk(bt2)], W=[kTb.k(j)])
                bk = P.bank()
                op("pe", MM([(PS[bk][:, d_ * 256 + hp * 128:d_ * 256 + (hp + 1) * 128], kz[:, d_, 2 * hp:2 * hp + 2, :],
                              vt[:, j, hp * 128:(hp + 1) * 128], True, True) for d_ in range(2) for hp in range(2)]),
                   R=[kzb[j % 2].k(0), kzb[j % 2].k(1), vtb.k(j)], W=[psk(bk)])
                for a in range(2):
                    srcv = bass.AP(PS[bk][:, :].tensor,
                                   PS[bk][a * 64:(a + 1) * 64, a * 64:a * 64 + 1].offset,
                                   [list(PS[bk][a * 64:(a + 1) * 64, :].ap[0]), [256, 2], [128, 2], [1, 64]])
                    op("act" if j % 2 == 0 else "dve",
                       (ACT(kva[a * 64:(a + 1) * 64, j], srcv, AF.Copy) if j % 2 == 0 else CP(kva[a * 64:(a + 1) * 64, j], srcv)),
                       R=[psk(bk)], W=[kvb.k((j, a))])

            P.phase = "ret_s1"
            stageA(0)
            for j in range(nts):
                if j + 1 < nts:
                    stageA(j + 1)
                stageB(j)
            rpb = None
            if _stop <= 2:
                continue
            P.phase = "ret_rec"
            KS = stf.k()
            if latent:
                for d_ in range(2):
                    for a in range(2):
                        op("sp", DMA(Sst[a * 64:(a + 1) * 64, d_], bass.AP(T["s0"].tensor, ((l * 2 + d_) * 4 + a) * 4096,
                                                                           [[64, 64], [2 * 4096, 2], [1, 64]])), W=[KS], dma=True)
            else:
                op("pool", MSET(P.f32(stf), 0.0), W=[KS])
            for d_ in range(2):
                order = range(nts) if d_ == 0 else range(nts - 1, -1, -1)
                for j in order:
                    op("act", ACT(sbf[:, j, d_], Sst[:, d_], AF.Copy), R=[KS], W=[sbb.k((j, d_))])
                    for hp in range(2):
                        op("dve", STT(Sst[:, d_, hp, :], Sst[:, d_, hp, :], gsel[:, d_, hp:hp + 1], kva[:, j, d_, hp, :], ALU.mult, ALU.add),
                           R=[KS, KT_, kvb.k((j, 0)), kvb.k((j, 1))], W=[KS])
            if not latent:
                pb = (t0 - 16) // 2
                for d_ in range(2):
                    for a in range(2):
                        op("sp", DMA(bass.AP(T["o_st"].tensor, (((pb * 2 + l) * 2 + d_) * 4 + a) * 4096, [[64, 64], [2 * 4096, 2], [1, 64]]),
                                     Sst[a * 64:(a + 1) * 64, d_]), R=[KS], dma=True)
            if _stop <= 3:
                continue
            P.free(kvb, qkb[0], qkb[1], kzb[0], kzb[1])
            if rpbs is not None:
                P.free(rpbs[0], rpbs[1])
            P.phase = "ret_out"
            G_ = 4
            ptb = [P.alloc(f"PT{i}", 256) for i in range(G_)]
            ob = [P.alloc(f"o_acc{i}", 256 * 3) for i in range(G_)]
            gnb = P.alloc("gn", 16 * G_)
            gn = P.f32(gnb)
            rob = [P.alloc(f"ret_o{i}", 128) for i in range(G_)]
            st_banks = {}

            def so1(j):
                jc = slice(j * 128, (j + 1) * 128)
                bsa = [P.bank(), P.bank()]
                PT = P.bf(ptb[j % G_])
                for a in range(2):
                    op("pe", MM([(PS[bsa[a]][:, hp * 128:(hp + 1) * 128], kT[a * 64:(a + 1) * 64, hp, jc],
                                  qT[a * 64:(a + 1) * 64, hp, jc], True, True) for hp in range(2)]),
                       R=[kTb.k(j), qTb.k(j)], W=[psk(bsa[a])])
                    op("dve", TT(PT[:, a * 256:(a + 1) * 256], PS[bsa[a]][:, 0:256], dmk[:, a * 256:(a + 1) * 256], ALU.mult),
                       R=[psk(bsa[a]), KT_], W=[ptb[j % G_].k(a)])

            def so2(j):
                jc = slice(j * 128, (j + 1) * 128)
                PT = P.bf(ptb[j % G_])
                bo = P.bank()
                op("pe", MM([(PS[bo][:, h * 64:(h + 1) * 64], PT[:, ((h % 2) * 2 + h // 2) * 128:((h % 2) * 2 + h // 2 + 1) * 128],
                              vt[:, j, h * 64:(h + 1) * 64], True, True) for h in range(4)]),
                   R=[ptb[j % G_].k(0), ptb[j % G_].k(1), vtb.k(j)], W=[psk(bo)])
                bxa = [P.bank(), P.bank()]
                for a in range(2):
                    op("pe", MM([(PS[bxa[a]][:, (d_ * 2 + hp) * 64:(d_ * 2 + hp + 1) * 64], qT[a * 64:(a + 1) * 64, hp, jc],
                                  sbf[a * 64:(a + 1) * 64, j, d_, hp, :], True, True) for d_ in range(2) for hp in range(2)]),
                       R=[qTb.k(j), sbb.k((j, 0)), sbb.k((j, 1))], W=[psk(bxa[a])])
                st_banks[j] = (bo, bxa)

            def so3(j):
                bo, bxa = st_banks[j]
                oa = P.f32(ob[j % G_]).rearrange("p (u h n) -> p u h n", u=3, h=4)
                ko = ob[j % G_].k()
                for a in range(2):
                    xif = bc(xz[:, 0, a:a + 1], [[2, 2], [0, 64]])
                    xib = bc(xz[:, 2, a:a + 1], [[2, 2], [0, 64]])
                    cf_ = PS[bxa[a]][:, 0:128].rearrange("p (q n) -> p q n", q=2)
                    cb_ = PS[bxa[a]][:, 128:256].rearrange("p (q n) -> p q n", q=2)
                    op("dve", TT(oa[:, 0, a::2, :], cf_, xif, ALU.mult), R=[psk(bxa[a]), KT_], W=[ko])
                    op("dve", TT(oa[:, 1, a::2, :], cb_, xib, ALU.mult), R=[psk(bxa[a]), KT_], W=[ko])
                op("pool", TT(oa[:, 0], oa[:, 0], oa[:, 1], ALU.add), R=[ko], W=[ko])
                op("dve", TT(oa[:, 0], oa[:, 0], PS[bo][:, 0:256].rearrange("p (h n) -> p h n", h=4), ALU.add), R=[ko, psk(bo)], W=[ko])

            def gnv(j):
                g0 = (j % G_) * 16
                return (gn[:, g0:g0 + 4], gn[:, g0 + 4:g0 + 8], gn[:, g0 + 8:g0 + 12], gn[:, g0 + 12:g0 + 16], gnb.k(j % G_))

            def so4(j):
                oa = P.f32(ob[j % G_]).rearrange("p (u h n) -> p u h n", u=3, h=4)
                ko = ob[j % G_].k()
                sm, ng_, vs, sd, kg = gnv(j)
                op("dve", RED(sm, oa[:, 0]), R=[ko], W=[kg])
                op("dve", TS(ng_, sm, -1.0 / 64, ALU.mult), R=[kg], W=[kg])
                op("pool", TT(oa[:, 1], oa[:, 0], bc(ng_, [[1, 4], [0, 64]]), ALU.add), R=[ko, kg], W=[ko])
                op("pool", TT(oa[:, 2], oa[:, 1], oa[:, 1], ALU.mult), R=[ko], W=[ko])

            def so5(j):
                oa = P.f32(ob[j % G_]).rearrange("p (u h n) -> p u h n", u=3, h=4)
                ko = ob[j % G_].k()
                sm, ng_, vs, sd, kg = gnv(j)
                op("dve", RED(vs, oa[:, 2]), R=[ko], W=[kg])
                op("act", ACT(sd, vs, AF.Sqrt, bias=epsc, scale=1.0 / 64), R=[kg, KC], W=[kg])
                op("dve", RECIP(sm, sd), R=[kg], W=[kg])

            def so6(j):
                t = t0 + j
                oa = P.f32(ob[j % G_]).rearrange("p (u h n) -> p u h n", u=3, h=4)
                ko = ob[j % G_].k()
                sm, ng_, vs, sd, kg = gnv(j)
                op("pool", TT(oa[:, 2], oa[:, 1], bc(sm, [[1, 4], [0, 64]]), ALU.mult), R=[ko, kg], W=[ko])
                ro = P.bf(rob[j % G_])
                op("dve", TT(ro.rearrange("p (h n) -> p h n", h=4), oa[:, 2], gt[:, j, :].rearrange("p (h n) -> p h n", h=4), ALU.mult),
                   R=[ko, gtb.k(j)], W=[rob[j % G_].k()])
                bt = P.bank()
                op("pe", MM([(PS[bt][:, i * 128:(i + 1) * 128], ro[:, i * 128:(i + 1) * 128], ident, True, True) for i in range(2)]),
                   R=[rob[j % G_].k(), KC], W=[psk(bt)])
                op("act", ACT(catT[:, 0:2, t * 128:(t + 1) * 128], PS[bt][:, 0:256].rearrange("p (q n) -> p q n", q=2), AF.Copy),
                   R=[psk(bt)], W=[catb.k(0), catb.k(1)])

            for j0 in range(0, nts, 2):
                js = list(range(j0, min(nts, j0 + 2)))
                for stage in (so1, so2, so3, so4, so5, so6):
                    for j in js:
                        stage(j)
            P.free(*ptb[2:], *ob[2:], *rob[2:])
            P.free(qTb, kTb, vtb, gtb, sbb, stf, ptb[0], ptb[1], ob[0], ob[1], gnb, rob[0], rob[1])
        P.free(wAb, rtb)

    def mla_all(l, hT, KH, catT, catb, KMC):
        import os as _os
        ropeb = ROPEB[0]
        P.phase = "mla1"
        wBb = P.alloc("wB", 8 * 416 // 2)
        wB = P.bf(wBb).rearrange("p (k n) -> p k n", k=8)
        op("pool", DMA(wB, T["w_in"][l][:, 1280:1696].rearrange("(k p) n -> p k n", p=128)), W=[wBb.k()], dma=True)
        wqb = P.alloc("wuq", 2 * 384 // 2 + 256)
        wuq = P.bf(wqb)[:, 0:768].rearrange("p (k n) -> p k n", k=2)
        wukv = P.bf(wqb)[:, 768:1280]
        op("pool", DMA(wuq, T["mla_w_uq"][l].rearrange("(k p) n -> p k n", p=128)), W=[wqb.k()], dma=True)
        op("pool", DMA(wukv, T["mla_w_ukv"][l]), W=[wqb.k()], dma=True)
        gnb_ = P.alloc("mla_g", 384)
        gq_b, gkv_b = P.f32(gnb_)[:, 0:256], P.f32(gnb_)[:, 256:384]
        op("sp", DMA(gq_b, bass.AP(T["mla_q_norm"].tensor, l * 256, [[0, 128], [1, 256]])), W=[gnb_.k()], dma=True)
        op("sp", DMA(gkv_b, bass.AP(T["mla_kv_norm"].tensor, l * 128, [[0, 128], [1, 128]])), W=[gnb_.k()], dma=True)
        msb = P.alloc("mla_stats", 16)
        ms = P.f32(msb)
        scale = 96.0 ** -0.5
        _seqs = [SEQS[int(i_)] for i_ in _os.environ.get("DBG_SEQS", "0,1,2").split(",")]
        for (t0, nts, latent) in _seqs:
            L = nts * 128
            nkt = nts + (2 if latent else 0)
            Lk = nkt * 128
            cqTb = P.alloc("cqT", L)
            cqT = P.bf(cqTb).rearrange("p (k n) -> p k n", k=2)
            ckTb = P.alloc("ckvT", Lk // 2)
            ckvT = P.bf(ckTb)
            KTb = P.alloc("KT", 2 * Lk)
            KT = P.bf(KTb).rearrange("p (h n) -> p h n", h=4)
            QTb = P.alloc("QT", 2 * L)
            QT = P.bf(QTb).rearrange("p (h n) -> p h n", h=4)
            Vb = P.alloc("Vaug", nkt * 130)
            Va = P.bf(Vb).rearrange("p (t h n) -> p t h n", t=nkt, h=4)
            atb = P.alloc("att_tok", nts * 128)
            att = P.bf(atb).rearrange("p (t n) -> p t n", t=nts)
            op("pool", MSET(P.bf(Vb), 1.0), W=[Vb.k()])
            kstb = [P.alloc(f"kst{i}", 128) for i in range(2)]
            cqnb = [P.alloc(f"cqn{i}", 128) for i in range(2)]
            ckfb = [P.alloc(f"ckvf{i}", 128 + 32) for i in range(2)]
            for i in range(2):
                op("pool", MSET(P.bf(kstb[i]), 0.0), W=[kstb[i].k()])
            pb = (t0 - 16) // 2
            P.phase = "mla1"
            for j in range(nkt):
                i2 = j % 2
                kst = P.bf(kstb[i2])
                kk = kstb[i2].k()
                kcols = slice(j * 128, (j + 1) * 128)
                if j < nts:
                    t = t0 + j
                    bp = P.bank()
                    op("pe", MM([(PS[bp][:, 0:416], hT[:, k, t * 128:(t + 1) * 128], wB[:, k, :], k == 0, k == 7) for k in range(8)]),
                       R=[KH, wBb.k()], W=[psk(bp)])
                    ssq, ssk, sq1, sq2, rq, rk = (ms[:, i2 * 8 + u:i2 * 8 + u + 1] for u in range(6))
                    kst_ = msb.k(i2)
                    cqn = P.bf(cqnb[i2])
                    ckf = P.f32(ckfb[i2])
                    op("act", ACT(cqn, PS[bp][:, 0:256], AF.Square, accum=ssq), R=[psk(bp)], W=[cqnb[i2].k(), kst_])
                    op("act", ACT(ckf[:, 0:128], PS[bp][:, 256:384], AF.Square, accum=ssk), R=[psk(bp)], W=[ckfb[i2].k(), kst_])
                    op("act", ACT(sq1, ssq, AF.Sqrt, bias=epsc, scale=1.0 / 256), R=[kst_, KC], W=[kst_])
                    op("act", ACT(sq2, ssk, AF.Sqrt, bias=epsc, scale=1.0 / 128), R=[kst_, KC], W=[kst_])
                    op("dve", RECIP(ms[:, i2 * 8 + 4:i2 * 8 + 6], ms[:, i2 * 8 + 2:i2 * 8 + 4]), R=[kst_], W=[kst_])
                    op("dve", STT(cqn, PS[bp][:, 0:256], rq, gq_b, ALU.mult, ALU.mult), R=[psk(bp), kst_, gnb_.k()], W=[cqnb[i2].k()])
                    op("dve", STT(ckf[:, 0:128], PS[bp][:, 256:384], rk, gkv_b, ALU.mult, ALU.mult), R=[psk(bp), kst_, gnb_.k()],
                       W=[ckfb[i2].k()])
                    op("pool", CP(kst[:, 0:128], ckf[:, 0:128]), R=[ckfb[i2].k()], W=[kk])
                    if latent:
                        op("dve", CP(ckf[:, 128:160], PS[bp][:, 384:416]), R=[psk(bp), kst_], W=[ckfb[i2].k("kr")])
                        kr3 = ckf[:, 128:160].rearrange("p (r x j) -> p r x j", r=2, x=2)
                        kd3 = kst[:, 192:224].rearrange("p (r x j) -> p r x j", r=2, x=2)
                        cs = MC["cos_mla"][:, j * 16:(j + 1) * 16].rearrange("p (r j) -> p r j", r=2)
                        sn = MC["sin_mla"][:, j * 16:(j + 1) * 16].rearrange("p (r j) -> p r j", r=2)
                        tmk = ms
                        rtb_ = ckfb[i2].k("rt")
                        tmp4 = P.f32(ropeb)[:, i2 * 64:(i2 + 1) * 64].rearrange("p (u r j) -> p u r j", u=4, r=2)
                        kro = ropeb.k(i2)
                        op("pool", TT(tmp4[:, 0], kr3[:, :, 0, :], cs, ALU.mult), R=[ckfb[i2].k("kr"), KMC], W=[kro])
                        op("pool", TT(tmp4[:, 1], kr3[:, :, 1, :], sn, ALU.mult), R=[ckfb[i2].k("kr"), KMC], W=[kro])
                        op("pool", TT(tmp4[:, 2], kr3[:, :, 1, :], cs, ALU.mult), R=[ckfb[i2].k("kr"), KMC], W=[kro])
                        op("pool", TT(tmp4[:, 3], kr3[:, :, 0, :], sn, ALU.mult), R=[ckfb[i2].k("kr"), KMC], W=[kro])
                        op("pool", TT(kd3[:, :, 0, :], tmp4[:, 0], tmp4[:, 1], ALU.subtract), R=[kro], W=[kk])
                        op("pool", TT(kd3[:, :, 1, :], tmp4[:, 2], tmp4[:, 3], ALU.add), R=[kro], W=[kk])
                    else:
                        op("dve", CP(ckf[:, 128:160], PS[bp][:, 384:416]), R=[psk(bp), kst_], W=[ckfb[i2].k("kr")])
                        op("pool", CP(kst[:, 192:224], ckf[:, 128:160]), R=[ckfb[i2].k("kr")], W=[kk])
                        rows = slice(j * 128, (j + 1) * 128)
                        op("sp", DMA(T["o_ckv"][pb, l, rows, :], ckf[:, 0:128]), R=[ckfb[i2].k()], dma=True)
                        op("sp", DMA(T["o_kr"][pb, l, rows, :], ckf[:, 128:160]), R=[ckfb[i2].k("kr")], dma=True)
                else:
                    rows = slice((j - nts) * 128, (j - nts + 1) * 128)
                    op("pool", DMA(kst[:, 0:128], T["ckv_ctx"][l, rows, :]), W=[kk], dma=True)
                    op("pool", DMA(kst[:, 192:224], T["kr_ctx"][l, rows, :]), W=[kk], dma=True)
                bT = P.bank()
                mms = []
                if j < nts:
                    mms += [(PS[bT][:, i * 128:(i + 1) * 128], P.bf(cqnb[i2])[:, i * 128:(i + 1) * 128], ident, True, True) for i in range(2)]
                mms += [(PS[bT][:, 256:384], kst[:, 0:128], ident, True, True),
                        (PS[bT][0:96, 384:512], kst[:, 128:224], ident, True, True)]
                op("pe", MM(mms), R=[cqnb[i2].k(), kk, KC], W=[psk(bT)])
                ev = "act" if j % 2 == 0 else "dve"
                cpy = (lambda o_, i_: ACT(o_, i_, AF.Copy)) if ev == "act" else CP
                if j < nts:
                    op(ev, cpy(cqT[:, :, j * 128:(j + 1) * 128], PS[bT][:, 0:256].rearrange("p (k n) -> p k n", k=2)), R=[psk(bT)], W=[cqTb.k(j)])
                op(ev, cpy(ckvT[:, kcols], PS[bT][:, 256:384]), R=[psk(bT)], W=[ckTb.k(j)])
                op(ev, cpy(KT[64:96, :, kcols], bc(PS[bT][64:96, 384:512], [[0, 4], [1, 128]])), R=[psk(bT)], W=[KTb.k(("r", j))])
            P.phase = "mla2"
            rawb = [P.alloc(f"qraw{i}", 384) for i in range(2)]
            qtkb = [P.alloc(f"qtok{i}", 192) for i in range(2)]
            for j in range(nts):
                i2 = j % 2
                jc = slice(j * 128, (j + 1) * 128)
                bq = P.bank()
                op("pe", MM([(PS[bq][:, 0:384], cqT[:, k, jc], wuq[:, k, :], k == 0, k == 1) for k in range(2)]),
                   R=[cqTb.k(j), wqb.k()], W=[psk(bq)])
                qtk = P.bf(qtkb[i2])
                kq_ = qtkb[i2].k()
                if latent:
                    raw = P.f32(rawb[i2])
                    op("act", ACT(raw, PS[bq][:, 0:384], AF.Copy), R=[psk(bq)], W=[rawb[i2].k()])
                    r4 = raw.rearrange("p (h n) -> p h n", h=4)
                    q4 = qtk.rearrange("p (h n) -> p h n", h=4)
                    op("dve", CP(q4[:, :, 0:64], r4[:, :, 0:64]), R=[rawb[i2].k()], W=[kq_])
                    def rv(base, off):
                        return bass.AP(base.tensor, base.offset + off, [list(base.ap[0]), [96, 4], [16, 2], [1, 8]])
                    x1, x2 = rv(raw, 64), rv(raw, 72)
                    o1_, o2_ = rv(qtk, 64), rv(qtk, 72)
                    cs = bc(MC["cos_mla"][:, j * 16:(j + 1) * 16], [[0, 4], [8, 2], [1, 8]])
                    sn = bc(MC["sin_mla"][:, j * 16:(j + 1) * 16], [[0, 4], [8, 2], [1, 8]])
                    tq = P.f32(ropeb)[:, 128 + i2 * 256:128 + (i2 + 1) * 256].rearrange("p (u h r j) -> p u h r j", u=4, h=4, r=2)
                    kro = ropeb.k(("q", i2))
                    op("pool", TT(tq[:, 0], x1, cs, ALU.mult), R=[rawb[i2].k(), KMC], W=[kro])
                    op("pool", TT(tq[:, 1], x2, sn, ALU.mult), R=[rawb[i2].k(), KMC], W=[kro])
                    op("pool", TT(tq[:, 2], x2, cs, ALU.mult), R=[rawb[i2].k(), KMC], W=[kro])
                    op("pool", TT(tq[:, 3], x1, sn, ALU.mult), R=[rawb[i2].k(), KMC], W=[kro])
                    op("pool", TT(o1_, tq[:, 0], tq[:, 1], ALU.subtract), R=[kro], W=[kq_])
                    op("pool", TT(o2_, tq[:, 2], tq[:, 3], ALU.add), R=[kro], W=[kq_])
                else:
                    op("act", ACT(qtk, PS[bq][:, 0:384], AF.Copy), R=[psk(bq)], W=[kq_])
                bT = P.bank()
                op("pe", MM([(PS[bT][0:96, h * 128:(h + 1) * 128], qtk[:, h * 96:(h + 1) * 96], ident, True, True) for h in range(4)]),
                   R=[kq_, KC], W=[psk(bT)])
                ev = "act" if j % 2 == 0 else "dve"
                cpy = (lambda o_, i_: ACT(o_, i_, AF.Copy)) if ev == "act" else CP
                op(ev, cpy(QT[0:96, :, jc], PS[bT][0:96, :].rearrange("p (h n) -> p h n", h=4)), R=[psk(bT)], W=[QTb.k(j)])
            P.free(rawb[0], rawb[1], qtkb[0], qtkb[1])
            P.phase = "mla3"
            wk4 = wukv.rearrange("p (h c n) -> p h c n", h=4, c=2)
            for c0 in range(0, Lk, 512):
                cw = min(512, Lk - c0)
                for h in range(4):
                    b = P.bank()
                    op("pe", MM([(PS[b][0:64, 0:cw], wk4[:, h, 0, :], ckvT[:, c0:c0 + cw], True, True)]),
                       R=[wqb.k()] + [ckTb.k(j_) for j_ in range(c0 // 128, (c0 + cw) // 128)], W=[psk(b)])
                    ev = "act" if h % 2 == 0 else "dve"
                    cpy = (lambda o_, i_: ACT(o_, i_, AF.Copy)) if ev == "act" else CP
                    op(ev, cpy(KT[0:64, h, c0:c0 + cw], PS[b][0:64, 0:cw]), R=[psk(b)], W=[KTb.k(("n", h, c0))])
            for kt in range(nkt):
                b = P.bank()
                op("pe", MM([(PS[b][:, 0:256], ckvT[:, kt * 128:(kt + 1) * 128], wk4[:, :, 1, :], True, True)]),
                   R=[wqb.k(), ckTb.k(kt)], W=[psk(b)])
                ev = "act" if kt % 2 == 0 else "dve"
                cpy = (lambda o_, i_: ACT(o_, i_, AF.Copy)) if ev == "act" else CP
                op(ev, cpy(Va[:, kt, :, 0:64], PS[b][:, 0:256].rearrange("p (h n) -> p h n", h=4)), R=[psk(b), Vb.k()], W=[Vb.k(kt)])
            P.phase = "mla_att"
            QC = min(512, L)
            nsub = QC // 128
            Eb = [P.alloc(f"E{i}", QC // 2) for i in range(3)]
            rcb = P.alloc("att_rec", 8)
            ei = 0
            allKT = [KTb.k(("r", j_)) for j_ in range(nkt)]
            for h in range(4):
                kth = allKT + [KTb.k(("n", h, c0)) for c0 in range(0, Lk, 512)]
                for qc in range(L // QC):
                    qcols = slice(qc * QC, (qc + 1) * QC)
                    bo = P.bank()
                    qdeps = [QTb.k(j_) for j_ in range(qc * nsub, (qc + 1) * nsub)]

                    def scores(kt):
                        bs = P.bank()
                        if bs == bo:
                            bs = P.bank()
                        op("pe", MM([(PS[bs][:, 0:QC], KT[0:96, h, kt * 128:(kt + 1) * 128], QT[0:96, h, qcols], True, True)]),
                           R=kth + qdeps, W=[psk(bs)])
                        return bs
                    pend = scores(0)
                    for kt in range(nkt):
                        bs = pend
                        if kt + 1 < nkt:
                            pend = scores(kt + 1)
                        E = P.bf(Eb[ei % 3])
                        ke = Eb[ei % 3].k()
                        ei += 1
                        op("act", ACT(E, PS[bs][:, 0:QC], AF.Exp, scale=scale), R=[psk(bs)], W=[ke])
                        op("pe", MM([(PS[bo][:, sub * 65:(sub + 1) * 65], E[:, sub * 128:(sub + 1) * 128], Va[:, kt, h, :],
                                      kt == 0 and sub == 0, kt == nkt - 1 and sub == nsub - 1)
                                     for sub in range(nsub)]), R=[ke, Vb.k(kt), Vb.k()], W=[psk(bo)])
                    o3 = PS[bo][:, 0:nsub * 65].rearrange("p (s n) -> p s n", s=nsub)
                    rec = P.f32(rcb)[:, (qc % 2) * 4:(qc % 2) * 4 + nsub]
                    op("dve", RECIP(rec, o3[:, :, 64]), R=[psk(bo)], W=[rcb.k(qc % 2)])
                    op("dve", TT(att[:, qc * nsub:(qc + 1) * nsub, h * 64:(h + 1) * 64], o3[:, :, 0:64], bc(rec, [[1, nsub], [0, 64]]), ALU.mult),
                       R=[psk(bo), rcb.k(qc % 2)], W=[atb.k((h, qc))])
            P.phase = "mla5"
            for j in range(nts):
                t = t0 + j
                bT = P.bank()
                op("pe", MM([(PS[bT][:, i * 128:(i + 1) * 128], att[:, j, i * 128:(i + 1) * 128], ident, True, True) for i in range(2)]),
                   R=[atb.k((h_, j // nsub)) for h_ in range(4)] + [KC], W=[psk(bT)])
                ev = "act" if j % 2 == 0 else "dve"
                cpy = (lambda o_, i_: ACT(o_, i_, AF.Copy)) if ev == "act" else CP
                op(ev, cpy(catT[:, 4:6, t * 128:(t + 1) * 128], PS[bT][:, 0:256].rearrange("p (q n) -> p q n", q=2)), R=[psk(bT)],
                   W=[catb.k(4), catb.k(5)])
            P.free(cqTb, ckTb, KTb, QTb, Vb, atb, kstb[0], kstb[1], cqnb[0], cqnb[1], ckfb[0], ckfb[1], Eb[0], Eb[1], Eb[2], rcb)
        P.free(wBb, wqb, gnb_, msb)


    def fourier_all(l, hT, KH, catT, catb):
        import os as _os
        P.phase = "four"
        wCb = P.alloc("wC", 8 * 256 // 2)
        wC = P.bf(wCb).rearrange("p (k n) -> p k n", k=8)
        op("pool", DMA(wC, T["w_in"][l][:, 1024:1280].rearrange("(k p) n -> p k n", p=128)), W=[wCb.k()], dma=True)
        _seqs = [SEQS[int(i_)] for i_ in _os.environ.get("DBG_SEQS", "0,1,2").split(",")]
        for (t0, nts, latent) in _seqs:
            L = nts * 128
            nj = nts // 2
            H = L // 2
            PB = min(512, H)
            base = t0 * 128
            ftb = P.alloc("f_tok", 2 * nj * 256 // 2)
            ft = P.bf(ftb).rearrange("p (a j n) -> p a j n", a=2, j=nj)
            for a in range(2):
                for j in range(nj):
                    b = P.bank()
                    c0 = base + a + 256 * j
                    op("pe", MM([(PS[b][:, 0:256], hT[:, k, c0:c0 + 255:2], wC[:, k, :], k == 0, k == 7) for k in range(8)]),
                       R=[KH, wCb.k()], W=[psk(b)])
                    ev = "act" if (a * nj + j) % 2 == 0 else "dve"
                    cpy = (lambda o_, i_: ACT(o_, i_, AF.Copy)) if ev == "act" else CP
                    op(ev, cpy(ft[:, a, j, :], PS[b][:, 0:256]), R=[psk(b)], W=[ftb.k((a, j))])
            tabC = T[f"f_cl{L}"].rearrange("p (a j n) -> p a j n", a=2, j=nj)
            tabS = T[f"f_sl{L}"].rearrange("p (a j n) -> p a j n", a=2, j=nj)
            mtb = [P.alloc(f"ftab{i}", 2 * nj * PB // 2) for i in range(2)]
            uvb = [P.alloc(f"fuv{i}", 4 * PB // 2) for i in range(2)]
            zob = P.alloc("fzo", 2 * PB)
            mi = 0
            for ph in range(H // PB):
                zbanks = {}
                for a in range(2):
                    mt = P.bf(mtb[mi % 2]).rearrange("p (q j n) -> p q j n", q=2, j=nj)
                    km = mtb[mi % 2].k()
                    op("sp", DMA(mt[:, 0], tabC[:, a, :, ph * PB:(ph + 1) * PB]), W=[km], dma=True)
                    op("sp", DMA(mt[:, 1], tabS[:, a, :, ph * PB:(ph + 1) * PB]), W=[km], dma=True)
                    uv = P.bf(uvb[mi % 2]).rearrange("p (c q n) -> p c q n", c=2, q=2)
                    ku = uvb[mi % 2].k()
                    mi += 1
                    for cc in range(2):
                        for q in range(2):
                            b = P.bank()
                            op("pe", MM([(PS[b][:, 0:PB], ft[:, a, j, cc * 128:(cc + 1) * 128], mt[:, q, j, :], j == 0, j == nj - 1)
                                         for j in range(nj)]), R=[ftb.k((a, j)) for j in range(nj)] + [km], W=[psk(b)])
                            ev = "act" if q == 0 else "dve"
                            cpy = (lambda o_, i_: ACT(o_, i_, AF.Copy)) if ev == "act" else CP
                            op(ev, cpy(uv[:, cc, q, :], PS[b][:, 0:PB]), R=[psk(b)], W=[uvb[(mi - 1) % 2].k((cc, q))])
                    for cc in range(2):
                        b = P.bank()
                        zbanks[(a, cc)] = b
                        kk_ = uvb[(mi - 1) % 2]
                        op("pe", MM([(PS[b][:, 0:PB], c64, uv[:, cc, 0, :], True, False), (PS[b][:, 0:PB], s64n, uv[:, cc, 1, :], False, True)]),
                           R=[kk_.k((cc, 0)), kk_.k((cc, 1)), KC], W=[psk(b)])
                zo = P.f32(zob).rearrange("p (c n) -> p c n", c=2)
                for cc in range(2):
                    be, bo_ = zbanks[(0, cc)], zbanks[(1, cc)]
                    op("act", ACT(zo[:, cc, :], PS[bo_][:, 0:PB], AF.Copy), R=[psk(bo_)], W=[zob.k(cc)])
                    p0 = base + ph * PB
                    op("dve", TT(catT[:, 2 + cc, p0:p0 + PB], PS[be][:, 0:PB], zo[:, cc, :], ALU.add), R=[psk(be), zob.k(cc)], W=[catb.k(2 + cc)])
                    op("dve", TT(catT[:, 2 + cc, p0 + H:p0 + H + PB], PS[be][:, 0:PB], zo[:, cc, :], ALU.subtract), R=[psk(be), zob.k(cc)],
                       W=[catb.k(2 + cc)])
            P.free(ftb, mtb[0], mtb[1], uvb[0], uvb[1], zob)
        P.free(wCb)


    def hyena_all(l, hT, KH, catT, catb, hTb):
        import os as _os
        TWO_PI = 2.0 * math.pi
        P.phase = "hy_proj"
        hpb = P.alloc("hy_par", 18 + 6 + 4 + 4 + 64 + 64 + 1024 + 8)
        hp = P.f32(hpb)
        KP = hpb.k()
        shw = hp[:, 0:18].rearrange("p (m k) -> p m k", m=6)
        shb = hp[:, 18:24]
        dsk = hp[:, 24:28].rearrange("p (o c) -> p o c", o=2)
        bcol = hp[:, 28:32]
        w1s = hp[0:33, 32:96]
        w2s = hp[0:64, 96:160]
        w3s = hp[0:64, 160:1184]
        for k in range(3):
            op("sp", DMAS(shw[:, :, k:k + 1], bass.AP(T["hy_short_w"].tensor, (l * 3 + k) * 768, [[1, 128], [128, 6], [1, 1]])), W=[KP], dma=True)
        op("sp", DMAS(shb.rearrange("p (m o) -> p m o", o=1), bass.AP(T["hy_short_b"].tensor, l * 768, [[1, 128], [128, 6], [1, 1]])), W=[KP], dma=True)
        op("sp", DMAS(hp[:, 24:28].rearrange("p (q o) -> p q o", o=1), bass.AP(T["hy_bias"].tensor, l * 512, [[1, 128], [128, 4], [1, 1]])), W=[KP], dma=True)
        op("sp", DMAS(bcol[0:64, 0:1], bass.AP(T["hy_b1"].tensor, l * 64, [[1, 64], [1, 1]])), W=[KP], dma=True)
        op("sp", DMAS(bcol[0:64, 1:2], bass.AP(T["hy_b2"].tensor, l * 64, [[1, 64], [1, 1]])), W=[KP], dma=True)
        op("sp", DMA(w1s, T["hy_w1"][l]), W=[KP], dma=True)
        op("sp", DMA(w2s, T["hy_w2"][l]), W=[KP], dma=True)
        op("sp", DMA(w3s, T["hy_w3"][l]), W=[KP], dma=True)
        bx = hp[:, 1184:1192]
        op("dve", TS(bx[0:64, 0:2], bcol[0:64, 0:2], 0.5, ALU.mult), R=[KP], W=[KP])
        op("dve", TS(bx[0:64, 2:4], bcol[0:64, 0:2], 0.25, ALU.mult), R=[KP], W=[KP])
        wDb = P.alloc("wD", 8 * 768 // 2)
        wD = P.bf(wDb).rearrange("p (k n) -> p k n", k=8)
        op("pool", DMA(wD, T["w_in"][l][:, 1696:2464].rearrange("(k p) n -> p k n", p=128)), W=[wDb.k()], dma=True)
        ucb = P.alloc("ucT", 6 * 2560 // 2)
        ucT = P.bf(ucb).rearrange("p (m n) -> p m n", m=6)
        upb = [P.alloc(f"upad{i}", 2050) for i in range(2)]
        ctb = P.alloc("convtmp", 2048)
        ui = 0
        for (t0, nts, latent) in SEQS:
            L = nts * 128
            base = t0 * 128
            for m in range(6):
                ub = upb[ui % 2]
                up = P.f32(ub)
                ku = ub.k()
                ui += 1
                op("pool", MSET(up[:, 0:1], 0.0), W=[ku])
                op("pool", MSET(up[:, L + 1:L + 2], 0.0), W=[ku])
                for c0 in range(0, L, 512):
                    cw = min(512, L - c0)
                    b = P.bank()
                    op("pe", MM([(PS[b][:, 0:cw], wD[:, k, m * 128:(m + 1) * 128], hT[:, k, base + c0:base + c0 + cw], k == 0, k == 7)
                                 for k in range(8)]), R=[KH, wDb.k()], W=[psk(b)])
                    op("act", ACT(up[:, 1 + c0:1 + c0 + cw], PS[b][:, 0:cw], AF.Copy), R=[psk(b)], W=[ku])
                ct = P.f32(ctb)[:, 0:L]
                eng = "dve"
                op(eng, TS(ct, up[:, 0:L], shw[:, m, 0:1], ALU.mult, shb[:, m:m + 1], ALU.add), R=[ku, KP], W=[ctb.k()])
                op(eng, STT(ct, up[:, 1:L + 1], shw[:, m, 1:2], ct, ALU.mult, ALU.add), R=[ku, KP, ctb.k()], W=[ctb.k()])
                op(eng, STT(ucT[:, m, base:base + L], up[:, 2:L + 2], shw[:, m, 2:3], ct, ALU.mult, ALU.add), R=[ku, KP, ctb.k()],
                   W=[ucb.k((m, t0))])
        P.free(hTb, wDb, upb[0], upb[1], ctb)
        dump(f"ucT{l}", ucT, [ucb.k((m, t0)) for m in range(6) for (t0, _, _) in SEQS])

        P.phase = "hy_filt"
        for L in (2048, 256):
            seqs = [sq for sq in SEQS if sq[1] * 128 == L]
            nj = L // 256
            ntt = 2 * nj
            NF = L // 256
            HB = L // 2
            TB = min(512, HB)
            zb = P.alloc("hy_z", L)
            h1b = P.alloc("hy_h1", L)
            zT = P.f32(zb)[0:33, :]
            h1T = P.f32(h1b)[0:64, :]
            op("sp", DMA(zT, T[f"h_zT{L}"]), W=[zb.k()], dma=True)
            h2b = P.alloc("hy_h2", L)
            h2T = P.f32(h2b)[0:64, :]
            sinb = P.alloc("hy_sint", 1024)
            for (src_, dst_, w_, kk_, bi_, kr_, kw_) in ((zT, h1T, w1s, 33, 0, zb.k(), h1b.k()), (h1T, h2T, w2s, 64, 1, h1b.k(), h2b.k())):
                for c0 in range(0, L, 512):
                    cw = min(512, L - c0)
                    b = P.bank()
                    op("pe", MM([(PS[b][0:64, 0:cw], w_, src_[:, c0:c0 + cw], True, True)]), R=[KP, kr_], W=[psk(b)])
                    s2 = P.f32(sinb)[0:64, 0:cw]
                    s4 = P.f32(sinb)[0:64, 512:512 + cw]
                    op("act", ACT(s2, PS[b][0:64, 0:cw], AF.Sin, bias=bx[0:64, bi_:bi_ + 1], scale=0.5), R=[psk(b), KP], W=[sinb.k(0)])
                    op("act", ACT(s4, PS[b][0:64, 0:cw], AF.Sin, bias=bx[0:64, 2 + bi_:3 + bi_], scale=0.25), R=[psk(b), KP], W=[sinb.k(1)])
                    op("dve", TT(s4, s4, s4, ALU.mult), R=[sinb.k(1)], W=[sinb.k(1)])
                    op("dve", TS(s4, s4, -2.0, ALU.mult, 1.0, ALU.add), R=[sinb.k(1)], W=[sinb.k(1)])
                    op("dve", STT(dst_[:, c0:c0 + cw], s2, 2.0, s4, ALU.mult, ALU.mult), R=[sinb.k(0), sinb.k(1)], W=[kw_])
            P.free(zb, h1b, sinb)
            for o in range(2):
                winb = P.alloc("hy_win", ntt * 256)
                win = P.f32(winb).rearrange("p (q n) -> p q n", q=ntt)
                op("sp", DMA(P.f32(winb), T[f"h_win{L}"]), W=[winb.k()], dma=True)
                P.phase = "hy_filt"
                abb = P.alloc("hy_ab", 2 * ntt * 256 // 2)
                ab = P.bf(abb).rearrange("p (s q n) -> p s q n", s=2, q=ntt)
                accb = P.alloc("hy_acc", 256)
                acc = P.f32(accb)
                op("pool", MSET(acc, 0.0), W=[accb.k()])
                g2b = [P.alloc(f"hy_g2{i}", 512) for i in range(2)]
                abs_b = [P.alloc(f"hy_abs{i}", 512) for i in range(2)]
                for q in range(ntt):
                    b = P.bank()
                    op("pe", MM([(PS[b][:, :], h2T[:, q * 128:(q + 1) * 128], w3s[:, o * 512:(o + 1) * 512], True, True)]),
                       R=[h2b.k(), KP], W=[psk(b)])
                    g2 = P.f32(g2b[q % 2]).rearrange("p (d n) -> p d n", d=2)
                    kg = g2b[q % 2].k()
                    op("dve", TT(g2, PS[b][:, :].rearrange("p (d n) -> p d n", d=2), bc(win[:, q, :], [[0, 2], [1, 256]]), ALU.mult),
                       R=[psk(b), winb.k()], W=[kg])
                    if q == 0:
                        op("pool", MSET(g2[0:1, 1, :], 0.0), R=[kg], W=[kg])
                    op("pool", TT(ab[:, 0, q, :], g2[:, 0, :], g2[:, 1, :], ALU.add), R=[kg], W=[abb.k((0, q))])
                    op("pool", TT(ab[:, 1, q, :], g2[:, 0, :], g2[:, 1, :], ALU.subtract), R=[kg], W=[abb.k((1, q))])
                    av = P.f32(abs_b[q % 2])
                    op("act", ACT(av, P.f32(g2b[q % 2]), AF.Abs), R=[kg], W=[abs_b[q % 2].k()])
                    op("dve", TT(acc, acc, av[:, 0:256], ALU.add), R=[abs_b[q % 2].k(), accb.k()], W=[accb.k()])
                    op("dve", TT(acc, acc, av[:, 256:512], ALU.add), R=[abs_b[q % 2].k(), accb.k()], W=[accb.k()])
                P.free(g2b[0], g2b[1], abs_b[0], abs_b[1], winb)
                rnb = P.alloc("hy_rn", 256 + 256)
                rnrow = P.f32(rnb)[0:1, 0:256]
                RN = P.f32(rnb)[:, 256:512]
                b = P.bank()
                op("pe", MM([(PS[b][0:1, 0:256], onesf[:, 0:1], acc, True, True)]), R=[accb.k(), KC], W=[psk(b)])
                op("dve", TS(rnrow, PS[b][0:1, 0:256], EPS, ALU.add), R=[psk(b)], W=[rnb.k("row")])
                op("dve", RECIP(rnrow, rnrow), R=[rnb.k("row")], W=[rnb.k("row")])
                b = P.bank()
                op("pe", MM([(PS[b][:, 0:256], onesf[0:1, 0:128], rnrow, True, True)]), R=[rnb.k("row"), KC], W=[psk(b)])
                op("act", ACT(RN, PS[b][:, 0:256], AF.Copy), R=[psk(b)], W=[rnb.k()])
                P.free(accb)
                hyt[0] = P.alloc("hy_ftmp", 1024)
                Gb = P.alloc("hy_G", NF * 4 * 256 // 2)
                G = P.bf(Gb).rearrange("p (f q n) -> p f q n", f=NF, q=4)
                ftb_ = [P.alloc(f"hy_ft{i}", 2 * ntt * 128 // 2) for i in range(2)]
                osb = P.alloc("hy_osb", 512)
                for fc in range(NF):
                    tb_ = ftb_[fc % 2]
                    tf = P.bf(tb_).rearrange("p (s q n) -> p s q n", s=2, q=ntt)
                    op("sp", DMA(tf[:, 0], T[f"h_cf{L}"][fc].rearrange("p (q n) -> p q n", q=ntt)), W=[tb_.k()], dma=True)
                    op("sp", DMA(tf[:, 1], T[f"h_sf{L}"][fc].rearrange("p (q n) -> p q n", q=ntt)), W=[tb_.k()], dma=True)
                    bE, bO = P.bank(), P.bank()
                    for (bk_, a_) in ((bE, 0), (bO, 1)):
                        mm = []
                        for j in range(nj):
                            q = a_ * nj + j
                            mm.append((PS[bk_][:, 0:256], tf[:, 0, q, :], ab[:, 0, q, :], j == 0, False))
                            mm.append((PS[bk_][:, 256:512], tf[:, 1, q, :], ab[:, 1, q, :], False, j == nj - 1))
                        op("pe", MM(mm), R=[tb_.k()] + [abb.k((s_, a_ * nj + j)) for s_ in range(2) for j in range(nj)], W=[psk(bk_)])
                    osv = P.f32(osb)
                    op("act", ACT(osv, PS[bO][:, :], AF.Copy), R=[psk(bO)], W=[osb.k()])
                    RN2 = bc(RN, [[0, 2], [1, 256]])
                    tsum = P.f32(osb)
                    tmpb_ = hyt[0]
                    tsv = P.f32(tmpb_)[:, 0:512]
                    tdv = P.f32(tmpb_)[:, 512:1024]
                    op("dve", TT(tsv, PS[bE][:, :], osv, ALU.add), R=[psk(bE), osb.k()], W=[tmpb_.k(0)])
                    op("dve", TT(tdv, PS[bE][:, :], osv, ALU.subtract), R=[psk(bE), osb.k()], W=[tmpb_.k(1)])
                    op("pool", TT(G[:, fc, 0:2, :], tsv.rearrange("p (q n) -> p q n", q=2), RN2, ALU.mult), R=[tmpb_.k(0), rnb.k()], W=[Gb.k(fc)])
                    op("pool", TT(G[:, fc, 2:4, :], tdv.rearrange("p (q n) -> p q n", q=2), RN2, ALU.mult), R=[tmpb_.k(1), rnb.k()], W=[Gb.k(fc)])
                P.free(abb, rnb, ftb_[0], ftb_[1], osb, hyt[0])
                dump(f"G{l}_{L}_{o}", G, [Gb.k(fc) for fc in range(NF)])
                for (t0, nts, latent) in seqs:
                    hyena_conv(l, o, L, t0, G, Gb, ucT, ucb, catT, catb, dsk, KP)
                P.free(Gb)
            P.free(h2b)
        P.free(hpb, ucb)
        for k_ in list(Z1.keys()):
            P.free(Z1.pop(k_)[0])

    hyt = [None]
    Z1 = {}

    def hyena_conv(l, o, L, t0, G, Gb, ucT, ucb, catT, catb, dsk, KP):
        P.phase = "hy_conv"
        nj = L // 256
        ntt = 2 * nj
        NF = L // 256
        HB = L // 2
        TB = min(512, HB)
        base = t0 * 128
        if o == 0:
            zb_ = P.alloc("hy_z1T", 2 * L // 2)
            Z1[t0] = (zb_, P.bf(zb_).rearrange("p (c n) -> p c n", c=2))
            vin = ucT[:, 0:2, base:base + L]
            kvin = [ucb.k((m, t0)) for m in (0, 1)]
            gate = ucT[:, 2:4, base:base + L]
            kgate = [ucb.k((m, t0)) for m in (2, 3)]
            outv = Z1[t0][1]
            kout = [Z1[t0][0].k(0), Z1[t0][0].k(1)]
        else:
            vin = Z1[t0][1]
            kvin = [Z1[t0][0].k(0), Z1[t0][0].k(1)]
            gate = ucT[:, 4:6, base:base + L]
            kgate = [ucb.k((m, t0)) for m in (4, 5)]
            outv = catT[:, 6:8, base:base + L]
            kout = [catb.k(6), catb.k(7)]
        vtb = P.alloc("hy_vtok", ntt * 256 // 2)
        vt = P.bf(vtb).rearrange("p (q n) -> p q n", q=ntt)
        for q in range(0, ntt, 2):
            b = P.bank()
            mm = []
            for qq in (q, q + 1):
                a_, j = qq // nj, qq % nj
                for cc in range(2):
                    mm.append((PS[b][:, (qq - q) * 256 + cc * 128:(qq - q) * 256 + (cc + 1) * 128],
                               vin[:, cc, a_ + 256 * j:a_ + 256 * j + 255:2], ident, True, True))
            op("pe", MM(mm), R=kvin + [KC], W=[psk(b)])
            ev = "act" if (q // 2) % 2 == 0 else "dve"
            cpy = (lambda o_, i_: ACT(o_, i_, AF.Copy)) if ev == "act" else CP
            op(ev, cpy(vt[:, q:q + 2, :], PS[b][:, :].rearrange("p (q n) -> p q n", q=2)), R=[psk(b)], W=[vtb.k(q // 2)])
        P.phase = "hy_fwd"
        pqb = P.alloc("hy_PQ", NF * 4 * 256 // 2)
        PQ = P.bf(pqb).rearrange("p (f q n) -> p f q n", f=NF, q=4)
        ftb_ = [P.alloc(f"hy_ft{i}", 2 * ntt * 128 // 2) for i in range(2)]
        osb = P.alloc("hy_osb", 512)
        tmb = P.alloc("hy_pw", 512 * 8)
        tm = P.f32(tmb)
        SSv, DDv, T1s, T2s, Av, Bv, T1d, T2d = (tm[:, i * 512:(i + 1) * 512] for i in range(8))
        allvt = [vtb.k(i) for i in range(nj)]
        for fc in range(NF):
            tb_ = ftb_[fc % 2]
            tf = P.bf(tb_).rearrange("p (s q n) -> p s q n", s=2, q=ntt)
            op("sp", DMA(tf[:, 0], T[f"h_cf{L}"][fc].rearrange("p (q n) -> p q n", q=ntt)), W=[tb_.k()], dma=True)
            op("sp", DMA(tf[:, 1], T[f"h_sf{L}"][fc].rearrange("p (q n) -> p q n", q=ntt)), W=[tb_.k()], dma=True)
            bE, bO = P.bank(), P.bank()
            for (bk_, a_) in ((bE, 0), (bO, 1)):
                mm = []
                for j in range(nj):
                    q = a_ * nj + j
                    mm.append((PS[bk_][:, 0:256], tf[:, 0, q, :], vt[:, q, :], j == 0, False))
                    mm.append((PS[bk_][:, 256:512], tf[:, 1, q, :], vt[:, q, :], False, j == nj - 1))
                op("pe", MM(mm), R=[tb_.k()] + allvt, W=[psk(bk_)])
            osv = P.f32(osb)
            op("act", ACT(osv, PS[bO][:, :], AF.Copy), R=[psk(bO)], W=[osb.k()])
            op("dve", TT(SSv, PS[bE][:, :], osv, ALU.add), R=[psk(bE), osb.k()], W=[tmb.k("S")])
            op("dve", TT(DDv, PS[bE][:, :], osv, ALU.subtract), R=[psk(bE), osb.k()], W=[tmb.k("D")])
            e1, e2 = ("dve", "pool") if fc % 2 == 0 else ("pool", "dve")
            for (X, gr, gi, dst, kx, e_, T1, T2) in ((SSv, 0, 1, Av, "S", e1, T1s, T2s), (DDv, 2, 3, Bv, "D", e2, T1d, T2d)):
                X2 = X.rearrange("p (q n) -> p q n", q=2)
                op(e_, TT(T1.rearrange("p (q n) -> p q n", q=2), X2, bc(G[:, fc, gr, :], [[0, 2], [1, 256]]), ALU.mult),
                   R=[tmb.k(kx), Gb.k(fc)], W=[tmb.k("T1" + kx)])
                op(e_, TT(T2.rearrange("p (q n) -> p q n", q=2), X2, bc(G[:, fc, gi, :], [[0, 2], [1, 256]]), ALU.mult),
                   R=[tmb.k(kx), Gb.k(fc)], W=[tmb.k("T2" + kx)])
                op(e_, TT(dst[:, 0:256], T1[:, 0:256], T2[:, 256:512], ALU.subtract), R=[tmb.k("T1" + kx), tmb.k("T2" + kx)], W=[tmb.k("A" + kx)])
                op(e_, TT(dst[:, 256:512], T2[:, 0:256], T1[:, 256:512], ALU.add), R=[tmb.k("T1" + kx), tmb.k("T2" + kx)], W=[tmb.k("A" + kx)])
            op("dve", TT(PQ[:, fc, 0:2, :], Av.rearrange("p (q n) -> p q n", q=2), Bv.rearrange("p (q n) -> p q n", q=2), ALU.add),
               R=[tmb.k("AS"), tmb.k("AD")], W=[pqb.k(fc)])
            op("pool", TT(PQ[:, fc, 2:4, :], Av.rearrange("p (q n) -> p q n", q=2), Bv.rearrange("p (q n) -> p q n", q=2), ALU.subtract),
               R=[tmb.k("AS"), tmb.k("AD")], W=[pqb.k(fc)])
        P.free(vtb, ftb_[0], ftb_[1], osb, tmb)
        P.phase = "hy_inv"
        itb = [P.alloc(f"hy_it{i}", 2 * NF * TB // 2) for i in range(2)]
        ytb = [P.alloc(f"hy_yt{i}", TB) for i in range(2)]
        CI4 = T[f"h_ci{L}"].rearrange("p (f a n) -> p f a n", f=NF, a=2)
        SI4 = T[f"h_si{L}"].rearrange("p (f a n) -> p f a n", f=NF, a=2)
        ii = 0
        allpq = [pqb.k(fc) for fc in range(NF)]
        for a_ in range(2):
            for tch in range(HB // TB):
                ib = itb[ii % 2]
                it = P.bf(ib).rearrange("p (s f n) -> p s f n", s=2, f=NF)
                op("sp", DMA(it[:, 0], CI4[:, :, a_, tch * TB:(tch + 1) * TB]), W=[ib.k()], dma=True)
                op("sp", DMA(it[:, 1], SI4[:, :, a_, tch * TB:(tch + 1) * TB]), W=[ib.k()], dma=True)
                ii += 1
                for cc in range(2):
                    b = P.bank()
                    mm = []
                    for fc in range(NF):
                        mm.append((PS[b][:, 0:TB], PQ[:, fc, 2 * a_, cc * 128:(cc + 1) * 128], it[:, 0, fc, :], fc == 0, False))
                        mm.append((PS[b][:, 0:TB], PQ[:, fc, 2 * a_ + 1, cc * 128:(cc + 1) * 128], it[:, 1, fc, :], False, fc == NF - 1))
                    op("pe", MM(mm), R=allpq + [ib.k()], W=[psk(b)])
                    tstart = a_ + 2 * tch * TB
                    sl = slice(tstart, tstart + 2 * TB - 1, 2)
                    yt = P.f32(ytb[cc])[:, 0:TB]
                    op("dve", STT(yt, vin[:, cc, sl], dsk[:, o, cc:cc + 1], PS[b][:, 0:TB], ALU.mult, ALU.add),
                       R=[psk(b), kvin[cc], KP], W=[ytb[cc].k()])
                    op("pool", TT(outv[:, cc, sl], yt, gate[:, cc, sl], ALU.mult), R=[ytb[cc].k(), kgate[cc]], W=[kout[cc]])
        P.free(pqb, itb[0], itb[1], ytb[0], ytb[1])


    return dict(nc=nc, P=P, T=T, DBG=DBG, layer_mod=layer_mod, ffn=ffn, mixer=mixer, loc=locals())


def build_full(dbg=()):
    B = build(dbg=dbg)
    T = B["T"]
    for l in range(2):
        B["P"].layer = l
        B["layer_mod"](l)
        B["mixer"](l, T["xin"] if l == 0 else T["xb"], T["xa"])
        B["ffn"](l, T["xa"], T["xb"] if l == 0 else T["y"])
    B["P"].emit()
    return B


_NC_CACHE = {}


def core_inputs(core, inp, consts):
    m = {nm: np.ascontiguousarray(inp[nm], dtype=np.float32) for nm, _ in WEIGHT_SPECS}
    m.update(consts)
    m["xin"] = np.ascontiguousarray(np.concatenate(
        [inp["x_sample"][core], inp["x_prompt"][2 * core], inp["x_prompt"][2 * core + 1]], 0), dtype=np.float32)
    m["ckv_ctx"] = np.ascontiguousarray(inp["cache_ckv"][core], dtype=np.float32)
    m["kr_ctx"] = np.ascontiguousarray(inp["cache_krope"][core], dtype=np.float32)
    m["s0"] = np.ascontiguousarray(inp["state_ret"][core], dtype=np.float32)
    m["cvec"] = np.ascontiguousarray(np.stack([inp["c_ctx"], inp["c"][core]]), dtype=np.float32)
    return m


def kernel(**inputs):
    inp = {k: np.asarray(v) for k, v in inputs.items()}
    consts = get_consts()
    if "nc" not in _NC_CACHE:
        _NC_CACHE["nc"] = build_full()["nc"]
    nc = _NC_CACHE["nc"]
    in_maps = [core_inputs(c, inp, consts) for c in range(8)]
    res = run_bass_kernel_spmd(nc, in_maps, core_ids=list(range(8)))
    R = res.results
    y_prompt = np.zeros((16, 256, 1024), np.float32)
    y_sample = np.zeros((8, 2048, 1024), np.float32)
    new_ckv = np.zeros((16, 2, 256, 128), np.float32)
    new_kr = np.zeros((16, 2, 256, 32), np.float32)
    new_st = np.zeros((16, 2, 2, 4, 64, 64), np.float32)
    for c in range(8):
        y = R[c]["y"]
        y_sample[c] = y[0:2048]
        y_prompt[2 * c] = y[2048:2304]
        y_prompt[2 * c + 1] = y[2304:2560]
        new_ckv[2 * c:2 * c + 2] = R[c]["o_ckv"]
        new_kr[2 * c:2 * c + 2] = R[c]["o_kr"]
        new_st[2 * c:2 * c + 2] = R[c]["o_st"]
    return (y_prompt, y_sample, new_ckv, new_kr, new_st)
```

```python
import contextlib
import math
import numpy as np
import ml_dtypes
import concourse.bass as bass
import concourse.mybir as mybir
from concourse.bass_utils import run_bass_kernel_spmd

F32 = mybir.dt.float32
BF16 = mybir.dt.bfloat16
ALU = mybir.AluOpType
AF = mybir.ActivationFunctionType
AX = mybir.AxisListType
NPBF = ml_dtypes.bfloat16

NDMA_SLOTS = 8
import os as _osmod
PROFILE_SCOPES = bool(_osmod.environ.get("KPROFILE"))
SCHEDULE = _osmod.environ.get("KSCHED", "1") == "1"
SCHED_WINDOW = int(_osmod.environ.get("KWIN", "600"))
D = 1024
DFF = 2816
NFF = 22
NT = 20
EPS = 1e-6
SEQS = [(0, 16, True), (16, 2, False), (18, 2, False)]


class Buf:
    def __init__(self, name, col0, ncols):
        self.name, self.col0, self.ncols = name, col0, ncols
        self.subs = set()
        self.inherit = set()

    def k(self, sub=None):
        self.subs.add(sub)
        return (self, sub)


class Prog:
    def __init__(self, nc):
        self.nc = nc
        self.ops = []
        self.last_writer = {}
        self.readers = {}
        self.stack = contextlib.ExitStack()
        self.live = []
        self.dead = []
        self.psn = 0

    def make_arena(self, ncols):
        self.arena = self.stack.enter_context(self.nc.sbuf_tensor("arena", [128, ncols], F32))
        self.arena_cols = ncols
        self.ps = [self.stack.enter_context(self.nc.psum_tensor(f"ps{i}", [128, 512], F32)) for i in range(8)]

    def bank(self):
        i = self.psn % 8
        self.psn += 1
        return i

    def alloc(self, name, ncols):
        ncols = int(math.ceil(ncols))
        segs = sorted((b.col0, b.ncols) for b in self.live)
        pos, found = 0, None
        for c0, n in segs:
            if c0 - pos >= ncols:
                found = pos
                break
            pos = max(pos, c0 + n)
        if found is None:
            if self.arena_cols - pos >= ncols:
                found = pos
            else:
                raise RuntimeError(f"arena OOM {name} {ncols}: live={[(b.name, b.ncols) for b in self.live]}")
        b = Buf(name, found, ncols)
        self.live.append(b)
        for ob in self.dead:
            if ob.col0 < found + ncols and found < ob.col0 + ob.ncols:
                for sk in ob.subs:
                    kk = (ob, sk)
                    w = self.last_writer.get(kk)
                    if w is not None:
                        b.inherit.add(w)
                    b.inherit.update(self.readers.get(kk, ()))
                b.inherit.update(ob.inherit)
        return b

    def free(self, *bs):
        for b in bs:
            self.live.remove(b)
            self.dead.append(b)

    def f32(self, b, p0=0, p1=128):
        return self.arena[p0:p1, b.col0:b.col0 + b.ncols]

    def bf(self, b, p0=0, p1=128):
        return self.arena[p0:p1, b.col0:b.col0 + b.ncols].bitcast(BF16)

    def op(self, eng, fn, R=(), W=(), dma=False):
        idx = len(self.ops)
        deps = set()
        for k in list(R) + list(W):
            if isinstance(k, tuple) and isinstance(k[0], Buf) and k[0].inherit:
                deps.update(k[0].inherit)
        for k in R:
            w = self.last_writer.get(k)
            if w is not None:
                deps.add(w)
        for k in W:
            w = self.last_writer.get(k)
            if w is not None:
                deps.add(w)
            deps.update(self.readers.get(k, ()))
        for k in R:
            self.readers.setdefault(k, []).append(idx)
        for k in W:
            self.last_writer[k] = idx
            self.readers[k] = []
        self.ops.append(dict(eng=eng, fn=fn, deps=deps, dma=dma, signal=False, cost=getattr(fn, "cost", 0.5), phase=getattr(self, "phase", "x") + str(getattr(self, "layer", ""))))
        return idx

    def schedule(self, engs):
        ops = self.ops
        n = len(ops)
        LAT = 1.2
        succ = [[] for _ in range(n)]
        indeg = [0] * n
        for i, o in enumerate(ops):
            for d in o["deps"]:
                succ[d].append(i)
            indeg[i] = len(o["deps"])
        rt = [0.0] * n
        fin = [0.0] * n
        avail = {e: [] for e in engs}
        for i, o in enumerate(ops):
            if indeg[i] == 0:
                avail[o["eng"]].append(i)
        clock = {e: 0.0 for e in engs}
        per = {e: [] for e in engs}
        done = 0
        WIN = SCHED_WINDOW
        while done < n:
            best = None
            for e in engs:
                av = avail[e]
                if not av:
                    continue
                c = clock[e]
                lo = min(av)
                cand = None
                for i in av:
                    if i - lo > WIN:
                        continue
                    if rt[i] <= c and (cand is None or i < cand):
                        cand = i
                if cand is None:
                    cand = min((i for i in av if i - lo <= WIN), key=lambda i: (rt[i], i))
                st = max(c, rt[cand])
                if best is None or (st, cand) < (best[0], best[1]):
                    best = (st, cand, e)
            st, i, e = best
            o = ops[i]
            avail[e].remove(i)
            per[e].append(i)
            if o["dma"]:
                clock[e] = st + 0.15
                fin[i] = st + o["cost"]
            else:
                fin[i] = st + o["cost"]
                clock[e] = fin[i]
            done += 1
            for s_ in succ[i]:
                indeg[s_] -= 1
                lat = LAT if (ops[s_]["eng"] != e or o["dma"]) else 0.25
                rt[s_] = max(rt[s_], fin[i] + lat)
                if indeg[s_] == 0:
                    avail[ops[s_]["eng"]].append(s_)
        self.est_us = max(fin)
        return per

    def emit(self):
        nc, ops = self.nc, self.ops
        engs = ["pe", "act", "dve", "pool", "sp"]
        per = self.schedule(engs) if SCHEDULE else None
        if per is None:
            per = {e: [] for e in engs}
            for i, o in enumerate(ops):
                per[o["eng"]].append(i)
        pos = {}
        for e in engs:
            for p_, i in enumerate(per[e]):
                pos[i] = p_
        for e in engs:
            seen = {pe_: -1 for pe_ in engs}
            for i in per[e]:
                o = ops[i]
                need = {}
                o["wdeps"] = []
                for d in o["deps"]:
                    od = ops[d]
                    if od["dma"]:
                        o["wdeps"].append(d)
                    else:
                        if od["eng"] not in need or pos[d] > pos[need[od["eng"]]]:
                            need[od["eng"]] = d
                for pe_, d in need.items():
                    if pos[d] > seen[pe_]:
                        seen[pe_] = pos[d]
                        o["wdeps"].append(d)
                        ops[d]["signal"] = True
        sems = {e: self.stack.enter_context(nc.semaphore("s_" + e)) for e in engs}
        dsems = {e: [self.stack.enter_context(nc.semaphore(f"d_{e}{i}")) for i in range(NDMA_SLOTS)]
                 for e in ("sp", "pool", "act")}
        cnt = {e: 0 for e in engs}
        dcnt = {e: 0 for e in dsems}
        for e in engs:
            for i in per[e]:
                o = ops[i]
                if o["dma"]:
                    j = dcnt[e]
                    dcnt[e] += 1
                    o["sem"] = dsems[e][j % NDMA_SLOTS]
                    o["val"] = 16 * (j // NDMA_SLOTS + 1)
                    o["prev"] = 16 * (j // NDMA_SLOTS)
                elif o["signal"]:
                    cnt[e] += 1
                    o["sem"] = sems[e]
                    o["val"] = cnt[e]
        nw = [0]

        def run_engine(ename, eobj):
            waited = {}
            for i in per[ename]:
                o = ops[i]
                wl = {}
                for d in o["wdeps"]:
                    od = ops[d]
                    s = od["sem"]
                    if od["val"] > wl.get(s.name, (s, 0))[1]:
                        wl[s.name] = (s, od["val"])
                if o["dma"] and o["prev"] > 0:
                    s = o["sem"]
                    if o["prev"] > wl.get(s.name, (s, 0))[1]:
                        wl[s.name] = (s, o["prev"])
                for nm, (s, v) in wl.items():
                    if waited.get(nm, 0) >= v:
                        continue
                    eobj.wait_ge(s, v)
                    nw[0] += 1
                    waited[nm] = v
                if PROFILE_SCOPES:
                    with nc.named_scope(o["phase"]):
                        ins = o["fn"](eobj)
                else:
                    ins = o["fn"](eobj)
                if o["dma"]:
                    ins.then_inc(o["sem"], 16)
                elif o["signal"]:
                    ins.then_inc(o["sem"], 1)
            if ename in dsems:
                n = dcnt[ename]
                for slot in range(NDMA_SLOTS):
                    k = (n - slot + NDMA_SLOTS - 1) // NDMA_SLOTS
                    if k > 0 and waited.get(dsems[ename][slot].name, 0) < 16 * k:
                        eobj.wait_ge(dsems[ename][slot], 16 * k)

        with nc.Block() as block:
            @block.tensor
            def _(e):
                run_engine("pe", e)

            @block.scalar
            def _(e):
                run_engine("act", e)

            @block.vector
            def _(e):
                run_engine("dve", e)

            @block.gpsimd
            def _(e):
                run_engine("pool", e)

            @block.sync
            def _(e):
                run_engine("sp", e)
        self.stats = dict(nops=len(ops), nwaits=nw[0], cnt=cnt, dcnt=dcnt, est_us=getattr(self, "est_us", None))


def _n(ap):
    n = 1
    for d in ap.shape[1:]:
        n *= d
    return n


def MM(specs):
    def f(e):
        ins = None
        for (o, l, r, st, sp) in specs:
            ins = e.matmul(o, lhsT=l, rhs=r, start=st, stop=sp)
        return ins
    f.cost = sum(max(_n(o), 64) / 2300.0 + 0.005 for (o, l, r, st, sp) in specs)
    return f


def ACT(out, in_, func, bias=None, scale=1.0, accum=None):
    def f(e):
        kw = {}
        if bias is not None:
            kw["bias"] = bias
        if accum is not None:
            kw["accum_out"] = accum
        return e.activation(out=out, in_=in_, func=func, scale=scale, **kw)
    f.cost = 0.19 + _n(in_) / 1200.0
    return f


def _v(fn, n, rate=960.0):
    fn.cost = 0.12 + n / rate
    return fn


def TT(out, a, b, op):
    return _v(lambda e: e.tensor_tensor(out=out, in0=a, in1=b, op=op), _n(out))


def TS(out, a, s1, op0, s2=None, op1=None):
    if op1 is None:
        return _v(lambda e: e.tensor_scalar(out=out, in0=a, scalar1=s1, scalar2=None, op0=op0), _n(out))
    return _v(lambda e: e.tensor_scalar(out=out, in0=a, scalar1=s1, scalar2=s2, op0=op0, op1=op1), _n(out))


def STT(out, a, s, b, op0, op1):
    return _v(lambda e: e.scalar_tensor_tensor(out=out, in0=a, scalar=s, in1=b, op0=op0, op1=op1), _n(out))


def CP(out, in_):
    return _v(lambda e: e.tensor_copy(out=out, in_=in_), _n(out))


def RED(out, in_, op=None):
    return _v(lambda e: e.tensor_reduce(out=out, in_=in_, axis=AX.X, op=op or ALU.add), _n(in_))


def RECIP(out, in_):
    return _v(lambda e: e.reciprocal(out=out, in_=in_), _n(out))


def MSET(out, v):
    return _v(lambda e: e.memset(out, v), _n(out), 2400.0)


def DMA(out, in_):
    f = lambda e: e.dma_start(out=out, in_=in_)
    f.cost = 2.0 + _n(out) * out.shape[0] * 4 / 200000.0
    return f


def DMAS(out, in_):
    f = lambda e: e.dma_start(out=out, in_=in_, allow_slow_non_contiguous=True)
    f.cost = 4.0
    return f


def bc(ap, dims):
    return bass.AP(ap.tensor, ap.offset, [list(ap.ap[0])] + [list(d) for d in dims])


def parity_perm(L):
    nj = L // 256
    idx = np.zeros((2, nj, 128), np.int64)
    for pi in range(2):
        for j in range(nj):
            idx[pi, j] = pi + 2 * (128 * j + np.arange(128))
    return idx


def host_consts():
    C = {}
    C["ident"] = np.eye(128, dtype=np.float32).astype(NPBF)
    tok = np.arange(2048)
    row, col = tok // 64, tok % 64
    for nm, half in (("ret", 16), ("mla", 8)):
        inv = 10000.0 ** (-np.arange(half, dtype=np.float64) / half)
        ang = np.stack([row[:, None] * inv[None], col[:, None] * inv[None]], axis=1)
        ang = ang.reshape(16, 128, 2, half).transpose(1, 0, 2, 3).reshape(128, 16 * 2 * half)
        C["cos_" + nm] = np.cos(ang).astype(np.float32)
        C["sin_" + nm] = np.sin(ang).astype(np.float32)
    m = np.arange(128)[:, None].astype(np.float64)
    c = np.arange(128)[None, :].astype(np.float64)
    C["ret_dpos"] = np.tile(np.maximum(c - m, 0), (1, 4)).astype(np.float32)
    C["ret_dneg"] = np.tile(np.maximum(m - c, 0), (1, 4)).astype(np.float32)
    C["ret_mge"] = np.tile((c >= m) * 0.125, (1, 4)).astype(np.float32)
    C["ret_mle"] = np.tile((c <= m) * 0.125, (1, 4)).astype(np.float32)
    p = np.arange(128, dtype=np.float64)
    C["ret_cols"] = np.stack([p + 1, 127 - p, 128 - p, p], axis=1).astype(np.float32)
    a = 2 * np.pi * np.outer(np.arange(64), np.arange(64)) / 64
    C64 = np.kron(np.eye(2), np.cos(a))
    S64 = np.kron(np.eye(2), np.sin(a))
    C["f_c64"] = C64.astype(NPBF)
    C["f_s64n"] = (-S64).astype(NPBF)
    for L in (2048, 256):
        idx = parity_perm(L)
        nj = L // 256
        l = idx.astype(np.float64)
        pp = np.arange(L // 2, dtype=np.float64)
        ang = 2 * np.pi * l[..., None] * pp / L
        sc = 1.0 / math.sqrt(L * 64)
        C[f"f_cl{L}"] = (np.cos(ang) * sc).transpose(2, 0, 1, 3).reshape(128, -1).astype(NPBF)
        C[f"f_sl{L}"] = (np.sin(ang) * sc).transpose(2, 0, 1, 3).reshape(128, -1).astype(NPBF)
        pos = idx.reshape(-1).astype(np.float64)
        t = pos / L
        bands = np.arange(1, 17, dtype=np.float64)
        ang2 = (2 * np.pi / L) * pos[:, None] * bands[None]
        z = np.concatenate([t[:, None], np.sin(ang2), np.cos(ang2)], axis=1)
        C[f"h_zT{L}"] = np.ascontiguousarray(z.T).astype(np.float32)
        deltas = np.abs(np.linspace(math.log(1e-2) / 1.5, math.log(1e-2) / 0.3, 256))
        win = np.exp(-t[:, None] * deltas[None])
        C[f"h_win{L}"] = win.reshape(2 * nj, 128, 256).transpose(1, 0, 2).reshape(128, -1).astype(np.float32)
        N = 2 * L
        nf = L // 256 if L >= 256 else 1
        F = L // 2
        f = np.arange(F, dtype=np.float64) + 0.5
        s = idx.astype(np.float64)
        psi = 2 * np.pi * s[..., None] * f / N
        fch = F // 128
        cf = np.cos(psi).reshape(2, nj, 128, fch, 128).transpose(3, 2, 0, 1, 4).reshape(fch, 128, -1)
        sf = (-np.sin(psi)).reshape(2, nj, 128, fch, 128).transpose(3, 2, 0, 1, 4).reshape(fch, 128, -1)
        C[f"h_cf{L}"] = cf.astype(NPBF)
        C[f"h_sf{L}"] = sf.astype(NPBF)
        tt = np.stack([2 * np.arange(L // 2), 2 * np.arange(L // 2) + 1]).astype(np.float64)
        psi2 = 2 * np.pi * f[:, None, None] * tt[None] / N
        ci = (2.0 / N) * np.cos(psi2)
        si = -(2.0 / N) * np.sin(psi2)
        C[f"h_ci{L}"] = ci.reshape(fch, 128, 2, L // 2).transpose(1, 0, 2, 3).reshape(128, -1).astype(NPBF)
        C[f"h_si{L}"] = si.reshape(fch, 128, 2, L // 2).transpose(1, 0, 2, 3).reshape(128, -1).astype(NPBF)
    return C


_CONSTS = None


def get_consts():
    global _CONSTS
    if _CONSTS is None:
        _CONSTS = host_consts()
    return _CONSTS


WEIGHT_SPECS = [
    ("w_ada", (2, 1024, 6144)), ("b_ada", (2, 6144)), ("norm_g", (2, 4, 1024)), ("w_in", (2, 1024, 2464)),
    ("w_out", (2, 1024, 1024)), ("ret_decay", (2, 2, 4)), ("mla_q_norm", (2, 256)), ("mla_kv_norm", (2, 128)),
    ("mla_w_uq", (2, 256, 384)), ("mla_w_ukv", (2, 128, 512)), ("hy_short_w", (2, 3, 768)),
    ("hy_short_b", (2, 768)), ("hy_w1", (2, 33, 64)), ("hy_b1", (2, 64)), ("hy_w2", (2, 64, 64)),
    ("hy_b2", (2, 64)), ("hy_w3", (2, 64, 1024)), ("hy_bias", (2, 2, 256)), ("w_gate", (2, 1024, 2816)),
    ("w_up", (2, 1024, 2816)), ("w_down", (2, 2816, 1024)),
]
CORE_SPECS = [("xin", (2560, 1024)), ("ckv_ctx", (2, 256, 128)), ("kr_ctx", (2, 256, 32)),
              ("s0", (2, 2, 4, 64, 64)), ("cvec", (2, 1024))]
OUT_SPECS = [("y", (2560, 1024)), ("o_ckv", (2, 2, 256, 128)), ("o_kr", (2, 2, 256, 32)),
             ("o_st", (2, 2, 2, 4, 64, 64))]


def build(dbg=(), stop_after=None):
    nc = bass.Bass("TRN2", target_bir_lowering=False)
    P = Prog(nc)
    C = get_consts()
    T = {}
    for nm, shp in WEIGHT_SPECS + CORE_SPECS:
        T[nm] = nc.dram_tensor(nm, list(shp), F32, kind="ExternalInput").ap()
    for nm, arr in C.items():
        T[nm] = nc.dram_tensor(nm, list(arr.shape), BF16 if arr.dtype == NPBF else F32, kind="ExternalInput").ap()
    for nm, shp in OUT_SPECS:
        T[nm] = nc.dram_tensor(nm, list(shp), F32, kind="ExternalOutput").ap()
    T["xa"] = nc.dram_tensor("xa", [2560, 1024], F32, kind="Internal").ap()
    T["xb"] = nc.dram_tensor("xb", [2560, 1024], F32, kind="Internal").ap()
    DBG = {}

    P.make_arena(53184)
    PS = P.ps
    op = P.op

    def psk(i):
        return ("ps", i)

    def dump(name, ap, keys, shape=None):
        if name not in dbg:
            return
        shape = list(shape or ap.shape)
        d = nc.dram_tensor("dbg_" + name, shape, F32, kind="ExternalOutput").ap()
        DBG[name] = d
        if len(shape) == 3:
            for i_ in range(shape[1]):
                op("pool", DMA(d[:, i_, :], ap[:, i_, :]), R=keys, dma=True)
        else:
            op("pool", DMA(d, ap), R=keys, dma=True)

    cb = P.alloc("consts", 64 + 1 + 8 + 64 * 3 + 128)
    cw = P.f32(cb)
    o = [0]

    def take(n, dt=F32, src=None):
        src = cw if src is None else src
        v = src[:, o[0]:o[0] + n]
        o[0] += n
        return v.bitcast(BF16) if dt == BF16 else v
    ident = take(64, BF16)
    epsc = take(1)
    cols8 = take(8)
    c64, s64n, onesb = take(64, BF16), take(64, BF16), take(64, BF16)
    onesf = take(128)
    KC = cb.k()
    for dst, nm in ((ident, "ident"), (c64, "f_c64"), (s64n, "f_s64n")):
        op("sp", DMA(dst, T[nm]), W=[KC], dma=True)
    op("pool", MSET(epsc, EPS), W=[KC])
    op("pool", MSET(cols8[:, 0:1], -math.pi), W=[KC])
    op("pool", MSET(onesb, 1.0), W=[KC])
    op("pool", MSET(onesf, 1.0), W=[KC])
    MC = {}

    def mixer_consts():
        mb = P.alloc("mconsts", 4 * 512 + 4 + 2 * 512 + 2 * 256)
        o[0] = 0
        mw = P.f32(mb)
        for nm, n in (("ret_dpos", 512), ("ret_dneg", 512), ("ret_mge", 512), ("ret_mle", 512), ("ret_cols", 4),
                      ("cos_ret", 512), ("sin_ret", 512), ("cos_mla", 256), ("sin_mla", 256)):
            MC[nm] = take(n, src=mw)
            op("sp", DMA(MC[nm], T[nm]), W=[mb.k()], dma=True)
        MC["buf"] = mb
        MC["key"] = mb.k()

    mcolb = P.alloc("modcols", 2 * 4 * 8)
    mcol = P.f32(mcolb).rearrange("p (r q k) -> p r q k", r=2, q=4)
    gbb = P.alloc("gbc", 4 * 1024)
    gbc = P.f32(gbb).rearrange("p (r q n) -> p r q n", r=2, q=2)
    PR = (0, 32)

    def layer_mod(l):
        P.phase = "mod"
        rb = P.alloc("rows", 6144 + 4096 + 6144)
        rows = P.f32(rb)[0:33, :]
        m = rows[:, 0:6144]
        ngr = rows[:, 6144:10240]
        rowt = rows[:, 10240:16384]
        KR = rb.k()
        scb = P.alloc("silu_c", 8 * 34 // 2)
        sct = P.bf(scb).rearrange("p (k r) -> p k r", r=34)
        cfb = P.alloc("c_f32", 16)
        cf32 = P.f32(cfb).rearrange("p (k r) -> p k r", r=2)
        for r in range(2):
            op("sp", DMAS(cf32[:, :, r:r + 1], bass.AP(T["cvec"].tensor, r * 1024, [[1, 128], [128, 8], [1, 1]])),
               W=[cfb.k()], dma=True)
        op("pool", MSET(sct, 0.0), W=[scb.k()])
        for r in range(2):
            op("act", ACT(sct[:, :, PR[r]:PR[r] + 1], cf32[:, :, r:r + 1], AF.Silu), R=[cfb.k()], W=[scb.k()])
        op("sp", DMA(ngr, bass.AP(T["norm_g"].tensor, l * 4096, [[0, 33], [1, 4096]])), W=[KR], dma=True)
        wab = [P.alloc(f"wada{i}", 8 * 512 // 2) for i in range(2)]
        badb = P.alloc("bada", 2 * 512)
        for nb in range(12):
            wb_ = wab[nb % 2]
            par = nb % 2
            wv = P.bf(wb_).rearrange("p (k n) -> p k n", k=8)
            op("pool", DMA(wv, T["w_ada"][l][:, nb * 512:(nb + 1) * 512].rearrange("(k p) n -> p k n", p=128)),
               W=[wb_.k()], dma=True)
            bv = P.f32(badb)[0:33, par * 512:par * 512 + 512]
            op("sp", DMA(bv, bass.AP(T["b_ada"].tensor, l * 6144 + nb * 512, [[0, 33], [1, 512]])),
               W=[badb.k(par)], dma=True)
            b = P.bank()
            op("pe", MM([(PS[b][0:33, :], sct[:, k, 0:33], wv[:, k, :], k == 0, k == 7) for k in range(8)]),
               R=[scb.k(), wb_.k()], W=[psk(b)])
            op("dve", TT(m[:, nb * 512:(nb + 1) * 512], PS[b][0:33, :], bv, ALU.add),
               R=[psk(b), badb.k(par)], W=[KR])
        for r in range(2):
            dump(f"mod{l}{r}", m[PR[r]:PR[r] + 1, :], [KR])
        op("dve", STT(rowt[:, 0:1024], m[:, 1024:2048], 1.0, ngr[:, 0:1024], ALU.add, ALU.mult), R=[KR], W=[KR])
        op("dve", CP(rowt[:, 1024:2048], m[:, 0:1024]), R=[KR], W=[KR])
        op("dve", STT(rowt[:, 2048:3072], m[:, 4096:5120], 1.0, ngr[:, 2048:3072], ALU.add, ALU.mult), R=[KR], W=[KR])
        op("dve", CP(rowt[:, 3072:4096], m[:, 3072:4096]), R=[KR], W=[KR])
        op("dve", TT(rowt[:, 4096:5120], m[:, 2048:3072], ngr[:, 1024:2048], ALU.mult), R=[KR], W=[KR])
        op("dve", TT(rowt[:, 5120:6144], m[:, 5120:6144], ngr[:, 3072:4096], ALU.mult), R=[KR], W=[KR])
        for r in range(2):
            pr = PR[r]
            b = P.bank()
            op("pe", MM([(PS[b][:, q * 8 + k:q * 8 + k + 1], rowt[pr:pr + 1, q * 1024 + k * 128:q * 1024 + (k + 1) * 128],
                          onesf[pr:pr + 1, 0:1], True, True) for q in range(4) for k in range(8)]),
               R=[KR, KC], W=[psk(b)])
            op("dve", CP(mcol[:, r], PS[b][:, 0:32].rearrange("p (q k) -> p q k", q=4)), R=[psk(b)], W=[mcolb.k()])
            for q in range(2):
                for hf in range(2):
                    b = P.bank()
                    c0 = (4 + q) * 1024 + hf * 512
                    op("pe", MM([(PS[b][:, :], onesf[pr:pr + 1, 0:128], rowt[pr:pr + 1, c0:c0 + 512], True, True)]),
                       R=[KR, KC], W=[psk(b)])
                    op("act", ACT(gbc[:, r, q, hf * 512:(hf + 1) * 512], PS[b][:, :], AF.Copy), R=[psk(b)], W=[gbb.k()])
        P.free(rb, scb, cfb, wab[0], wab[1], badb)

    def rstd_from_ss(ss, out, n, keys_r, keys_w, tmp):
        op("act", ACT(tmp, ss, AF.Sqrt, bias=epsc[0:ss.shape[0], :], scale=1.0 / n), R=keys_r + [KC], W=[keys_w[1]])
        op("dve", RECIP(out, tmp), R=[keys_w[1]], W=[keys_w[0]])

    NXT = 4
    xtb = [None] * NXT
    xnb = [P.alloc(f"xn{i}", 512) for i in range(2)]
    stb = P.alloc("stats", 64)
    stv = P.f32(stb)
    tcount = [0]

    def norm_transpose(src, tile, r, q0, dstT, dcol, xkeep=None):
        i = tcount[0] % len(xnb)
        tcount[0] += 1
        xt = P.f32(xtb[i]) if xkeep is None else xkeep[0]
        kx = xtb[i].k() if xkeep is None else xkeep[1]
        op("sp", DMA(xt, src[tile * 128:(tile + 1) * 128, :]), R=[("dram", src.tensor.name, tile)], W=[kx], dma=True)
        ss, rs, tm = stv[:, i * 3:i * 3 + 1], stv[:, i * 3 + 1:i * 3 + 2], stv[:, i * 3 + 2:i * 3 + 3]
        op("act", ACT(P.bf(xnb[i]), xt, AF.Square, accum=ss), R=[kx], W=[xnb[i].k(), stb.k(("ss", i))])
        rstd_from_ss(ss, rs, 1024.0, [stb.k(("ss", i))], [stb.k(("rs", i)), stb.k(("tm", i))], tm)
        xn = P.bf(xnb[i])
        op("dve", TS(xn, xt, rs, ALU.mult), R=[kx, stb.k(("rs", i))], W=[xnb[i].k()])
        for half in range(2):
            b = P.bank()
            op("pe", MM([(PS[b][:, j * 128:(j + 1) * 128], xn[:, (half * 4 + j) * 128:(half * 4 + j + 1) * 128], ident, True, True)
                         for j in range(4)]), R=[xnb[i].k(), KC], W=[psk(b)])
            for j in range(4):
                kc = half * 4 + j
                o_ = dstT[:, kc, dcol:dcol + 128]
                src_ps = PS[b][:, j * 128:(j + 1) * 128]
                A, B = mcol[:, r, q0, kc:kc + 1], mcol[:, r, q0 + 1, kc:kc + 1]
                if half == 0:
                    op("act", ACT(o_, src_ps, AF.Identity, bias=B, scale=A), R=[psk(b), mcolb.k()], W=[dstT_key[0]])
                else:
                    op("dve", TS(o_, src_ps, A, ALU.mult, B, ALU.add), R=[psk(b), mcolb.k()], W=[dstT_key[0]])

    dstT_key = [None]
    J2 = [None]
    ROPEB = [None]

    def resid_update(ps2, tile, r, q, xt, kx, dst, ri):
        junk2b = J2[0]
        ssa, ssb, ss, rs, tm = (stv[:, 16 + ri * 8 + j:16 + ri * 8 + j + 1] for j in range(5))
        kk = stb.k(("ru", ri))
        jo = ri * 1024 if junk2b.ncols >= 2048 else 0
        for hf, sx in ((0, ssa), (1, ssb)):
            op("act", ACT(P.f32(junk2b)[:, jo + hf * 512:jo + (hf + 1) * 512], PS[ps2[hf]][:, :], AF.Square, accum=sx),
               R=[psk(ps2[hf])], W=[junk2b.k((jo, hf)), kk])
        op("dve", TT(ss, ssa, ssb, ALU.add), R=[kk], W=[kk])
        op("act", ACT(tm, ss, AF.Sqrt, bias=epsc, scale=1.0 / 1024), R=[kk, KC], W=[kk])
        op("dve", RECIP(rs, tm), R=[kk], W=[kk])
        for hf in range(2):
            tmp = P.f32(junk2b)[:, jo + hf * 512:jo + (hf + 1) * 512]
            op("dve", STT(tmp, PS[ps2[hf]][:, :], rs, gbc[:, r, q, hf * 512:(hf + 1) * 512], ALU.mult, ALU.mult),
               R=[psk(ps2[hf]), kk, gbb.k()], W=[junk2b.k((jo, hf))])
            op("pool", TT(xt[:, hf * 512:(hf + 1) * 512], tmp, xt[:, hf * 512:(hf + 1) * 512], ALU.add),
               R=[junk2b.k((jo, hf)), kx], W=[kx])
        op("sp", DMA(dst[tile * 128:(tile + 1) * 128, :], xt), R=[kx], W=[("dram", dst.tensor.name, tile)], dma=True)

    junk2b = None

    def ffn(l, src, dst):
        P.phase = "ffn"
        wgb = P.alloc("wg", 8 * DFF // 2)
        wub = P.alloc("wu", 8 * DFF // 2)
        wdb = P.alloc("wd", NFF * 1024 // 2)
        wg = P.bf(wgb).rearrange("p (k n) -> p k n", k=8)
        wu = P.bf(wub).rearrange("p (k n) -> p k n", k=8)
        wd = P.bf(wdb).rearrange("p (k n) -> p k n", k=NFF)
        FB = 4
        for f0 in range(0, NFF, FB):
            f1 = min(NFF, f0 + FB)
            for (wv_, nm_, wb__) in ((wg, "w_gate", wgb), (wu, "w_up", wub)):
                op("pool", DMA(wv_[:, :, f0 * 128:f1 * 128], T[nm_][l][:, f0 * 128:f1 * 128].rearrange("(k p) n -> p k n", p=128)),
                   W=[wb__.k(f0 // FB)], dma=True)
        for k in range(0, NFF, 2):
            op("pool", DMA(wd[:, k:k + 2, :], T["w_down"][l][k * 128:(k + 2) * 128, :].rearrange("(k p) n -> p k n", p=128)),
               W=[wdb.k(k // 2)], dma=True)
        h2b = P.alloc("h2T", 8 * 512 // 2)
        h2T = P.bf(h2b).rearrange("p (k n) -> p k n", k=8)
        aTb = P.alloc("aT", NFF * 512 // 2)
        aT = P.bf(aTb).rearrange("p (k n) -> p k n", k=NFF)
        xgb = P.alloc("xgrp", 4 * 1024)
        sgb = [P.alloc(f"sg{i}", 256) for i in range(2)]
        J2[0] = P.alloc("junk2", 1024)
        for g in range(5):
            r = 1 if g < 4 else 0
            for j in range(4):
                dstT_key[0] = h2b.k(j)
                tile = g * 4 + j
                xt = P.f32(xgb)[:, j * 1024:(j + 1) * 1024]
                norm_transpose(src, tile, r, 2, h2T, j * 128, xkeep=(xt, xgb.k(j)))
            for fc in range(NFF):
                bg, bu = P.bank(), P.bank()
                op("pe", MM([(PS[bg][:, :], wg[:, k, fc * 128:(fc + 1) * 128], h2T[:, k, :], k == 0, k == 7) for k in range(8)]),
                   R=[wgb.k(fc // FB)] + [h2b.k(j_) for j_ in range(4)], W=[psk(bg)])
                op("pe", MM([(PS[bu][:, :], wu[:, k, fc * 128:(fc + 1) * 128], h2T[:, k, :], k == 0, k == 7) for k in range(8)]),
                   R=[wub.k(fc // FB)] + [h2b.k(j_) for j_ in range(4)], W=[psk(bu)])
                sg = P.bf(sgb[fc % 2])
                op("act", ACT(sg, PS[bg][:, :], AF.Silu), R=[psk(bg)], W=[sgb[fc % 2].k()])
                op("dve", TT(aT[:, fc, :], sg, PS[bu][:, :], ALU.mult), R=[sgb[fc % 2].k(), psk(bu)], W=[aTb.k(fc)])
            for j in range(4):
                tile = g * 4 + j
                b0, b1 = P.bank(), P.bank()
                for hf, b in ((0, b0), (1, b1)):
                    op("pe", MM([(PS[b][:, :], aT[:, fc, j * 128:(j + 1) * 128], wd[:, fc, hf * 512:(hf + 1) * 512], fc == 0, fc == NFF - 1)
                                 for fc in range(NFF)]), R=[aTb.k(fc) for fc in range(NFF)] + [wdb.k(k_) for k_ in range(NFF // 2)], W=[psk(b)])
                xt = P.f32(xgb)[:, j * 1024:(j + 1) * 1024]
                resid_update((b0, b1), tile, r, 1, xt, xgb.k(j), dst, j % 2)
        P.free(wgb, wub, wdb, h2b, aTb, xgb, sgb[0], sgb[1], J2[0])

    def mixer(l, src, dst, parts=("ret", "mla", "four", "hy"), do_out=True):
        mixer_consts()
        KMC = MC["key"]
        for i_ in range(NXT):
            xtb[i_] = P.alloc(f"xt{i_}", 1024)
        xnb.extend([P.alloc(f"xn{i_}", 512) for i_ in range(2, NXT)])
        hTb = P.alloc("hT", 8 * 2560 // 2)
        hT = P.bf(hTb).rearrange("p (k n) -> p k n", k=8)
        catb = P.alloc("catT", 8 * 2560 // 2)
        catT = P.bf(catb).rearrange("p (k n) -> p k n", k=8)
        KH = hTb
        if dbg:
            op("pool", MSET(catT, 0.0), W=[catb.k((c, t_)) for c in range(8) for t_ in range(NT)])
        P.phase = "phaseA"
        for t in range(NT):
            dstT_key[0] = hTb.k(t)
            norm_transpose(src, t, 1 if t < 16 else 0, 0, hT, t * 128)
        dump(f"hT{l}", hT, [hTb.k(t_) for t_ in range(NT)])
        P.free(*xtb)
        P.free(*xnb[2:])
        del xnb[2:]
        ROPEB[0] = P.alloc("ropeb", 128 + 512)
        if "ret" in parts:
            retention_all(l, hT, KH, catT, catb, KMC)
        if "mla" in parts:
            mla_all(l, hT, KH, catT, catb, KMC)
        P.free(ROPEB[0], MC["buf"])
        if "four" in parts:
            fourier_all(l, hT, KH, catT, catb)
        if "hy" in parts:
            hyena_all(l, hT, KH, catT, catb, hTb)
        else:
            P.free(hTb)
        dump(f"catT{l}", catT, [catb.k((c, t_)) for c in range(8) for t_ in range(NT)])
        if do_out:
            P.phase = "phaseC"
            for i_ in range(NXT):
                xtb[i_] = P.alloc(f"xt{i_}", 1024)
            J2[0] = P.alloc("junk2", 2048)
            wob = P.alloc("wout", 8 * 1024 // 2)
            wo = P.bf(wob).rearrange("p (k n) -> p k n", k=8)
            op("pool", DMA(wo, T["w_out"][l].rearrange("(k p) n -> p k n", p=128)), W=[wob.k()], dma=True)
            for t in range(NT):
                r = 1 if t < 16 else 0
                i2 = t % NXT
                xt = P.f32(xtb[i2])
                op("sp", DMA(xt, src[t * 128:(t + 1) * 128, :]), R=[("dram", src.tensor.name, t)], W=[xtb[i2].k()], dma=True)
                b0, b1 = P.bank(), P.bank()
                for hf, b in ((0, b0), (1, b1)):
                    op("pe", MM([(PS[b][:, :], catT[:, k, t * 128:(t + 1) * 128], wo[:, k, hf * 512:(hf + 1) * 512], k == 0, k == 7)
                                 for k in range(8)]), R=[catb.k((c, t)) for c in range(8)] + [wob.k()], W=[psk(b)])
                resid_update((b0, b1), t, r, 0, xt, xtb[i2].k(), dst, t % 2)
            P.free(wob, J2[0], *xtb)
        P.free(catb)

    def retention_all(l, hT, KH, catT, catb, KMC):
        P.phase = "ret_tab"
        wAb = P.alloc("wA", 8 * 1024 // 2)
        wA = P.bf(wAb).rearrange("p (k n) -> p k n", k=8)
        op("pool", DMA(wA, T["w_in"][l][:, 0:1024].rearrange("(k p) n -> p k n", p=128)), W=[wAb.k()], dma=True)
        rtb = P.alloc("rtabs", 8 + 8 + 8 + 16 + 512 + 4 + 4)
        rt = P.f32(rtb)
        KT_ = rtb.k()
        decb = rt[:, 0:8]
        lgb = rt[:, 8:16]
        tmp8 = rt[:, 16:24]
        xz = rt[:, 24:40].rearrange("p (q h) -> p q h", q=4)
        dmk = rt[:, 40:552]
        lgsel = rt[:, 552:556].rearrange("p (d q) -> p d q", d=2)
        gsel = rt[:, 556:560].rearrange("p (d q) -> p d q", d=2)
        op("sp", DMA(decb, bass.AP(T["ret_decay"].tensor, l * 8, [[0, 128], [1, 8]])), W=[KT_], dma=True)
        op("act", ACT(tmp8, decb, AF.Exp, scale=-1.0), R=[KT_], W=[KT_])
        op("act", ACT(tmp8, tmp8, AF.Ln, bias=onesf[:, 0:1], scale=1.0), R=[KT_, KC], W=[KT_])
        op("dve", TS(lgb, tmp8, -1.0, ALU.mult), R=[KT_], W=[KT_])
        rc = MC["ret_cols"]
        op("act", ACT(xz[:, 0, :], lgb[:, 0:4], AF.Exp, scale=rc[:, 0:1]), R=[KT_, KMC], W=[KT_])
        op("act", ACT(xz[:, 1, :], lgb[:, 0:4], AF.Exp, scale=rc[:, 1:2]), R=[KT_, KMC], W=[KT_])
        op("act", ACT(xz[:, 2, :], lgb[:, 4:8], AF.Exp, scale=rc[:, 2:3]), R=[KT_, KMC], W=[KT_])
        op("act", ACT(xz[:, 3, :], lgb[:, 4:8], AF.Exp, scale=rc[:, 3:4]), R=[KT_, KMC], W=[KT_])
        for qq in (1, 3):
            op("dve", TS(xz[:, qq, :], xz[:, qq, :], 0.125, ALU.mult), R=[KT_], W=[KT_])
        t1b = P.alloc("rt_tmp", 1024)
        t1 = P.f32(t1b)
        for h in range(4):
            hs = (h % 2) * 2 + h // 2
            sl = slice(hs * 128, (hs + 1) * 128)
            op("act", ACT(t1[:, sl], MC["ret_dpos"][:, sl], AF.Exp, scale=lgb[:, h:h + 1]), R=[KT_, KMC], W=[t1b.k()])
            op("act", ACT(t1[:, 512 + hs * 128:512 + (hs + 1) * 128], MC["ret_dneg"][:, sl], AF.Exp, scale=lgb[:, 4 + h:5 + h]),
               R=[KT_, KMC], W=[t1b.k()])
        op("dve", TT(t1[:, 0:512], t1[:, 0:512], MC["ret_mge"], ALU.mult), R=[t1b.k(), KMC], W=[t1b.k()])
        op("dve", TT(t1[:, 512:1024], t1[:, 512:1024], MC["ret_mle"], ALU.mult), R=[t1b.k(), KMC], W=[t1b.k()])
        op("dve", TT(dmk, t1[:, 0:512], t1[:, 512:1024], ALU.add), R=[t1b.k()], W=[KT_])
        P.free(t1b)
        dv = lgb.rearrange("p (d q a) -> p d q a", d=2, a=2)
        for a in range(2):
            op("dve", CP(lgsel[a * 64:(a + 1) * 64], dv[a * 64:(a + 1) * 64, :, :, a]), R=[KT_], W=[KT_])
        op("act", ACT(gsel, lgsel, AF.Exp, scale=128.0), R=[KT_], W=[KT_])

        import os as _os
        _seqs = [SEQS[int(i_)] for i_ in _os.environ.get("DBG_SEQS", "0,1,2").split(",")]
        _stop = int(_os.environ.get("RET_STOP", "9"))
        if _stop <= 1:
            return
        for (t0, nts, latent) in _seqs:
            if latent:
                nts = int(_os.environ.get("DBG_NT0", nts))
            L = nts * 128
            qTb = P.alloc("qT", 2 * L // 2)
            kTb = P.alloc("kT", 2 * L // 2)
            qT = P.bf(qTb).rearrange("p (q n) -> p q n", q=2)
            kT = P.bf(kTb).rearrange("p (q n) -> p q n", q=2)
            vtb = P.alloc("v_tok", nts * 256 // 2)
            vt = P.bf(vtb).rearrange("p (t n) -> p t n", t=nts)
            gtb = P.alloc("gate_tok", nts * 256 // 2)
            gt = P.bf(gtb).rearrange("p (t n) -> p t n", t=nts)
            kvb = P.alloc("kv_all", nts * 256)
            kva = P.f32(kvb).rearrange("p (t d q n) -> p t d q n", t=nts, d=2, q=2)
            sbb = P.alloc("S_bf", nts * 256 // 2)
            sbf = P.bf(sbb).rearrange("p (t d q n) -> p t d q n", t=nts, d=2, q=2)
            stf = P.alloc("S_f32", 256)
            Sst = P.f32(stf).rearrange("p (d q n) -> p d q n", d=2, q=2)
            qkb = [P.alloc(f"qkrot{i}", 256) for i in range(2)]
            kzb = [P.alloc(f"kz{i}", 256) for i in range(2)]
            rpbs = [P.alloc(f"ropetmp{i}", 1024) for i in range(2)] if latent else None

            def stageA(j):
                t = t0 + j
                cols = slice(t * 128, (t + 1) * 128)
                bq, bv = P.bank(), P.bank()
                op("pe", MM([(PS[bq][:, :], hT[:, k, cols], wA[:, k, 0:512], k == 0, k == 7) for k in range(8)]),
                   R=[KH.k(t), wAb.k()], W=[psk(bq)])
                op("pe", MM([(PS[bv][:, :], hT[:, k, cols], wA[:, k, 512:1024], k == 0, k == 7) for k in range(8)]),
                   R=[KH.k(t), wAb.k()], W=[psk(bv)])
                qk = P.bf(qkb[j % 2])
                kq = qkb[j % 2].k()
                if latent:
                    rpb = rpbs[j % 2]
                    rp = P.f32(rpb)
                    raw = rp[:, 0:512]
                    op("act", ACT(raw, PS[bq][:, :], AF.Copy), R=[psk(bq)], W=[rpb.k(0)])
                    src5 = raw.rearrange("p (h r x j) -> p h r x j", h=8, r=2, x=2)
                    dst5 = qk.rearrange("p (h r x j) -> p h r x j", h=8, r=2, x=2)
                    tm5 = rp[:, 512:1024].rearrange("p (u h r j) -> p u h r j", u=2, h=8, r=2)
                    cs = bc(MC["cos_ret"][:, j * 32:(j + 1) * 32], [[0, 8], [16, 2], [1, 16]])
                    sn = bc(MC["sin_ret"][:, j * 32:(j + 1) * 32], [[0, 8], [16, 2], [1, 16]])
                    x1, x2 = src5[:, :, :, 0, :], src5[:, :, :, 1, :]
                    op("pool", TT(tm5[:, 0], x1, cs, ALU.mult), R=[rpb.k(0), KMC], W=[rpb.k(1)])
                    op("pool", TT(tm5[:, 1], x2, sn, ALU.mult), R=[rpb.k(0), KMC], W=[rpb.k(2)])
                    op("pool", TT(dst5[:, :, :, 0, :], tm5[:, 0], tm5[:, 1], ALU.subtract), R=[rpb.k(1), rpb.k(2)], W=[kq])
                    op("pool", TT(tm5[:, 0], x2, cs, ALU.mult), R=[rpb.k(0), KMC], W=[rpb.k(1)])
                    op("pool", TT(tm5[:, 1], x1, sn, ALU.mult), R=[rpb.k(0), KMC], W=[rpb.k(2)])
                    op("pool", TT(dst5[:, :, :, 1, :], tm5[:, 0], tm5[:, 1], ALU.add), R=[rpb.k(1), rpb.k(2)], W=[kq])
                else:
                    op("act", ACT(qk, PS[bq][:, :], AF.Copy), R=[psk(bq)], W=[kq])
                op("act", ACT(vt[:, j, :], PS[bv][:, 0:256], AF.Copy), R=[psk(bv)], W=[vtb.k(j)])
                op("act", ACT(gt[:, j, :], PS[bv][:, 256:512], AF.Silu), R=[psk(bv)], W=[gtb.k(j)])
                kz = P.bf(kzb[j % 2]).rearrange("p (d h n) -> p d h n", d=2, h=4)
                k4 = qk[:, 256:512].rearrange("p (h n) -> p h n", h=4)
                for d_, qq in ((0, 1), (1, 3)):
                    op("pool", TT(kz[:, d_], k4, bc(xz[:, qq, :], [[1, 4], [0, 64]]), ALU.mult), R=[kq, KT_], W=[kzb[j % 2].k(d_)])

            def stageB(j):
                qk = P.bf(qkb[j % 2])
                kq = qkb[j % 2].k()
                kz = P.bf(kzb[j % 2]).rearrange("p (d h n) -> p d h n", d=2, h=4)
                bt, bt2 = P.bank(), P.bank()
                op("pe", MM([(PS[bt][:, i * 128:(i + 1) * 128], qk[:, i * 128:(i + 1) * 128], ident, True, True) for i in range(2)]),
                   R=[kq, KC], W=[psk(bt)])
                op("pe", MM([(PS[bt2][:, i * 128:(i + 1) * 128], qk[:, (2 + i) * 128:(3 + i) * 128], ident, True, True) for i in range(2)]),
                   R=[kq, KC], W=[psk(bt2)])
                jc = slice(j * 128, (j + 1) * 128)
                op("act", ACT(qT[:, :, jc], PS[bt][:, 0:256].rearrange("p (q n) -> p q n", q=2), AF.Copy), R=[psk(bt)], W=[qTb.k(j)])
                op("dve", CP(kT[:, :, jc], PS[bt2][:, 0:256].rearrange("p (q n) -> p q n", q=2)), R=[psk(bt2)], W=[kTb.k(j)])
                bk = P.bank()
                op("pe", MM([(PS[bk][:, d_ * 256 + hp * 128:d_ * 256 + (hp + 1) * 128], kz[:, d_, 2 * hp:2 * hp + 2, :],
                              vt[:, j, hp * 128:(hp + 1) * 128], True, True) for d_ in range(2) for hp in range(2)]),
                   R=[kzb[j % 2].k(0), kzb[j % 2].k(1), vtb.k(j)], W=[psk(bk)])
                for a in range(2):
                    srcv = bass.AP(PS[bk][:, :].tensor,
                                   PS[bk][a * 64:(a + 1) * 64, a * 64:a * 64 + 1].offset,
                                   [list(PS[bk][a * 64:(a + 1) * 64, :].ap[0]), [256, 2], [128, 2], [1, 64]])
                    op("act" if j % 2 == 0 else "dve",
                       (ACT(kva[a * 64:(a + 1) * 64, j], srcv, AF.Copy) if j % 2 == 0 else CP(kva[a * 64:(a + 1) * 64, j], srcv)),
                       R=[psk(bk)], W=[kvb.k((j, a))])

            P.phase = "ret_s1"
            stageA(0)
            for j in range(nts):
                if j + 1 < nts:
                    stageA(j + 1)
                stageB(j)
            rpb = None
            if _stop <= 2:
                continue
            P.phase = "ret_rec"
            KS = stf.k()
            if latent:
                for d_ in range(2):
                    for a in range(2):
                        op("sp", DMA(Sst[a * 64:(a + 1) * 64, d_], bass.AP(T["s0"].tensor, ((l * 2 + d_) * 4 + a) * 4096,
                                                                           [[64, 64], [2 * 4096, 2], [1, 64]])), W=[KS], dma=True)
            else:
                op("pool", MSET(P.f32(stf), 0.0), W=[KS])
            for d_ in range(2):
                order = range(nts) if d_ == 0 else range(nts - 1, -1, -1)
                for j in order:
                    op("act", ACT(sbf[:, j, d_], Sst[:, d_], AF.Copy), R=[KS], W=[sbb.k((j, d_))])
                    for hp in range(2):
                        op("dve", STT(Sst[:, d_, hp, :], Sst[:, d_, hp, :], gsel[:, d_, hp:hp + 1], kva[:, j, d_, hp, :], ALU.mult, ALU.add),
                           R=[KS, KT_, kvb.k((j, 0)), kvb.k((j, 1))], W=[KS])
            if not latent:
                pb = (t0 - 16) // 2
                for d_ in range(2):
                    for a in range(2):
                        op("sp", DMA(bass.AP(T["o_st"].tensor, (((pb * 2 + l) * 2 + d_) * 4 + a) * 4096, [[64, 64], [2 * 4096, 2], [1, 64]]),
                                     Sst[a * 64:(a + 1) * 64, d_]), R=[KS], dma=True)
            if _stop <= 3:
                continue
            P.free(kvb, qkb[0], qkb[1], kzb[0], kzb[1])
            if rpbs is not None:
                P.free(rpbs[0], rpbs[1])
            P.phase = "ret_out"
            G_ = 4
            ptb = [P.alloc(f"PT{i}", 256) for i in range(G_)]
            ob = [P.alloc(f"o_acc{i}", 256 * 3) for i in range(G_)]
            gnb = P.alloc("gn", 16 * G_)
            gn = P.f32(gnb)
            rob = [P.alloc(f"ret_o{i}", 128) for i in range(G_)]
            st_banks = {}

            def so1(j):
                jc = slice(j * 128, (j + 1) * 128)
                bsa = [P.bank(), P.bank()]
                PT = P.bf(ptb[j % G_])
                for a in range(2):
                    op("pe", MM([(PS[bsa[a]][:, hp * 128:(hp + 1) * 128], kT[a * 64:(a + 1) * 64, hp, jc],
                                  qT[a * 64:(a + 1) * 64, hp, jc], True, True) for hp in range(2)]),
                       R=[kTb.k(j), qTb.k(j)], W=[psk(bsa[a])])
                    op("dve", TT(PT[:, a * 256:(a + 1) * 256], PS[bsa[a]][:, 0:256], dmk[:, a * 256:(a + 1) * 256], ALU.mult),
                       R=[psk(bsa[a]), KT_], W=[ptb[j % G_].k(a)])

            def so2(j):
                jc = slice(j * 128, (j + 1) * 128)
                PT = P.bf(ptb[j % G_])
                bo = P.bank()
                op("pe", MM([(PS[bo][:, h * 64:(h + 1) * 64], PT[:, ((h % 2) * 2 + h // 2) * 128:((h % 2) * 2 + h // 2 + 1) * 128],
                              vt[:, j, h * 64:(h + 1) * 64], True, True) for h in range(4)]),
                   R=[ptb[j % G_].k(0), ptb[j % G_].k(1), vtb.k(j)], W=[psk(bo)])
                bxa = [P.bank(), P.bank()]
                for a in range(2):
                    op("pe", MM([(PS[bxa[a]][:, (d_ * 2 + hp) * 64:(d_ * 2 + hp + 1) * 64], qT[a * 64:(a + 1) * 64, hp, jc],
                                  sbf[a * 64:(a + 1) * 64, j, d_, hp, :], True, True) for d_ in range(2) for hp in range(2)]),
                       R=[qTb.k(j), sbb.k((j, 0)), sbb.k((j, 1))], W=[psk(bxa[a])])
                st_banks[j] = (bo, bxa)

            def so3(j):
                bo, bxa = st_banks[j]
                oa = P.f32(ob[j % G_]).rearrange("p (u h n) -> p u h n", u=3, h=4)
                ko = ob[j % G_].k()
                for a in range(2):
                    xif = bc(xz[:, 0, a:a + 1], [[2, 2], [0, 64]])
                    xib = bc(xz[:, 2, a:a + 1], [[2, 2], [0, 64]])
                    cf_ = PS[bxa[a]][:, 0:128].rearrange("p (q n) -> p q n", q=2)
                    cb_ = PS[bxa[a]][:, 128:256].rearrange("p (q n) -> p q n", q=2)
                    op("dve", TT(oa[:, 0, a::2, :], cf_, xif, ALU.mult), R=[psk(bxa[a]), KT_], W=[ko])
                    op("dve", TT(oa[:, 1, a::2, :], cb_, xib, ALU.mult), R=[psk(bxa[a]), KT_], W=[ko])
                op("pool", TT(oa[:, 0], oa[:, 0], oa[:, 1], ALU.add), R=[ko], W=[ko])
                op("dve", TT(oa[:, 0], oa[:, 0], PS[bo][:, 0:256].rearrange("p (h n) -> p h n", h=4), ALU.add), R=[ko, psk(bo)], W=[ko])

            def gnv(j):
                g0 = (j % G_) * 16
                return (gn[:, g0:g0 + 4], gn[:, g0 + 4:g0 + 8], gn[:, g0 + 8:g0 + 12], gn[:, g0 + 12:g0 + 16], gnb.k(j % G_))

            def so4(j):
                oa = P.f32(ob[j % G_]).rearrange("p (u h n) -> p u h n", u=3, h=4)
                ko = ob[j % G_].k()
                sm, ng_, vs, sd, kg = gnv(j)
                op("dve", RED(sm, oa[:, 0]), R=[ko], W=[kg])
                op("dve", TS(ng_, sm, -1.0 / 64, ALU.mult), R=[kg], W=[kg])
                op("pool", TT(oa[:, 1], oa[:, 0], bc(ng_, [[1, 4], [0, 64]]), ALU.add), R=[ko, kg], W=[ko])
                op("pool", TT(oa[:, 2], oa[:, 1], oa[:, 1], ALU.mult), R=[ko], W=[ko])

            def so5(j):
                oa = P.f32(ob[j % G_]).rearrange("p (u h n) -> p u h n", u=3, h=4)
                ko = ob[j % G_].k()
                sm, ng_, vs, sd, kg = gnv(j)
                op("dve", RED(vs, oa[:, 2]), R=[ko], W=[kg])
                op("act", ACT(sd, vs, AF.Sqrt, bias=epsc, scale=1.0 / 64), R=[kg, KC], W=[kg])
                op("dve", RECIP(sm, sd), R=[kg], W=[kg])

            def so6(j):
                t = t0 + j
                oa = P.f32(ob[j % G_]).rearrange("p (u h n) -> p u h n", u=3, h=4)
                ko = ob[j % G_].k()
                sm, ng_, vs, sd, kg = gnv(j)
                op("pool", TT(oa[:, 2], oa[:, 1], bc(sm, [[1, 4], [0, 64]]), ALU.mult), R=[ko, kg], W=[ko])
                ro = P.bf(rob[j % G_])
                op("dve", TT(ro.rearrange("p (h n) -> p h n", h=4), oa[:, 2], gt[:, j, :].rearrange("p (h n) -> p h n", h=4), ALU.mult),
                   R=[ko, gtb.k(j)], W=[rob[j % G_].k()])
                bt = P.bank()
                op("pe", MM([(PS[bt][:, i * 128:(i + 1) * 128], ro[:, i * 128:(i + 1) * 128], ident, True, True) for i in range(2)]),
                   R=[rob[j % G_].k(), KC], W=[psk(bt)])
                op("act", ACT(catT[:, 0:2, t * 128:(t + 1) * 128], PS[bt][:, 0:256].rearrange("p (q n) -> p q n", q=2), AF.Copy),
                   R=[psk(bt)], W=[catb.k((0, t)), catb.k((1, t))])

            for j0 in range(0, nts, 2):
                js = list(range(j0, min(nts, j0 + 2)))
                for stage in (so1, so2, so3, so4, so5, so6):
                    for j in js:
                        stage(j)
            P.free(*ptb[2:], *ob[2:], *rob[2:])
            P.free(qTb, kTb, vtb, gtb, sbb, stf, ptb[0], ptb[1], ob[0], ob[1], gnb, rob[0], rob[1])
        P.free(wAb, rtb)

    def mla_all(l, hT, KH, catT, catb, KMC):
        import os as _os
        ropeb = ROPEB[0]
        P.phase = "mla1"
        wBb = P.alloc("wB", 8 * 416 // 2)
        wB = P.bf(wBb).rearrange("p (k n) -> p k n", k=8)
        op("pool", DMA(wB, T["w_in"][l][:, 1280:1696].rearrange("(k p) n -> p k n", p=128)), W=[wBb.k()], dma=True)
        wqb = P.alloc("wuq", 2 * 384 // 2 + 256)
        wuq = P.bf(wqb)[:, 0:768].rearrange("p (k n) -> p k n", k=2)
        wukv = P.bf(wqb)[:, 768:1280]
        op("pool", DMA(wuq, T["mla_w_uq"][l].rearrange("(k p) n -> p k n", p=128)), W=[wqb.k()], dma=True)
        op("pool", DMA(wukv, T["mla_w_ukv"][l]), W=[wqb.k()], dma=True)
        gnb_ = P.alloc("mla_g", 384)
        gq_b, gkv_b = P.f32(gnb_)[:, 0:256], P.f32(gnb_)[:, 256:384]
        op("sp", DMA(gq_b, bass.AP(T["mla_q_norm"].tensor, l * 256, [[0, 128], [1, 256]])), W=[gnb_.k()], dma=True)
        op("sp", DMA(gkv_b, bass.AP(T["mla_kv_norm"].tensor, l * 128, [[0, 128], [1, 128]])), W=[gnb_.k()], dma=True)
        msb = P.alloc("mla_stats", 32)
        ms = P.f32(msb)
        scale = 96.0 ** -0.5
        _seqs = [SEQS[int(i_)] for i_ in _os.environ.get("DBG_SEQS", "0,1,2").split(",")]
        for (t0, nts, latent) in _seqs:
            L = nts * 128
            nkt = nts + (2 if latent else 0)
            Lk = nkt * 128
            cqTb = P.alloc("cqT", L)
            cqT = P.bf(cqTb).rearrange("p (k n) -> p k n", k=2)
            ckTb = P.alloc("ckvT", Lk // 2)
            ckvT = P.bf(ckTb)
            KTb = P.alloc("KT", 2 * Lk)
            KT = P.bf(KTb).rearrange("p (h n) -> p h n", h=4)
            QTb = P.alloc("QT", 2 * L)
            QT = P.bf(QTb).rearrange("p (h n) -> p h n", h=4)
            Vb = P.alloc("Vaug", nkt * 130)
            Va = P.bf(Vb).rearrange("p (t h n) -> p t h n", t=nkt, h=4)
            atb = P.alloc("att_tok", nts * 128)
            att = P.bf(atb).rearrange("p (t n) -> p t n", t=nts)
            op("pool", MSET(P.bf(Vb), 1.0), W=[Vb.k()])
            NS = 4
            kstb = [P.alloc(f"kst{i}", 128) for i in range(NS)]
            cqnb = [P.alloc(f"cqn{i}", 128) for i in range(NS)]
            ckfb = [P.alloc(f"ckvf{i}", 128 + 32) for i in range(NS)]
            for i in range(NS):
                op("pool", MSET(P.bf(kstb[i]), 0.0), W=[kstb[i].k()])
            pb = (t0 - 16) // 2
            P.phase = "mla1"
            for j in range(nkt):
                i2 = j % NS
                kst = P.bf(kstb[i2])
                kk = kstb[i2].k()
                kcols = slice(j * 128, (j + 1) * 128)
                if j < nts:
                    t = t0 + j
                    bp = P.bank()
                    op("pe", MM([(PS[bp][:, 0:416], hT[:, k, t * 128:(t + 1) * 128], wB[:, k, :], k == 0, k == 7) for k in range(8)]),
                       R=[KH.k(t), wBb.k()], W=[psk(bp)])
                    ssq, ssk, sq1, sq2, rq, rk = (ms[:, i2 * 8 + u:i2 * 8 + u + 1] for u in range(6))
                    kst_ = msb.k(i2)
                    cqn = P.bf(cqnb[i2])
                    ckf = P.f32(ckfb[i2])
                    op("act", ACT(cqn, PS[bp][:, 0:256], AF.Square, accum=ssq), R=[psk(bp)], W=[cqnb[i2].k(), kst_])
                    op("act", ACT(ckf[:, 0:128], PS[bp][:, 256:384], AF.Square, accum=ssk), R=[psk(bp)], W=[ckfb[i2].k(), kst_])
                    op("act", ACT(sq1, ssq, AF.Sqrt, bias=epsc, scale=1.0 / 256), R=[kst_, KC], W=[kst_])
                    op("act", ACT(sq2, ssk, AF.Sqrt, bias=epsc, scale=1.0 / 128), R=[kst_, KC], W=[kst_])
                    op("dve", RECIP(ms[:, i2 * 8 + 4:i2 * 8 + 6], ms[:, i2 * 8 + 2:i2 * 8 + 4]), R=[kst_], W=[kst_])
                    op("dve", STT(cqn, PS[bp][:, 0:256], rq, gq_b, ALU.mult, ALU.mult), R=[psk(bp), kst_, gnb_.k()], W=[cqnb[i2].k()])
                    op("dve", STT(ckf[:, 0:128], PS[bp][:, 256:384], rk, gkv_b, ALU.mult, ALU.mult), R=[psk(bp), kst_, gnb_.k()],
                       W=[ckfb[i2].k()])
                    op("pool", CP(kst[:, 0:128], ckf[:, 0:128]), R=[ckfb[i2].k()], W=[kk])
                    if latent:
                        op("dve", CP(ckf[:, 128:160], PS[bp][:, 384:416]), R=[psk(bp), kst_], W=[ckfb[i2].k("kr")])
                        kr3 = ckf[:, 128:160].rearrange("p (r x j) -> p r x j", r=2, x=2)
                        kd3 = kst[:, 192:224].rearrange("p (r x j) -> p r x j", r=2, x=2)
                        cs = MC["cos_mla"][:, j * 16:(j + 1) * 16].rearrange("p (r j) -> p r j", r=2)
                        sn = MC["sin_mla"][:, j * 16:(j + 1) * 16].rearrange("p (r j) -> p r j", r=2)
                        tmk = ms
                        rtb_ = ckfb[i2].k("rt")
                        tmp4 = P.f32(ropeb)[:, (i2 % 2) * 64:(i2 % 2 + 1) * 64].rearrange("p (u r j) -> p u r j", u=4, r=2)
                        kro = ropeb.k(i2 % 2)
                        op("pool", TT(tmp4[:, 0], kr3[:, :, 0, :], cs, ALU.mult), R=[ckfb[i2].k("kr"), KMC], W=[kro])
                        op("pool", TT(tmp4[:, 1], kr3[:, :, 1, :], sn, ALU.mult), R=[ckfb[i2].k("kr"), KMC], W=[kro])
                        op("pool", TT(tmp4[:, 2], kr3[:, :, 1, :], cs, ALU.mult), R=[ckfb[i2].k("kr"), KMC], W=[kro])
                        op("pool", TT(tmp4[:, 3], kr3[:, :, 0, :], sn, ALU.mult), R=[ckfb[i2].k("kr"), KMC], W=[kro])
                        op("pool", TT(kd3[:, :, 0, :], tmp4[:, 0], tmp4[:, 1], ALU.subtract), R=[kro], W=[kk])
                        op("pool", TT(kd3[:, :, 1, :], tmp4[:, 2], tmp4[:, 3], ALU.add), R=[kro], W=[kk])
                    else:
                        op("dve", CP(ckf[:, 128:160], PS[bp][:, 384:416]), R=[psk(bp), kst_], W=[ckfb[i2].k("kr")])
                        op("pool", CP(kst[:, 192:224], ckf[:, 128:160]), R=[ckfb[i2].k("kr")], W=[kk])
                        rows = slice(j * 128, (j + 1) * 128)
                        op("sp", DMA(T["o_ckv"][pb, l, rows, :], ckf[:, 0:128]), R=[ckfb[i2].k()], dma=True)
                        op("sp", DMA(T["o_kr"][pb, l, rows, :], ckf[:, 128:160]), R=[ckfb[i2].k("kr")], dma=True)
                else:
                    rows = slice((j - nts) * 128, (j - nts + 1) * 128)
                    op("pool", DMA(kst[:, 0:128], T["ckv_ctx"][l, rows, :]), W=[kk], dma=True)
                    op("pool", DMA(kst[:, 192:224], T["kr_ctx"][l, rows, :]), W=[kk], dma=True)
                bT = P.bank()
                mms = []
                if j < nts:
                    mms += [(PS[bT][:, i * 128:(i + 1) * 128], P.bf(cqnb[i2])[:, i * 128:(i + 1) * 128], ident, True, True) for i in range(2)]
                mms += [(PS[bT][:, 256:384], kst[:, 0:128], ident, True, True),
                        (PS[bT][0:96, 384:512], kst[:, 128:224], ident, True, True)]
                op("pe", MM(mms), R=[cqnb[i2].k(), kk, KC], W=[psk(bT)])
                ev = "act" if j % 2 == 0 else "dve"
                cpy = (lambda o_, i_: ACT(o_, i_, AF.Copy)) if ev == "act" else CP
                if j < nts:
                    op(ev, cpy(cqT[:, :, j * 128:(j + 1) * 128], PS[bT][:, 0:256].rearrange("p (k n) -> p k n", k=2)), R=[psk(bT)], W=[cqTb.k(j)])
                op(ev, cpy(ckvT[:, kcols], PS[bT][:, 256:384]), R=[psk(bT)], W=[ckTb.k(j)])
                op(ev, cpy(KT[64:96, :, kcols], bc(PS[bT][64:96, 384:512], [[0, 4], [1, 128]])), R=[psk(bT)], W=[KTb.k(("r", j))])
            P.phase = "mla2"
            rawb = [P.alloc(f"qraw{i}", 384) for i in range(2)]
            qtkb = [P.alloc(f"qtok{i}", 192) for i in range(2)]
            for j in range(nts):
                i2 = j % 2
                jc = slice(j * 128, (j + 1) * 128)
                bq = P.bank()
                op("pe", MM([(PS[bq][:, 0:384], cqT[:, k, jc], wuq[:, k, :], k == 0, k == 1) for k in range(2)]),
                   R=[cqTb.k(j), wqb.k()], W=[psk(bq)])
                qtk = P.bf(qtkb[i2])
                kq_ = qtkb[i2].k()
                if latent:
                    raw = P.f32(rawb[i2])
                    op("act", ACT(raw, PS[bq][:, 0:384], AF.Copy), R=[psk(bq)], W=[rawb[i2].k()])
                    r4 = raw.rearrange("p (h n) -> p h n", h=4)
                    q4 = qtk.rearrange("p (h n) -> p h n", h=4)
                    op("dve", CP(q4[:, :, 0:64], r4[:, :, 0:64]), R=[rawb[i2].k()], W=[kq_])
                    def rv(base, off):
                        return bass.AP(base.tensor, base.offset + off, [list(base.ap[0]), [96, 4], [16, 2], [1, 8]])
                    x1, x2 = rv(raw, 64), rv(raw, 72)
                    o1_, o2_ = rv(qtk, 64), rv(qtk, 72)
                    cs = bc(MC["cos_mla"][:, j * 16:(j + 1) * 16], [[0, 4], [8, 2], [1, 8]])
                    sn = bc(MC["sin_mla"][:, j * 16:(j + 1) * 16], [[0, 4], [8, 2], [1, 8]])
                    tq = P.f32(ropeb)[:, 128 + i2 * 256:128 + (i2 + 1) * 256].rearrange("p (u h r j) -> p u h r j", u=4, h=4, r=2)
                    kro = ropeb.k(("q", i2))
                    op("pool", TT(tq[:, 0], x1, cs, ALU.mult), R=[rawb[i2].k(), KMC], W=[kro])
                    op("pool", TT(tq[:, 1], x2, sn, ALU.mult), R=[rawb[i2].k(), KMC], W=[kro])
                    op("pool", TT(tq[:, 2], x2, cs, ALU.mult), R=[rawb[i2].k(), KMC], W=[kro])
                    op("pool", TT(tq[:, 3], x1, sn, ALU.mult), R=[rawb[i2].k(), KMC], W=[kro])
                    op("pool", TT(o1_, tq[:, 0], tq[:, 1], ALU.subtract), R=[kro], W=[kq_])
                    op("pool", TT(o2_, tq[:, 2], tq[:, 3], ALU.add), R=[kro], W=[kq_])
                else:
                    op("act", ACT(qtk, PS[bq][:, 0:384], AF.Copy), R=[psk(bq)], W=[kq_])
                bT = P.bank()
                op("pe", MM([(PS[bT][0:96, h * 128:(h + 1) * 128], qtk[:, h * 96:(h + 1) * 96], ident, True, True) for h in range(4)]),
                   R=[kq_, KC], W=[psk(bT)])
                ev = "act" if j % 2 == 0 else "dve"
                cpy = (lambda o_, i_: ACT(o_, i_, AF.Copy)) if ev == "act" else CP
                op(ev, cpy(QT[0:96, :, jc], PS[bT][0:96, :].rearrange("p (h n) -> p h n", h=4)), R=[psk(bT)], W=[QTb.k(j)])
            P.free(rawb[0], rawb[1], qtkb[0], qtkb[1])
            P.phase = "mla3"
            wk4 = wukv.rearrange("p (h c n) -> p h c n", h=4, c=2)
            for c0 in range(0, Lk, 512):
                cw = min(512, Lk - c0)
                for h in range(4):
                    b = P.bank()
                    op("pe", MM([(PS[b][0:64, 0:cw], wk4[:, h, 0, :], ckvT[:, c0:c0 + cw], True, True)]),
                       R=[wqb.k()] + [ckTb.k(j_) for j_ in range(c0 // 128, (c0 + cw) // 128)], W=[psk(b)])
                    ev = "act" if h % 2 == 0 else "dve"
                    cpy = (lambda o_, i_: ACT(o_, i_, AF.Copy)) if ev == "act" else CP
                    op(ev, cpy(KT[0:64, h, c0:c0 + cw], PS[b][0:64, 0:cw]), R=[psk(b)], W=[KTb.k(("n", h, c0))])
            for kt in range(nkt):
                b = P.bank()
                op("pe", MM([(PS[b][:, 0:256], ckvT[:, kt * 128:(kt + 1) * 128], wk4[:, :, 1, :], True, True)]),
                   R=[wqb.k(), ckTb.k(kt)], W=[psk(b)])
                ev = "act" if kt % 2 == 0 else "dve"
                cpy = (lambda o_, i_: ACT(o_, i_, AF.Copy)) if ev == "act" else CP
                op(ev, cpy(Va[:, kt, :, 0:64], PS[b][:, 0:256].rearrange("p (h n) -> p h n", h=4)), R=[psk(b), Vb.k()], W=[Vb.k(kt)])
            P.phase = "mla_att"
            QC = min(512, L)
            nsub = QC // 128
            Eb = [P.alloc(f"E{i}", QC // 2) for i in range(4)]
            rcb = P.alloc("att_rec", 8)
            ei = 0
            allKT = [KTb.k(("r", j_)) for j_ in range(nkt)]
            for h in range(4):
                kth = allKT + [KTb.k(("n", h, c0)) for c0 in range(0, Lk, 512)]
                for qc in range(L // QC):
                    qcols = slice(qc * QC, (qc + 1) * QC)
                    bo = P.bank()
                    qdeps = [QTb.k(j_) for j_ in range(qc * nsub, (qc + 1) * nsub)]

                    def scores(kt):
                        bs = P.bank()
                        if bs == bo:
                            bs = P.bank()
                        op("pe", MM([(PS[bs][:, 0:QC], KT[0:96, h, kt * 128:(kt + 1) * 128], QT[0:96, h, qcols], True, True)]),
                           R=kth + qdeps, W=[psk(bs)])
                        return bs
                    pend = scores(0)
                    for kt in range(nkt):
                        bs = pend
                        if kt + 1 < nkt:
                            pend = scores(kt + 1)
                        E = P.bf(Eb[ei % 4])
                        ke = Eb[ei % 4].k()
                        ei += 1
                        op("act", ACT(E, PS[bs][:, 0:QC], AF.Exp, scale=scale), R=[psk(bs)], W=[ke])
                        op("pe", MM([(PS[bo][:, sub * 65:(sub + 1) * 65], E[:, sub * 128:(sub + 1) * 128], Va[:, kt, h, :],
                                      kt == 0 and sub == 0, kt == nkt - 1 and sub == nsub - 1)
                                     for sub in range(nsub)]), R=[ke, Vb.k(kt), Vb.k()], W=[psk(bo)])
                    o3 = PS[bo][:, 0:nsub * 65].rearrange("p (s n) -> p s n", s=nsub)
                    rec = P.f32(rcb)[:, (qc % 2) * 4:(qc % 2) * 4 + nsub]
                    op("dve", RECIP(rec, o3[:, :, 64]), R=[psk(bo)], W=[rcb.k(qc % 2)])
                    op("dve", TT(att[:, qc * nsub:(qc + 1) * nsub, h * 64:(h + 1) * 64], o3[:, :, 0:64], bc(rec, [[1, nsub], [0, 64]]), ALU.mult),
                       R=[psk(bo), rcb.k(qc % 2)], W=[atb.k((h, qc))])
            P.phase = "mla5"
            for j in range(nts):
                t = t0 + j
                bT = P.bank()
                op("pe", MM([(PS[bT][:, i * 128:(i + 1) * 128], att[:, j, i * 128:(i + 1) * 128], ident, True, True) for i in range(2)]),
                   R=[atb.k((h_, j // nsub)) for h_ in range(4)] + [KC], W=[psk(bT)])
                ev = "act" if j % 2 == 0 else "dve"
                cpy = (lambda o_, i_: ACT(o_, i_, AF.Copy)) if ev == "act" else CP
                op(ev, cpy(catT[:, 4:6, t * 128:(t + 1) * 128], PS[bT][:, 0:256].rearrange("p (q n) -> p q n", q=2)), R=[psk(bT)],
                   W=[catb.k((4, t)), catb.k((5, t))])
            P.free(cqTb, ckTb, KTb, QTb, Vb, atb, *kstb, *cqnb, *ckfb, *Eb, rcb)
        P.free(wBb, wqb, gnb_, msb)


    def fourier_all(l, hT, KH, catT, catb):
        import os as _os
        P.phase = "four"
        wCb = P.alloc("wC", 8 * 256 // 2)
        wC = P.bf(wCb).rearrange("p (k n) -> p k n", k=8)
        op("pool", DMA(wC, T["w_in"][l][:, 1024:1280].rearrange("(k p) n -> p k n", p=128)), W=[wCb.k()], dma=True)
        _seqs = [SEQS[int(i_)] for i_ in _os.environ.get("DBG_SEQS", "0,1,2").split(",")]
        for (t0, nts, latent) in _seqs:
            L = nts * 128
            nj = nts // 2
            H = L // 2
            PB = min(512, H)
            base = t0 * 128
            ftb = P.alloc("f_tok", 2 * nj * 256 // 2)
            ft = P.bf(ftb).rearrange("p (a j n) -> p a j n", a=2, j=nj)
            for a in range(2):
                for j in range(nj):
                    b = P.bank()
                    c0 = base + a + 256 * j
                    op("pe", MM([(PS[b][:, 0:256], hT[:, k, c0:c0 + 255:2], wC[:, k, :], k == 0, k == 7) for k in range(8)]),
                       R=[KH.k(c0 // 128), KH.k(c0 // 128 + 1), wCb.k()], W=[psk(b)])
                    ev = "act" if (a * nj + j) % 2 == 0 else "dve"
                    cpy = (lambda o_, i_: ACT(o_, i_, AF.Copy)) if ev == "act" else CP
                    op(ev, cpy(ft[:, a, j, :], PS[b][:, 0:256]), R=[psk(b)], W=[ftb.k((a, j))])
            tabC = T[f"f_cl{L}"].rearrange("p (a j n) -> p a j n", a=2, j=nj)
            tabS = T[f"f_sl{L}"].rearrange("p (a j n) -> p a j n", a=2, j=nj)
            mtb = [P.alloc(f"ftab{i}", 2 * nj * PB // 2) for i in range(2)]
            uvb = [P.alloc(f"fuv{i}", 4 * PB // 2) for i in range(2)]
            zob = P.alloc("fzo", 2 * PB)
            mi = 0
            for ph in range(H // PB):
                zbanks = {}
                for a in range(2):
                    mt = P.bf(mtb[mi % 2]).rearrange("p (q j n) -> p q j n", q=2, j=nj)
                    km = mtb[mi % 2].k()
                    op("sp", DMA(mt[:, 0], tabC[:, a, :, ph * PB:(ph + 1) * PB]), W=[km], dma=True)
                    op("sp", DMA(mt[:, 1], tabS[:, a, :, ph * PB:(ph + 1) * PB]), W=[km], dma=True)
                    uv = P.bf(uvb[mi % 2]).rearrange("p (c q n) -> p c q n", c=2, q=2)
                    ku = uvb[mi % 2].k()
                    mi += 1
                    for cc in range(2):
                        for q in range(2):
                            b = P.bank()
                            op("pe", MM([(PS[b][:, 0:PB], ft[:, a, j, cc * 128:(cc + 1) * 128], mt[:, q, j, :], j == 0, j == nj - 1)
                                         for j in range(nj)]), R=[ftb.k((a, j)) for j in range(nj)] + [km], W=[psk(b)])
                            ev = "act" if q == 0 else "dve"
                            cpy = (lambda o_, i_: ACT(o_, i_, AF.Copy)) if ev == "act" else CP
                            op(ev, cpy(uv[:, cc, q, :], PS[b][:, 0:PB]), R=[psk(b)], W=[uvb[(mi - 1) % 2].k((cc, q))])
                    for cc in range(2):
                        b = P.bank()
                        zbanks[(a, cc)] = b
                        kk_ = uvb[(mi - 1) % 2]
                        op("pe", MM([(PS[b][:, 0:PB], c64, uv[:, cc, 0, :], True, False), (PS[b][:, 0:PB], s64n, uv[:, cc, 1, :], False, True)]),
                           R=[kk_.k((cc, 0)), kk_.k((cc, 1)), KC], W=[psk(b)])
                zo = P.f32(zob).rearrange("p (c n) -> p c n", c=2)
                for cc in range(2):
                    be, bo_ = zbanks[(0, cc)], zbanks[(1, cc)]
                    op("act", ACT(zo[:, cc, :], PS[bo_][:, 0:PB], AF.Copy), R=[psk(bo_)], W=[zob.k(cc)])
                    p0 = base + ph * PB
                    op("dve", TT(catT[:, 2 + cc, p0:p0 + PB], PS[be][:, 0:PB], zo[:, cc, :], ALU.add), R=[psk(be), zob.k(cc)],
                       W=[catb.k((2 + cc, t_)) for t_ in range(p0 // 128, (p0 + PB) // 128)])
                    op("dve", TT(catT[:, 2 + cc, p0 + H:p0 + H + PB], PS[be][:, 0:PB], zo[:, cc, :], ALU.subtract), R=[psk(be), zob.k(cc)],
                       W=[catb.k((2 + cc, t_)) for t_ in range((p0 + H) // 128, (p0 + H + PB) // 128)])
            P.free(ftb, mtb[0], mtb[1], uvb[0], uvb[1], zob)
        P.free(wCb)


    def hyena_all(l, hT, KH, catT, catb, hTb):
        import os as _os
        TWO_PI = 2.0 * math.pi
        P.phase = "hy_proj"
        hpb = P.alloc("hy_par", 18 + 6 + 4 + 4 + 64 + 64 + 1024 + 8)
        hp = P.f32(hpb)
        KP = hpb.k()
        shw = hp[:, 0:18].rearrange("p (m k) -> p m k", m=6)
        shb = hp[:, 18:24]
        dsk = hp[:, 24:28].rearrange("p (o c) -> p o c", o=2)
        bcol = hp[:, 28:32]
        w1s = hp[0:33, 32:96]
        w2s = hp[0:64, 96:160]
        w3s = hp[0:64, 160:1184]
        for k in range(3):
            op("sp", DMAS(shw[:, :, k:k + 1], bass.AP(T["hy_short_w"].tensor, (l * 3 + k) * 768, [[1, 128], [128, 6], [1, 1]])), W=[KP], dma=True)
        op("sp", DMAS(shb.rearrange("p (m o) -> p m o", o=1), bass.AP(T["hy_short_b"].tensor, l * 768, [[1, 128], [128, 6], [1, 1]])), W=[KP], dma=True)
        op("sp", DMAS(hp[:, 24:28].rearrange("p (q o) -> p q o", o=1), bass.AP(T["hy_bias"].tensor, l * 512, [[1, 128], [128, 4], [1, 1]])), W=[KP], dma=True)
        op("sp", DMAS(bcol[0:64, 0:1], bass.AP(T["hy_b1"].tensor, l * 64, [[1, 64], [1, 1]])), W=[KP], dma=True)
        op("sp", DMAS(bcol[0:64, 1:2], bass.AP(T["hy_b2"].tensor, l * 64, [[1, 64], [1, 1]])), W=[KP], dma=True)
        op("sp", DMA(w1s, T["hy_w1"][l]), W=[KP], dma=True)
        op("sp", DMA(w2s, T["hy_w2"][l]), W=[KP], dma=True)
        op("sp", DMA(w3s, T["hy_w3"][l]), W=[KP], dma=True)
        bx = hp[:, 1184:1192]
        op("dve", TS(bx[0:64, 0:2], bcol[0:64, 0:2], 0.5, ALU.mult), R=[KP], W=[KP])
        op("dve", TS(bx[0:64, 2:4], bcol[0:64, 0:2], 0.25, ALU.mult), R=[KP], W=[KP])
        wDb = P.alloc("wD", 8 * 768 // 2)
        wD = P.bf(wDb).rearrange("p (k n) -> p k n", k=8)
        op("pool", DMA(wD, T["w_in"][l][:, 1696:2464].rearrange("(k p) n -> p k n", p=128)), W=[wDb.k()], dma=True)
        ucb = P.alloc("ucT", 6 * 2560 // 2)
        ucT = P.bf(ucb).rearrange("p (m n) -> p m n", m=6)
        upb = [P.alloc(f"upad{i}", 2050) for i in range(2)]
        ctb = P.alloc("convtmp", 2048)
        ui = 0
        for (t0, nts, latent) in SEQS:
            L = nts * 128
            base = t0 * 128
            for m in range(6):
                ub = upb[ui % 2]
                up = P.f32(ub)
                ku = ub.k()
                ui += 1
                op("pool", MSET(up[:, 0:1], 0.0), W=[ku])
                op("pool", MSET(up[:, L + 1:L + 2], 0.0), W=[ku])
                for c0 in range(0, L, 512):
                    cw = min(512, L - c0)
                    b = P.bank()
                    op("pe", MM([(PS[b][:, 0:cw], wD[:, k, m * 128:(m + 1) * 128], hT[:, k, base + c0:base + c0 + cw], k == 0, k == 7)
                                 for k in range(8)]), R=[KH.k(t_) for t_ in range((base + c0) // 128, (base + c0 + cw) // 128)] + [wDb.k()], W=[psk(b)])
                    op("act", ACT(up[:, 1 + c0:1 + c0 + cw], PS[b][:, 0:cw], AF.Copy), R=[psk(b)], W=[ku])
                ct = P.f32(ctb)[:, 0:L]
                eng = "dve"
                op(eng, TS(ct, up[:, 0:L], shw[:, m, 0:1], ALU.mult, shb[:, m:m + 1], ALU.add), R=[ku, KP], W=[ctb.k()])
                op(eng, STT(ct, up[:, 1:L + 1], shw[:, m, 1:2], ct, ALU.mult, ALU.add), R=[ku, KP, ctb.k()], W=[ctb.k()])
                op(eng, STT(ucT[:, m, base:base + L], up[:, 2:L + 2], shw[:, m, 2:3], ct, ALU.mult, ALU.add), R=[ku, KP, ctb.k()],
                   W=[ucb.k((m, t0))])
        P.free(hTb, wDb, upb[0], upb[1], ctb)
        dump(f"ucT{l}", ucT, [ucb.k((m, t0)) for m in range(6) for (t0, _, _) in SEQS])

        P.phase = "hy_filt"
        for L in (2048, 256):
            seqs = [sq for sq in SEQS if sq[1] * 128 == L]
            nj = L // 256
            ntt = 2 * nj
            NF = L // 256
            HB = L // 2
            TB = min(512, HB)
            zb = P.alloc("hy_z", L)
            h1b = P.alloc("hy_h1", L)
            zT = P.f32(zb)[0:33, :]
            h1T = P.f32(h1b)[0:64, :]
            op("sp", DMA(zT, T[f"h_zT{L}"]), W=[zb.k()], dma=True)
            h2b = P.alloc("hy_h2", L)
            h2T = P.f32(h2b)[0:64, :]
            sinb = P.alloc("hy_sint", 1024)
            for (src_, dst_, w_, kk_, bi_, kr_, kw_) in ((zT, h1T, w1s, 33, 0, zb.k(), h1b.k()), (h1T, h2T, w2s, 64, 1, h1b.k(), h2b.k())):
                for c0 in range(0, L, 512):
                    cw = min(512, L - c0)
                    b = P.bank()
                    op("pe", MM([(PS[b][0:64, 0:cw], w_, src_[:, c0:c0 + cw], True, True)]), R=[KP, kr_], W=[psk(b)])
                    s2 = P.f32(sinb)[0:64, 0:cw]
                    s4 = P.f32(sinb)[0:64, 512:512 + cw]
                    op("act", ACT(s2, PS[b][0:64, 0:cw], AF.Sin, bias=bx[0:64, bi_:bi_ + 1], scale=0.5), R=[psk(b), KP], W=[sinb.k(0)])
                    op("act", ACT(s4, PS[b][0:64, 0:cw], AF.Sin, bias=bx[0:64, 2 + bi_:3 + bi_], scale=0.25), R=[psk(b), KP], W=[sinb.k(1)])
                    op("dve", TT(s4, s4, s4, ALU.mult), R=[sinb.k(1)], W=[sinb.k(1)])
                    op("dve", TS(s4, s4, -2.0, ALU.mult, 1.0, ALU.add), R=[sinb.k(1)], W=[sinb.k(1)])
                    op("dve", STT(dst_[:, c0:c0 + cw], s2, 2.0, s4, ALU.mult, ALU.mult), R=[sinb.k(0), sinb.k(1)], W=[kw_])
            P.free(zb, h1b, sinb)
            for o in range(2):
                winb = P.alloc("hy_win", ntt * 256)
                win = P.f32(winb).rearrange("p (q n) -> p q n", q=ntt)
                op("sp", DMA(P.f32(winb), T[f"h_win{L}"]), W=[winb.k()], dma=True)
                P.phase = "hy_filt"
                abb = P.alloc("hy_ab", 2 * ntt * 256 // 2)
                ab = P.bf(abb).rearrange("p (s q n) -> p s q n", s=2, q=ntt)
                accb = P.alloc("hy_acc", 256)
                acc = P.f32(accb)
                op("pool", MSET(acc, 0.0), W=[accb.k()])
                g2b = [P.alloc(f"hy_g2{i}", 512) for i in range(2)]
                abs_b = [P.alloc(f"hy_abs{i}", 512) for i in range(2)]
                for q in range(ntt):
                    b = P.bank()
                    op("pe", MM([(PS[b][:, :], h2T[:, q * 128:(q + 1) * 128], w3s[:, o * 512:(o + 1) * 512], True, True)]),
                       R=[h2b.k(), KP], W=[psk(b)])
                    g2 = P.f32(g2b[q % 2]).rearrange("p (d n) -> p d n", d=2)
                    kg = g2b[q % 2].k()
                    op("dve", TT(g2, PS[b][:, :].rearrange("p (d n) -> p d n", d=2), bc(win[:, q, :], [[0, 2], [1, 256]]), ALU.mult),
                       R=[psk(b), winb.k()], W=[kg])
                    if q == 0:
                        op("pool", MSET(g2[0:1, 1, :], 0.0), R=[kg], W=[kg])
                    op("pool", TT(ab[:, 0, q, :], g2[:, 0, :], g2[:, 1, :], ALU.add), R=[kg], W=[abb.k((0, q))])
                    op("pool", TT(ab[:, 1, q, :], g2[:, 0, :], g2[:, 1, :], ALU.subtract), R=[kg], W=[abb.k((1, q))])
                    av = P.f32(abs_b[q % 2])
                    op("act", ACT(av, P.f32(g2b[q % 2]), AF.Abs), R=[kg], W=[abs_b[q % 2].k()])
                    op("dve", TT(acc, acc, av[:, 0:256], ALU.add), R=[abs_b[q % 2].k(), accb.k()], W=[accb.k()])
                    op("dve", TT(acc, acc, av[:, 256:512], ALU.add), R=[abs_b[q % 2].k(), accb.k()], W=[accb.k()])
                P.free(g2b[0], g2b[1], abs_b[0], abs_b[1], winb)
                rnb = P.alloc("hy_rn", 256 + 256)
                rnrow = P.f32(rnb)[0:1, 0:256]
                RN = P.f32(rnb)[:, 256:512]
                b = P.bank()
                op("pe", MM([(PS[b][0:1, 0:256], onesf[:, 0:1], acc, True, True)]), R=[accb.k(), KC], W=[psk(b)])
                op("dve", TS(rnrow, PS[b][0:1, 0:256], EPS, ALU.add), R=[psk(b)], W=[rnb.k("row")])
                op("dve", RECIP(rnrow, rnrow), R=[rnb.k("row")], W=[rnb.k("row")])
                b = P.bank()
                op("pe", MM([(PS[b][:, 0:256], onesf[0:1, 0:128], rnrow, True, True)]), R=[rnb.k("row"), KC], W=[psk(b)])
                op("act", ACT(RN, PS[b][:, 0:256], AF.Copy), R=[psk(b)], W=[rnb.k()])
                P.free(accb)
                hyt[0] = P.alloc("hy_ftmp", 1024)
                Gb = P.alloc("hy_G", NF * 4 * 256 // 2)
                G = P.bf(Gb).rearrange("p (f q n) -> p f q n", f=NF, q=4)
                ftb_ = [P.alloc(f"hy_ft{i}", 2 * ntt * 128 // 2) for i in range(2)]
                osb = P.alloc("hy_osb", 512)
                for fc in range(NF):
                    tb_ = ftb_[fc % 2]
                    tf = P.bf(tb_).rearrange("p (s q n) -> p s q n", s=2, q=ntt)
                    op("sp", DMA(tf[:, 0], T[f"h_cf{L}"][fc].rearrange("p (q n) -> p q n", q=ntt)), W=[tb_.k()], dma=True)
                    op("sp", DMA(tf[:, 1], T[f"h_sf{L}"][fc].rearrange("p (q n) -> p q n", q=ntt)), W=[tb_.k()], dma=True)
                    bE, bO = P.bank(), P.bank()
                    for (bk_, a_) in ((bE, 0), (bO, 1)):
                        mm = []
                        for j in range(nj):
                            q = a_ * nj + j
                            mm.append((PS[bk_][:, 0:256], tf[:, 0, q, :], ab[:, 0, q, :], j == 0, False))
                            mm.append((PS[bk_][:, 256:512], tf[:, 1, q, :], ab[:, 1, q, :], False, j == nj - 1))
                        op("pe", MM(mm), R=[tb_.k()] + [abb.k((s_, a_ * nj + j)) for s_ in range(2) for j in range(nj)], W=[psk(bk_)])
                    osv = P.f32(osb)
                    op("act", ACT(osv, PS[bO][:, :], AF.Copy), R=[psk(bO)], W=[osb.k()])
                    RN2 = bc(RN, [[0, 2], [1, 256]])
                    tsum = P.f32(osb)
                    tmpb_ = hyt[0]
                    tsv = P.f32(tmpb_)[:, 0:512]
                    tdv = P.f32(tmpb_)[:, 512:1024]
                    op("dve", TT(tsv, PS[bE][:, :], osv, ALU.add), R=[psk(bE), osb.k()], W=[tmpb_.k(0)])
                    op("dve", TT(tdv, PS[bE][:, :], osv, ALU.subtract), R=[psk(bE), osb.k()], W=[tmpb_.k(1)])
                    op("pool", TT(G[:, fc, 0:2, :], tsv.rearrange("p (q n) -> p q n", q=2), RN2, ALU.mult), R=[tmpb_.k(0), rnb.k()], W=[Gb.k(fc)])
                    op("pool", TT(G[:, fc, 2:4, :], tdv.rearrange("p (q n) -> p q n", q=2), RN2, ALU.mult), R=[tmpb_.k(1), rnb.k()], W=[Gb.k(fc)])
                P.free(abb, rnb, ftb_[0], ftb_[1], osb, hyt[0])
                dump(f"G{l}_{L}_{o}", G, [Gb.k(fc) for fc in range(NF)])
                for (t0, nts, latent) in seqs:
                    hyena_conv(l, o, L, t0, G, Gb, ucT, ucb, catT, catb, dsk, KP)
                P.free(Gb)
            P.free(h2b)
        P.free(hpb, ucb)
        for k_ in list(Z1.keys()):
            P.free(Z1.pop(k_)[0])

    hyt = [None]
    Z1 = {}

    def hyena_conv(l, o, L, t0, G, Gb, ucT, ucb, catT, catb, dsk, KP):
        P.phase = "hy_conv"
        nj = L // 256
        ntt = 2 * nj
        NF = L // 256
        HB = L // 2
        TB = min(512, HB)
        base = t0 * 128
        if o == 0:
            zb_ = P.alloc("hy_z1T", 2 * L // 2)
            Z1[t0] = (zb_, P.bf(zb_).rearrange("p (c n) -> p c n", c=2))
            vin = ucT[:, 0:2, base:base + L]
            kvin = [ucb.k((m, t0)) for m in (0, 1)]
            gate = ucT[:, 2:4, base:base + L]
            kgate = [ucb.k((m, t0)) for m in (2, 3)]
            outv = Z1[t0][1]
            kout = [Z1[t0][0].k(0), Z1[t0][0].k(1)]
        else:
            vin = Z1[t0][1]
            kvin = [Z1[t0][0].k(0), Z1[t0][0].k(1)]
            gate = ucT[:, 4:6, base:base + L]
            kgate = [ucb.k((m, t0)) for m in (4, 5)]
            outv = catT[:, 6:8, base:base + L]
            kout = [None, None]
        vtb = P.alloc("hy_vtok", ntt * 256 // 2)
        vt = P.bf(vtb).rearrange("p (q n) -> p q n", q=ntt)
        for q in range(0, ntt, 2):
            b = P.bank()
            mm = []
            for qq in (q, q + 1):
                a_, j = qq // nj, qq % nj
                for cc in range(2):
                    mm.append((PS[b][:, (qq - q) * 256 + cc * 128:(qq - q) * 256 + (cc + 1) * 128],
                               vin[:, cc, a_ + 256 * j:a_ + 256 * j + 255:2], ident, True, True))
            op("pe", MM(mm), R=kvin + [KC], W=[psk(b)])
            ev = "act" if (q // 2) % 2 == 0 else "dve"
            cpy = (lambda o_, i_: ACT(o_, i_, AF.Copy)) if ev == "act" else CP
            op(ev, cpy(vt[:, q:q + 2, :], PS[b][:, :].rearrange("p (q n) -> p q n", q=2)), R=[psk(b)], W=[vtb.k(q // 2)])
        P.phase = "hy_fwd"
        pqb = P.alloc("hy_PQ", NF * 4 * 256 // 2)
        PQ = P.bf(pqb).rearrange("p (f q n) -> p f q n", f=NF, q=4)
        ftb_ = [P.alloc(f"hy_ft{i}", 2 * ntt * 128 // 2) for i in range(2)]
        osb = P.alloc("hy_osb", 512)
        tmb = P.alloc("hy_pw", 512 * 8)
        tm = P.f32(tmb)
        SSv, DDv, T1s, T2s, Av, Bv, T1d, T2d = (tm[:, i * 512:(i + 1) * 512] for i in range(8))
        allvt = [vtb.k(i) for i in range(nj)]
        for fc in range(NF):
            tb_ = ftb_[fc % 2]
            tf = P.bf(tb_).rearrange("p (s q n) -> p s q n", s=2, q=ntt)
            op("sp", DMA(tf[:, 0], T[f"h_cf{L}"][fc].rearrange("p (q n) -> p q n", q=ntt)), W=[tb_.k()], dma=True)
            op("sp", DMA(tf[:, 1], T[f"h_sf{L}"][fc].rearrange("p (q n) -> p q n", q=ntt)), W=[tb_.k()], dma=True)
            bE, bO = P.bank(), P.bank()
            for (bk_, a_) in ((bE, 0), (bO, 1)):
                mm = []
                for j in range(nj):
                    q = a_ * nj + j
                    mm.append((PS[bk_][:, 0:256], tf[:, 0, q, :], vt[:, q, :], j == 0, False))
                    mm.append((PS[bk_][:, 256:512], tf[:, 1, q, :], vt[:, q, :], False, j == nj - 1))
                op("pe", MM(mm), R=[tb_.k()] + allvt, W=[psk(bk_)])
            osv = P.f32(osb)
            op("act", ACT(osv, PS[bO][:, :], AF.Copy), R=[psk(bO)], W=[osb.k()])
            op("dve", TT(SSv, PS[bE][:, :], osv, ALU.add), R=[psk(bE), osb.k()], W=[tmb.k("S")])
            op("dve", TT(DDv, PS[bE][:, :], osv, ALU.subtract), R=[psk(bE), osb.k()], W=[tmb.k("D")])
            e1, e2 = ("dve", "pool") if fc % 2 == 0 else ("pool", "dve")
            for (X, gr, gi, dst, kx, e_, T1, T2) in ((SSv, 0, 1, Av, "S", e1, T1s, T2s), (DDv, 2, 3, Bv, "D", e2, T1d, T2d)):
                X2 = X.rearrange("p (q n) -> p q n", q=2)
                op(e_, TT(T1.rearrange("p (q n) -> p q n", q=2), X2, bc(G[:, fc, gr, :], [[0, 2], [1, 256]]), ALU.mult),
                   R=[tmb.k(kx), Gb.k(fc)], W=[tmb.k("T1" + kx)])
                op(e_, TT(T2.rearrange("p (q n) -> p q n", q=2), X2, bc(G[:, fc, gi, :], [[0, 2], [1, 256]]), ALU.mult),
                   R=[tmb.k(kx), Gb.k(fc)], W=[tmb.k("T2" + kx)])
                op(e_, TT(dst[:, 0:256], T1[:, 0:256], T2[:, 256:512], ALU.subtract), R=[tmb.k("T1" + kx), tmb.k("T2" + kx)], W=[tmb.k("A" + kx)])
                op(e_, TT(dst[:, 256:512], T2[:, 0:256], T1[:, 256:512], ALU.add), R=[tmb.k("T1" + kx), tmb.k("T2" + kx)], W=[tmb.k("A" + kx)])
            op("dve", TT(PQ[:, fc, 0:2, :], Av.rearrange("p (q n) -> p q n", q=2), Bv.rearrange("p (q n) -> p q n", q=2), ALU.add),
               R=[tmb.k("AS"), tmb.k("AD")], W=[pqb.k(fc)])
            op("pool", TT(PQ[:, fc, 2:4, :], Av.rearrange("p (q n) -> p q n", q=2), Bv.rearrange("p (q n) -> p q n", q=2), ALU.subtract),
               R=[tmb.k("AS"), tmb.k("AD")], W=[pqb.k(fc)])
        P.free(vtb, ftb_[0], ftb_[1], osb, tmb)
        P.phase = "hy_inv"
        itb = [P.alloc(f"hy_it{i}", 2 * NF * TB // 2) for i in range(2)]
        ytb = [P.alloc(f"hy_yt{i}", TB) for i in range(2)]
        CI4 = T[f"h_ci{L}"].rearrange("p (f a n) -> p f a n", f=NF, a=2)
        SI4 = T[f"h_si{L}"].rearrange("p (f a n) -> p f a n", f=NF, a=2)
        ii = 0
        allpq = [pqb.k(fc) for fc in range(NF)]
        for a_ in range(2):
            for tch in range(HB // TB):
                ib = itb[ii % 2]
                it = P.bf(ib).rearrange("p (s f n) -> p s f n", s=2, f=NF)
                op("sp", DMA(it[:, 0], CI4[:, :, a_, tch * TB:(tch + 1) * TB]), W=[ib.k()], dma=True)
                op("sp", DMA(it[:, 1], SI4[:, :, a_, tch * TB:(tch + 1) * TB]), W=[ib.k()], dma=True)
                ii += 1
                for cc in range(2):
                    b = P.bank()
                    mm = []
                    for fc in range(NF):
                        mm.append((PS[b][:, 0:TB], PQ[:, fc, 2 * a_, cc * 128:(cc + 1) * 128], it[:, 0, fc, :], fc == 0, False))
                        mm.append((PS[b][:, 0:TB], PQ[:, fc, 2 * a_ + 1, cc * 128:(cc + 1) * 128], it[:, 1, fc, :], False, fc == NF - 1))
                    op("pe", MM(mm), R=allpq + [ib.k()], W=[psk(b)])
                    tstart = a_ + 2 * tch * TB
                    sl = slice(tstart, tstart + 2 * TB - 1, 2)
                    yt = P.f32(ytb[cc])[:, 0:TB]
                    op("dve", STT(yt, vin[:, cc, sl], dsk[:, o, cc:cc + 1], PS[b][:, 0:TB], ALU.mult, ALU.add),
                       R=[psk(b), kvin[cc], KP], W=[ytb[cc].k()])
                    if o == 0:
                        wk_ = [kout[cc]]
                    else:
                        wk_ = [catb.k((6 + cc, t_)) for t_ in range((base + tstart) // 128, (base + tstart + 2 * TB - 1) // 128 + 1)]
                    op("pool", TT(outv[:, cc, sl], yt, gate[:, cc, sl], ALU.mult), R=[ytb[cc].k(), kgate[cc]], W=wk_)
        P.free(pqb, itb[0], itb[1], ytb[0], ytb[1])


    return dict(nc=nc, P=P, T=T, DBG=DBG, layer_mod=layer_mod, ffn=ffn, mixer=mixer, loc=locals())


def build_full(dbg=()):
    B = build(dbg=dbg)
    T = B["T"]
    for l in range(2):
        B["P"].layer = l
        B["layer_mod"](l)
        B["mixer"](l, T["xin"] if l == 0 else T["xb"], T["xa"])
        B["ffn"](l, T["xa"], T["xb"] if l == 0 else T["y"])
    B["P"].emit()
    return B


_NC_CACHE = {}


def core_inputs(core, inp, consts):
    m = {nm: np.ascontiguousarray(inp[nm], dtype=np.float32) for nm, _ in WEIGHT_SPECS}
    m.update(consts)
    m["xin"] = np.ascontiguousarray(np.concatenate(
        [inp["x_sample"][core], inp["x_prompt"][2 * core], inp["x_prompt"][2 * core + 1]], 0), dtype=np.float32)
    m["ckv_ctx"] = np.ascontiguousarray(inp["cache_ckv"][core], dtype=np.float32)
    m["kr_ctx"] = np.ascontiguousarray(inp["cache_krope"][core], dtype=np.float32)
    m["s0"] = np.ascontiguousarray(inp["state_ret"][core], dtype=np.float32)
    m["cvec"] = np.ascontiguousarray(np.stack([inp["c_ctx"], inp["c"][core]]), dtype=np.float32)
    return m


def kernel(**inputs):
    inp = {k: np.asarray(v) for k, v in inputs.items()}
    consts = get_consts()
    if "nc" not in _NC_CACHE:
        _NC_CACHE["nc"] = build_full()["nc"]
    nc = _NC_CACHE["nc"]
    in_maps = [core_inputs(c, inp, consts) for c in range(8)]
    res = run_bass_kernel_spmd(nc, in_maps, core_ids=list(range(8)))
    R = res.results
    y_prompt = np.zeros((16, 256, 1024), np.float32)
    y_sample = np.zeros((8, 2048, 1024), np.float32)
    new_ckv = np.zeros((16, 2, 256, 128), np.float32)
    new_kr = np.zeros((16, 2, 256, 32), np.float32)
    new_st = np.zeros((16, 2, 2, 4, 64, 64), np.float32)
    for c in range(8):
        y = R[c]["y"]
        y_sample[c] = y[0:2048]
        y_prompt[2 * c] = y[2048:2304]
        y_prompt[2 * c + 1] = y[2304:2560]
        new_ckv[2 * c:2 * c + 2] = R[c]["o_ckv"]
        new_kr[2 * c:2 * c + 2] = R[c]["o_kr"]
        new_st[2 * c:2 * c + 2] = R[c]["o_st"]
    return (y_prompt, y_sample, new_ckv, new_kr, new_st)
```
